# Optimizing a Trainium2 kernel written in Bass

```python
import math
import jax, jax.numpy as jnp
from jax import lax
import numpy as np

D_MODEL = 2048
BATCH = 2
SEQ = 8192
DEPTH = 1

DA_HEADS = 8
DA_HEAD_DIM = 64
DA_V_DIM = 2 * DA_HEAD_DIM
NSA_HEADS = 16
NSA_KV_GROUPS = 4
NSA_HEADS_PER_GROUP = NSA_HEADS // NSA_KV_GROUPS
NSA_HEAD_DIM = 64
CMP_BLOCK = 32
CMP_STRIDE = 16
CMP_HIDDEN = 256
SLC_BLOCK = 64
SLC_TOPK = 16
N_LOCAL_BLOCKS = 2
WINDOW = 512
Q_BLOCK = 128
REL_BUCKETS = 32
REL_MAX_DIST = 128
REL_HEADS = DA_HEADS + NSA_HEADS
PEER_HEADS = 8
PEER_NKEYS = 128
PEER_EXPERTS = PEER_NKEYS * PEER_NKEYS
PEER_KEY_DIM = 128
PEER_TOPK = 16
PEER_CHUNK = 128
DN_ALPHA = (2 * DEPTH) ** 0.25
DN_BETA = (8 * DEPTH) ** -0.25
LN_EPS = 1e-5
NEG = -1e30
FORCE_SCORE = 1e4

DA_QK_W = DA_HEADS * 2 * DA_HEAD_DIM
DA_V_W = DA_HEADS * DA_V_DIM
NSA_Q_W = NSA_HEADS * NSA_HEAD_DIM
NSA_KV_W = NSA_KV_GROUPS * NSA_HEAD_DIM
NSA_GATE_W = 3 * NSA_HEADS
IN_SIZES = (DA_QK_W, DA_QK_W, DA_V_W, NSA_Q_W, NSA_KV_W, NSA_KV_W, NSA_KV_W, NSA_KV_W, NSA_KV_W, NSA_KV_W, NSA_GATE_W, D_MODEL, D_MODEL)
IN_IS_VALUE = (False, False, True, False, False, True, False, True, False, True, False, False, False)
IN_TOTAL = sum(IN_SIZES)

kernel_name = 'hybrid_diffattn_nsa_peer_deepnorm'


def layer_norm(x, g, b):
    xf = x.astype(jnp.float32)
    mu = jnp.mean(xf, -1, keepdims=True)
    var = jnp.mean(jnp.square(xf - mu), -1, keepdims=True)
    return ((xf - mu) * lax.rsqrt(var + LN_EPS)).astype(x.dtype) * g + b


def rms_norm(x, g):
    xf = x.astype(jnp.float32)
    return (xf * lax.rsqrt(jnp.mean(xf * xf, -1, keepdims=True) + LN_EPS)).astype(x.dtype) * g


def masked_softmax(logits, mask):
    logits = jnp.where(mask, logits.astype(jnp.float32), NEG)
    m = jnp.max(logits, -1, keepdims=True)
    p = jnp.where(mask, jnp.exp(logits - m), 0.0)
    return p / jnp.maximum(jnp.sum(p, -1, keepdims=True), 1e-30)


def rel_bucket(dist):
    n = jnp.maximum(dist, 0)
    max_exact = REL_BUCKETS // 2
    nf = jnp.maximum(n, 1).astype(jnp.float32)
    large = max_exact + (jnp.log(nf / max_exact) / math.log(REL_MAX_DIST / max_exact)
                         * (REL_BUCKETS - max_exact)).astype(jnp.int32)
    large = jnp.minimum(large, REL_BUCKETS - 1)
    return jnp.where(n < max_exact, n, large)


def split_columns(proj):
    outs, start = [], 0
    for w in IN_SIZES:
        outs.append(proj[..., start:start + w])
        start += w
    return outs


def diff_attention(q, k, v, lam, subln_g, lam_init, rel_table):
    B, S = q.shape[:2]
    nb = S // Q_BLOCK
    scale = DA_HEAD_DIM ** -0.5
    qb = q.reshape(B, nb, Q_BLOCK, DA_HEADS, 2, DA_HEAD_DIM).swapaxes(0, 1)
    k_pos = jnp.arange(S)
    table = rel_table.astype(jnp.float32)

    def block(args):
        qi, i = args
        q_pos = i * Q_BLOCK + jnp.arange(Q_BLOCK)
        dist = q_pos[:, None] - k_pos[None, :]
        bias = table[rel_bucket(dist)].transpose(2, 0, 1)
        logits = jnp.einsum('bqhmd,bkhmd->bhmqk', qi, k).astype(jnp.float32) * scale + bias[None, :, None]
        p = masked_softmax(logits, dist >= 0)
        w = p[:, :, 0] - lam * p[:, :, 1]
        return jnp.einsum('bhqk,bkhe->bqhe', w.astype(v.dtype), v)

    o = lax.map(block, (qb, jnp.arange(nb)))
    o = o.swapaxes(0, 1).reshape(B, S, DA_HEADS, DA_V_DIM)
    o = rms_norm(o, subln_g) * (1.0 - lam_init)
    return o.reshape(B, S, DA_V_W)


def compress_blocks(t, pe, w1, w2):
    B, S, G, d = t.shape
    r = CMP_BLOCK // CMP_STRIDE
    nc = S // CMP_STRIDE - r + 1
    ch = t.reshape(B, S // CMP_STRIDE, CMP_STRIDE, G, d)
    blocks = jnp.concatenate([ch[:, j:j + nc] for j in range(r)], axis=2) + pe[:, None, :]
    flat = blocks.transpose(0, 1, 3, 2, 4).reshape(B, nc, G, CMP_BLOCK * d)
    return jax.nn.gelu(flat @ w1) @ w2


def nsa_attention(q, kc, vc, ks, vs, kw, vw, gates, rel_table):
    B, S = q.shape[:2]
    G, HPG, d = NSA_KV_GROUPS, NSA_HEADS_PER_GROUP, NSA_HEAD_DIM
    nb = S // Q_BLOCK
    nc = kc.shape[1]
    ns = S // SLC_BLOCK
    k_sel = min(SLC_TOPK, ns)
    span = WINDOW + Q_BLOCK
    scale = d ** -0.5
    table = rel_table.astype(jnp.float32)
    table_g = table.reshape(REL_BUCKETS, G, HPG).transpose(1, 0, 2)
    cmp_start = jnp.arange(nc) * CMP_STRIDE
    cmp_end = cmp_start + CMP_BLOCK - 1
    slc_start = jnp.arange(ns) * SLC_BLOCK
    overlap = ((cmp_start[:, None] <= slc_start[None, :] + SLC_BLOCK - 1)
               & (cmp_end[:, None] >= slc_start[None, :])).astype(jnp.float32)
    ks_blk = ks.reshape(B, ns, SLC_BLOCK, G, d).transpose(0, 3, 1, 2, 4)
    vs_blk = vs.reshape(B, ns, SLC_BLOCK, G, d).transpose(0, 3, 1, 2, 4)
    kw_pad = jnp.pad(kw, ((0, 0), (WINDOW, 0), (0, 0), (0, 0)))
    vw_pad = jnp.pad(vw, ((0, 0), (WINDOW, 0), (0, 0), (0, 0)))
    gather = jax.vmap(jax.vmap(lambda tab, ix: tab[ix]))
    blk = jnp.arange(ns)
    g_idx = jnp.arange(G)[None, :, None, None, None]
    qb = q.reshape(B, nb, Q_BLOCK, NSA_HEADS, d).swapaxes(0, 1)
    gb = gates.reshape(B, nb, Q_BLOCK, NSA_HEADS, 3).swapaxes(0, 1)

    def block(args):
        qi, gi, i = args
        q0 = i * Q_BLOCK
        q_pos = q0 + jnp.arange(Q_BLOCK)
        qg = qi.reshape(B, Q_BLOCK, G, HPG, d)
        lg = jnp.einsum('bqghd,bcgd->bghqc', qg, kc).astype(jnp.float32) * scale
        p_cmp = masked_softmax(lg, cmp_end[None, :] <= q_pos[:, None])
        o_cmp = jnp.einsum('bghqc,bcgd->bqghd', p_cmp.astype(vc.dtype), vc)
        imp = jnp.einsum('bghqc,cs->bgqs', p_cmp, overlap)
        cur = q_pos // SLC_BLOCK
        valid = blk[None, :] <= cur[:, None]
        forced = valid & ((blk[None, :] == 0) | (blk[None, :] > cur[:, None] - N_LOCAL_BLOCKS))
        score = jnp.where(forced, FORCE_SCORE, jnp.where(valid, imp, NEG))
        _, idx = lax.top_k(score, k_sel)
        kg = gather(ks_blk, idx)
        vg = gather(vs_blk, idx)
        tok = idx[..., None] * SLC_BLOCK + jnp.arange(SLC_BLOCK)
        dist = q_pos[:, None, None] - tok
        smask = (dist >= 0).reshape(B, G, 1, Q_BLOCK, k_sel * SLC_BLOCK)
        sbias = table_g[g_idx, rel_bucket(dist)]
        sbias = jnp.moveaxis(sbias, -1, 2).reshape(B, G, HPG, Q_BLOCK, k_sel * SLC_BLOCK)
        lg = jnp.einsum('bqghd,bgqksd->bghqks', qg, kg).astype(jnp.float32)
        lg = lg.reshape(B, G, HPG, Q_BLOCK, k_sel * SLC_BLOCK) * scale + sbias
        p = masked_softmax(lg, smask)
        o_slc = jnp.einsum('bghqn,bgqnd->bqghd', p.astype(vg.dtype),
                           vg.reshape(B, G, Q_BLOCK, k_sel * SLC_BLOCK, d))
        kwi = lax.dynamic_slice_in_dim(kw_pad, q0, span, axis=1)
        vwi = lax.dynamic_slice_in_dim(vw_pad, q0, span, axis=1)
        k_pos = q0 - WINDOW + jnp.arange(span)
        wdist = q_pos[:, None] - k_pos[None, :]
        wmask = (wdist >= 0) & (wdist < WINDOW) & (k_pos[None, :] >= 0)
        wbias = table[rel_bucket(wdist)].reshape(Q_BLOCK, span, G, HPG).transpose(2, 3, 0, 1)
        lg = jnp.einsum('bqghd,blgd->bghql', qg, kwi).astype(jnp.float32) * scale + wbias
        p = masked_softmax(lg, wmask)
        o_win = jnp.einsum('bghql,blgd->bqghd', p.astype(vwi.dtype), vwi)
        g = jax.nn.sigmoid(gi).reshape(B, Q_BLOCK, G, HPG, 3)
        o = g[..., 0:1] * o_cmp + g[..., 1:2] * o_slc + g[..., 2:3] * o_win
        return o.reshape(B, Q_BLOCK, NSA_Q_W)

    o = lax.map(block, (qb, gb, jnp.arange(nb)))
    return o.swapaxes(0, 1).reshape(B, S, NSA_Q_W)


def peer_ffn(y, w_q, sub_k1, sub_k2, u_tab, v_tab):
    B, S, D = y.shape
    T = B * S
    n_chunks = T // PEER_CHUNK
    half = PEER_KEY_DIM // 2

    def chunk(yc):
        q = (yc @ w_q).reshape(PEER_CHUNK, PEER_HEADS, 2, half)
        s1 = jnp.einsum('chd,nd->chn', q[:, :, 0], sub_k1).astype(jnp.float32)
        s2 = jnp.einsum('chd,nd->chn', q[:, :, 1], sub_k2).astype(jnp.float32)
        v1, i1 = lax.top_k(s1, PEER_TOPK)
        v2, i2 = lax.top_k(s2, PEER_TOPK)
        cand = (v1[..., :, None] + v2[..., None, :]).reshape(PEER_CHUNK, PEER_HEADS, PEER_TOPK * PEER_TOPK)
        cid = (i1[..., :, None] * PEER_NKEYS + i2[..., None, :]).reshape(PEER_CHUNK, PEER_HEADS, PEER_TOPK * PEER_TOPK)
        sc, j = lax.top_k(cand, PEER_TOPK)
        eid = jnp.take_along_axis(cid, j, axis=-1)
        gate = jax.nn.softmax(sc, axis=-1)
        act = jax.nn.gelu(jnp.einsum('chkd,cd->chk', u_tab[eid], yc))
        return jnp.einsum('chk,chkd->cd', (gate * act).astype(yc.dtype), v_tab[eid])

    out = lax.map(chunk, y.reshape(n_chunks, PEER_CHUNK, D))
    return out.reshape(B, S, D)


def setup_inputs(seed: int = 0) -> dict:
    key = jax.random.key(seed)
    ks = jax.random.split(key, 26)
    f32 = jnp.float32

    def nrm(k, shape, s):
        return jax.random.normal(k, shape, f32) * s

    d = NSA_HEAD_DIM
    col_scale = jnp.concatenate([jnp.full((w,), DN_BETA if is_v else 1.0, f32)
                                 for w, is_v in zip(IN_SIZES, IN_IS_VALUE)])
    return {
        'x': nrm(ks[0], (BATCH, SEQ, D_MODEL), 1.0),
        'w_in': nrm(ks[1], (DEPTH, D_MODEL, IN_TOTAL), D_MODEL ** -0.5) * col_scale,
        'da_lam_q': nrm(ks[2], (DEPTH, 2, DA_HEAD_DIM), 0.1),
        'da_lam_k': nrm(ks[3], (DEPTH, 2, DA_HEAD_DIM), 0.1),
        'da_subln_g': 1.0 + nrm(ks[4], (DEPTH, DA_V_DIM), 0.02),
        'cmp_pe_k': nrm(ks[5], (DEPTH, CMP_BLOCK, d), 0.1),
        'cmp_w1_k': nrm(ks[6], (DEPTH, CMP_BLOCK * d, CMP_HIDDEN), (CMP_BLOCK * d) ** -0.5),
        'cmp_w2_k': nrm(ks[7], (DEPTH, CMP_HIDDEN, d), CMP_HIDDEN ** -0.5),
        'cmp_pe_v': nrm(ks[8], (DEPTH, CMP_BLOCK, d), 0.1),
        'cmp_w1_v': nrm(ks[9], (DEPTH, CMP_BLOCK * d, CMP_HIDDEN), (CMP_BLOCK * d) ** -0.5),
        'cmp_w2_v': nrm(ks[10], (DEPTH, CMP_HIDDEN, d), CMP_HIDDEN ** -0.5),
        'w_branch_da': nrm(ks[11], (DEPTH, DA_V_W, D_MODEL), DA_V_W ** -0.5),
        'w_branch_nsa': nrm(ks[12], (DEPTH, NSA_Q_W, D_MODEL), NSA_Q_W ** -0.5),
        'w_out': nrm(ks[13], (DEPTH, D_MODEL, D_MODEL), D_MODEL ** -0.5 * DN_BETA),
        'ln1_g': 1.0 + nrm(ks[14], (DEPTH, D_MODEL), 0.02),
        'ln1_b': nrm(ks[15], (DEPTH, D_MODEL), 0.02),
        'peer_wq': nrm(ks[16], (DEPTH, D_MODEL, PEER_HEADS * PEER_KEY_DIM), D_MODEL ** -0.5),
        'peer_subkey1': nrm(ks[17], (DEPTH, PEER_NKEYS, PEER_KEY_DIM // 2), (PEER_KEY_DIM // 2) ** -0.5),
        'peer_subkey2': nrm(ks[18], (DEPTH, PEER_NKEYS, PEER_KEY_DIM // 2), (PEER_KEY_DIM // 2) ** -0.5),
        'peer_u': nrm(ks[19], (DEPTH, PEER_EXPERTS, D_MODEL), D_MODEL ** -0.5),
        'peer_v': nrm(ks[20], (DEPTH, PEER_EXPERTS, D_MODEL), DN_BETA * PEER_HEADS ** -0.5),
        'ln2_g': 1.0 + nrm(ks[21], (DEPTH, D_MODEL), 0.02),
        'ln2_b': nrm(ks[22], (DEPTH, D_MODEL), 0.02),
        'rel_bias': nrm(ks[23], (REL_BUCKETS, REL_HEADS), 0.5),
    }


def reference(x, w_in, da_lam_q, da_lam_k, da_subln_g, cmp_pe_k, cmp_w1_k, cmp_w2_k,
              cmp_pe_v, cmp_w1_v, cmp_w2_v, w_branch_da, w_branch_nsa, w_out, ln1_g, ln1_b,
              peer_wq, peer_subkey1, peer_subkey2, peer_u, peer_v, ln2_g, ln2_b, rel_bias):
    B, S, _ = x.shape
    G, d = NSA_KV_GROUPS, NSA_HEAD_DIM
    rel_da = rel_bias[:, :DA_HEADS]
    rel_nsa = rel_bias[:, DA_HEADS:]
    for l in range(DEPTH):
        lam_init = 0.8 - 0.6 * math.exp(-0.3 * l)
        (qd, kd, vd, qn, kc, vc, ksl, vsl, kwn, vwn,
         g_nsa_br, g_da_merge, g_nsa_merge) = split_columns(x @ w_in[l])
        lam_e = jnp.exp(jnp.sum(da_lam_q[l].astype(jnp.float32) * da_lam_k[l].astype(jnp.float32), -1))
        lam = lam_e[0] - lam_e[1] + lam_init
        o_da = diff_attention(qd.reshape(B, S, DA_HEADS, 2, DA_HEAD_DIM),
                              kd.reshape(B, S, DA_HEADS, 2, DA_HEAD_DIM),
                              vd.reshape(B, S, DA_HEADS, DA_V_DIM),
                              lam, da_subln_g[l], lam_init, rel_da)
        kc_c = compress_blocks(kc.reshape(B, S, G, d), cmp_pe_k[l], cmp_w1_k[l], cmp_w2_k[l])
        vc_c = compress_blocks(vc.reshape(B, S, G, d), cmp_pe_v[l], cmp_w1_v[l], cmp_w2_v[l])
        o_nsa = nsa_attention(qn.reshape(B, S, NSA_HEADS, d), kc_c, vc_c,
                              ksl.reshape(B, S, G, d), vsl.reshape(B, S, G, d),
                              kwn.reshape(B, S, G, d), vwn.reshape(B, S, G, d),
                              g_nsa_br.reshape(B, S, NSA_HEADS, 3), rel_nsa)
        mixed = (jax.nn.sigmoid(g_da_merge) * (o_da @ w_branch_da[l])
                 + jax.nn.sigmoid(g_nsa_merge) * (o_nsa @ w_branch_nsa[l]))
        h = layer_norm(DN_ALPHA * x + mixed @ w_out[l], ln1_g[l], ln1_b[l])
        x = layer_norm(DN_ALPHA * h + peer_ffn(h, peer_wq[l], peer_subkey1[l], peer_subkey2[l],
                                              peer_u[l], peer_v[l]), ln2_g[l], ln2_b[l])
    return x
```

```python
import math
from contextlib import ExitStack

import numpy as np
import concourse.bass as bass
import concourse.mybir as mybir
from concourse.bass_utils import run_bass_kernel_spmd

F32 = mybir.dt.float32
BF16 = mybir.dt.bfloat16
AF = mybir.ActivationFunctionType
ALU = mybir.AluOpType
AX = mybir.AxisListType

ENGS = ['pe', 'act', 'dve', 'pool', 'sp']
EPOCH = 16000
RING = {'sp': 40, 'pool': 16}
NEGM = -30000.0
NPOOL = 5
ALPHA = 2.0 ** 0.25
LAM_INIT = 0.8 - 0.6 * math.exp(0.0)


class Prog:
    def __init__(self, nc, stack):
        self.nc = nc
        self.stack = stack
        self.ops = {e: [] for e in ENGS}
        self.cnt = {e: 0 for e in ENGS}
        self.esems = {e: [] for e in ENGS}
        self.rings = {}
        self.ring_pos = {}
        self.ring_use = {}
        for q, n in RING.items():
            self.rings[q] = [stack.enter_context(nc.semaphore('r%s%d' % (q, i))) for i in range(n)]
            self.ring_pos[q] = 0
            self.ring_use[q] = [0] * n
        self.seen = {e: {} for e in ENGS}
        self.lastw = {}
        self.readers = {}
        self.last_ev = {e: None for e in ENGS}

    def _esem(self, eng, epoch):
        while len(self.esems[eng]) <= epoch:
            self.esems[eng].append(self.stack.enter_context(
                self.nc.semaphore('e%s%d' % (eng, len(self.esems[eng])))))
        return self.esems[eng][epoch]

    def op(self, eng, fn, reads=(), writes=(), dma=False):
        deps = {}

        def add(ev):
            if ev is None:
                return
            s, v = ev
            if v > deps.get(id(s), (None, 0))[1]:
                deps[id(s)] = (s, v)
        for k in reads:
            add(self.lastw.get(k))
        for k in writes:
            add(self.lastw.get(k))
            for ev in self.readers.get(k, {}).values():
                add(ev)
        if eng == 'pe':
            for t in self.esems['pe']:
                deps.pop(id(t), None)
        if dma:
            q = eng
            pos = self.ring_pos[q]
            self.ring_pos[q] = (pos + 1) % len(self.rings[q])
            sem = self.rings[q][pos]
            if self.ring_use[q][pos] > 0:
                add((sem, 16 * self.ring_use[q][pos]))
            self.ring_use[q][pos] += 1
            ev = (sem, 16 * self.ring_use[q][pos])
            inc = 16
        else:
            i = self.cnt[eng]
            self.cnt[eng] += 1
            sem = self._esem(eng, i // EPOCH)
            ev = (sem, i % EPOCH + 1)
            inc = 1
            self.last_ev[eng] = ev
        waits = []
        seen = self.seen[eng]
        for s, v in deps.values():
            if seen.get(id(s), 0) < v:
                seen[id(s)] = v
                waits.append((s, v))
        self.ops[eng].append((fn, waits, sem, inc))
        for k in reads:
            self.readers.setdefault(k, {})[(eng, id(sem))] = ev
        for k in writes:
            self.lastw[k] = ev
            self.readers[k] = {}
        return ev

    def barrier(self):
        evs = [ev for ev in self.last_ev.values() if ev is not None]
        for q in self.rings:
            for i, s in enumerate(self.rings[q]):
                if self.ring_use[q][i] > 0:
                    evs.append((s, 16 * self.ring_use[q][i]))
        for eng in ENGS:
            waits = []
            seen = self.seen[eng]
            for s, v in evs:
                if seen.get(id(s), 0) < v:
                    seen[id(s)] = v
                    waits.append((s, v))
            self.ops[eng].append((None, waits, None, 0))
        self.lastw = {}
        self.readers = {}

    def emit(self):
        nc = self.nc
        ops = self.ops
        self.ops = {e: [] for e in ENGS}
        with nc.Block() as block:
            def run(engname):
                def body(e):
                    for fn, waits, sem, inc in ops[engname]:
                        for s, v in waits:
                            e.wait_ge(s, v)
                        if fn is not None:
                            fn(e).then_inc(sem, inc)
                return body
            block.tensor(run('pe'))
            block.scalar(run('act'))
            block.vector(run('dve'))
            block.gpsimd(run('pool'))
            block.sync(run('sp'))


class Ctx:
    pass


def build_program(phases=('p0i', 'kv0', 'kv1', 'q', 'da', 'nsa', 'e1', 'e2', 'peer', 'g2'), dbg=False, dbg_src='aT'):
    nc = bass.Bass("TRN2", target_bir_lowering=False)
    C = Ctx()
    C.nc = nc

    def din(name, shape, dt=F32):
        return nc.dram_tensor(name, list(shape), dt, kind="ExternalInput").ap()

    def dscr(name, shape, dt=BF16):
        return nc.dram_tensor(name, list(shape), dt, kind="Internal").ap()

    I = {}
    I['xT'] = din('xT', [16, 128, 16, 512])
    I['xTo'] = din('xTo', [4, 128, 16, 512])
    I['xo'] = din('xo', [16, 128, 2048])
    I['w_dakv'] = din('w_dakv', [128, 16, 2048])
    I['w_nkv'] = din('w_nkv', [128, 16, 1536])
    I['w_q'] = din('w_q', [128, 16, 2048])
    I['w_gate'] = din('w_gate', [128, 16, 48])
    I['w_mg'] = din('w_mg', [128, 16, 4096])
    I['raw_da'] = din('raw_da', [8, 128, 2560])
    I['mneg'] = din('mneg', [128, 2944])
    I['wneg'] = din('wneg', [128, 2944])
    I['raw_nsa'] = din('raw_nsa', [16, 128, 2944])
    I['c_nsa'] = din('c_nsa', [128, 16])
    I['w1k'] = din('w1k', [64, 32, 256])
    I['w1v'] = din('w1v', [64, 32, 256])
    I['w2k'] = din('w2k', [128, 2, 64])
    I['w2v'] = din('w2v', [128, 2, 64])
    I['pekT'] = din('pekT', [64, 32])
    I['pevT'] = din('pevT', [64, 32])
    I['ovl'] = din('ovl', [128, 4, 129])
    I['cm'] = din('cm', [128, 2, 512])
    I['vmul'] = din('vmul', [128, 16, 128])
    I['vadd'] = din('vadd', [128, 16, 128])
    I['onehot'] = din('onehot', [64, 8192], BF16)
    I['w_bda'] = din('w_bda', [128, 8, 2048])
    I['w_bnsa'] = din('w_bnsa', [128, 8, 2048])
    I['w_out'] = din('w_out', [128, 16, 2048])
    I['ln1g'] = din('ln1g', [128, 2048])
    I['ln1b'] = din('ln1b', [128, 2048])
    I['ln2g'] = din('ln2g', [128, 2048])
    I['ln2b'] = din('ln2b', [128, 2048])
    I['wq'] = din('wq', [16, 128, 16, 64])
    I['skT'] = din('skT', [64, 2, 128])
    I['puT'] = din('puT', [128, 128, 16, 128])
    I['pv'] = din('pv', [128, 128, 2048])
    I['c_da'] = din('c_da', [128, 8])
    I['lamq'] = din('lamq', [128, 128])
    I['lamk'] = din('lamk', [128, 128])
    I['subg'] = din('subg', [128, 128])
    I['ident'] = din('ident', [128, 128])
    out = nc.dram_tensor('out', [16, 128, 2048], F32, kind="ExternalOutput").ap()
    dbg_out = None
    if dbg:
        dbg_out = nc.dram_tensor('dbg', [128, 8, 2048], BF16, kind="ExternalOutput").ap()
        dbg2_out = nc.dram_tensor('dbg2', [128, 4, 258], F32, kind="ExternalOutput").ap()

    S = {}
    S['kT'] = dscr('s_kT', [8, 128, 8192])
    S['v'] = dscr('s_v', [8, 8192, 128])
    S['nkT'] = dscr('s_nkT', [4, 256, 8192])
    S['nv'] = dscr('s_nv', [2, 4, 8192, 64])
    S['qT'] = dscr('s_qT', [16, 128, 2048])
    S['h'] = dscr('s_h', [16, 128, 2048], F32)
    S['mixT'] = dscr('s_mixT', [4, 128, 16, 512])
    S['hT'] = dscr('s_hT', [4, 128, 16, 512])
    S['pe'] = dscr('s_pe', [16, 128, 2048], F32)
    S['puT'] = dscr('s_puT', [128, 128, 16, 128])
    S['pv'] = dscr('s_pv', [128, 128, 2048])

    with ExitStack() as top:
        P = Prog(nc, top)

        def sbt(st, name, shape, dt):
            return st.enter_context(nc.sbuf_tensor(name, list(shape), dt))

        ps = [top.enter_context(nc.psum_tensor('ps%d' % i, [128, 512], F32)) for i in range(8)]

        def MM(o, lhsT, rhs, start, stop, r, w):
            P.op('pe', lambda e: e.matmul(o, lhsT, rhs, start=start, stop=stop), r, w)

        def ACT(o, i, func, r, w, **kw):
            P.op('act', lambda e: e.activation(o, i, func, **kw), r, w)

        def CP(eng, o, i, r, w):
            if eng == 'act':
                P.op('act', lambda e: e.copy(o, i), r, w)
            else:
                P.op(eng, lambda e: e.tensor_copy(o, i), r, w)

        def TS(eng, o, i0, s1, s2, op0, op1, r, w, **kw):
            if op1 is None:
                P.op(eng, lambda e: e.tensor_scalar(o, i0, s1, None, op0, **kw), r, w)
            else:
                P.op(eng, lambda e: e.tensor_scalar(o, i0, s1, s2, op0, op1, **kw), r, w)

        def STT(eng, o, i0, sc, i1, op0, op1, r, w):
            P.op(eng, lambda e: e.scalar_tensor_tensor(o, i0, sc, i1, op0, op1), r, w)

        def TT(eng, o, i0, i1, op, r, w):
            P.op(eng, lambda e: e.tensor_tensor(o, i0, i1, op), r, w)

        def DMA(q, o, i, r, w):
            return P.op(q, lambda e: e.dma_start(out=o, in_=i), r, w, dma=True)

        def MEMSET(eng, o, val, w):
            P.op(eng, lambda e: e.memset(o, val), (), w)

        gates = sbt(top, 'gates', [128, 16, 48], F32)
        identb = sbt(top, 'identb', [128, 128], BF16)
        neglam = sbt(top, 'neglam', [128, 1], F32)
        gs = sbt(top, 'gs', [128, 128], F32)
        cda = sbt(top, 'cda', [128, 8], F32)
        dbg2sb = sbt(top, 'dbg2sb', [128, 4, 258], F32) if dbg else None
        mid = ExitStack()
        aT = sbt(mid, 'aT', [128, 8, 2048], BF16)
        nT = sbt(mid, 'nT', [128, 8, 2048], BF16)

        with ExitStack() as ph:
            idf = sbt(ph, 'idf', [128, 128], F32)
            lq = sbt(ph, 'lq', [128, 128], F32)
            lk = sbt(ph, 'lk', [128, 128], F32)
            lp = sbt(ph, 'lp', [128, 128], F32)
            l2 = sbt(ph, 'l2', [128, 2], F32)
            DMA('sp', idf[:], I['ident'], [], ['idf'])
            DMA('sp', lq[:], I['lamq'], [], ['lq'])
            DMA('sp', lk[:], I['lamk'], [], ['lk'])
            DMA('sp', gs[:], I['subg'], [], ['gs'])
            DMA('sp', cda[:], I['c_da'], [], ['cda'])
            CP('dve', identb[:], idf[:], ['idf'], ['identb'])
            TT('dve', lp[:], lq[:], lk[:], ALU.mult, ['lq', 'lk'], ['lp'])
            P.op('dve', lambda e: e.tensor_reduce(l2[:], lp[:].rearrange('p (a b) -> p a b', a=2), AX.X, ALU.add),
                 ['lp'], ['l2'])
            ACT(l2[:], l2[:], AF.Exp, ['l2'], ['l2'])
            STT('dve', neglam[:], l2[:, 0:1], -1.0, l2[:, 1:2], ALU.mult, ALU.add, ['l2'], ['neglam'])
            TS('dve', neglam[:], neglam[:], -LAM_INIT, None, ALU.add, None, ['neglam'], ['neglam'])
            TS('dve', gs[:], gs[:], 1.0 - LAM_INIT, None, ALU.mult, None, ['gs'], ['gs'])
            P.barrier()
            P.emit()

        psrot = [0]

        def nextps():
            i = psrot[0]
            psrot[0] = (i + 1) % 8
            return i

        def phase_kv(passno):
            ncw = 2048 if passno == 0 else 1536
            wsrc = I['w_dakv'] if passno == 0 else I['w_nkv']
            with ExitStack() as ph:
                wb = sbt(ph, 'A%d_wb' % passno, [128, 16, ncw], BF16)
                wst = [sbt(ph, 'A%d_wst%d' % (passno, i), [128, ncw], F32) for i in range(2)]
                xs = [sbt(ph, 'A%d_xs%d' % (passno, i), [128, 4, 512], F32) for i in range(2)]
                xb = [sbt(ph, 'A%d_xb%d' % (passno, i), [128, 16, 512], BF16) for i in range(2)]
                evs = [sbt(ph, 'A%d_ev%d' % (passno, i), [128, 512], BF16) for i in range(4)]
                evi = [0]
                for kc in range(16):
                    b = kc % 2
                    DMA('sp', wst[b][:], wsrc[:, kc, :], [], ['wst%d' % b])
                    CP('pool', wb[:, kc, :], wst[b][:], ['wst%d' % b], ['wb'])

                def evac_store(pi, npart, ncol, dst_fn):
                    k = evi[0]
                    evi[0] = (k + 1) % 4
                    CP('act', evs[k][0:npart, 0:ncol], ps[pi][0:npart, 0:ncol], ['ps%d' % pi], ['ev%d' % k])
                    dst_fn(evs[k], k)

                for tile in range(16):
                    xbk = 'xb%d' % (tile % 2)
                    xbt = xb[tile % 2]
                    for qq in range(4):
                        half = qq % 2
                        DMA('sp', xs[half][:], I['xT'][tile, :, qq * 4:(qq + 1) * 4, :], [], ['xs%d' % half])
                        CP('dve', xbt[:, qq * 4:(qq + 1) * 4, :], xs[half][:], ['xs%d' % half], [xbk])
                    tsl = slice(tile * 512, (tile + 1) * 512)
                    if passno == 0:
                        fm = [(c * 128, S['kT'][c, :, tsl]) for c in range(8)]
                    else:
                        fm = []
                        for kind, cb in enumerate((0, 256, 512, 1024)):
                            for c2 in range(2):
                                fm.append((cb + c2 * 128, S['nkT'][kind, c2 * 128:(c2 + 1) * 128, tsl]))
                    for col0, dst in fm:
                        pi = nextps()
                        for kc in range(16):
                            MM(ps[pi][:, :], wb[:, kc, col0:col0 + 128], xbt[:, kc, :], kc == 0, kc == 15,
                               ['wb', xbk], ['ps%d' % pi])
                        evac_store(pi, 128, 512,
                                   lambda ev, k, dst=dst: DMA('pool', dst, ev[:, :], ['ev%d' % k], []))
                    for blk in range(4):
                        t0 = tile * 512 + blk * 128
                        if passno == 0:
                            tm = [(1024 + g4 * 512, 512,
                                   S['v'][g4 * 4:(g4 + 1) * 4, t0:t0 + 128, :].rearrange('h t e -> t h e'), 4)
                                  for g4 in range(2)]
                        else:
                            tm = [(768, 256, S['nv'][0, :, t0:t0 + 128, :].rearrange('g t e -> t g e'), 4),
                                  (1280, 256, S['nv'][1, :, t0:t0 + 128, :].rearrange('g t e -> t g e'), 4)]
                        for col0, ncol, dst, nh in tm:
                            pi = nextps()
                            for kc in range(16):
                                MM(ps[pi][:, 0:ncol], xbt[:, kc, blk * 128:(blk + 1) * 128], wb[:, kc, col0:col0 + ncol],
                                   kc == 0, kc == 15, ['wb', xbk], ['ps%d' % pi])
                            evac_store(pi, 128, ncol,
                                       lambda ev, k, dst=dst, ncol=ncol, nh=nh: DMA(
                                           'pool', dst, ev[:, 0:ncol].rearrange('t (h e) -> t h e', h=nh),
                                           ['ev%d' % k], []))
                P.barrier()
                P.emit()

        def phase_q():
            with ExitStack() as ph:
                wb = sbt(ph, 'B_wb', [128, 16, 2048], BF16)
                wgb = sbt(ph, 'B_wgb', [128, 16, 48], BF16)
                wst = [sbt(ph, 'B_wst%d' % i, [128, 2048], F32) for i in range(1)]
                wgs = sbt(ph, 'B_wgs', [128, 16, 48], F32)
                xs = [sbt(ph, 'B_xs%d' % i, [128, 4, 512], F32) for i in range(2)]
                xb = [sbt(ph, 'B_xb%d' % i, [128, 16, 512], BF16) for i in range(2)]
                evs = [sbt(ph, 'B_ev%d' % i, [128, 512], BF16) for i in range(4)]
                evi = [0]
                for kc in range(16):
                    b = 0
                    DMA('sp', wst[b][:], I['w_q'][:, kc, :], [], ['wst%d' % b])
                    CP('pool', wb[:, kc, :], wst[b][:], ['wst%d' % b], ['wb'])
                DMA('sp', wgs[:], I['w_gate'], [], ['wgs'])
                CP('pool', wgb[:], wgs[:], ['wgs'], ['wgb'])
                for tile in range(4):
                    xbk = 'xb%d' % (tile % 2)
                    xbt = xb[tile % 2]
                    for qq in range(4):
                        half = qq % 2
                        DMA('sp', xs[half][:], I['xTo'][tile, :, qq * 4:(qq + 1) * 4, :], [], ['xs%d' % half])
                        CP('dve', xbt[:, qq * 4:(qq + 1) * 4, :], xs[half][:], ['xs%d' % half], [xbk])
                    tsl = slice(tile * 512, (tile + 1) * 512)
                    for c in range(16):
                        pi = nextps()
                        for kc in range(16):
                            MM(ps[pi][:, :], wb[:, kc, c * 128:(c + 1) * 128], xbt[:, kc, :], kc == 0, kc == 15,
                               ['wb', xbk], ['ps%d' % pi])
                        k = evi[0]
                        evi[0] = (k + 1) % 4
                        CP('act', evs[k][:, :], ps[pi][:, :], ['ps%d' % pi], ['ev%d' % k])
                        DMA('pool', S['qT'][c, :, tsl], evs[k][:, :], ['ev%d' % k], [])
                    for blk in range(4):
                        pi = nextps()
                        for kc in range(16):
                            MM(ps[pi][:, 0:48], xbt[:, kc, blk * 128:(blk + 1) * 128], wgb[:, kc, :], kc == 0, kc == 15,
                               ['wgb', xbk], ['ps%d' % pi])
                        ACT(gates[:, tile * 4 + blk, :], ps[pi][:, 0:48], AF.Sigmoid, ['ps%d' % pi], ['gates'])
                P.barrier()
                P.emit()

        def phase_da():
            with ExitStack() as ph:
                KtF = sbt(ph, 'C_KtF', [128, 8192], BF16)
                Vh = sbt(ph, 'C_Vh', [128, 64, 129], BF16)
                strip = sbt(ph, 'C_strip', [128, 2560], F32)
                mneg = sbt(ph, 'C_mneg', [128, 2560], F32)
                QT = [sbt(ph, 'C_QT%d' % m, [128, 2048], BF16) for m in range(2)]
                pT = [sbt(ph, 'C_pT%d' % b, [128, 512], BF16) for b in range(4)]
                tmp = [sbt(ph, 'C_tmp%d' % b, [128, 512], F32) for b in range(3)]
                fz = [sbt(ph, 'C_fz%d' % i, [128, 4], F32) for i in range(2)]
                o0 = sbt(ph, 'C_o0', [128, 4, 129], F32)
                fu = [sbt(ph, 'C_fu%d' % i, [128, 128], F32) for i in range(2)]
                fo = [sbt(ph, 'C_fo%d' % i, [128, 128], F32) for i in range(2)]
                fj = [sbt(ph, 'C_fj%d' % i, [128, 128], F32) for i in range(2)]
                fon = [sbt(ph, 'C_fon%d' % i, [128, 128], BF16) for i in range(2)]
                DMA('sp', mneg[:], I['mneg'][:, 0:2560], [], ['mneg'])
                MEMSET('pool', Vh[:, :, 128:129], 1.0, ['Vh'])
                MEMSET('pool', QT[0][:], 0.0, ['QT0'])
                MEMSET('pool', QT[1][:], 0.0, ['QT1'])
                p0q = []
                if 'p0i' in phases:
                    pus = [sbt(ph, 'C_pus%d' % i, [128, 2048], F32) for i in range(3)]
                    pub = [sbt(ph, 'C_pub%d' % i, [128, 2048], BF16) for i in range(3)]
                    p0q = p0_units(pus, pub, ['pool'])
                st_ = {'slot': 0, 'fin': 0}
                pipe = []

                def push(pv):
                    pipe.append(pv)
                    if len(pipe) > 2:
                        pipe.pop(0)()

                def flush():
                    while pipe:
                        pipe.pop(0)()

                def da_finalize(h, t):
                    for sub in range(4):
                        f = st_['fin'] % 2
                        st_['fin'] += 1
                        acc = ps[4 + sub]
                        ak = 'ps%d' % (4 + sub)
                        if dbg and h == 0 and t == 0:
                            CP('dve', dbg2sb[:, sub, 0:129], o0[:, sub, :], ['o0_%d' % sub], ['dbg2sb'])
                            CP('dve', dbg2sb[:, sub, 129:258], acc[:, 0:129], [ak], ['dbg2sb'])
                        P.op('dve', lambda e, f=f, sub=sub: e.reciprocal(fz[f][:, 0:1], o0[:, sub, 128:129]), ['o0_%d' % sub], ['fz%d' % f])
                        P.op('dve', lambda e, f=f, acc=acc: e.reciprocal(fz[f][:, 1:2], acc[:, 128:129]), [ak], ['fz%d' % f])
                        TT('dve', fz[f][:, 2:3], fz[f][:, 1:2], neglam[:], ALU.mult, ['fz%d' % f, 'neglam'], ['fz%d' % f])
                        TS('dve', fu[f][:], acc[:, 0:128], fz[f][:, 2:3], None, ALU.mult, None, [ak, 'fz%d' % f], ['fu%d' % f])
                        STT('dve', fo[f][:], o0[:, sub, 0:128], fz[f][:, 0:1], fu[f][:], ALU.mult, ALU.add,
                            ['o0_%d' % sub, 'fz%d' % f, 'fu%d' % f], ['fo%d' % f])
                        TT('pool', fj[f][:], fo[f][:], fo[f][:], ALU.mult, ['fo%d' % f], ['fj%d' % f])
                        P.op('dve', lambda e, f=f: e.tensor_reduce(fz[f][:, 3:4], fj[f][:], AX.X, ALU.add), ['fj%d' % f], ['fz3_%d' % f])
                        TS('dve', fz[f][:, 3:4], fz[f][:, 3:4], 1.0 / 128.0, 1e-5, ALU.mult, ALU.add, ['fz3_%d' % f], ['fz3_%d' % f])
                        ACT(fz[f][:, 3:4], fz[f][:, 3:4], AF.Sqrt, ['fz3_%d' % f], ['fz3_%d' % f])
                        P.op('dve', lambda e, f=f: e.reciprocal(fz[f][:, 3:4], fz[f][:, 3:4]), ['fz3_%d' % f], ['fz3_%d' % f])
                        STT('dve', fon[f][:], fo[f][:], fz[f][:, 3:4], gs[:], ALU.mult, ALU.mult,
                            ['fo%d' % f, 'fz3_%d' % f, 'gs'], ['fon%d' % f])
                        MM(ps[3][:, 0:128], fon[f][:], identb[:], True, True, ['fon%d' % f, 'identb'], ['ps3'])
                        blk = t * 4 + sub
                        CP('act', aT[:, h, blk * 128:(blk + 1) * 128], ps[3][:, 0:128], ['ps3'], ['aT'])

                for h in range(8):
                    flush()
                    DMA('sp', KtF[:], S['kT'][h], [], ['KtF'])
                    for m in range(2):
                        DMA('sp', QT[m][m * 64:(m + 1) * 64, :], S['qT'][h, m * 64:(m + 1) * 64, :], [], ['QT%d' % m])
                    for q4 in range(4):
                        DMA('sp', Vh[:, q4 * 16:(q4 + 1) * 16, 0:128],
                            S['v'][h, q4 * 2048:(q4 + 1) * 2048, :].rearrange('(s p) e -> p s e', p=128), [], ['Vh'])
                    DMA('sp', strip[:], I['raw_da'][h], [], ['strip'])
                    STT('dve', strip[:], strip[:], cda[:, h:h + 1], mneg[:], ALU.subtract, ALU.add,
                        ['strip', 'cda', 'mneg'], ['strip'])
                    for t in range(4):
                        nsl = 16 * (t + 1)
                        for m in range(2):
                            for s in range(nsl):
                                c = st_['slot']
                                st_['slot'] += 1
                                if p0q and c % 10 == 0:
                                    p0q.pop(0)()
                                b2 = c % 3
                                b3 = c % 4
                                near = s >= 16 * t - 1
                                pi = b2
                                MM(ps[pi][:, :], KtF[:, s * 128:(s + 1) * 128], QT[m][:, t * 512:(t + 1) * 512],
                                   True, True, ['KtF', 'QT%d' % m], ['ps%d' % pi])
                                if near:
                                    x0 = 128 * (15 - (s - 16 * t))
                                    STT('dve', tmp[b2][:], ps[pi][:, :], 0.125, strip[:, x0:x0 + 512], ALU.mult, ALU.add,
                                        ['ps%d' % pi, 'strip'], ['tmp%d' % b2])
                                    ACT(pT[b3][:], tmp[b2][:], AF.Exp, ['tmp%d' % b2], ['pT%d' % b3])
                                else:
                                    ACT(pT[b3][:], ps[pi][:, :], AF.Exp, ['ps%d' % pi], ['pT%d' % b3], scale=0.125)

                                def pv(h=h, t=t, m=m, s=s, b3=b3, nsl=nsl):
                                    for sub in range(4):
                                        MM(ps[4 + sub][:, 0:129], pT[b3][:, sub * 128:(sub + 1) * 128],
                                           Vh[:, s, :], s == 0, s == nsl - 1, ['pT%d' % b3, 'Vh'], ['ps%d' % (4 + sub)])
                                    if s == nsl - 1:
                                        if m == 0:
                                            for sub in range(4):
                                                CP('dve', o0[:, sub, :], ps[4 + sub][:, 0:129], ['ps%d' % (4 + sub)], ['o0_%d' % sub])
                                        else:
                                            da_finalize(h, t)
                                push(pv)
                flush()
                while p0q:
                    p0q.pop(0)()
                P.barrier()
                P.emit()


        def phase_nsa():
            with ExitStack() as ph:
                KCT = sbt(ph, 'D_KCT', [64, 4, 512], BF16)
                Rg = sbt(ph, 'D_Rg', [128, 4, 4, 193], BF16)
                cns = sbt(ph, 'D_cns', [128, 16], F32)
                MEMSET('pool', KCT[:], 0.0, ['KCT'])
                MEMSET('pool', Rg[:], 0.0, ['Rg'])
                DMA('sp', cns[:], I['c_nsa'], [], ['cns'])
                with ExitStack() as p0:
                    cT = sbt(p0, 'D0_cT', [64, 8192], BF16)
                    w1s = sbt(p0, 'D0_w1s', [64, 8, 256], F32)
                    w1b = sbt(p0, 'D0_w1b', [64, 32, 256], BF16)
                    w2s = sbt(p0, 'D0_w2s', [128, 2, 64], F32)
                    w2b = sbt(p0, 'D0_w2b', [128, 2, 64], BF16)
                    pes = sbt(p0, 'D0_pes', [64, 32], F32)
                    peb = sbt(p0, 'D0_peb', [64, 32], BF16)
                    b1 = sbt(p0, 'D0_b1', [128, 2], F32)
                    hT = sbt(p0, 'D0_hT', [128, 2, 512], BF16)
                    ovs = sbt(p0, 'D0_ovs', [128, 4, 129], F32)
                    DMA('sp', ovs[:], I['ovl'], [], ['ovs'])
                    for g in range(4):
                        CP('pool', Rg[:, :, g, 0:129], ovs[:], ['ovs'], ['Rg'])
                    for kind in range(2):
                        w1src = I['w1k'] if kind == 0 else I['w1v']
                        for p8 in range(4):
                            DMA('sp', w1s[:], w1src[:, p8 * 8:(p8 + 1) * 8, :], [], ['w1s'])
                            CP('pool', w1b[:, p8 * 8:(p8 + 1) * 8, :], w1s[:], ['w1s'], ['w1b'])
                        DMA('sp', w2s[:], I['w2k'] if kind == 0 else I['w2v'], [], ['w2s'])
                        CP('pool', w2b[:], w2s[:], ['w2s'], ['w2b'])
                        DMA('sp', pes[:], I['pekT'] if kind == 0 else I['pevT'], [], ['pes'])
                        CP('pool', peb[:], pes[:], ['pes'], ['peb'])
                        for hc in range(2):
                            pi = nextps()
                            for p in range(32):
                                MM(ps[pi][:, 0:1], w1b[:, p, hc * 128:(hc + 1) * 128], peb[:, p:p + 1], p == 0, p == 31,
                                   ['w1b', 'peb'], ['ps%d' % pi])
                            CP('dve', b1[:, hc:hc + 1], ps[pi][:, 0:1], ['ps%d' % pi], ['b1'])
                        for g in range(4):
                            DMA('sp', cT[:], S['nkT'][kind, g * 64:(g + 1) * 64, :], [], ['cT'])
                            for hc in range(2):
                                pi = nextps()
                                for p in range(32):
                                    MM(ps[pi][:, 0:511], w1b[:, p, hc * 128:(hc + 1) * 128], cT[:, p:p + 16 * 510 + 1:16],
                                       p == 0, p == 31, ['w1b', 'cT'], ['ps%d' % pi])
                                ACT(hT[:, hc, 0:511], ps[pi][:, 0:511], AF.Gelu_apprx_tanh, ['ps%d' % pi, 'b1'], ['hT'],
                                    bias=b1[:, hc:hc + 1])
                            if kind == 0:
                                pi = nextps()
                                for hc in range(2):
                                    MM(ps[pi][0:64, 0:511], w2b[:, hc, :], hT[:, hc, 0:511], hc == 0, hc == 1,
                                       ['w2b', 'hT'], ['ps%d' % pi])
                                CP('act', KCT[:, g, 0:511], ps[pi][0:64, 0:511], ['ps%d' % pi], ['KCT'])
                            else:
                                for cc in range(4):
                                    ncl = 128 if cc < 3 else 127
                                    pi = nextps()
                                    for hc in range(2):
                                        MM(ps[pi][0:ncl, 0:64], hT[:, hc, cc * 128:cc * 128 + ncl], w2b[:, hc, :], hc == 0, hc == 1,
                                           ['w2b', 'hT'], ['ps%d' % pi])
                                    CP('act', Rg[0:ncl, cc, g, 129:193], ps[pi][0:ncl, 0:64], ['ps%d' % pi], ['Rg'])
                    P.barrier()
                    P.emit()
                cm = sbt(ph, 'D_cm', [128, 2, 512], F32)
                vmul = sbt(ph, 'D_vmul', [128, 16, 128], F32)
                vadd = sbt(ph, 'D_vadd', [128, 16, 128], F32)
                Kbuf = sbt(ph, 'D_Kbuf', [128, 8192], BF16)
                Vbuf = sbt(ph, 'D_Vbuf', [128, 64, 65], BF16)
                strip = sbt(ph, 'D_strip', [128, 2944], F32)
                neg = sbt(ph, 'D_neg', [128, 2944], F32)
                QTn = [sbt(ph, 'D_QTn%d' % i, [64, 2048], BF16) for i in range(4)]
                QA = [sbt(ph, 'D_QA%d' % i, [128, 512], BF16) for i in range(2)]
                QB = [sbt(ph, 'D_QB%d' % i, [128, 512], BF16) for i in range(2)]
                selT = [sbt(ph, 'D_selT%d' % i, [128, 512], BF16) for i in range(4)]
                onsa = sbt(ph, 'D_onsa', [128, 16, 4, 64], F32)
                impacc = sbt(ph, 'D_impacc', [128, 4, 128], F32)
                pT = [sbt(ph, 'D_pT%d' % b, [128, 512], BF16) for b in range(4)]
                tmp = [sbt(ph, 'D_tmp%d' % b, [128, 512], F32) for b in range(3)]
                fz = [sbt(ph, 'D_fz%d' % i, [128, 4], F32) for i in range(2)]
                sc = [sbt(ph, 'D_sc%d' % i, [128, 128], F32) for i in range(2)]
                sc2 = [sbt(ph, 'D_sc2%d' % i, [128, 128], F32) for i in range(2)]
                m8 = [sbt(ph, 'D_m8%d' % i, [128, 16], F32) for i in range(2)]
                sng = [sbt(ph, 'D_sng%d' % i, [128, 128], BF16) for i in range(2)]
                onb = [sbt(ph, 'D_onb%d' % i, [128, 128], BF16) for i in range(2)]
                accT_sb = [sbt(ph, 'D_accT%d' % i, [65, 512], F32) for i in range(1)]
                QZ = [sbt(ph, 'D_QZ%d' % i, [128, 512], BF16) for i in range(2)]
                MEMSET('pool', QZ[0][:], 0.0, ['QZ0'])
                MEMSET('pool', QZ[1][:], 0.0, ['QZ1'])
                identf = sbt(ph, 'D_identf', [128, 128], F32)
                DMA('sp', identf[:], I['ident'], [], ['identf'])
                DMA('sp', cm[:], I['cm'], [], ['cm'])
                DMA('sp', vmul[:], I['vmul'], [], ['vmul'])
                DMA('sp', vadd[:], I['vadd'], [], ['vadd'])
                DMA('sp', Kbuf[64:128, :], I['onehot'], [], ['KbufHi'])
                MEMSET('pool', Vbuf[:, :, 64:65], 1.0, ['Vbuf'])
                cnt = {'slot': 0, 'fin': 0, 'q': 0, 'tr': 0, 'grp': 0}

                pipe = []

                def push(pv):
                    pipe.append(pv)
                    if len(pipe) > 2:
                        pipe.pop(0)()

                def flush():
                    while pipe:
                        pipe.pop(0)()

                def attn_slot(lhsT, rhs, rkeys, bias_ap, vrhs, vkeys, ncolv, first, last, after=None, tbank=None):
                    c = cnt['slot']
                    cnt['slot'] = c + 1
                    b2, b3 = c % 3, c % 4
                    MM(ps[b2][:, :], lhsT, rhs, True, True, rkeys, ['ps%d' % b2])
                    if bias_ap is not None:
                        STT('dve', tmp[b2][:], ps[b2][:, :], 0.125, bias_ap, ALU.mult, ALU.add,
                            ['ps%d' % b2, 'strip', 'cm'], ['tmp%d' % b2])
                        ACT(pT[b3][:], tmp[b2][:], AF.Exp, ['tmp%d' % b2], ['pT%d' % b3])
                    else:
                        ACT(pT[b3][:], ps[b2][:, :], AF.Exp, ['ps%d' % b2], ['pT%d' % b3], scale=0.125)

                    def pv():
                        if tbank is None:
                            for sub in range(4):
                                MM(ps[4 + sub][:, 0:ncolv], pT[b3][:, sub * 128:(sub + 1) * 128], vrhs, first, last,
                                   ['pT%d' % b3] + vkeys, ['ps%d' % (4 + sub)])
                        else:
                            MM(ps[tbank][0:ncolv, :], vrhs, pT[b3][:, :], first, last, ['pT%d' % b3] + vkeys, ['ps%d' % tbank])
                        if after is not None:
                            after()
                    push(pv)

                def fin_branch(t, hh, n, gidx, dcol, ncol0, first_branch, tb=None):
                    if tb is not None:
                        k = cnt['tr'] % 2
                        cnt['tr'] += 1
                        CP('act', accT_sb[0][0:65, :], ps[tb][0:65, :], ['ps%d' % tb], ['accT0'])
                        for sub in range(4):
                            MM(ps[6 + k][:, sub * 65:(sub + 1) * 65], accT_sb[0][0:65, sub * 128:(sub + 1) * 128], identf[0:65, 0:65],
                               True, True, ['accT0', 'identf'], ['ps%d' % (6 + k)])
                    for sub in range(4):
                        f = cnt['fin'] % 2
                        cnt['fin'] += 1
                        blk = 4 * t + sub
                        if tb is None:
                            acc = ps[4 + sub]
                            ak = 'ps%d' % (4 + sub)
                            c0 = 0
                        else:
                            acc = ps[6 + k]
                            ak = 'ps%d' % (6 + k)
                            c0 = sub * 65
                        fk = 'fz%d' % f
                        TS('dve', fz[f][:, 0:1], acc[:, c0 + dcol:c0 + dcol + 1], 1e-30, None, ALU.max, None, [ak], [fk])
                        P.op('dve', lambda e, f=f: e.reciprocal(fz[f][:, 1:2], fz[f][:, 0:1]), [fk], [fk])
                        TT('dve', fz[f][:, 2:3], fz[f][:, 1:2], gates[:, blk, n * 3 + gidx:n * 3 + gidx + 1], ALU.mult,
                           [fk, 'gates'], [fk])
                        if first_branch:
                            if hh == 0:
                                TS('dve', impacc[:, sub, :], acc[:, 0:128], fz[f][:, 1:2], None, ALU.mult, None,
                                   [ak, fk], ['impacc%d' % sub])
                            else:
                                STT('dve', impacc[:, sub, :], acc[:, 0:128], fz[f][:, 1:2], impacc[:, sub, :], ALU.mult, ALU.add,
                                    [ak, fk, 'impacc%d' % sub], ['impacc%d' % sub])
                            TS('dve', onsa[:, blk, hh, :], acc[:, ncol0:ncol0 + 64], fz[f][:, 2:3], None, ALU.mult, None,
                               [ak, fk], ['onsa'])
                        else:
                            STT('dve', onsa[:, blk, hh, :], acc[:, c0 + ncol0:c0 + ncol0 + 64], fz[f][:, 2:3], onsa[:, blk, hh, :],
                                ALU.mult, ALU.add, [ak, fk, 'onsa'], ['onsa'])

                for g in range(4):
                    for hh in range(4):
                        n = 4 * g + hh
                        DMA('sp', QTn[hh][:], S['qT'][8 + n // 2, (n % 2) * 64:(n % 2) * 64 + 64, :], [], ['QTn%d' % hh])
                    def topk_code(g, t):
                        for sub in range(4):
                            f = cnt['fin'] % 2
                            cnt['fin'] += 1
                            blk = 4 * t + sub
                            TT('dve', sc[f][:], impacc[:, sub, :], vmul[:, blk, :], ALU.mult, ['impacc%d' % sub, 'vmul'], ['sc%d' % f])
                            TT('dve', sc[f][:], sc[f][:], vadd[:, blk, :], ALU.add, ['sc%d' % f, 'vadd'], ['sc%d' % f])
                            P.op('dve', lambda e, f=f: e.max(out=m8[f][:, 0:8], in_=sc[f][:]), ['sc%d' % f], ['m8_%d' % f])
                            P.op('dve', lambda e, f=f: e.match_replace(out=sc2[f][:], in_to_replace=m8[f][:, 0:8],
                                                                      in_values=sc[f][:], imm_value=-3.0e38),
                                 ['sc%d' % f, 'm8_%d' % f], ['sc2%d' % f])
                            P.op('dve', lambda e, f=f: e.max(out=m8[f][:, 8:16], in_=sc2[f][:]), ['sc2%d' % f], ['m8_%d' % f])
                            TS('dve', sng[f][:], sc[f][:], m8[f][:, 15:16], -240000.0, ALU.is_lt, ALU.mult,
                               ['sc%d' % f, 'm8_%d' % f], ['sng%d' % f])
                            MM(ps[3][:, 0:128], sng[f][:], identb[:], True, True, ['sng%d' % f, 'identb'], ['ps3'])
                            CP('act', selT[t][:, sub * 128:(sub + 1) * 128], ps[3][:, 0:128], ['ps3'], ['selT%d' % t])
                            if dbg and g == 0 and t == 0:
                                CP('dve', dbg2sb[:, sub, 0:128], sc[f][:], ['sc%d' % f], ['dbg2sb'])
                                CP('dve', dbg2sb[:, sub, 129:145], m8[f][:], ['m8_%d' % f], ['dbg2sb'])

                    for t in range(4):
                        for hh in range(4):
                            n = 4 * g + hh
                            for cc in range(t + 1):
                                bias_ap = cm[:, cc - t + 1, :] if cc >= t - 1 else None
                                aft = None
                                if cc == t:
                                    def aft(t=t, hh=hh, n=n, g=g):
                                        fin_branch(t, hh, n, 0, 128, 129, True)
                                        if hh == 3:
                                            topk_code(g, t)
                                attn_slot(KCT[:, g, cc * 128:(cc + 1) * 128], QTn[hh][:, t * 512:(t + 1) * 512],
                                          ['KCT', 'QTn%d' % hh], bias_ap, Rg[:, cc, g, :], ['Rg'], 193, cc == 0, cc == t, after=aft)
                    flush()
                    DMA('sp', Kbuf[0:64, :], S['nkT'][2, g * 64:(g + 1) * 64, :], [], ['KbufLo'])
                    for q4 in range(4):
                        DMA('sp', Vbuf[:, q4 * 16:(q4 + 1) * 16, 0:64],
                            S['nv'][0, g, q4 * 2048:(q4 + 1) * 2048, :].rearrange('(s p) e -> p s e', p=128), [], ['Vbuf'])
                    DMA('sp', neg[:], I['mneg'], [], ['neg'])
                    for hh in range(4):
                        n = 4 * g + hh
                        DMA('sp', strip[:], I['raw_nsa'][n], [], ['strip'])
                        STT('dve', strip[:], strip[:], cns[:, n:n + 1], neg[:], ALU.subtract, ALU.add,
                            ['strip', 'cns', 'neg'], ['strip'])
                        for t in range(4):
                            qb = cnt['q'] % 2
                            cnt['q'] += 1
                            CP('pool', QA[qb][0:64, :], QTn[hh][:, t * 512:(t + 1) * 512], ['QTn%d' % hh], ['QA%d' % qb])
                            CP('pool', QB[qb][0:64, :], QTn[hh][:, t * 512:(t + 1) * 512], ['QTn%d' % hh], ['QB%d' % qb])
                            DMA('sp', QA[qb][64:128, :], selT[t][0:64, :], ['selT%d' % t], ['QA%d' % qb])
                            CP('pool', QB[qb][64:128, :], selT[t][64:128, :], ['selT%d' % t], ['QB%d' % qb])
                            nsl = 16 * (t + 1)
                            tb = 4 + (cnt['grp'] % 2)
                            cnt['grp'] += 1
                            for s_ in range(nsl):
                                near = s_ >= 16 * t - 1
                                bias_ap = strip[:, 128 * (15 - (s_ - 16 * t)):128 * (15 - (s_ - 16 * t)) + 512] if near else None
                                qq = QA[qb] if s_ < 32 else QB[qb]
                                qk = ('QA%d' if s_ < 32 else 'QB%d') % qb
                                aft = None
                                if s_ == nsl - 1:
                                    def aft(t=t, hh=hh, n=n, tb=tb):
                                        fin_branch(t, hh, n, 1, 64, 0, False)
                                attn_slot(Kbuf[:, s_ * 128:(s_ + 1) * 128], qq[:, :], ['KbufLo', 'KbufHi', qk], bias_ap,
                                          Vbuf[:, s_, :], ['Vbuf'], 65, s_ == 0, s_ == nsl - 1, after=aft)
                    flush()
                    DMA('sp', Kbuf[0:64, :], S['nkT'][3, g * 64:(g + 1) * 64, :], [], ['KbufLo'])
                    for q4 in range(4):
                        DMA('sp', Vbuf[:, q4 * 16:(q4 + 1) * 16, 0:64],
                            S['nv'][1, g, q4 * 2048:(q4 + 1) * 2048, :].rearrange('(s p) e -> p s e', p=128), [], ['Vbuf'])
                    DMA('sp', neg[:], I['wneg'], [], ['neg'])
                    for hh in range(4):
                        n = 4 * g + hh
                        DMA('sp', strip[:], I['raw_nsa'][n], [], ['strip'])
                        STT('dve', strip[:], strip[:], cns[:, n:n + 1], neg[:], ALU.subtract, ALU.add,
                            ['strip', 'cns', 'neg'], ['strip'])
                        for t in range(4):
                            s0 = max(0, 16 * t - 4)
                            s1 = 16 * t + 15
                            tb = 4 + (cnt['grp'] % 2)
                            cnt['grp'] += 1
                            qz = cnt['grp'] % 2
                            CP('pool', QZ[qz][0:64, :], QTn[hh][:, t * 512:(t + 1) * 512], ['QTn%d' % hh], ['QZ%d' % qz])
                            for s_ in range(s0, s1 + 1):
                                x0 = 128 * (15 - (s_ - 16 * t))
                                aft = None
                                if s_ == s1:
                                    def aft(t=t, hh=hh, n=n, tb=tb):
                                        fin_branch(t, hh, n, 2, 64, 0, False)
                                attn_slot(Kbuf[:, s_ * 128:(s_ + 1) * 128], QZ[qz][:, :],
                                          ['KbufLo', 'KbufHi', 'QZ%d' % qz], strip[:, x0:x0 + 512], Vbuf[:, s_, :], ['Vbuf'], 65,
                                          s_ == s0, s_ == s1, after=aft)
                    flush()
                    for blk in range(16):
                        for pr in range(2):
                            f = cnt['fin'] % 2
                            cnt['fin'] += 1
                            CP('pool', onb[f][:].rearrange('p (h e) -> p h e', h=2), onsa[:, blk, 2 * pr:2 * pr + 2, :],
                               ['onsa'], ['onb%d' % f])
                            pi = 3
                            MM(ps[pi][:, 0:128], onb[f][:], identb[:], True, True, ['onb%d' % f, 'identb'], ['ps%d' % pi])
                            CP('act', nT[:, 2 * g + pr, blk * 128:(blk + 1) * 128], ps[pi][:, 0:128], ['ps%d' % pi], ['nT'])
                P.barrier()
                P.emit()


        def phase_e1():
            with ExitStack() as ph:
                xs = [sbt(ph, 'E_xs%d' % i, [128, 2, 512], F32) for i in range(2)]
                xb = sbt(ph, 'E_xb', [128, 16, 2048], BF16)
                wgs = [sbt(ph, 'E_wgs%d' % i, [128, 16, 128], F32) for i in range(2)]
                wbs = [sbt(ph, 'E_wbs%d' % i, [128, 8, 128], F32) for i in range(2)]
                wga = [sbt(ph, 'E_wga%d' % i, [128, 16, 128], BF16) for i in range(2)]
                wgn = [sbt(ph, 'E_wgn%d' % i, [128, 16, 128], BF16) for i in range(2)]
                wba = [sbt(ph, 'E_wba%d' % i, [128, 8, 128], BF16) for i in range(2)]
                wbn = [sbt(ph, 'E_wbn%d' % i, [128, 8, 128], BF16) for i in range(2)]
                sA = [sbt(ph, 'E_sA%d' % i, [128, 512], F32) for i in range(2)]
                sB = [sbt(ph, 'E_sB%d' % i, [128, 512], F32) for i in range(2)]
                m1 = [sbt(ph, 'E_m1%d' % i, [128, 512], F32) for i in range(2)]
                mx = [sbt(ph, 'E_mx%d' % i, [128, 512], BF16) for i in range(3)]
                k = 0
                for tile in range(4):
                    for qq in range(8):
                        half = k % 2
                        k += 1
                        DMA('sp', xs[half][:], I['xTo'][tile, :, qq * 2:(qq + 1) * 2, :], [], ['xs%d' % half])
                        CP('dve' if qq % 2 == 0 else 'act', xb[:, qq * 2:(qq + 1) * 2, tile * 512:(tile + 1) * 512], xs[half][:],
                           ['xs%d' % half], ['xb'])
                it = 0
                mi = 0
                for c in range(16):
                    b = c % 2
                    DMA('sp', wgs[0][:], I['w_mg'][:, :, c * 128:(c + 1) * 128], [], ['wgs0'])
                    CP('pool', wga[b][:], wgs[0][:], ['wgs0'], ['wga%d' % b])
                    DMA('sp', wgs[1][:], I['w_mg'][:, :, 2048 + c * 128:2048 + (c + 1) * 128], [], ['wgs1'])
                    CP('dve', wgn[b][:], wgs[1][:], ['wgs1'], ['wgn%d' % b])
                    DMA('sp', wbs[0][:], I['w_bda'][:, :, c * 128:(c + 1) * 128], [], ['wbs0'])
                    CP('pool', wba[b][:], wbs[0][:], ['wbs0'], ['wba%d' % b])
                    DMA('sp', wbs[1][:], I['w_bnsa'][:, :, c * 128:(c + 1) * 128], [], ['wbs1'])
                    CP('dve', wbn[b][:], wbs[1][:], ['wbs1'], ['wbn%d' % b])
                    for tile in range(4):
                        tsl = slice(tile * 512, (tile + 1) * 512)
                        p2 = it % 2
                        it += 1
                        pA, pB, pC, pD = (4 * p2 + 0), (4 * p2 + 1), (4 * p2 + 2), (4 * p2 + 3)
                        for kc in range(16):
                            MM(ps[pA][:, :], wga[b][:, kc, :], xb[:, kc, tsl], kc == 0, kc == 15, ['wga%d' % b, 'xb'], ['ps%d' % pA])
                        for kc in range(16):
                            MM(ps[pB][:, :], wgn[b][:, kc, :], xb[:, kc, tsl], kc == 0, kc == 15, ['wgn%d' % b, 'xb'], ['ps%d' % pB])
                        for kc in range(8):
                            MM(ps[pC][:, :], wba[b][:, kc, :], aT[:, kc, tsl], kc == 0, kc == 7, ['wba%d' % b, 'aT'], ['ps%d' % pC])
                        for kc in range(8):
                            MM(ps[pD][:, :], wbn[b][:, kc, :], nT[:, kc, tsl], kc == 0, kc == 7, ['wbn%d' % b, 'nT'], ['ps%d' % pD])
                        ACT(sA[p2][:], ps[pA][:, :], AF.Sigmoid, ['ps%d' % pA], ['sA%d' % p2])
                        ACT(sB[p2][:], ps[pB][:, :], AF.Sigmoid, ['ps%d' % pB], ['sB%d' % p2])
                        TT('dve', m1[p2][:], sA[p2][:], ps[pC][:, :], ALU.mult, ['sA%d' % p2, 'ps%d' % pC], ['m1%d' % p2])
                        TT('dve', sB[p2][:], sB[p2][:], ps[pD][:, :], ALU.mult, ['sB%d' % p2, 'ps%d' % pD], ['sB%d' % p2])
                        m3 = mi % 3
                        mi += 1
                        TT('pool', mx[m3][:], m1[p2][:], sB[p2][:], ALU.add, ['m1%d' % p2, 'sB%d' % p2], ['mx%d' % m3])
                        DMA('pool', S['mixT'][tile, :, c, :], mx[m3][:], ['mx%d' % m3], [])
                P.barrier()
                P.emit()

        def phase_e2():
            with ExitStack() as ph:
                wos = sbt(ph, 'F_wos', [128, 4, 512], F32)
                wob = [sbt(ph, 'F_wob%d' % i, [128, 16, 512], BF16) for i in range(2)]
                mxt = sbt(ph, 'F_mxt', [128, 16, 512], BF16)
                xc = [sbt(ph, 'F_xc%d' % i, [128, 512], F32) for i in range(3)]
                hpre2 = [sbt(ph, 'F_hpre%d' % i, [128, 4, 2048], F32) for i in range(2)]
                junk = sbt(ph, 'F_junk', [128, 2048], F32)
                hn = [sbt(ph, 'F_hn%d' % i, [128, 2048], F32) for i in range(2)]
                hb = sbt(ph, 'F_hb', [128, 2048], BF16)
                gB = sbt(ph, 'F_gB', [128, 2048], F32)
                bB = sbt(ph, 'F_bB', [128, 2048], F32)
                st = [sbt(ph, 'F_st%d' % i, [128, 4], F32) for i in range(2)]
                hTt = sbt(ph, 'F_hTt', [128, 16, 512], BF16)
                DMA('sp', gB[:], I['ln1g'], [], ['gB'])
                DMA('sp', bB[:], I['ln1b'], [], ['bB'])
                wi = 0
                xi = 0
                for tile in range(4):
                    hpre = hpre2[tile % 2]
                    hk = 'hpre%d_' % (tile % 2)
                    DMA('sp', mxt[:], S['mixT'][tile], [], ['mxt'])
                    for dc in range(4):
                        b = wi % 2
                        wi += 1
                        for k4 in range(4):
                            DMA('sp', wos[:], I['w_out'][:, k4 * 4:(k4 + 1) * 4, dc * 512:(dc + 1) * 512], [], ['wos'])
                            CP('act' if k4 % 2 == 0 else 'pool', wob[b][:, k4 * 4:(k4 + 1) * 4, :], wos[:], ['wos'], ['wob%d' % b])
                        for sub in range(4):
                            blk = tile * 4 + sub
                            pi = nextps()
                            for kc in range(16):
                                MM(ps[pi][:, :], mxt[:, kc, sub * 128:(sub + 1) * 128], wob[b][:, kc, :], kc == 0, kc == 15,
                                   ['mxt', 'wob%d' % b], ['ps%d' % pi])
                            x3 = xi % 3
                            xi += 1
                            DMA('sp', xc[x3][:], I['xo'][blk, :, dc * 512:(dc + 1) * 512], [], ['xc%d' % x3])
                            STT('dve', hpre[:, sub, dc * 512:(dc + 1) * 512], xc[x3][:], ALPHA, ps[pi][:, :], ALU.mult, ALU.add,
                                ['xc%d' % x3, 'ps%d' % pi], [hk + str(sub)])
                    for sub in range(4):
                        blk = tile * 4 + sub
                        layer_norm_block(hpre[:, sub, :], hk + str(sub), gB, bB, junk, hn, st, blk)
                        f = blk % len(hn)
                        DMA('pool', S['h'][blk], hn[f][:], ['hn%d' % f], [])
                        CP('pool', hb[:], hn[f][:], ['hn%d' % f], ['hb'])
                        for f4 in range(4):
                            pi = nextps()
                            for ff in range(4):
                                fc = f4 * 4 + ff
                                MM(ps[pi][:, ff * 128:(ff + 1) * 128], hb[:, fc * 128:(fc + 1) * 128], identb[:], True, True,
                                   ['hb', 'identb'], ['ps%d' % pi])
                            CP('act', hTt[:, f4 * 4:(f4 + 1) * 4, sub * 128:(sub + 1) * 128],
                               ps[pi][:, :].rearrange('p (a b) -> p a b', a=4), ['ps%d' % pi], ['hTt'])
                    DMA('pool', S['hT'][tile], hTt[:], ['hTt'], [])
                P.barrier()
                P.emit()

        def layer_norm_block(src, srck, gB_, bB_, junk, hn, st, blk, junkk='junk'):
            f = blk % len(hn)
            sk = 'st%d' % f
            P.op('dve', lambda e: e.tensor_reduce(st[f][:, 0:1], src, AX.X, ALU.add), [srck], [sk])
            TS('dve', st[f][:, 0:1], st[f][:, 0:1], 1.0 / 2048.0, None, ALU.mult, None, [sk], [sk])
            TS('dve', src, src, st[f][:, 0:1], None, ALU.subtract, None, [srck, sk], [srck])
            TT('pool', junk[:], src, src, ALU.mult, [srck], [junkk])
            P.op('dve', lambda e: e.tensor_reduce(st[f][:, 1:2], junk[:], AX.X, ALU.add), [junkk], [sk])
            TS('dve', st[f][:, 1:2], st[f][:, 1:2], 1.0 / 2048.0, 1e-5, ALU.mult, ALU.add, [sk], [sk])
            ACT(st[f][:, 1:2], st[f][:, 1:2], AF.Sqrt, [sk], [sk])
            P.op('dve', lambda e: e.reciprocal(st[f][:, 2:3], st[f][:, 1:2]), [sk], [sk])
            STT('dve', hn[f][:], src, st[f][:, 2:3], gB_[:], ALU.mult, ALU.mult, [srck, sk, 'gB'], ['hn%d' % f])
            TT('pool', hn[f][:], hn[f][:], bB_[:], ALU.add, ['hn%d' % f, 'bB'], ['hn%d' % f])


        def p0_units(us, ub, engs):
            units = []
            for ec in range(128):
                for which in range(2):
                    def unit(ec=ec, which=which, k=len(units)):
                        b = k % len(us)
                        src = I['puT'][ec].rearrange('p a b -> p (a b)') if which == 0 else I['pv'][ec]
                        dst = S['puT'][ec].rearrange('p a b -> p (a b)') if which == 0 else S['pv'][ec]
                        DMA('sp', us[b][:], src, [], ['us%d' % b])
                        CP(engs[k % len(engs)], ub[b][:], us[b][:], ['us%d' % b], ['ub%d' % b])
                        DMA('pool', dst, ub[b][:], ['ub%d' % b], [])
                    units.append(unit)
            return units

        def phase_p0():
            with ExitStack() as ph:
                us = [sbt(ph, 'P_us%d' % i, [128, 2048], F32) for i in range(3)]
                ub = [sbt(ph, 'P_ub%d' % i, [128, 2048], BF16) for i in range(3)]
                for u in p0_units(us, ub, ['dve', 'act', 'pool']):
                    u()
                P.barrier()
                P.emit()

        def phase_peer(fin_evs):
            with ExitStack() as ph:
                hTt = sbt(ph, 'G_hTt', [128, 16, 512], BF16)
                wqs = [sbt(ph, 'G_wqs%d' % i, [128, 16, 64], F32) for i in range(2)]
                wqb = [sbt(ph, 'G_wqb%d' % i, [128, 16, 64], BF16) for i in range(2)]
                sks = sbt(ph, 'G_sks', [64, 2, 128], F32)
                skb = sbt(ph, 'G_skb', [64, 2, 128], BF16)
                qTu = [sbt(ph, 'G_qTu%d' % i, [64, 512], BF16) for i in range(2)]
                sAll = sbt(ph, 'G_sAll', [128, 4, 16, 128], F32)
                tau = sbt(ph, 'G_tau', [128, 4, 8], F32)
                negc = sbt(ph, 'G_negc', [128, 4, 8], F32)
                kap = sbt(ph, 'G_kap', [128, 4, 8], F32)
                m1 = [sbt(ph, 'G_m1%d' % i, [128, 16], F32) for i in range(4)]
                m2 = [sbt(ph, 'G_m2%d' % i, [128, 16], F32) for i in range(4)]
                mc = [sbt(ph, 'G_mc%d' % i, [128, 16], F32) for i in range(4)]
                t1 = [sbt(ph, 'G_t1%d' % i, [128, 256], F32) for i in range(4)]
                cand = [sbt(ph, 'G_cand%d' % i, [128, 256], F32) for i in range(4)]
                sm = [sbt(ph, 'G_sm%d' % i, [128, 4], F32) for i in range(4)]
                e16 = [sbt(ph, 'G_e16%d' % i, [128, 16], F32) for i in range(4)]
                eb = [sbt(ph, 'G_e%d' % i, [128, 4, 128], F32) for i in range(5)]
                Wall = [sbt(ph, 'G_W%d' % i, [128, 8, 4, 128], BF16) for i in range(2)]
                GT = [sbt(ph, 'G_GT%d' % i, [128, 4, 512], BF16) for i in range(3)]
                Gs = [sbt(ph, 'G_Gs%d' % i, [128, 512], BF16) for i in range(2)]
                uch = [sbt(ph, 'G_uch%d' % i, [128, 16, 128], BF16) for i in range(2)]
                vch = [sbt(ph, 'G_vch%d' % i, [128, 4, 2048], BF16) for i in range(2)]
                ga = [sbt(ph, 'G_ga%d' % i, [128, 512], F32) for i in range(4)]
                GA = [sbt(ph, 'G_GA%d' % i, [128, 4, 512], BF16) for i in range(2)]
                acc = sbt(ph, 'G_acc', [128, 4, 2048], F32)
                DMA('sp', sks[:], I['skT'], [], ['sks'])
                CP('dve', skb[:], sks[:], ['sks'], ['skb'])
                c_ = {'u': 0, 'k': 0, 'w': 0, 's': 0, 'v': 0, 'g': 0}
                for tile in range(4):
                    DMA('sp', hTt[:], S['hT'][tile], [], ['hTt'])
                    for u in range(16):
                        b = c_['u'] % 2
                        c_['u'] += 1
                        DMA('sp', wqs[b][:], I['wq'][u], [], ['wqs%d' % b])
                        CP('act', wqb[b][:], wqs[b][:], ['wqs%d' % b], ['wqb%d' % b])
                        pi = nextps()
                        for kc in range(16):
                            MM(ps[pi][0:64, :], wqb[b][:, kc, :], hTt[:, kc, :], kc == 0, kc == 15, ['wqb%d' % b, 'hTt'], ['ps%d' % pi])
                        CP('act', qTu[b][:], ps[pi][0:64, :], ['ps%d' % pi], ['qTu%d' % b])
                        pi = nextps()
                        for blk in range(4):
                            MM(ps[pi][:, blk * 128:(blk + 1) * 128], qTu[b][:, blk * 128:(blk + 1) * 128], skb[:, u % 2, :], True, True,
                               ['qTu%d' % b, 'skb'], ['ps%d' % pi])
                        CP('dve', sAll[:, :, u, :], ps[pi][:, :].rearrange('p (a b) -> p a b', a=4), ['ps%d' % pi], ['sAll'])
                    def chain(blk, h, f):
                        steps = []
                        s1 = sAll[:, blk, 2 * h, :]
                        s2 = sAll[:, blk, 2 * h + 1, :]
                        for (sx, mm_, mk, tk_, tt_) in ((s1, m1[f], 'm1_%d' % f, 't1a_%d' % f, t1[f][:, 0:128]),
                                                        (s2, m2[f], 'm2_%d' % f, 't1b_%d' % f, t1[f][:, 128:256])):
                            steps.append(lambda sx=sx, mm_=mm_, mk=mk: P.op('dve', lambda e: e.max(out=mm_[:, 0:8], in_=sx), ['sAll'], [mk]))
                            steps.append(lambda sx=sx, mm_=mm_, mk=mk, tk_=tk_, tt_=tt_: P.op(
                                'dve', lambda e: e.match_replace(out=tt_, in_to_replace=mm_[:, 0:8], in_values=sx, imm_value=-3.0e38),
                                ['sAll', mk], [tk_]))
                            steps.append(lambda mm_=mm_, mk=mk, tk_=tk_, tt_=tt_: P.op(
                                'dve', lambda e: e.max(out=mm_[:, 8:16], in_=tt_), [tk_], [mk]))
                        steps.append(lambda: TT('pool', cand[f][:].rearrange('p (a b) -> p a b', a=16),
                                                m1[f][:].unsqueeze(2).broadcast_to([128, 16, 16]), m2[f][:].unsqueeze(1).broadcast_to([128, 16, 16]),
                                                ALU.add, ['m1_%d' % f, 'm2_%d' % f], ['cand%d' % f]))
                        steps.append(lambda: P.op('dve', lambda e: e.max(out=mc[f][:, 0:8], in_=cand[f][:]), ['cand%d' % f], ['mc%d' % f]))
                        steps.append(lambda: P.op('dve', lambda e: e.match_replace(out=t1[f][:], in_to_replace=mc[f][:, 0:8], in_values=cand[f][:],
                                                                                  imm_value=-3.0e38),
                                                  ['cand%d' % f, 'mc%d' % f, 't1a_%d' % f, 't1b_%d' % f], ['t1a_%d' % f, 't1b_%d' % f]))
                        steps.append(lambda: P.op('dve', lambda e: e.max(out=mc[f][:, 8:16], in_=t1[f][:]), ['t1a_%d' % f, 't1b_%d' % f], ['mc%d' % f]))
                        steps.append(lambda: CP('dve', tau[:, blk, h:h + 1], mc[f][:, 15:16], ['mc%d' % f], ['tau']))
                        steps.append(lambda: TS('dve', sm[f][:, 0:1], mc[f][:, 0:1], -1.0, None, ALU.mult, None, ['mc%d' % f], ['sm%d' % f]))
                        steps.append(lambda: ACT(e16[f][:], mc[f][:], AF.Exp, ['mc%d' % f, 'sm%d' % f], ['e16_%d' % f], bias=sm[f][:, 0:1]))
                        steps.append(lambda: P.op('dve', lambda e: e.tensor_reduce(sm[f][:, 1:2], e16[f][:], AX.X, ALU.add), ['e16_%d' % f], ['sm%d' % f]))
                        steps.append(lambda: ACT(sm[f][:, 2:3], sm[f][:, 1:2], AF.Ln, ['sm%d' % f], ['sm%d' % f]))
                        steps.append(lambda: TT('dve', negc[:, blk, h:h + 1], sm[f][:, 0:1], sm[f][:, 2:3], ALU.subtract, ['sm%d' % f], ['negc%d' % f]))
                        steps.append(lambda: TT('dve', sm[f][:, 3:4], mc[f][:, 15:16], negc[:, blk, h:h + 1], ALU.add,
                                                ['mc%d' % f, 'negc%d' % f], ['sm%d' % f]))
                        steps.append(lambda: ACT(sm[f][:, 3:4], sm[f][:, 3:4], AF.Exp, ['sm%d' % f], ['sm%d' % f]))
                        steps.append(lambda: TS('dve', kap[:, blk, h:h + 1], sm[f][:, 3:4], 0.9999, None, ALU.mult, None, ['sm%d' % f], ['kap']))
                        steps.append(lambda: TS('dve', sAll[:, blk, 2 * h, :], sAll[:, blk, 2 * h, :], negc[:, blk, h:h + 1], None, ALU.add, None,
                                                ['negc%d' % f, 'm1_%d' % f, 't1a_%d' % f], ['sAllw%d' % f]))
                        return steps
                    pairs = [(blk, h) for blk in range(4) for h in range(8)]
                    for g4 in range(0, 32, 4):
                        chains = [chain(blk, h, f) for f, (blk, h) in enumerate(pairs[g4:g4 + 4])]
                        for i in range(len(chains[0])):
                            for ch in chains:
                                ch[i]()
                    P.op('dve', lambda e: e.tensor_copy(sm[0][:, 0:1], sm[0][:, 0:1]),
                         ['sAllw0', 'sAllw1', 'sAllw2', 'sAllw3', 'negc0', 'negc1', 'negc2', 'negc3', 'sm0'], ['sAll', 'negc', 'sm0'])
                    def opsA(eg, blk, h):
                        wb_ = (4 * eg + blk) % 2
                        sb_ = c_['s'] % 5
                        c_['s'] += 1
                        if h < NPOOL:
                            TT('pool', eb[sb_][:],
                               sAll[:, blk, 2 * h, 4 * eg:4 * eg + 4].unsqueeze(2).broadcast_to([128, 4, 128]),
                               sAll[:, blk, 2 * h + 1, :].unsqueeze(1).broadcast_to([128, 4, 128]),
                               ALU.add, ['sAll'], ['e%d' % sb_])
                            ACT(eb[sb_][:], eb[sb_][:], AF.Exp, ['e%d' % sb_], ['e%d' % sb_])
                        else:
                            for c in range(4):
                                ACT(eb[sb_][:, c, :], sAll[:, blk, 2 * h + 1, :], AF.Exp, ['sAll'], ['e%d' % sb_],
                                    bias=sAll[:, blk, 2 * h, 4 * eg + c:4 * eg + c + 1])
                        STT('dve', Wall[wb_][:, h, :, :], eb[sb_][:], kap[:, blk, h:h + 1], eb[sb_][:], ALU.is_ge, ALU.mult,
                            ['kap', 'e%d' % sb_], ['W%d' % wb_])

                    def stageB(eg, blk):
                        k = 4 * eg + blk
                        wb_ = k % 2
                        pb = k % 2
                        for c in range(4):
                            for h in range(8):
                                MM(ps[pb][:, c * 128:(c + 1) * 128], Wall[wb_][:, h, c, :], identb[:], h == 0, h == 7,
                                   ['W%d' % wb_, 'identb'], ['ps%d' % pb])
                        def evac(eg=eg, blk=blk, pb=pb):
                            CP('act', GT[eg % 3][:, :, blk * 128:(blk + 1) * 128], ps[pb][:, :].rearrange('p (a b) -> p a b', a=4),
                               ['ps%d' % pb], ['GT%d' % (eg % 3)])
                        pend.append(evac)

                    def stageC_pe(eg, c):
                        ec = 4 * eg + c
                        u2 = c % 2
                        DMA('sp', uch[u2][:], S['puT'][ec], [], ['uch%d' % u2])
                        pa = 2 + c
                        for kc in range(16):
                            MM(ps[pa][:, :], uch[u2][:, kc, :], hTt[:, kc, :], kc == 0, kc == 15, ['uch%d' % u2, 'hTt'], ['ps%d' % pa])

                    def stageC_post(eg):
                        gb = eg % 2
                        for c in range(4):
                            ACT(ga[c][:], ps[2 + c][:, :], AF.Gelu_apprx_tanh, ['ps%d' % (2 + c)], ['ga%d' % c])
                        for c in range(4):
                            TT('pool', GA[gb][:, c, :], ga[c][:], GT[eg % 3][:, c, :], ALU.mult, ['ga%d' % c, 'GT%d' % (eg % 3)], ['GA%d' % gb])

                    def stageD1(eg, blk, dc):
                        gb = eg % 2
                        vb = eg % 2
                        pv_ = 6 + (c_['v'] % 2)
                        c_['v'] += 1
                        for c in range(4):
                            MM(ps[pv_][:, :], GA[gb][:, c, blk * 128:(blk + 1) * 128], vch[vb][:, c, dc * 512:(dc + 1) * 512],
                               c == 0, c == 3, ['GA%d' % gb, 'vch%d' % vb], ['ps%d' % pv_])
                        if eg == 0:
                            CP('dve', acc[:, blk, dc * 512:(dc + 1) * 512], ps[pv_][:, :], ['ps%d' % pv_], ['acc%d' % blk])
                        else:
                            TT('dve', acc[:, blk, dc * 512:(dc + 1) * 512], acc[:, blk, dc * 512:(dc + 1) * 512], ps[pv_][:, :],
                               ALU.add, ['ps%d' % pv_, 'acc%d' % blk], ['acc%d' % blk])

                    pend = []
                    for it in range(35):
                        doA = it < 32
                        if 2 <= it <= 33:
                            eg_ = it - 2
                            DMA('sp', vch[eg_ % 2][:], S['pv'][4 * eg_:4 * eg_ + 4].rearrange('c p d -> p c d'), [], ['vch%d' % (eg_ % 2)])
                        for blk in range(4):
                            for h in range(8):
                                if doA:
                                    opsA(it, blk, h)
                                if h == 3 or not doA:
                                    while pend:
                                        pend.pop(0)()
                                if blk == 0 and h == 3 and 2 <= it <= 33:
                                    stageC_post(it - 2)
                                if h % 2 == 1 and 3 <= it <= 34:
                                    stageD1(it - 3, blk, h // 2)
                            if 1 <= it <= 32:
                                stageC_pe(it - 1, blk)
                            if doA:
                                stageB(it, blk)
                    DMA('pool', S['pe'][tile * 4:(tile + 1) * 4].rearrange('b p d -> p b d'), acc[:], ['acc0', 'acc1', 'acc2', 'acc3'], [])
                P.barrier()
                P.emit()


        def phase_g2(fin_evs):
            with ExitStack() as ph:
                hblk = [sbt(ph, 'H_hblk%d' % i, [128, 2048], F32) for i in range(2)]
                pblk = [sbt(ph, 'H_pblk%d' % i, [128, 2048], F32) for i in range(2)]
                junk = sbt(ph, 'H_junk', [128, 2048], F32)
                hn = [sbt(ph, 'H_hn%d' % i, [128, 2048], F32) for i in range(2)]
                gB = sbt(ph, 'H_gB', [128, 2048], F32)
                bB = sbt(ph, 'H_bB', [128, 2048], F32)
                st = [sbt(ph, 'H_st%d' % i, [128, 4], F32) for i in range(2)]
                DMA('sp', gB[:], I['ln2g'], [], ['gB'])
                DMA('sp', bB[:], I['ln2b'], [], ['bB'])
                for gblk in range(16):
                    f = gblk % 2
                    DMA('sp', hblk[f][:], S['h'][gblk], [], ['hblk%d' % f])
                    DMA('sp', pblk[f][:], S['pe'][gblk], [], ['pblk%d' % f])
                    STT('dve', pblk[f][:], hblk[f][:], ALPHA, pblk[f][:], ALU.mult, ALU.add, ['hblk%d' % f, 'pblk%d' % f], ['pblk%d' % f])
                    layer_norm_block(pblk[f][:], 'pblk%d' % f, gB, bB, junk, hn, st, gblk)
                    fin_evs.append(DMA('pool', out[gblk], hn[f][:], ['hn%d' % f], []))
                P.barrier()
                P.emit()

        if 'p0' in phases:
            phase_p0()
        if 'kv0' in phases:
            phase_kv(0)
        if 'kv1' in phases:
            phase_kv(1)
        if 'q' in phases:
            phase_q()
        if 'da' in phases:
            phase_da()
        if 'nsa' in phases:
            phase_nsa()
        if 'e1' in phases:
            phase_e1()
        fin_evs = []
        if dbg and dbg_src in ('aT', 'nT'):
            fin_evs.append(DMA('pool', dbg_out, (aT if dbg_src == 'aT' else nT)[:], ['aT', 'nT'], []))
            fin_evs.append(DMA('pool', dbg2_out, dbg2sb[:], ['dbg2sb'], []))
            P.barrier()
            P.emit()
        mid.close()
        if 'e2' in phases:
            phase_e2()
        if 'peer' in phases:
            phase_peer(fin_evs)
        if 'g2' in phases:
            phase_g2(fin_evs)
        for ev in fin_evs:
            pass
        P.ops['sp'].append((None, [ev for ev in fin_evs], None, 0))
        P.emit()
    return nc


def rel_bucket_np(dist):
    n = np.maximum(dist, 0)
    nf = np.maximum(n, 1).astype(np.float32)
    large = 16 + (np.log(nf / np.float32(16)) / np.float32(math.log(8.0)) * np.float32(16)).astype(np.int32)
    large = np.minimum(large, 31)
    return np.where(n < 16, n, large)


def prep_inputs(inputs):
    x = np.asarray(inputs['x'], np.float32)
    w_in = np.asarray(inputs['w_in'], np.float32)[0]
    rel = np.asarray(inputs['rel_bias'], np.float32)
    wr = np.ascontiguousarray(w_in.reshape(16, 128, 9776).transpose(1, 0, 2))
    common = {
        'w_dakv': np.ascontiguousarray(wr[:, :, 1024:3072]),
        'w_nkv': np.ascontiguousarray(wr[:, :, 4096:5632]),
        'w_q': np.ascontiguousarray(np.concatenate([wr[:, :, 0:1024], wr[:, :, 3072:4096]], axis=2)),
        'w_gate': np.ascontiguousarray(wr[:, :, 5632:5680]),
        'w_mg': np.ascontiguousarray(wr[:, :, 5680:9776]),
        'c_da': np.ascontiguousarray(np.broadcast_to(rel[31, 0:8][None, :], (128, 8))),
        'lamq': np.ascontiguousarray(np.broadcast_to(np.asarray(inputs['da_lam_q'], np.float32)[0].reshape(1, 128), (128, 128))),
        'lamk': np.ascontiguousarray(np.broadcast_to(np.asarray(inputs['da_lam_k'], np.float32)[0].reshape(1, 128), (128, 128))),
        'subg': np.ascontiguousarray(np.broadcast_to(np.asarray(inputs['da_subln_g'], np.float32)[0].reshape(1, 128), (128, 128))),
        'ident': np.eye(128, dtype=np.float32),
        'w_bda': np.ascontiguousarray(np.asarray(inputs['w_branch_da'], np.float32)[0].reshape(8, 128, 2048).transpose(1, 0, 2)),
        'w_bnsa': np.ascontiguousarray(np.asarray(inputs['w_branch_nsa'], np.float32)[0].reshape(8, 128, 2048).transpose(1, 0, 2)),
        'w_out': np.ascontiguousarray(np.asarray(inputs['w_out'], np.float32)[0].reshape(16, 128, 2048).transpose(1, 0, 2)),
        'ln1g': np.ascontiguousarray(np.broadcast_to(np.asarray(inputs['ln1_g'], np.float32)[0][None, :], (128, 2048))),
        'ln1b': np.ascontiguousarray(np.broadcast_to(np.asarray(inputs['ln1_b'], np.float32)[0][None, :], (128, 2048))),
        'ln2g': np.ascontiguousarray(np.broadcast_to(np.asarray(inputs['ln2_g'], np.float32)[0][None, :], (128, 2048))),
        'ln2b': np.ascontiguousarray(np.broadcast_to(np.asarray(inputs['ln2_b'], np.float32)[0][None, :], (128, 2048))),
        'wq': np.ascontiguousarray(np.asarray(inputs['peer_wq'], np.float32)[0].reshape(16, 128, 16, 64).transpose(2, 1, 0, 3)),
        'skT': np.ascontiguousarray(np.stack([np.asarray(inputs['peer_subkey1'], np.float32)[0].T,
                                              np.asarray(inputs['peer_subkey2'], np.float32)[0].T], axis=1)),
        'puT': np.ascontiguousarray(np.asarray(inputs['peer_u'], np.float32)[0].reshape(128, 128, 16, 128).transpose(0, 3, 2, 1)),
        'pv': np.ascontiguousarray(np.asarray(inputs['peer_v'], np.float32)[0].reshape(128, 128, 2048)),
        'c_nsa': np.ascontiguousarray(np.broadcast_to(rel[31, 8:24][None, :], (128, 16))),
        'w1k': np.ascontiguousarray(np.asarray(inputs['cmp_w1_k'], np.float32)[0].reshape(32, 64, 256).transpose(1, 0, 2)),
        'w1v': np.ascontiguousarray(np.asarray(inputs['cmp_w1_v'], np.float32)[0].reshape(32, 64, 256).transpose(1, 0, 2)),
        'w2k': np.ascontiguousarray(np.asarray(inputs['cmp_w2_k'], np.float32)[0].reshape(2, 128, 64).transpose(1, 0, 2)),
        'w2v': np.ascontiguousarray(np.asarray(inputs['cmp_w2_v'], np.float32)[0].reshape(2, 128, 64).transpose(1, 0, 2)),
        'pekT': np.ascontiguousarray(np.asarray(inputs['cmp_pe_k'], np.float32)[0].T),
        'pevT': np.ascontiguousarray(np.asarray(inputs['cmp_pe_v'], np.float32)[0].T),
    }
    import ml_dtypes
    cidx = np.arange(512)
    sidx = np.arange(128)
    ov = ((cidx[:, None] * 16 <= sidx[None, :] * 64 + 63) & (cidx[:, None] * 16 + 31 >= sidx[None, :] * 64)).astype(np.float32)
    ovl = np.concatenate([ov, np.ones((512, 1), np.float32)], axis=1)
    ovl[511] = 0.0
    common['ovl'] = np.ascontiguousarray(ovl.reshape(4, 128, 129).transpose(1, 0, 2))
    kk = np.arange(8192)
    common['onehot'] = (((kk[None, :] // 64) % 64) == np.arange(64)[:, None]).astype(ml_dtypes.bfloat16)
    xTs = []
    for b in range(2):
        xTs.append(np.ascontiguousarray(x[b].reshape(16, 512, 16, 128).transpose(0, 3, 2, 1)))
    in_maps = []
    kl = np.arange(128)[:, None]
    xx = np.arange(2944)[None, :]
    for c in range(8):
        b, j = c // 4, c % 4
        tiles = [4 * t + j for t in range(4)]
        m = dict(common)
        m['xT'] = xTs[b]
        m['xTo'] = np.ascontiguousarray(xTs[b][tiles])
        m['xo'] = np.ascontiguousarray(
            np.concatenate([x[b, 512 * T:512 * (T + 1)] for T in tiles], axis=0).reshape(16, 128, 2048))
        d = xx - kl + 512 * j - 1920
        bk = rel_bucket_np(d)
        rb = rel[bk]
        m['raw_da'] = np.ascontiguousarray(rb[:, 0:2560, 0:8].transpose(2, 0, 1))
        m['raw_nsa'] = np.ascontiguousarray(rb[:, :, 8:24].transpose(2, 0, 1))
        m['mneg'] = np.where(d < 0, np.float32(NEGM), np.float32(0.0)).astype(np.float32)
        m['wneg'] = np.where((d < 0) | (d >= 512), np.float32(NEGM), np.float32(0.0)).astype(np.float32)
        cl = np.arange(128)[:, None, None]
        dl = np.arange(2)[None, :, None] - 1
        ql = np.arange(512)[None, None, :]
        m['cm'] = np.where(16 * cl + 31 + 2048 * dl <= 512 * j + ql, np.float32(0.0), np.float32(NEGM)).astype(np.float32)
        qpos = (512 * np.array(tiles)[:, None, None] + 128 * np.arange(4)[None, :, None] + np.arange(128)[None, None, :]).reshape(16, 128)
        cur = qpos // 64
        sb_ = np.arange(128)[None, None, :]
        valid = sb_ <= cur[:, :, None]
        forced = valid & ((sb_ == 0) | (sb_ > cur[:, :, None] - 2))
        vmul = (valid & ~forced).astype(np.float32)
        vadd = np.where(forced, np.float32(1e4) + sb_.astype(np.float32), np.where(valid, np.float32(0.0), np.float32(-1e30))).astype(np.float32)
        m['vmul'] = np.ascontiguousarray(vmul.transpose(1, 0, 2))
        m['vadd'] = np.ascontiguousarray(vadd.transpose(1, 0, 2))
        in_maps.append(m)
    return in_maps


_NC_CACHE = {}


def kernel(**inputs):
    in_maps = prep_inputs(inputs)
    if 'nc' not in _NC_CACHE:
        _NC_CACHE['nc'] = build_program()
    nc = _NC_CACHE['nc']
    res = run_bass_kernel_spmd(nc, in_maps, core_ids=list(range(8)))
    outp = np.zeros((2, 8192, 2048), np.float32)
    for c in range(8):
        b, j = c // 4, c % 4
        o = np.asarray(res.results[c]['out']).reshape(4, 512, 2048)
        for t in range(4):
            T = 4 * t + j
            outp[b, 512 * T:512 * (T + 1)] = o[t]
    return outp
```

```python
import math
from contextlib import ExitStack

import numpy as np
import concourse.bass as bass
import concourse.mybir as mybir
from concourse.bass_utils import run_bass_kernel_spmd

F32 = mybir.dt.float32
BF16 = mybir.dt.bfloat16
AF = mybir.ActivationFunctionType
ALU = mybir.AluOpType
AX = mybir.AxisListType

ENGS = ['pe', 'act', 'dve', 'pool', 'sp']
EPOCH = 16000
RING = {'sp': 40, 'pool': 16}
NEGM = -30000.0
NPOOL = 5
ALPHA = 2.0 ** 0.25
LAM_INIT = 0.8 - 0.6 * math.exp(0.0)


class Prog:
    def __init__(self, nc, stack):
        self.nc = nc
        self.stack = stack
        self.ops = {e: [] for e in ENGS}
        self.cnt = {e: 0 for e in ENGS}
        self.esems = {e: [] for e in ENGS}
        self.rings = {}
        self.ring_pos = {}
        self.ring_use = {}
        for q, n in RING.items():
            self.rings[q] = [stack.enter_context(nc.semaphore('r%s%d' % (q, i))) for i in range(n)]
            self.ring_pos[q] = 0
            self.ring_use[q] = [0] * n
        self.seen = {e: {} for e in ENGS}
        self.lastw = {}
        self.readers = {}
        self.last_ev = {e: None for e in ENGS}

    def _esem(self, eng, epoch):
        while len(self.esems[eng]) <= epoch:
            self.esems[eng].append(self.stack.enter_context(
                self.nc.semaphore('e%s%d' % (eng, len(self.esems[eng])))))
        return self.esems[eng][epoch]

    def op(self, eng, fn, reads=(), writes=(), dma=False):
        deps = {}

        def add(ev):
            if ev is None:
                return
            s, v = ev
            if v > deps.get(id(s), (None, 0))[1]:
                deps[id(s)] = (s, v)
        for k in reads:
            add(self.lastw.get(k))
        for k in writes:
            add(self.lastw.get(k))
            for ev in self.readers.get(k, {}).values():
                add(ev)
        if eng == 'pe':
            for t in self.esems['pe']:
                deps.pop(id(t), None)
        if dma:
            q = eng
            pos = self.ring_pos[q]
            self.ring_pos[q] = (pos + 1) % len(self.rings[q])
            sem = self.rings[q][pos]
            if self.ring_use[q][pos] > 0:
                add((sem, 16 * self.ring_use[q][pos]))
            self.ring_use[q][pos] += 1
            ev = (sem, 16 * self.ring_use[q][pos])
            inc = 16
        else:
            i = self.cnt[eng]
            self.cnt[eng] += 1
            sem = self._esem(eng, i // EPOCH)
            ev = (sem, i % EPOCH + 1)
            inc = 1
            self.last_ev[eng] = ev
        waits = []
        seen = self.seen[eng]
        for s, v in deps.values():
            if seen.get(id(s), 0) < v:
                seen[id(s)] = v
                waits.append((s, v))
        self.ops[eng].append((fn, waits, sem, inc))
        for k in reads:
            self.readers.setdefault(k, {})[(eng, id(sem))] = ev
        for k in writes:
            self.lastw[k] = ev
            self.readers[k] = {}
        return ev

    def barrier(self):
        evs = [ev for ev in self.last_ev.values() if ev is not None]
        for q in self.rings:
            for i, s in enumerate(self.rings[q]):
                if self.ring_use[q][i] > 0:
                    evs.append((s, 16 * self.ring_use[q][i]))
        for eng in ENGS:
            waits = []
            seen = self.seen[eng]
            for s, v in evs:
                if seen.get(id(s), 0) < v:
                    seen[id(s)] = v
                    waits.append((s, v))
            self.ops[eng].append((None, waits, None, 0))
        self.lastw = {}
        self.readers = {}

    def emit(self):
        nc = self.nc
        ops = self.ops
        self.ops = {e: [] for e in ENGS}
        with nc.Block() as block:
            def run(engname):
                def body(e):
                    for fn, waits, sem, inc in ops[engname]:
                        for s, v in waits:
                            e.wait_ge(s, v)
                        if fn is not None:
                            fn(e).then_inc(sem, inc)
                return body
            block.tensor(run('pe'))
            block.scalar(run('act'))
            block.vector(run('dve'))
            block.gpsimd(run('pool'))
            block.sync(run('sp'))


class Ctx:
    pass


def build_program(phases=('p0i', 'kv0', 'kv1', 'q', 'da', 'nsa', 'e1', 'e2', 'peer', 'g2'), dbg=False, dbg_src='aT'):
    nc = bass.Bass("TRN2", target_bir_lowering=False)
    C = Ctx()
    C.nc = nc

    def din(name, shape, dt=F32):
        return nc.dram_tensor(name, list(shape), dt, kind="ExternalInput").ap()

    def dscr(name, shape, dt=BF16):
        return nc.dram_tensor(name, list(shape), dt, kind="Internal").ap()

    I = {}
    I['xT'] = din('xT', [16, 128, 16, 512])
    I['xTo'] = din('xTo', [4, 128, 16, 512])
    I['xo'] = din('xo', [16, 128, 2048])
    I['w_dakv'] = din('w_dakv', [128, 16, 2048])
    I['w_nkv'] = din('w_nkv', [128, 16, 1536])
    I['w_q'] = din('w_q', [128, 16, 2048])
    I['w_gate'] = din('w_gate', [128, 16, 48])
    I['w_mg'] = din('w_mg', [128, 16, 4096])
    I['raw_da'] = din('raw_da', [8, 128, 2560])
    I['mneg'] = din('mneg', [128, 2944])
    I['wneg'] = din('wneg', [128, 2944])
    I['raw_nsa'] = din('raw_nsa', [16, 128, 2944])
    I['c_nsa'] = din('c_nsa', [128, 16])
    I['w1k'] = din('w1k', [64, 32, 256])
    I['w1v'] = din('w1v', [64, 32, 256])
    I['w2k'] = din('w2k', [128, 2, 64])
    I['w2v'] = din('w2v', [128, 2, 64])
    I['pekT'] = din('pekT', [64, 32])
    I['pevT'] = din('pevT', [64, 32])
    I['ovl'] = din('ovl', [128, 4, 129])
    I['cm'] = din('cm', [128, 2, 512])
    I['vmul'] = din('vmul', [128, 16, 128])
    I['vadd'] = din('vadd', [128, 16, 128])
    I['onehot'] = din('onehot', [64, 8192], BF16)
    I['w_bda'] = din('w_bda', [128, 8, 2048])
    I['w_bnsa'] = din('w_bnsa', [128, 8, 2048])
    I['w_out'] = din('w_out', [128, 16, 2048])
    I['ln1g'] = din('ln1g', [128, 2048])
    I['ln1b'] = din('ln1b', [128, 2048])
    I['ln2g'] = din('ln2g', [128, 2048])
    I['ln2b'] = din('ln2b', [128, 2048])
    I['wq'] = din('wq', [16, 128, 16, 64])
    I['skT'] = din('skT', [64, 2, 128])
    I['puT'] = din('puT', [128, 128, 16, 128])
    I['pv'] = din('pv', [128, 128, 2048])
    I['c_da'] = din('c_da', [128, 8])
    I['lamq'] = din('lamq', [128, 128])
    I['lamk'] = din('lamk', [128, 128])
    I['subg'] = din('subg', [128, 128])
    I['ident'] = din('ident', [128, 128])
    out = nc.dram_tensor('out', [16, 128, 2048], F32, kind="ExternalOutput").ap()
    dbg_out = None
    if dbg:
        dbg_out = nc.dram_tensor('dbg', [128, 8, 2048], BF16, kind="ExternalOutput").ap()
        dbg2_out = nc.dram_tensor('dbg2', [128, 4, 258], F32, kind="ExternalOutput").ap()

    S = {}
    S['kT'] = dscr('s_kT', [8, 128, 8192])
    S['v'] = dscr('s_v', [8, 8192, 128])
    S['nkT'] = dscr('s_nkT', [4, 256, 8192])
    S['nv'] = dscr('s_nv', [2, 4, 8192, 64])
    S['qT'] = dscr('s_qT', [16, 128, 2048])
    S['h'] = dscr('s_h', [16, 128, 2048], F32)
    S['mixT'] = dscr('s_mixT', [4, 128, 16, 512])
    S['hT'] = dscr('s_hT', [4, 128, 16, 512])
    S['pe'] = dscr('s_pe', [16, 128, 2048], F32)
    S['puT'] = dscr('s_puT', [128, 128, 16, 128])
    S['pv'] = dscr('s_pv', [128, 128, 2048])

    with ExitStack() as top:
        P = Prog(nc, top)

        def sbt(st, name, shape, dt):
            return st.enter_context(nc.sbuf_tensor(name, list(shape), dt))

        ps = [top.enter_context(nc.psum_tensor('ps%d' % i, [128, 512], F32)) for i in range(8)]

        def MM(o, lhsT, rhs, start, stop, r, w):
            P.op('pe', lambda e: e.matmul(o, lhsT, rhs, start=start, stop=stop), r, w)

        def ACT(o, i, func, r, w, **kw):
            P.op('act', lambda e: e.activation(o, i, func, **kw), r, w)

        def CP(eng, o, i, r, w):
            if eng == 'act':
                P.op('act', lambda e: e.copy(o, i), r, w)
            else:
                P.op(eng, lambda e: e.tensor_copy(o, i), r, w)

        def TS(eng, o, i0, s1, s2, op0, op1, r, w, **kw):
            if op1 is None:
                P.op(eng, lambda e: e.tensor_scalar(o, i0, s1, None, op0, **kw), r, w)
            else:
                P.op(eng, lambda e: e.tensor_scalar(o, i0, s1, s2, op0, op1, **kw), r, w)

        def STT(eng, o, i0, sc, i1, op0, op1, r, w):
            P.op(eng, lambda e: e.scalar_tensor_tensor(o, i0, sc, i1, op0, op1), r, w)

        def TT(eng, o, i0, i1, op, r, w):
            P.op(eng, lambda e: e.tensor_tensor(o, i0, i1, op), r, w)

        def DMA(q, o, i, r, w):
            return P.op(q, lambda e: e.dma_start(out=o, in_=i), r, w, dma=True)

        def MEMSET(eng, o, val, w):
            P.op(eng, lambda e: e.memset(o, val), (), w)

        gates = sbt(top, 'gates', [128, 16, 48], F32)
        identb = sbt(top, 'identb', [128, 128], BF16)
        neglam = sbt(top, 'neglam', [128, 1], F32)
        gs = sbt(top, 'gs', [128, 128], F32)
        cda = sbt(top, 'cda', [128, 8], F32)
        dbg2sb = sbt(top, 'dbg2sb', [128, 4, 258], F32) if dbg else None
        mid = ExitStack()
        aT = sbt(mid, 'aT', [128, 8, 2048], BF16)
        nT = sbt(mid, 'nT', [128, 8, 2048], BF16)

        with ExitStack() as ph:
            idf = sbt(ph, 'idf', [128, 128], F32)
            lq = sbt(ph, 'lq', [128, 128], F32)
            lk = sbt(ph, 'lk', [128, 128], F32)
            lp = sbt(ph, 'lp', [128, 128], F32)
            l2 = sbt(ph, 'l2', [128, 2], F32)
            DMA('sp', idf[:], I['ident'], [], ['idf'])
            DMA('sp', lq[:], I['lamq'], [], ['lq'])
            DMA('sp', lk[:], I['lamk'], [], ['lk'])
            DMA('sp', gs[:], I['subg'], [], ['gs'])
            DMA('sp', cda[:], I['c_da'], [], ['cda'])
            CP('dve', identb[:], idf[:], ['idf'], ['identb'])
            TT('dve', lp[:], lq[:], lk[:], ALU.mult, ['lq', 'lk'], ['lp'])
            P.op('dve', lambda e: e.tensor_reduce(l2[:], lp[:].rearrange('p (a b) -> p a b', a=2), AX.X, ALU.add),
                 ['lp'], ['l2'])
            ACT(l2[:], l2[:], AF.Exp, ['l2'], ['l2'])
            STT('dve', neglam[:], l2[:, 0:1], -1.0, l2[:, 1:2], ALU.mult, ALU.add, ['l2'], ['neglam'])
            TS('dve', neglam[:], neglam[:], -LAM_INIT, None, ALU.add, None, ['neglam'], ['neglam'])
            TS('dve', gs[:], gs[:], 1.0 - LAM_INIT, None, ALU.mult, None, ['gs'], ['gs'])
            P.barrier()
            P.emit()

        psrot = [0]

        def nextps():
            i = psrot[0]
            psrot[0] = (i + 1) % 8
            return i

        def phase_kv(passno):
            ncw = 2048 if passno == 0 else 1536
            wsrc = I['w_dakv'] if passno == 0 else I['w_nkv']
            with ExitStack() as ph:
                wb = sbt(ph, 'A%d_wb' % passno, [128, 16, ncw], BF16)
                wst = [sbt(ph, 'A%d_wst%d' % (passno, i), [128, ncw], F32) for i in range(2)]
                xs = [sbt(ph, 'A%d_xs%d' % (passno, i), [128, 4, 512], F32) for i in range(2)]
                xb = [sbt(ph, 'A%d_xb%d' % (passno, i), [128, 16, 512], BF16) for i in range(2)]
                evs = [sbt(ph, 'A%d_ev%d' % (passno, i), [128, 512], BF16) for i in range(4)]
                evi = [0]
                for kc in range(16):
                    b = kc % 2
                    DMA('sp', wst[b][:], wsrc[:, kc, :], [], ['wst%d' % b])
                    CP('pool', wb[:, kc, :], wst[b][:], ['wst%d' % b], ['wb'])

                def evac_store(pi, npart, ncol, dst_fn):
                    k = evi[0]
                    evi[0] = (k + 1) % 4
                    CP('act', evs[k][0:npart, 0:ncol], ps[pi][0:npart, 0:ncol], ['ps%d' % pi], ['ev%d' % k])
                    dst_fn(evs[k], k)

                for tile in range(16):
                    xbk = 'xb%d' % (tile % 2)
                    xbt = xb[tile % 2]
                    for qq in range(4):
                        half = qq % 2
                        DMA('sp', xs[half][:], I['xT'][tile, :, qq * 4:(qq + 1) * 4, :], [], ['xs%d' % half])
                        CP('dve', xbt[:, qq * 4:(qq + 1) * 4, :], xs[half][:], ['xs%d' % half], [xbk])
                    tsl = slice(tile * 512, (tile + 1) * 512)
                    if passno == 0:
                        fm = [(c * 128, S['kT'][c, :, tsl]) for c in range(8)]
                    else:
                        fm = []
                        for kind, cb in enumerate((0, 256, 512, 1024)):
                            for c2 in range(2):
                                fm.append((cb + c2 * 128, S['nkT'][kind, c2 * 128:(c2 + 1) * 128, tsl]))
                    for col0, dst in fm:
                        pi = nextps()
                        for kc in range(16):
                            MM(ps[pi][:, :], wb[:, kc, col0:col0 + 128], xbt[:, kc, :], kc == 0, kc == 15,
                               ['wb', xbk], ['ps%d' % pi])
                        evac_store(pi, 128, 512,
                                   lambda ev, k, dst=dst: DMA('pool', dst, ev[:, :], ['ev%d' % k], []))
                    for blk in range(4):
                        t0 = tile * 512 + blk * 128
                        if passno == 0:
                            tm = [(1024 + g4 * 512, 512,
                                   S['v'][g4 * 4:(g4 + 1) * 4, t0:t0 + 128, :].rearrange('h t e -> t h e'), 4)
                                  for g4 in range(2)]
                        else:
                            tm = [(768, 256, S['nv'][0, :, t0:t0 + 128, :].rearrange('g t e -> t g e'), 4),
                                  (1280, 256, S['nv'][1, :, t0:t0 + 128, :].rearrange('g t e -> t g e'), 4)]
                        for col0, ncol, dst, nh in tm:
                            pi = nextps()
                            for kc in range(16):
                                MM(ps[pi][:, 0:ncol], xbt[:, kc, blk * 128:(blk + 1) * 128], wb[:, kc, col0:col0 + ncol],
                                   kc == 0, kc == 15, ['wb', xbk], ['ps%d' % pi])
                            evac_store(pi, 128, ncol,
                                       lambda ev, k, dst=dst, ncol=ncol, nh=nh: DMA(
                                           'pool', dst, ev[:, 0:ncol].rearrange('t (h e) -> t h e', h=nh),
                                           ['ev%d' % k], []))
                P.barrier()
                P.emit()

        def phase_q():
            with ExitStack() as ph:
                wb = sbt(ph, 'B_wb', [128, 16, 2048], BF16)
                wgb = sbt(ph, 'B_wgb', [128, 16, 48], BF16)
                wst = [sbt(ph, 'B_wst%d' % i, [128, 2048], F32) for i in range(1)]
                wgs = sbt(ph, 'B_wgs', [128, 16, 48], F32)
                xs = [sbt(ph, 'B_xs%d' % i, [128, 4, 512], F32) for i in range(2)]
                xb = [sbt(ph, 'B_xb%d' % i, [128, 16, 512], BF16) for i in range(2)]
                evs = [sbt(ph, 'B_ev%d' % i, [128, 512], BF16) for i in range(4)]
                evi = [0]
                for kc in range(16):
                    b = 0
                    DMA('sp', wst[b][:], I['w_q'][:, kc, :], [], ['wst%d' % b])
                    CP('pool', wb[:, kc, :], wst[b][:], ['wst%d' % b], ['wb'])
                DMA('sp', wgs[:], I['w_gate'], [], ['wgs'])
                CP('pool', wgb[:], wgs[:], ['wgs'], ['wgb'])
                for tile in range(4):
                    xbk = 'xb%d' % (tile % 2)
                    xbt = xb[tile % 2]
                    for qq in range(4):
                        half = qq % 2
                        DMA('sp', xs[half][:], I['xTo'][tile, :, qq * 4:(qq + 1) * 4, :], [], ['xs%d' % half])
                        CP('dve', xbt[:, qq * 4:(qq + 1) * 4, :], xs[half][:], ['xs%d' % half], [xbk])
                    tsl = slice(tile * 512, (tile + 1) * 512)
                    for c in range(16):
                        pi = nextps()
                        for kc in range(16):
                            MM(ps[pi][:, :], wb[:, kc, c * 128:(c + 1) * 128], xbt[:, kc, :], kc == 0, kc == 15,
                               ['wb', xbk], ['ps%d' % pi])
                        k = evi[0]
                        evi[0] = (k + 1) % 4
                        CP('act', evs[k][:, :], ps[pi][:, :], ['ps%d' % pi], ['ev%d' % k])
                        DMA('pool', S['qT'][c, :, tsl], evs[k][:, :], ['ev%d' % k], [])
                    for blk in range(4):
                        pi = nextps()
                        for kc in range(16):
                            MM(ps[pi][:, 0:48], xbt[:, kc, blk * 128:(blk + 1) * 128], wgb[:, kc, :], kc == 0, kc == 15,
                               ['wgb', xbk], ['ps%d' % pi])
                        ACT(gates[:, tile * 4 + blk, :], ps[pi][:, 0:48], AF.Sigmoid, ['ps%d' % pi], ['gates'])
                P.barrier()
                P.emit()

        def phase_da():
            with ExitStack() as ph:
                KtF = sbt(ph, 'C_KtF', [128, 8192], BF16)
                Vh = sbt(ph, 'C_Vh', [128, 64, 129], BF16)
                strip = sbt(ph, 'C_strip', [128, 2560], F32)
                mneg = sbt(ph, 'C_mneg', [128, 2560], F32)
                QT = [sbt(ph, 'C_QT%d' % m, [128, 2048], BF16) for m in range(2)]
                pT = [sbt(ph, 'C_pT%d' % b, [128, 512], BF16) for b in range(4)]
                tmp = [sbt(ph, 'C_tmp%d' % b, [128, 512], F32) for b in range(3)]
                fz = [sbt(ph, 'C_fz%d' % i, [128, 4], F32) for i in range(2)]
                o0 = sbt(ph, 'C_o0', [128, 4, 129], F32)
                fu = [sbt(ph, 'C_fu%d' % i, [128, 128], F32) for i in range(2)]
                fo = [sbt(ph, 'C_fo%d' % i, [128, 128], F32) for i in range(2)]
                fj = [sbt(ph, 'C_fj%d' % i, [128, 128], F32) for i in range(2)]
                fon = [sbt(ph, 'C_fon%d' % i, [128, 128], BF16) for i in range(2)]
                DMA('sp', mneg[:], I['mneg'][:, 0:2560], [], ['mneg'])
                MEMSET('pool', Vh[:, :, 128:129], 1.0, ['Vh'])
                MEMSET('pool', QT[0][:], 0.0, ['QT0'])
                MEMSET('pool', QT[1][:], 0.0, ['QT1'])
                p0q = []
                if 'p0i' in phases:
                    pus = [sbt(ph, 'C_pus%d' % i, [128, 2048], F32) for i in range(3)]
                    pub = [sbt(ph, 'C_pub%d' % i, [128, 2048], BF16) for i in range(3)]
                    p0q = p0_units(pus, pub, ['pool'])
                st_ = {'slot': 0, 'fin': 0}
                pipe = []

                def push(pv):
                    pipe.append(pv)
                    if len(pipe) > 2:
                        pipe.pop(0)()

                def flush():
                    while pipe:
                        pipe.pop(0)()

                def da_finalize(h, t):
                    for sub in range(4):
                        f = st_['fin'] % 2
                        st_['fin'] += 1
                        acc = ps[4 + sub]
                        ak = 'ps%d' % (4 + sub)
                        if dbg and h == 0 and t == 0:
                            CP('dve', dbg2sb[:, sub, 0:129], o0[:, sub, :], ['o0_%d' % sub], ['dbg2sb'])
                            CP('dve', dbg2sb[:, sub, 129:258], acc[:, 0:129], [ak], ['dbg2sb'])
                        P.op('dve', lambda e, f=f, sub=sub: e.reciprocal(fz[f][:, 0:1], o0[:, sub, 128:129]), ['o0_%d' % sub], ['fz%d' % f])
                        P.op('dve', lambda e, f=f, acc=acc: e.reciprocal(fz[f][:, 1:2], acc[:, 128:129]), [ak], ['fz%d' % f])
                        TT('dve', fz[f][:, 2:3], fz[f][:, 1:2], neglam[:], ALU.mult, ['fz%d' % f, 'neglam'], ['fz%d' % f])
                        TS('dve', fu[f][:], acc[:, 0:128], fz[f][:, 2:3], None, ALU.mult, None, [ak, 'fz%d' % f], ['fu%d' % f])
                        STT('dve', fo[f][:], o0[:, sub, 0:128], fz[f][:, 0:1], fu[f][:], ALU.mult, ALU.add,
                            ['o0_%d' % sub, 'fz%d' % f, 'fu%d' % f], ['fo%d' % f])
                        TT('pool', fj[f][:], fo[f][:], fo[f][:], ALU.mult, ['fo%d' % f], ['fj%d' % f])
                        P.op('dve', lambda e, f=f: e.tensor_reduce(fz[f][:, 3:4], fj[f][:], AX.X, ALU.add), ['fj%d' % f], ['fz3_%d' % f])
                        TS('dve', fz[f][:, 3:4], fz[f][:, 3:4], 1.0 / 128.0, 1e-5, ALU.mult, ALU.add, ['fz3_%d' % f], ['fz3_%d' % f])
                        ACT(fz[f][:, 3:4], fz[f][:, 3:4], AF.Sqrt, ['fz3_%d' % f], ['fz3_%d' % f])
                        P.op('dve', lambda e, f=f: e.reciprocal(fz[f][:, 3:4], fz[f][:, 3:4]), ['fz3_%d' % f], ['fz3_%d' % f])
                        STT('dve', fon[f][:], fo[f][:], fz[f][:, 3:4], gs[:], ALU.mult, ALU.mult,
                            ['fo%d' % f, 'fz3_%d' % f, 'gs'], ['fon%d' % f])
                        MM(ps[3][:, 0:128], fon[f][:], identb[:], True, True, ['fon%d' % f, 'identb'], ['ps3'])
                        blk = t * 4 + sub
                        CP('act', aT[:, h, blk * 128:(blk + 1) * 128], ps[3][:, 0:128], ['ps3'], ['aT'])

                for h in range(8):
                    flush()
                    DMA('sp', KtF[:], S['kT'][h], [], ['KtF'])
                    for m in range(2):
                        DMA('sp', QT[m][m * 64:(m + 1) * 64, :], S['qT'][h, m * 64:(m + 1) * 64, :], [], ['QT%d' % m])
                    for q4 in range(4):
                        DMA('sp', Vh[:, q4 * 16:(q4 + 1) * 16, 0:128],
                            S['v'][h, q4 * 2048:(q4 + 1) * 2048, :].rearrange('(s p) e -> p s e', p=128), [], ['Vh'])
                    DMA('sp', strip[:], I['raw_da'][h], [], ['strip'])
                    STT('dve', strip[:], strip[:], cda[:, h:h + 1], mneg[:], ALU.subtract, ALU.add,
                        ['strip', 'cda', 'mneg'], ['strip'])
                    for t in range(4):
                        nsl = 16 * (t + 1)
                        for m in range(2):
                            for s in range(nsl):
                                c = st_['slot']
                                st_['slot'] += 1
                                if p0q and c % 10 == 0:
                                    p0q.pop(0)()
                                b2 = c % 3
                                b3 = c % 4
                                near = s >= 16 * t - 1
                                pi = b2
                                MM(ps[pi][:, :], KtF[:, s * 128:(s + 1) * 128], QT[m][:, t * 512:(t + 1) * 512],
                                   True, True, ['KtF', 'QT%d' % m], ['ps%d' % pi])
                                if near:
                                    x0 = 128 * (15 - (s - 16 * t))
                                    STT('dve', tmp[b2][:], ps[pi][:, :], 0.125, strip[:, x0:x0 + 512], ALU.mult, ALU.add,
                                        ['ps%d' % pi, 'strip'], ['tmp%d' % b2])
                                    ACT(pT[b3][:], tmp[b2][:], AF.Exp, ['tmp%d' % b2], ['pT%d' % b3])
                                else:
                                    ACT(pT[b3][:], ps[pi][:, :], AF.Exp, ['ps%d' % pi], ['pT%d' % b3], scale=0.125)

                                def pv(h=h, t=t, m=m, s=s, b3=b3, nsl=nsl):
                                    for sub in range(4):
                                        MM(ps[4 + sub][:, 0:129], pT[b3][:, sub * 128:(sub + 1) * 128],
                                           Vh[:, s, :], s == 0, s == nsl - 1, ['pT%d' % b3, 'Vh'], ['ps%d' % (4 + sub)])
                                    if s == nsl - 1:
                                        if m == 0:
                                            for sub in range(4):
                                                CP('dve', o0[:, sub, :], ps[4 + sub][:, 0:129], ['ps%d' % (4 + sub)], ['o0_%d' % sub])
                                        else:
                                            da_finalize(h, t)
                                push(pv)
                flush()
                while p0q:
                    p0q.pop(0)()
                P.barrier()
                P.emit()


        def phase_nsa():
            with ExitStack() as ph:
                KCT = sbt(ph, 'D_KCT', [64, 4, 512], BF16)
                Rg = sbt(ph, 'D_Rg', [128, 4, 4, 193], BF16)
                cns = sbt(ph, 'D_cns', [128, 16], F32)
                MEMSET('pool', KCT[:], 0.0, ['KCT'])
                MEMSET('pool', Rg[:], 0.0, ['Rg'])
                DMA('sp', cns[:], I['c_nsa'], [], ['cns'])
                with ExitStack() as p0:
                    cT = sbt(p0, 'D0_cT', [64, 8192], BF16)
                    w1s = sbt(p0, 'D0_w1s', [64, 8, 256], F32)
                    w1b = sbt(p0, 'D0_w1b', [64, 32, 256], BF16)
                    w2s = sbt(p0, 'D0_w2s', [128, 2, 64], F32)
                    w2b = sbt(p0, 'D0_w2b', [128, 2, 64], BF16)
                    pes = sbt(p0, 'D0_pes', [64, 32], F32)
                    peb = sbt(p0, 'D0_peb', [64, 32], BF16)
                    b1 = sbt(p0, 'D0_b1', [128, 2], F32)
                    hT = sbt(p0, 'D0_hT', [128, 2, 512], BF16)
                    ovs = sbt(p0, 'D0_ovs', [128, 4, 129], F32)
                    DMA('sp', ovs[:], I['ovl'], [], ['ovs'])
                    for g in range(4):
                        CP('pool', Rg[:, :, g, 0:129], ovs[:], ['ovs'], ['Rg'])
                    for kind in range(2):
                        w1src = I['w1k'] if kind == 0 else I['w1v']
                        for p8 in range(4):
                            DMA('sp', w1s[:], w1src[:, p8 * 8:(p8 + 1) * 8, :], [], ['w1s'])
                            CP('pool', w1b[:, p8 * 8:(p8 + 1) * 8, :], w1s[:], ['w1s'], ['w1b'])
                        DMA('sp', w2s[:], I['w2k'] if kind == 0 else I['w2v'], [], ['w2s'])
                        CP('pool', w2b[:], w2s[:], ['w2s'], ['w2b'])
                        DMA('sp', pes[:], I['pekT'] if kind == 0 else I['pevT'], [], ['pes'])
                        CP('pool', peb[:], pes[:], ['pes'], ['peb'])
                        for hc in range(2):
                            pi = nextps()
                            for p in range(32):
                                MM(ps[pi][:, 0:1], w1b[:, p, hc * 128:(hc + 1) * 128], peb[:, p:p + 1], p == 0, p == 31,
                                   ['w1b', 'peb'], ['ps%d' % pi])
                            CP('dve', b1[:, hc:hc + 1], ps[pi][:, 0:1], ['ps%d' % pi], ['b1'])
                        for g in range(4):
                            DMA('sp', cT[:], S['nkT'][kind, g * 64:(g + 1) * 64, :], [], ['cT'])
                            for hc in range(2):
                                pi = nextps()
                                for p in range(32):
                                    MM(ps[pi][:, 0:511], w1b[:, p, hc * 128:(hc + 1) * 128], cT[:, p:p + 16 * 510 + 1:16],
                                       p == 0, p == 31, ['w1b', 'cT'], ['ps%d' % pi])
                                ACT(hT[:, hc, 0:511], ps[pi][:, 0:511], AF.Gelu_apprx_tanh, ['ps%d' % pi, 'b1'], ['hT'],
                                    bias=b1[:, hc:hc + 1])
                            if kind == 0:
                                pi = nextps()
                                for hc in range(2):
                                    MM(ps[pi][0:64, 0:511], w2b[:, hc, :], hT[:, hc, 0:511], hc == 0, hc == 1,
                                       ['w2b', 'hT'], ['ps%d' % pi])
                                CP('act', KCT[:, g, 0:511], ps[pi][0:64, 0:511], ['ps%d' % pi], ['KCT'])
                            else:
                                for cc in range(4):
                                    ncl = 128 if cc < 3 else 127
                                    pi = nextps()
                                    for hc in range(2):
                                        MM(ps[pi][0:ncl, 0:64], hT[:, hc, cc * 128:cc * 128 + ncl], w2b[:, hc, :], hc == 0, hc == 1,
                                           ['w2b', 'hT'], ['ps%d' % pi])
                                    CP('act', Rg[0:ncl, cc, g, 129:193], ps[pi][0:ncl, 0:64], ['ps%d' % pi], ['Rg'])
                    P.barrier()
                    P.emit()
                cm = sbt(ph, 'D_cm', [128, 2, 512], F32)
                vmul = sbt(ph, 'D_vmul', [128, 16, 128], F32)
                vadd = sbt(ph, 'D_vadd', [128, 16, 128], F32)
                Kbuf = sbt(ph, 'D_Kbuf', [128, 8192], BF16)
                Vbuf = sbt(ph, 'D_Vbuf', [128, 64, 65], BF16)
                strip = sbt(ph, 'D_strip', [128, 2944], F32)
                neg = sbt(ph, 'D_neg', [128, 2944], F32)
                QTn = [sbt(ph, 'D_QTn%d' % i, [64, 2048], BF16) for i in range(4)]
                QA = [sbt(ph, 'D_QA%d' % i, [128, 512], BF16) for i in range(2)]
                QB = [sbt(ph, 'D_QB%d' % i, [128, 512], BF16) for i in range(2)]
                selT = [sbt(ph, 'D_selT%d' % i, [128, 512], BF16) for i in range(4)]
                onsa = sbt(ph, 'D_onsa', [128, 16, 4, 64], F32)
                impacc = sbt(ph, 'D_impacc', [128, 4, 128], F32)
                pT = [sbt(ph, 'D_pT%d' % b, [128, 512], BF16) for b in range(5)]
                tmp = [sbt(ph, 'D_tmp%d' % b, [128, 512], F32) for b in range(4)]
                fz = [sbt(ph, 'D_fz%d' % i, [128, 4], F32) for i in range(2)]
                sc = [sbt(ph, 'D_sc%d' % i, [128, 128], F32) for i in range(2)]
                sc2 = [sbt(ph, 'D_sc2%d' % i, [128, 128], F32) for i in range(2)]
                m8 = [sbt(ph, 'D_m8%d' % i, [128, 16], F32) for i in range(2)]
                sng = [sbt(ph, 'D_sng%d' % i, [128, 128], BF16) for i in range(2)]
                onb = [sbt(ph, 'D_onb%d' % i, [128, 128], BF16) for i in range(2)]
                accT_sb = [sbt(ph, 'D_accT%d' % i, [65, 512], F32) for i in range(1)]
                QZ = [sbt(ph, 'D_QZ%d' % i, [128, 512], BF16) for i in range(2)]
                MEMSET('pool', QZ[0][:], 0.0, ['QZ0'])
                MEMSET('pool', QZ[1][:], 0.0, ['QZ1'])
                identf = sbt(ph, 'D_identf', [128, 128], F32)
                DMA('sp', identf[:], I['ident'], [], ['identf'])
                DMA('sp', cm[:], I['cm'], [], ['cm'])
                DMA('sp', vmul[:], I['vmul'], [], ['vmul'])
                DMA('sp', vadd[:], I['vadd'], [], ['vadd'])
                DMA('sp', Kbuf[64:128, :], I['onehot'], [], ['KbufHi'])
                MEMSET('pool', Vbuf[:, :, 64:65], 1.0, ['Vbuf'])
                cnt = {'slot': 0, 'fin': 0, 'q': 0, 'tr': 0, 'grp': 0}

                pipe = []

                def push(pv):
                    pipe.append(pv)
                    if len(pipe) > 3:
                        pipe.pop(0)()

                def flush():
                    while pipe:
                        pipe.pop(0)()

                def attn_slot(lhsT, rhs, rkeys, bias_ap, vrhs, vkeys, ncolv, first, last, after=None, tbank=None):
                    c = cnt['slot']
                    cnt['slot'] = c + 1
                    b2, b3 = c % 4, c % 5
                    MM(ps[b2][:, :], lhsT, rhs, True, True, rkeys, ['ps%d' % b2])
                    if bias_ap is not None:
                        STT('dve', tmp[b2][:], ps[b2][:, :], 0.125, bias_ap, ALU.mult, ALU.add,
                            ['ps%d' % b2, 'strip', 'cm'], ['tmp%d' % b2])
                        ACT(pT[b3][:], tmp[b2][:], AF.Exp, ['tmp%d' % b2], ['pT%d' % b3])
                    else:
                        ACT(pT[b3][:], ps[b2][:, :], AF.Exp, ['ps%d' % b2], ['pT%d' % b3], scale=0.125)

                    def pv():
                        if tbank is None:
                            for sub in range(4):
                                MM(ps[4 + sub][:, 0:ncolv], pT[b3][:, sub * 128:(sub + 1) * 128], vrhs, first, last,
                                   ['pT%d' % b3] + vkeys, ['ps%d' % (4 + sub)])
                        else:
                            MM(ps[tbank][0:ncolv, :], vrhs, pT[b3][:, :], first, last, ['pT%d' % b3] + vkeys, ['ps%d' % tbank])
                        if after is not None:
                            after()
                    push(pv)

                def fin_branch(t, hh, n, gidx, dcol, ncol0, first_branch, tb=None):
                    if tb is not None:
                        k = cnt['tr'] % 2
                        cnt['tr'] += 1
                        CP('act', accT_sb[0][0:65, :], ps[tb][0:65, :], ['ps%d' % tb], ['accT0'])
                        for sub in range(4):
                            MM(ps[6 + k][:, sub * 65:(sub + 1) * 65], accT_sb[0][0:65, sub * 128:(sub + 1) * 128], identf[0:65, 0:65],
                               True, True, ['accT0', 'identf'], ['ps%d' % (6 + k)])
                    for sub in range(4):
                        f = cnt['fin'] % 2
                        cnt['fin'] += 1
                        blk = 4 * t + sub
                        if tb is None:
                            acc = ps[4 + sub]
                            ak = 'ps%d' % (4 + sub)
                            c0 = 0
                        else:
                            acc = ps[6 + k]
                            ak = 'ps%d' % (6 + k)
                            c0 = sub * 65
                        fk = 'fz%d' % f
                        TS('dve', fz[f][:, 0:1], acc[:, c0 + dcol:c0 + dcol + 1], 1e-30, None, ALU.max, None, [ak], [fk])
                        P.op('dve', lambda e, f=f: e.reciprocal(fz[f][:, 1:2], fz[f][:, 0:1]), [fk], [fk])
                        TT('dve', fz[f][:, 2:3], fz[f][:, 1:2], gates[:, blk, n * 3 + gidx:n * 3 + gidx + 1], ALU.mult,
                           [fk, 'gates'], [fk])
                        if first_branch:
                            if hh == 0:
                                TS('dve', impacc[:, sub, :], acc[:, 0:128], fz[f][:, 1:2], None, ALU.mult, None,
                                   [ak, fk], ['impacc%d' % sub])
                            else:
                                STT('dve', impacc[:, sub, :], acc[:, 0:128], fz[f][:, 1:2], impacc[:, sub, :], ALU.mult, ALU.add,
                                    [ak, fk, 'impacc%d' % sub], ['impacc%d' % sub])
                            TS('dve', onsa[:, blk, hh, :], acc[:, ncol0:ncol0 + 64], fz[f][:, 2:3], None, ALU.mult, None,
                               [ak, fk], ['onsa'])
                        else:
                            STT('dve', onsa[:, blk, hh, :], acc[:, c0 + ncol0:c0 + ncol0 + 64], fz[f][:, 2:3], onsa[:, blk, hh, :],
                                ALU.mult, ALU.add, [ak, fk, 'onsa'], ['onsa'])

                for g in range(4):
                    for hh in range(4):
                        n = 4 * g + hh
                        DMA('sp', QTn[hh][:], S['qT'][8 + n // 2, (n % 2) * 64:(n % 2) * 64 + 64, :], [], ['QTn%d' % hh])
                    def topk_code(g, t):
                        for sub in range(4):
                            f = cnt['fin'] % 2
                            cnt['fin'] += 1
                            blk = 4 * t + sub
                            TT('dve', sc[f][:], impacc[:, sub, :], vmul[:, blk, :], ALU.mult, ['impacc%d' % sub, 'vmul'], ['sc%d' % f])
                            TT('dve', sc[f][:], sc[f][:], vadd[:, blk, :], ALU.add, ['sc%d' % f, 'vadd'], ['sc%d' % f])
                            P.op('dve', lambda e, f=f: e.max(out=m8[f][:, 0:8], in_=sc[f][:]), ['sc%d' % f], ['m8_%d' % f])
                            P.op('dve', lambda e, f=f: e.match_replace(out=sc2[f][:], in_to_replace=m8[f][:, 0:8],
                                                                      in_values=sc[f][:], imm_value=-3.0e38),
                                 ['sc%d' % f, 'm8_%d' % f], ['sc2%d' % f])
                            P.op('dve', lambda e, f=f: e.max(out=m8[f][:, 8:16], in_=sc2[f][:]), ['sc2%d' % f], ['m8_%d' % f])
                            TS('dve', sng[f][:], sc[f][:], m8[f][:, 15:16], -240000.0, ALU.is_lt, ALU.mult,
                               ['sc%d' % f, 'm8_%d' % f], ['sng%d' % f])
                            MM(ps[3][:, 0:128], sng[f][:], identb[:], True, True, ['sng%d' % f, 'identb'], ['ps3'])
                            CP('act', selT[t][:, sub * 128:(sub + 1) * 128], ps[3][:, 0:128], ['ps3'], ['selT%d' % t])
                            if dbg and g == 0 and t == 0:
                                CP('dve', dbg2sb[:, sub, 0:128], sc[f][:], ['sc%d' % f], ['dbg2sb'])
                                CP('dve', dbg2sb[:, sub, 129:145], m8[f][:], ['m8_%d' % f], ['dbg2sb'])

                    for t in range(4):
                        for hh in range(4):
                            n = 4 * g + hh
                            for cc in range(t + 1):
                                bias_ap = cm[:, cc - t + 1, :] if cc >= t - 1 else None
                                aft = None
                                if cc == t:
                                    def aft(t=t, hh=hh, n=n, g=g):
                                        fin_branch(t, hh, n, 0, 128, 129, True)
                                        if hh == 3:
                                            topk_code(g, t)
                                attn_slot(KCT[:, g, cc * 128:(cc + 1) * 128], QTn[hh][:, t * 512:(t + 1) * 512],
                                          ['KCT', 'QTn%d' % hh], bias_ap, Rg[:, cc, g, :], ['Rg'], 193, cc == 0, cc == t, after=aft)
                    flush()
                    DMA('sp', Kbuf[0:64, :], S['nkT'][2, g * 64:(g + 1) * 64, :], [], ['KbufLo'])
                    for q4 in range(4):
                        DMA('sp', Vbuf[:, q4 * 16:(q4 + 1) * 16, 0:64],
                            S['nv'][0, g, q4 * 2048:(q4 + 1) * 2048, :].rearrange('(s p) e -> p s e', p=128), [], ['Vbuf'])
                    DMA('sp', neg[:], I['mneg'], [], ['neg'])
                    for hh in range(4):
                        n = 4 * g + hh
                        DMA('sp', strip[:], I['raw_nsa'][n], [], ['strip'])
                        STT('dve', strip[:], strip[:], cns[:, n:n + 1], neg[:], ALU.subtract, ALU.add,
                            ['strip', 'cns', 'neg'], ['strip'])
                        for t in range(4):
                            qb = cnt['q'] % 2
                            cnt['q'] += 1
                            CP('pool', QA[qb][0:64, :], QTn[hh][:, t * 512:(t + 1) * 512], ['QTn%d' % hh], ['QA%d' % qb])
                            CP('pool', QB[qb][0:64, :], QTn[hh][:, t * 512:(t + 1) * 512], ['QTn%d' % hh], ['QB%d' % qb])
                            DMA('sp', QA[qb][64:128, :], selT[t][0:64, :], ['selT%d' % t], ['QA%d' % qb])
                            CP('pool', QB[qb][64:128, :], selT[t][64:128, :], ['selT%d' % t], ['QB%d' % qb])
                            nsl = 16 * (t + 1)
                            tb = 4 + (cnt['grp'] % 2)
                            cnt['grp'] += 1
                            for s_ in range(nsl):
                                near = s_ >= 16 * t - 1
                                bias_ap = strip[:, 128 * (15 - (s_ - 16 * t)):128 * (15 - (s_ - 16 * t)) + 512] if near else None
                                qq = QA[qb] if s_ < 32 else QB[qb]
                                qk = ('QA%d' if s_ < 32 else 'QB%d') % qb
                                aft = None
                                if s_ == nsl - 1:
                                    def aft(t=t, hh=hh, n=n, tb=tb):
                                        fin_branch(t, hh, n, 1, 64, 0, False, tb=tb)
                                attn_slot(Kbuf[:, s_ * 128:(s_ + 1) * 128], qq[:, :], ['KbufLo', 'KbufHi', qk], bias_ap,
                                          Vbuf[:, s_, :], ['Vbuf'], 65, s_ == 0, s_ == nsl - 1, after=aft, tbank=tb)
                    flush()
                    DMA('sp', Kbuf[0:64, :], S['nkT'][3, g * 64:(g + 1) * 64, :], [], ['KbufLo'])
                    for q4 in range(4):
                        DMA('sp', Vbuf[:, q4 * 16:(q4 + 1) * 16, 0:64],
                            S['nv'][1, g, q4 * 2048:(q4 + 1) * 2048, :].rearrange('(s p) e -> p s e', p=128), [], ['Vbuf'])
                    DMA('sp', neg[:], I['wneg'], [], ['neg'])
                    for hh in range(4):
                        n = 4 * g + hh
                        DMA('sp', strip[:], I['raw_nsa'][n], [], ['strip'])
                        STT('dve', strip[:], strip[:], cns[:, n:n + 1], neg[:], ALU.subtract, ALU.add,
                            ['strip', 'cns', 'neg'], ['strip'])
                        for t in range(4):
                            s0 = max(0, 16 * t - 4)
                            s1 = 16 * t + 15
                            tb = 4 + (cnt['grp'] % 2)
                            cnt['grp'] += 1
                            qz = cnt['grp'] % 2
                            CP('pool', QZ[qz][0:64, :], QTn[hh][:, t * 512:(t + 1) * 512], ['QTn%d' % hh], ['QZ%d' % qz])
                            for s_ in range(s0, s1 + 1):
                                x0 = 128 * (15 - (s_ - 16 * t))
                                aft = None
                                if s_ == s1:
                                    def aft(t=t, hh=hh, n=n, tb=tb):
                                        fin_branch(t, hh, n, 2, 64, 0, False, tb=tb)
                                attn_slot(Kbuf[:, s_ * 128:(s_ + 1) * 128], QZ[qz][:, :],
                                          ['KbufLo', 'KbufHi', 'QZ%d' % qz], strip[:, x0:x0 + 512], Vbuf[:, s_, :], ['Vbuf'], 65,
                                          s_ == s0, s_ == s1, after=aft, tbank=tb)
                    flush()
                    for blk in range(16):
                        for pr in range(2):
                            f = cnt['fin'] % 2
                            cnt['fin'] += 1
                            CP('pool', onb[f][:].rearrange('p (h e) -> p h e', h=2), onsa[:, blk, 2 * pr:2 * pr + 2, :],
                               ['onsa'], ['onb%d' % f])
                            pi = 3
                            MM(ps[pi][:, 0:128], onb[f][:], identb[:], True, True, ['onb%d' % f, 'identb'], ['ps%d' % pi])
                            CP('act', nT[:, 2 * g + pr, blk * 128:(blk + 1) * 128], ps[pi][:, 0:128], ['ps%d' % pi], ['nT'])
                P.barrier()
                P.emit()


        def phase_e1():
            with ExitStack() as ph:
                xs = [sbt(ph, 'E_xs%d' % i, [128, 2, 512], F32) for i in range(2)]
                xb = sbt(ph, 'E_xb', [128, 16, 2048], BF16)
                wgs = [sbt(ph, 'E_wgs%d' % i, [128, 16, 128], F32) for i in range(2)]
                wbs = [sbt(ph, 'E_wbs%d' % i, [128, 8, 128], F32) for i in range(2)]
                wga = [sbt(ph, 'E_wga%d' % i, [128, 16, 128], BF16) for i in range(2)]
                wgn = [sbt(ph, 'E_wgn%d' % i, [128, 16, 128], BF16) for i in range(2)]
                wba = [sbt(ph, 'E_wba%d' % i, [128, 8, 128], BF16) for i in range(2)]
                wbn = [sbt(ph, 'E_wbn%d' % i, [128, 8, 128], BF16) for i in range(2)]
                sA = [sbt(ph, 'E_sA%d' % i, [128, 512], F32) for i in range(2)]
                sB = [sbt(ph, 'E_sB%d' % i, [128, 512], F32) for i in range(2)]
                m1 = [sbt(ph, 'E_m1%d' % i, [128, 512], F32) for i in range(2)]
                mx = [sbt(ph, 'E_mx%d' % i, [128, 512], BF16) for i in range(3)]
                k = 0
                for tile in range(4):
                    for qq in range(8):
                        half = k % 2
                        k += 1
                        DMA('sp', xs[half][:], I['xTo'][tile, :, qq * 2:(qq + 1) * 2, :], [], ['xs%d' % half])
                        CP('dve' if qq % 2 == 0 else 'act', xb[:, qq * 2:(qq + 1) * 2, tile * 512:(tile + 1) * 512], xs[half][:],
                           ['xs%d' % half], ['xb'])
                it = 0
                mi = 0
                for c in range(16):
                    b = c % 2
                    DMA('sp', wgs[0][:], I['w_mg'][:, :, c * 128:(c + 1) * 128], [], ['wgs0'])
                    CP('pool', wga[b][:], wgs[0][:], ['wgs0'], ['wga%d' % b])
                    DMA('sp', wgs[1][:], I['w_mg'][:, :, 2048 + c * 128:2048 + (c + 1) * 128], [], ['wgs1'])
                    CP('dve', wgn[b][:], wgs[1][:], ['wgs1'], ['wgn%d' % b])
                    DMA('sp', wbs[0][:], I['w_bda'][:, :, c * 128:(c + 1) * 128], [], ['wbs0'])
                    CP('pool', wba[b][:], wbs[0][:], ['wbs0'], ['wba%d' % b])
                    DMA('sp', wbs[1][:], I['w_bnsa'][:, :, c * 128:(c + 1) * 128], [], ['wbs1'])
                    CP('dve', wbn[b][:], wbs[1][:], ['wbs1'], ['wbn%d' % b])
                    for tile in range(4):
                        tsl = slice(tile * 512, (tile + 1) * 512)
                        p2 = it % 2
                        it += 1
                        pA, pB, pC, pD = (4 * p2 + 0), (4 * p2 + 1), (4 * p2 + 2), (4 * p2 + 3)
                        for kc in range(16):
                            MM(ps[pA][:, :], wga[b][:, kc, :], xb[:, kc, tsl], kc == 0, kc == 15, ['wga%d' % b, 'xb'], ['ps%d' % pA])
                        for kc in range(16):
                            MM(ps[pB][:, :], wgn[b][:, kc, :], xb[:, kc, tsl], kc == 0, kc == 15, ['wgn%d' % b, 'xb'], ['ps%d' % pB])
                        for kc in range(8):
                            MM(ps[pC][:, :], wba[b][:, kc, :], aT[:, kc, tsl], kc == 0, kc == 7, ['wba%d' % b, 'aT'], ['ps%d' % pC])
                        for kc in range(8):
                            MM(ps[pD][:, :], wbn[b][:, kc, :], nT[:, kc, tsl], kc == 0, kc == 7, ['wbn%d' % b, 'nT'], ['ps%d' % pD])
                        ACT(sA[p2][:], ps[pA][:, :], AF.Sigmoid, ['ps%d' % pA], ['sA%d' % p2])
                        ACT(sB[p2][:], ps[pB][:, :], AF.Sigmoid, ['ps%d' % pB], ['sB%d' % p2])
                        TT('dve', m1[p2][:], sA[p2][:], ps[pC][:, :], ALU.mult, ['sA%d' % p2, 'ps%d' % pC], ['m1%d' % p2])
                        TT('dve', sB[p2][:], sB[p2][:], ps[pD][:, :], ALU.mult, ['sB%d' % p2, 'ps%d' % pD], ['sB%d' % p2])
                        m3 = mi % 3
                        mi += 1
                        TT('pool', mx[m3][:], m1[p2][:], sB[p2][:], ALU.add, ['m1%d' % p2, 'sB%d' % p2], ['mx%d' % m3])
                        DMA('pool', S['mixT'][tile, :, c, :], mx[m3][:], ['mx%d' % m3], [])
                P.barrier()
                P.emit()

        def phase_e2():
            with ExitStack() as ph:
                wos = sbt(ph, 'F_wos', [128, 4, 512], F32)
                wob = [sbt(ph, 'F_wob%d' % i, [128, 16, 512], BF16) for i in range(2)]
                mxt = sbt(ph, 'F_mxt', [128, 16, 512], BF16)
                xc = [sbt(ph, 'F_xc%d' % i, [128, 512], F32) for i in range(3)]
                hpre2 = [sbt(ph, 'F_hpre%d' % i, [128, 4, 2048], F32) for i in range(2)]
                junk = sbt(ph, 'F_junk', [128, 2048], F32)
                hn = [sbt(ph, 'F_hn%d' % i, [128, 2048], F32) for i in range(2)]
                hb = sbt(ph, 'F_hb', [128, 2048], BF16)
                gB = sbt(ph, 'F_gB', [128, 2048], F32)
                bB = sbt(ph, 'F_bB', [128, 2048], F32)
                st = [sbt(ph, 'F_st%d' % i, [128, 4], F32) for i in range(2)]
                hTt = sbt(ph, 'F_hTt', [128, 16, 512], BF16)
                DMA('sp', gB[:], I['ln1g'], [], ['gB'])
                DMA('sp', bB[:], I['ln1b'], [], ['bB'])
                wi = 0
                xi = 0
                for tile in range(4):
                    hpre = hpre2[tile % 2]
                    hk = 'hpre%d_' % (tile % 2)
                    DMA('sp', mxt[:], S['mixT'][tile], [], ['mxt'])
                    for dc in range(4):
                        b = wi % 2
                        wi += 1
                        for k4 in range(4):
                            DMA('sp', wos[:], I['w_out'][:, k4 * 4:(k4 + 1) * 4, dc * 512:(dc + 1) * 512], [], ['wos'])
                            CP('act' if k4 % 2 == 0 else 'pool', wob[b][:, k4 * 4:(k4 + 1) * 4, :], wos[:], ['wos'], ['wob%d' % b])
                        for sub in range(4):
                            blk = tile * 4 + sub
                            pi = nextps()
                            for kc in range(16):
                                MM(ps[pi][:, :], mxt[:, kc, sub * 128:(sub + 1) * 128], wob[b][:, kc, :], kc == 0, kc == 15,
                                   ['mxt', 'wob%d' % b], ['ps%d' % pi])
                            x3 = xi % 3
                            xi += 1
                            DMA('sp', xc[x3][:], I['xo'][blk, :, dc * 512:(dc + 1) * 512], [], ['xc%d' % x3])
                            STT('dve', hpre[:, sub, dc * 512:(dc + 1) * 512], xc[x3][:], ALPHA, ps[pi][:, :], ALU.mult, ALU.add,
                                ['xc%d' % x3, 'ps%d' % pi], [hk + str(sub)])
                    for sub in range(4):
                        blk = tile * 4 + sub
                        layer_norm_block(hpre[:, sub, :], hk + str(sub), gB, bB, junk, hn, st, blk)
                        f = blk % len(hn)
                        DMA('pool', S['h'][blk], hn[f][:], ['hn%d' % f], [])
                        CP('pool', hb[:], hn[f][:], ['hn%d' % f], ['hb'])
                        for f4 in range(4):
                            pi = nextps()
                            for ff in range(4):
                                fc = f4 * 4 + ff
                                MM(ps[pi][:, ff * 128:(ff + 1) * 128], hb[:, fc * 128:(fc + 1) * 128], identb[:], True, True,
                                   ['hb', 'identb'], ['ps%d' % pi])
                            CP('act', hTt[:, f4 * 4:(f4 + 1) * 4, sub * 128:(sub + 1) * 128],
                               ps[pi][:, :].rearrange('p (a b) -> p a b', a=4), ['ps%d' % pi], ['hTt'])
                    DMA('pool', S['hT'][tile], hTt[:], ['hTt'], [])
                P.barrier()
                P.emit()

        def layer_norm_block(src, srck, gB_, bB_, junk, hn, st, blk, junkk='junk'):
            f = blk % len(hn)
            sk = 'st%d' % f
            P.op('dve', lambda e: e.tensor_reduce(st[f][:, 0:1], src, AX.X, ALU.add), [srck], [sk])
            TS('dve', st[f][:, 0:1], st[f][:, 0:1], 1.0 / 2048.0, None, ALU.mult, None, [sk], [sk])
            TS('dve', src, src, st[f][:, 0:1], None, ALU.subtract, None, [srck, sk], [srck])
            TT('pool', junk[:], src, src, ALU.mult, [srck], [junkk])
            P.op('dve', lambda e: e.tensor_reduce(st[f][:, 1:2], junk[:], AX.X, ALU.add), [junkk], [sk])
            TS('dve', st[f][:, 1:2], st[f][:, 1:2], 1.0 / 2048.0, 1e-5, ALU.mult, ALU.add, [sk], [sk])
            ACT(st[f][:, 1:2], st[f][:, 1:2], AF.Sqrt, [sk], [sk])
            P.op('dve', lambda e: e.reciprocal(st[f][:, 2:3], st[f][:, 1:2]), [sk], [sk])
            STT('dve', hn[f][:], src, st[f][:, 2:3], gB_[:], ALU.mult, ALU.mult, [srck, sk, 'gB'], ['hn%d' % f])
            TT('pool', hn[f][:], hn[f][:], bB_[:], ALU.add, ['hn%d' % f, 'bB'], ['hn%d' % f])


        def p0_units(us, ub, engs):
            units = []
            for ec in range(128):
                for which in range(2):
                    def unit(ec=ec, which=which, k=len(units)):
                        b = k % len(us)
                        src = I['puT'][ec].rearrange('p a b -> p (a b)') if which == 0 else I['pv'][ec]
                        dst = S['puT'][ec].rearrange('p a b -> p (a b)') if which == 0 else S['pv'][ec]
                        DMA('sp', us[b][:], src, [], ['us%d' % b])
                        CP(engs[k % len(engs)], ub[b][:], us[b][:], ['us%d' % b], ['ub%d' % b])
                        DMA('pool', dst, ub[b][:], ['ub%d' % b], [])
                    units.append(unit)
            return units

        def phase_p0():
            with ExitStack() as ph:
                us = [sbt(ph, 'P_us%d' % i, [128, 2048], F32) for i in range(3)]
                ub = [sbt(ph, 'P_ub%d' % i, [128, 2048], BF16) for i in range(3)]
                for u in p0_units(us, ub, ['dve', 'act', 'pool']):
                    u()
                P.barrier()
                P.emit()

        def phase_peer(fin_evs):
            with ExitStack() as ph:
                hTt = sbt(ph, 'G_hTt', [128, 16, 512], BF16)
                wqs = [sbt(ph, 'G_wqs%d' % i, [128, 16, 64], F32) for i in range(2)]
                wqb = [sbt(ph, 'G_wqb%d' % i, [128, 16, 64], BF16) for i in range(2)]
                sks = sbt(ph, 'G_sks', [64, 2, 128], F32)
                skb = sbt(ph, 'G_skb', [64, 2, 128], BF16)
                qTu = [sbt(ph, 'G_qTu%d' % i, [64, 512], BF16) for i in range(2)]
                sAll = sbt(ph, 'G_sAll', [128, 4, 16, 128], F32)
                tau = sbt(ph, 'G_tau', [128, 4, 8], F32)
                negc = sbt(ph, 'G_negc', [128, 4, 8], F32)
                kap = sbt(ph, 'G_kap', [128, 4, 8], F32)
                m1 = [sbt(ph, 'G_m1%d' % i, [128, 16], F32) for i in range(4)]
                m2 = [sbt(ph, 'G_m2%d' % i, [128, 16], F32) for i in range(4)]
                mc = [sbt(ph, 'G_mc%d' % i, [128, 16], F32) for i in range(4)]
                t1 = [sbt(ph, 'G_t1%d' % i, [128, 256], F32) for i in range(4)]
                cand = [sbt(ph, 'G_cand%d' % i, [128, 256], F32) for i in range(4)]
                sm = [sbt(ph, 'G_sm%d' % i, [128, 4], F32) for i in range(4)]
                e16 = [sbt(ph, 'G_e16%d' % i, [128, 16], F32) for i in range(4)]
                eb = [sbt(ph, 'G_e%d' % i, [128, 4, 128], F32) for i in range(5)]
                Wall = [sbt(ph, 'G_W%d' % i, [128, 8, 4, 128], BF16) for i in range(2)]
                GT = [sbt(ph, 'G_GT%d' % i, [128, 4, 512], BF16) for i in range(3)]
                Gs = [sbt(ph, 'G_Gs%d' % i, [128, 512], BF16) for i in range(2)]
                uch = [sbt(ph, 'G_uch%d' % i, [128, 16, 128], BF16) for i in range(2)]
                vch = [sbt(ph, 'G_vch%d' % i, [128, 4, 2048], BF16) for i in range(2)]
                ga = [sbt(ph, 'G_ga%d' % i, [128, 512], F32) for i in range(4)]
                GA = [sbt(ph, 'G_GA%d' % i, [128, 4, 512], BF16) for i in range(2)]
                acc = sbt(ph, 'G_acc', [128, 4, 2048], F32)
                DMA('sp', sks[:], I['skT'], [], ['sks'])
                CP('dve', skb[:], sks[:], ['sks'], ['skb'])
                c_ = {'u': 0, 'k': 0, 'w': 0, 's': 0, 'v': 0, 'g': 0}
                for tile in range(4):
                    DMA('sp', hTt[:], S['hT'][tile], [], ['hTt'])
                    for u in range(16):
                        b = c_['u'] % 2
                        c_['u'] += 1
                        DMA('sp', wqs[b][:], I['wq'][u], [], ['wqs%d' % b])
                        CP('act', wqb[b][:], wqs[b][:], ['wqs%d' % b], ['wqb%d' % b])
                        pi = nextps()
                        for kc in range(16):
                            MM(ps[pi][0:64, :], wqb[b][:, kc, :], hTt[:, kc, :], kc == 0, kc == 15, ['wqb%d' % b, 'hTt'], ['ps%d' % pi])
                        CP('act', qTu[b][:], ps[pi][0:64, :], ['ps%d' % pi], ['qTu%d' % b])
                        pi = nextps()
                        for blk in range(4):
                            MM(ps[pi][:, blk * 128:(blk + 1) * 128], qTu[b][:, blk * 128:(blk + 1) * 128], skb[:, u % 2, :], True, True,
                               ['qTu%d' % b, 'skb'], ['ps%d' % pi])
                        CP('dve', sAll[:, :, u, :], ps[pi][:, :].rearrange('p (a b) -> p a b', a=4), ['ps%d' % pi], ['sAll'])
                    def chain(blk, h, f):
                        steps = []
                        s1 = sAll[:, blk, 2 * h, :]
                        s2 = sAll[:, blk, 2 * h + 1, :]
                        for (sx, mm_, mk, tk_, tt_) in ((s1, m1[f], 'm1_%d' % f, 't1a_%d' % f, t1[f][:, 0:128]),
                                                        (s2, m2[f], 'm2_%d' % f, 't1b_%d' % f, t1[f][:, 128:256])):
                            steps.append(lambda sx=sx, mm_=mm_, mk=mk: P.op('dve', lambda e: e.max(out=mm_[:, 0:8], in_=sx), ['sAll'], [mk]))
                            steps.append(lambda sx=sx, mm_=mm_, mk=mk, tk_=tk_, tt_=tt_: P.op(
                                'dve', lambda e: e.match_replace(out=tt_, in_to_replace=mm_[:, 0:8], in_values=sx, imm_value=-3.0e38),
                                ['sAll', mk], [tk_]))
                            steps.append(lambda mm_=mm_, mk=mk, tk_=tk_, tt_=tt_: P.op(
                                'dve', lambda e: e.max(out=mm_[:, 8:16], in_=tt_), [tk_], [mk]))
                        steps.append(lambda: TT('pool', cand[f][:].rearrange('p (a b) -> p a b', a=16),
                                                m1[f][:].unsqueeze(2).broadcast_to([128, 16, 16]), m2[f][:].unsqueeze(1).broadcast_to([128, 16, 16]),
                                                ALU.add, ['m1_%d' % f, 'm2_%d' % f], ['cand%d' % f]))
                        steps.append(lambda: P.op('dve', lambda e: e.max(out=mc[f][:, 0:8], in_=cand[f][:]), ['cand%d' % f], ['mc%d' % f]))
                        steps.append(lambda: P.op('dve', lambda e: e.match_replace(out=t1[f][:], in_to_replace=mc[f][:, 0:8], in_values=cand[f][:],
                                                                                  imm_value=-3.0e38),
                                                  ['cand%d' % f, 'mc%d' % f, 't1a_%d' % f, 't1b_%d' % f], ['t1a_%d' % f, 't1b_%d' % f]))
                        steps.append(lambda: P.op('dve', lambda e: e.max(out=mc[f][:, 8:16], in_=t1[f][:]), ['t1a_%d' % f, 't1b_%d' % f], ['mc%d' % f]))
                        steps.append(lambda: CP('dve', tau[:, blk, h:h + 1], mc[f][:, 15:16], ['mc%d' % f], ['tau']))
                        steps.append(lambda: TS('dve', sm[f][:, 0:1], mc[f][:, 0:1], -1.0, None, ALU.mult, None, ['mc%d' % f], ['sm%d' % f]))
                        steps.append(lambda: ACT(e16[f][:], mc[f][:], AF.Exp, ['mc%d' % f, 'sm%d' % f], ['e16_%d' % f], bias=sm[f][:, 0:1]))
                        steps.append(lambda: P.op('dve', lambda e: e.tensor_reduce(sm[f][:, 1:2], e16[f][:], AX.X, ALU.add), ['e16_%d' % f], ['sm%d' % f]))
                        steps.append(lambda: ACT(sm[f][:, 2:3], sm[f][:, 1:2], AF.Ln, ['sm%d' % f], ['sm%d' % f]))
                        steps.append(lambda: TT('dve', negc[:, blk, h:h + 1], sm[f][:, 0:1], sm[f][:, 2:3], ALU.subtract, ['sm%d' % f], ['negc%d' % f]))
                        steps.append(lambda: TT('dve', sm[f][:, 3:4], mc[f][:, 15:16], negc[:, blk, h:h + 1], ALU.add,
                                                ['mc%d' % f, 'negc%d' % f], ['sm%d' % f]))
                        steps.append(lambda: ACT(sm[f][:, 3:4], sm[f][:, 3:4], AF.Exp, ['sm%d' % f], ['sm%d' % f]))
                        steps.append(lambda: TS('dve', kap[:, blk, h:h + 1], sm[f][:, 3:4], 0.9999, None, ALU.mult, None, ['sm%d' % f], ['kap']))
                        steps.append(lambda: TS('dve', sAll[:, blk, 2 * h, :], sAll[:, blk, 2 * h, :], negc[:, blk, h:h + 1], None, ALU.add, None,
                                                ['negc%d' % f, 'm1_%d' % f, 't1a_%d' % f], ['sAllw%d' % f]))
                        return steps
                    pairs = [(blk, h) for blk in range(4) for h in range(8)]
                    for g4 in range(0, 32, 4):
                        chains = [chain(blk, h, f) for f, (blk, h) in enumerate(pairs[g4:g4 + 4])]
                        for i in range(len(chains[0])):
                            for ch in chains:
                                ch[i]()
                    P.op('dve', lambda e: e.tensor_copy(sm[0][:, 0:1], sm[0][:, 0:1]),
                         ['sAllw0', 'sAllw1', 'sAllw2', 'sAllw3', 'negc0', 'negc1', 'negc2', 'negc3', 'sm0'], ['sAll', 'negc', 'sm0'])
                    def opsA(eg, blk, h):
                        wb_ = (4 * eg + blk) % 2
                        sb_ = c_['s'] % 5
                        c_['s'] += 1
                        if h < NPOOL:
                            TT('pool', eb[sb_][:],
                               sAll[:, blk, 2 * h, 4 * eg:4 * eg + 4].unsqueeze(2).broadcast_to([128, 4, 128]),
                               sAll[:, blk, 2 * h + 1, :].unsqueeze(1).broadcast_to([128, 4, 128]),
                               ALU.add, ['sAll'], ['e%d' % sb_])
                            ACT(eb[sb_][:], eb[sb_][:], AF.Exp, ['e%d' % sb_], ['e%d' % sb_])
                        else:
                            for c in range(4):
                                ACT(eb[sb_][:, c, :], sAll[:, blk, 2 * h + 1, :], AF.Exp, ['sAll'], ['e%d' % sb_],
                                    bias=sAll[:, blk, 2 * h, 4 * eg + c:4 * eg + c + 1])
                        STT('dve', Wall[wb_][:, h, :, :], eb[sb_][:], kap[:, blk, h:h + 1], eb[sb_][:], ALU.is_ge, ALU.mult,
                            ['kap', 'e%d' % sb_], ['W%d' % wb_])

                    def stageB(eg, blk):
                        k = 4 * eg + blk
                        wb_ = k % 2
                        pb = k % 2
                        for c in range(4):
                            for h in range(8):
                                MM(ps[pb][:, c * 128:(c + 1) * 128], Wall[wb_][:, h, c, :], identb[:], h == 0, h == 7,
                                   ['W%d' % wb_, 'identb'], ['ps%d' % pb])
                        def evac(eg=eg, blk=blk, pb=pb):
                            CP('act', GT[eg % 3][:, :, blk * 128:(blk + 1) * 128], ps[pb][:, :].rearrange('p (a b) -> p a b', a=4),
                               ['ps%d' % pb], ['GT%d' % (eg % 3)])
                        pend.append(evac)

                    def stageC_pe(eg, c):
                        ec = 4 * eg + c
                        u2 = c % 2
                        DMA('sp', uch[u2][:], S['puT'][ec], [], ['uch%d' % u2])
                        pa = 2 + c
                        for kc in range(16):
                            MM(ps[pa][:, :], uch[u2][:, kc, :], hTt[:, kc, :], kc == 0, kc == 15, ['uch%d' % u2, 'hTt'], ['ps%d' % pa])

                    def stageC_post(eg):
                        gb = eg % 2
                        for c in range(4):
                            ACT(ga[c][:], ps[2 + c][:, :], AF.Gelu_apprx_tanh, ['ps%d' % (2 + c)], ['ga%d' % c])
                        for c in range(4):
                            TT('pool', GA[gb][:, c, :], ga[c][:], GT[eg % 3][:, c, :], ALU.mult, ['ga%d' % c, 'GT%d' % (eg % 3)], ['GA%d' % gb])

                    def stageD1(eg, blk, dc):
                        gb = eg % 2
                        vb = eg % 2
                        pv_ = 6 + (c_['v'] % 2)
                        c_['v'] += 1
                        for c in range(4):
                            MM(ps[pv_][:, :], GA[gb][:, c, blk * 128:(blk + 1) * 128], vch[vb][:, c, dc * 512:(dc + 1) * 512],
                               c == 0, c == 3, ['GA%d' % gb, 'vch%d' % vb], ['ps%d' % pv_])
                        if eg == 0:
                            CP('dve', acc[:, blk, dc * 512:(dc + 1) * 512], ps[pv_][:, :], ['ps%d' % pv_], ['acc%d' % blk])
                        else:
                            TT('dve', acc[:, blk, dc * 512:(dc + 1) * 512], acc[:, blk, dc * 512:(dc + 1) * 512], ps[pv_][:, :],
                               ALU.add, ['ps%d' % pv_, 'acc%d' % blk], ['acc%d' % blk])

                    pend = []
                    for it in range(35):
                        doA = it < 32
                        if 2 <= it <= 33:
                            eg_ = it - 2
                            DMA('sp', vch[eg_ % 2][:], S['pv'][4 * eg_:4 * eg_ + 4].rearrange('c p d -> p c d'), [], ['vch%d' % (eg_ % 2)])
                        for blk in range(4):
                            for h in range(8):
                                if doA:
                                    opsA(it, blk, h)
                                if h == 3 or not doA:
                                    while pend:
                                        pend.pop(0)()
                                if blk == 0 and h == 3 and 2 <= it <= 33:
                                    stageC_post(it - 2)
                                if h % 2 == 1 and 3 <= it <= 34:
                                    stageD1(it - 3, blk, h // 2)
                            if 1 <= it <= 32:
                                stageC_pe(it - 1, blk)
                            if doA:
                                stageB(it, blk)
                    DMA('pool', S['pe'][tile * 4:(tile + 1) * 4].rearrange('b p d -> p b d'), acc[:], ['acc0', 'acc1', 'acc2', 'acc3'], [])
                P.barrier()
                P.emit()


        def phase_g2(fin_evs):
            with ExitStack() as ph:
                hblk = [sbt(ph, 'H_hblk%d' % i, [128, 2048], F32) for i in range(2)]
                pblk = [sbt(ph, 'H_pblk%d' % i, [128, 2048], F32) for i in range(2)]
                junk = sbt(ph, 'H_junk', [128, 2048], F32)
                hn = [sbt(ph, 'H_hn%d' % i, [128, 2048], F32) for i in range(2)]
                gB = sbt(ph, 'H_gB', [128, 2048], F32)
                bB = sbt(ph, 'H_bB', [128, 2048], F32)
                st = [sbt(ph, 'H_st%d' % i, [128, 4], F32) for i in range(2)]
                DMA('sp', gB[:], I['ln2g'], [], ['gB'])
                DMA('sp', bB[:], I['ln2b'], [], ['bB'])
                for gblk in range(16):
                    f = gblk % 2
                    DMA('sp', hblk[f][:], S['h'][gblk], [], ['hblk%d' % f])
                    DMA('sp', pblk[f][:], S['pe'][gblk], [], ['pblk%d' % f])
                    STT('dve', pblk[f][:], hblk[f][:], ALPHA, pblk[f][:], ALU.mult, ALU.add, ['hblk%d' % f, 'pblk%d' % f], ['pblk%d' % f])
                    layer_norm_block(pblk[f][:], 'pblk%d' % f, gB, bB, junk, hn, st, gblk)
                    fin_evs.append(DMA('pool', out[gblk], hn[f][:], ['hn%d' % f], []))
                P.barrier()
                P.emit()

        if 'p0' in phases:
            phase_p0()
        if 'kv0' in phases:
            phase_kv(0)
        if 'kv1' in phases:
            phase_kv(1)
        if 'q' in phases:
            phase_q()
        if 'da' in phases:
            phase_da()
        if 'nsa' in phases:
            phase_nsa()
        if 'e1' in phases:
            phase_e1()
        fin_evs = []
        if dbg and dbg_src in ('aT', 'nT'):
            fin_evs.append(DMA('pool', dbg_out, (aT if dbg_src == 'aT' else nT)[:], ['aT', 'nT'], []))
            fin_evs.append(DMA('pool', dbg2_out, dbg2sb[:], ['dbg2sb'], []))
            P.barrier()
            P.emit()
        mid.close()
        if 'e2' in phases:
            phase_e2()
        if 'peer' in phases:
            phase_peer(fin_evs)
        if 'g2' in phases:
            phase_g2(fin_evs)
        for ev in fin_evs:
            pass
        P.ops['sp'].append((None, [ev for ev in fin_evs], None, 0))
        P.emit()
    return nc


def rel_bucket_np(dist):
    n = np.maximum(dist, 0)
    nf = np.maximum(n, 1).astype(np.float32)
    large = 16 + (np.log(nf / np.float32(16)) / np.float32(math.log(8.0)) * np.float32(16)).astype(np.int32)
    large = np.minimum(large, 31)
    return np.where(n < 16, n, large)


def prep_inputs(inputs):
    x = np.asarray(inputs['x'], np.float32)
    w_in = np.asarray(inputs['w_in'], np.float32)[0]
    rel = np.asarray(inputs['rel_bias'], np.float32)
    wr = np.ascontiguousarray(w_in.reshape(16, 128, 9776).transpose(1, 0, 2))
    common = {
        'w_dakv': np.ascontiguousarray(wr[:, :, 1024:3072]),
        'w_nkv': np.ascontiguousarray(wr[:, :, 4096:5632]),
        'w_q': np.ascontiguousarray(np.concatenate([wr[:, :, 0:1024], wr[:, :, 3072:4096]], axis=2)),
        'w_gate': np.ascontiguousarray(wr[:, :, 5632:5680]),
        'w_mg': np.ascontiguousarray(wr[:, :, 5680:9776]),
        'c_da': np.ascontiguousarray(np.broadcast_to(rel[31, 0:8][None, :], (128, 8))),
        'lamq': np.ascontiguousarray(np.broadcast_to(np.asarray(inputs['da_lam_q'], np.float32)[0].reshape(1, 128), (128, 128))),
        'lamk': np.ascontiguousarray(np.broadcast_to(np.asarray(inputs['da_lam_k'], np.float32)[0].reshape(1, 128), (128, 128))),
        'subg': np.ascontiguousarray(np.broadcast_to(np.asarray(inputs['da_subln_g'], np.float32)[0].reshape(1, 128), (128, 128))),
        'ident': np.eye(128, dtype=np.float32),
        'w_bda': np.ascontiguousarray(np.asarray(inputs['w_branch_da'], np.float32)[0].reshape(8, 128, 2048).transpose(1, 0, 2)),
        'w_bnsa': np.ascontiguousarray(np.asarray(inputs['w_branch_nsa'], np.float32)[0].reshape(8, 128, 2048).transpose(1, 0, 2)),
        'w_out': np.ascontiguousarray(np.asarray(inputs['w_out'], np.float32)[0].reshape(16, 128, 2048).transpose(1, 0, 2)),
        'ln1g': np.ascontiguousarray(np.broadcast_to(np.asarray(inputs['ln1_g'], np.float32)[0][None, :], (128, 2048))),
        'ln1b': np.ascontiguousarray(np.broadcast_to(np.asarray(inputs['ln1_b'], np.float32)[0][None, :], (128, 2048))),
        'ln2g': np.ascontiguousarray(np.broadcast_to(np.asarray(inputs['ln2_g'], np.float32)[0][None, :], (128, 2048))),
        'ln2b': np.ascontiguousarray(np.broadcast_to(np.asarray(inputs['ln2_b'], np.float32)[0][None, :], (128, 2048))),
        'wq': np.ascontiguousarray(np.asarray(inputs['peer_wq'], np.float32)[0].reshape(16, 128, 16, 64).transpose(2, 1, 0, 3)),
        'skT': np.ascontiguousarray(np.stack([np.asarray(inputs['peer_subkey1'], np.float32)[0].T,
                                              np.asarray(inputs['peer_subkey2'], np.float32)[0].T], axis=1)),
        'puT': np.ascontiguousarray(np.asarray(inputs['peer_u'], np.float32)[0].reshape(128, 128, 16, 128).transpose(0, 3, 2, 1)),
        'pv': np.ascontiguousarray(np.asarray(inputs['peer_v'], np.float32)[0].reshape(128, 128, 2048)),
        'c_nsa': np.ascontiguousarray(np.broadcast_to(rel[31, 8:24][None, :], (128, 16))),
        'w1k': np.ascontiguousarray(np.asarray(inputs['cmp_w1_k'], np.float32)[0].reshape(32, 64, 256).transpose(1, 0, 2)),
        'w1v': np.ascontiguousarray(np.asarray(inputs['cmp_w1_v'], np.float32)[0].reshape(32, 64, 256).transpose(1, 0, 2)),
        'w2k': np.ascontiguousarray(np.asarray(inputs['cmp_w2_k'], np.float32)[0].reshape(2, 128, 64).transpose(1, 0, 2)),
        'w2v': np.ascontiguousarray(np.asarray(inputs['cmp_w2_v'], np.float32)[0].reshape(2, 128, 64).transpose(1, 0, 2)),
        'pekT': np.ascontiguousarray(np.asarray(inputs['cmp_pe_k'], np.float32)[0].T),
        'pevT': np.ascontiguousarray(np.asarray(inputs['cmp_pe_v'], np.float32)[0].T),
    }
    import ml_dtypes
    cidx = np.arange(512)
    sidx = np.arange(128)
    ov = ((cidx[:, None] * 16 <= sidx[None, :] * 64 + 63) & (cidx[:, None] * 16 + 31 >= sidx[None, :] * 64)).astype(np.float32)
    ovl = np.concatenate([ov, np.ones((512, 1), np.float32)], axis=1)
    ovl[511] = 0.0
    common['ovl'] = np.ascontiguousarray(ovl.reshape(4, 128, 129).transpose(1, 0, 2))
    kk = np.arange(8192)
    common['onehot'] = (((kk[None, :] // 64) % 64) == np.arange(64)[:, None]).astype(ml_dtypes.bfloat16)
    xTs = []
    for b in range(2):
        xTs.append(np.ascontiguousarray(x[b].reshape(16, 512, 16, 128).transpose(0, 3, 2, 1)))
    in_maps = []
    kl = np.arange(128)[:, None]
    xx = np.arange(2944)[None, :]
    for c in range(8):
        b, j = c // 4, c % 4
        tiles = [4 * t + j for t in range(4)]
        m = dict(common)
        m['xT'] = xTs[b]
        m['xTo'] = np.ascontiguousarray(xTs[b][tiles])
        m['xo'] = np.ascontiguousarray(
            np.concatenate([x[b, 512 * T:512 * (T + 1)] for T in tiles], axis=0).reshape(16, 128, 2048))
        d = xx - kl + 512 * j - 1920
        bk = rel_bucket_np(d)
        rb = rel[bk]
        m['raw_da'] = np.ascontiguousarray(rb[:, 0:2560, 0:8].transpose(2, 0, 1))
        m['raw_nsa'] = np.ascontiguousarray(rb[:, :, 8:24].transpose(2, 0, 1))
        m['mneg'] = np.where(d < 0, np.float32(NEGM), np.float32(0.0)).astype(np.float32)
        m['wneg'] = np.where((d < 0) | (d >= 512), np.float32(NEGM), np.float32(0.0)).astype(np.float32)
        cl = np.arange(128)[:, None, None]
        dl = np.arange(2)[None, :, None] - 1
        ql = np.arange(512)[None, None, :]
        m['cm'] = np.where(16 * cl + 31 + 2048 * dl <= 512 * j + ql, np.float32(0.0), np.float32(NEGM)).astype(np.float32)
        qpos = (512 * np.array(tiles)[:, None, None] + 128 * np.arange(4)[None, :, None] + np.arange(128)[None, None, :]).reshape(16, 128)
        cur = qpos // 64
        sb_ = np.arange(128)[None, None, :]
        valid = sb_ <= cur[:, :, None]
        forced = valid & ((sb_ == 0) | (sb_ > cur[:, :, None] - 2))
        vmul = (valid & ~forced).astype(np.float32)
        vadd = np.where(forced, np.float32(1e4) + sb_.astype(np.float32), np.where(valid, np.float32(0.0), np.float32(-1e30))).astype(np.float32)
        m['vmul'] = np.ascontiguousarray(vmul.transpose(1, 0, 2))
        m['vadd'] = np.ascontiguousarray(vadd.transpose(1, 0, 2))
        in_maps.append(m)
    return in_maps


_NC_CACHE = {}


def kernel(**inputs):
    in_maps = prep_inputs(inputs)
    if 'nc' not in _NC_CACHE:
        _NC_CACHE['nc'] = build_program()
    nc = _NC_CACHE['nc']
    res = run_bass_kernel_spmd(nc, in_maps, core_ids=list(range(8)))
    outp = np.zeros((2, 8192, 2048), np.float32)
    for c in range(8):
        b, j = c // 4, c % 4
        o = np.asarray(res.results[c]['out']).reshape(4, 512, 2048)
        for t in range(4):
            T = 4 * t + j
            outp[b, 512 * T:512 * (T + 1)] = o[t]
    return outp
```

```python
import math
from contextlib import ExitStack

import numpy as np
import concourse.bass as bass
import concourse.mybir as mybir
from concourse.bass_utils import run_bass_kernel_spmd

F32 = mybir.dt.float32
BF16 = mybir.dt.bfloat16
AF = mybir.ActivationFunctionType
ALU = mybir.AluOpType
AX = mybir.AxisListType

ENGS = ['pe', 'act', 'dve', 'pool', 'sp']
EPOCH = 16000
RING = {'sp': 40, 'pool': 16}
NEGM = -30000.0
NPOOL = 5
POOL_STT = ()
ALPHA = 2.0 ** 0.25
LAM_INIT = 0.8 - 0.6 * math.exp(0.0)


class Prog:
    def __init__(self, nc, stack):
        self.nc = nc
        self.stack = stack
        self.ops = {e: [] for e in ENGS}
        self.cnt = {e: 0 for e in ENGS}
        self.esems = {e: [] for e in ENGS}
        self.rings = {}
        self.ring_pos = {}
        self.ring_use = {}
        for q, n in RING.items():
            self.rings[q] = [stack.enter_context(nc.semaphore('r%s%d' % (q, i))) for i in range(n)]
            self.ring_pos[q] = 0
            self.ring_use[q] = [0] * n
        self.seen = {e: {} for e in ENGS}
        self.lastw = {}
        self.readers = {}
        self.last_ev = {e: None for e in ENGS}

    def _esem(self, eng, epoch):
        while len(self.esems[eng]) <= epoch:
            self.esems[eng].append(self.stack.enter_context(
                self.nc.semaphore('e%s%d' % (eng, len(self.esems[eng])))))
        return self.esems[eng][epoch]

    def op(self, eng, fn, reads=(), writes=(), dma=False):
        deps = {}

        def add(ev):
            if ev is None:
                return
            s, v = ev
            if v > deps.get(id(s), (None, 0))[1]:
                deps[id(s)] = (s, v)
        for k in reads:
            add(self.lastw.get(k))
        for k in writes:
            add(self.lastw.get(k))
            for ev in self.readers.get(k, {}).values():
                add(ev)
        if eng == 'pe':
            for t in self.esems['pe']:
                deps.pop(id(t), None)
        if dma:
            q = eng
            pos = self.ring_pos[q]
            self.ring_pos[q] = (pos + 1) % len(self.rings[q])
            sem = self.rings[q][pos]
            if self.ring_use[q][pos] > 0:
                add((sem, 16 * self.ring_use[q][pos]))
            self.ring_use[q][pos] += 1
            ev = (sem, 16 * self.ring_use[q][pos])
            inc = 16
        else:
            i = self.cnt[eng]
            self.cnt[eng] += 1
            sem = self._esem(eng, i // EPOCH)
            ev = (sem, i % EPOCH + 1)
            inc = 1
            self.last_ev[eng] = ev
        waits = []
        seen = self.seen[eng]
        for s, v in deps.values():
            if seen.get(id(s), 0) < v:
                seen[id(s)] = v
                waits.append((s, v))
        self.ops[eng].append((fn, waits, sem, inc))
        for k in reads:
            self.readers.setdefault(k, {})[(eng, id(sem))] = ev
        for k in writes:
            self.lastw[k] = ev
            self.readers[k] = {}
        return ev

    def barrier(self):
        evs = [ev for ev in self.last_ev.values() if ev is not None]
        for q in self.rings:
            for i, s in enumerate(self.rings[q]):
                if self.ring_use[q][i] > 0:
                    evs.append((s, 16 * self.ring_use[q][i]))
        for eng in ENGS:
            waits = []
            seen = self.seen[eng]
            for s, v in evs:
                if seen.get(id(s), 0) < v:
                    seen[id(s)] = v
                    waits.append((s, v))
            self.ops[eng].append((None, waits, None, 0))
        self.lastw = {}
        self.readers = {}

    def emit(self):
        nc = self.nc
        ops = self.ops
        self.ops = {e: [] for e in ENGS}
        with nc.Block() as block:
            def run(engname):
                def body(e):
                    for fn, waits, sem, inc in ops[engname]:
                        for s, v in waits:
                            e.wait_ge(s, v)
                        if fn is not None:
                            fn(e).then_inc(sem, inc)
                return body
            block.tensor(run('pe'))
            block.scalar(run('act'))
            block.vector(run('dve'))
            block.gpsimd(run('pool'))
            block.sync(run('sp'))


class Ctx:
    pass


def build_program(phases=('p0i', 'kv0', 'kv1', 'q', 'da', 'nsa', 'e1', 'e2', 'peer', 'g2'), dbg=False, dbg_src='aT'):
    nc = bass.Bass("TRN2", target_bir_lowering=False)
    C = Ctx()
    C.nc = nc

    def din(name, shape, dt=F32):
        return nc.dram_tensor(name, list(shape), dt, kind="ExternalInput").ap()

    def dscr(name, shape, dt=BF16):
        return nc.dram_tensor(name, list(shape), dt, kind="Internal").ap()

    I = {}
    I['xT'] = din('xT', [16, 128, 16, 512])
    I['xTo'] = din('xTo', [4, 128, 16, 512])
    I['xo'] = din('xo', [16, 128, 2048])
    I['w_dakv'] = din('w_dakv', [128, 16, 2048])
    I['w_nkv'] = din('w_nkv', [128, 16, 1536])
    I['w_q'] = din('w_q', [128, 16, 2048])
    I['w_gate'] = din('w_gate', [128, 16, 48])
    I['w_mg'] = din('w_mg', [128, 16, 4096])
    I['raw_da'] = din('raw_da', [8, 128, 2560])
    I['mneg'] = din('mneg', [128, 2944])
    I['wneg'] = din('wneg', [128, 2944])
    I['raw_nsa'] = din('raw_nsa', [16, 128, 2944])
    I['c_nsa'] = din('c_nsa', [128, 16])
    I['w1k'] = din('w1k', [64, 32, 256])
    I['w1v'] = din('w1v', [64, 32, 256])
    I['w2k'] = din('w2k', [128, 2, 64])
    I['w2v'] = din('w2v', [128, 2, 64])
    I['pekT'] = din('pekT', [64, 32])
    I['pevT'] = din('pevT', [64, 32])
    I['ovl'] = din('ovl', [128, 4, 129])
    I['cm'] = din('cm', [128, 2, 512])
    I['vmul'] = din('vmul', [128, 16, 128])
    I['vadd'] = din('vadd', [128, 16, 128])
    I['onehot'] = din('onehot', [64, 8192], BF16)
    I['w_bda'] = din('w_bda', [128, 8, 2048])
    I['w_bnsa'] = din('w_bnsa', [128, 8, 2048])
    I['w_out'] = din('w_out', [128, 16, 2048])
    I['ln1g'] = din('ln1g', [128, 2048])
    I['ln1b'] = din('ln1b', [128, 2048])
    I['ln2g'] = din('ln2g', [128, 2048])
    I['ln2b'] = din('ln2b', [128, 2048])
    I['wq'] = din('wq', [16, 128, 16, 64])
    I['skT'] = din('skT', [64, 2, 128])
    I['puT'] = din('puT', [128, 128, 16, 128])
    I['pv'] = din('pv', [128, 128, 2048])
    I['c_da'] = din('c_da', [128, 8])
    I['lamq'] = din('lamq', [128, 128])
    I['lamk'] = din('lamk', [128, 128])
    I['subg'] = din('subg', [128, 128])
    I['ident'] = din('ident', [128, 128])
    out = nc.dram_tensor('out', [16, 128, 2048], F32, kind="ExternalOutput").ap()
    dbg_out = None
    if dbg:
        dbg_out = nc.dram_tensor('dbg', [128, 8, 2048], BF16, kind="ExternalOutput").ap()
        dbg2_out = nc.dram_tensor('dbg2', [128, 4, 258], F32, kind="ExternalOutput").ap()

    S = {}
    S['kT'] = dscr('s_kT', [8, 128, 8192])
    S['v'] = dscr('s_v', [8, 8192, 128])
    S['nkT'] = dscr('s_nkT', [4, 256, 8192])
    S['nv'] = dscr('s_nv', [2, 4, 8192, 64])
    S['qT'] = dscr('s_qT', [16, 128, 2048])
    S['h'] = dscr('s_h', [16, 128, 2048], F32)
    S['mixT'] = dscr('s_mixT', [4, 128, 16, 512])
    S['hT'] = dscr('s_hT', [4, 128, 16, 512])
    S['pe'] = dscr('s_pe', [16, 128, 2048], F32)
    S['puT'] = dscr('s_puT', [128, 128, 16, 128])
    S['pv'] = dscr('s_pv', [128, 128, 2048])

    with ExitStack() as top:
        P = Prog(nc, top)

        def sbt(st, name, shape, dt):
            return st.enter_context(nc.sbuf_tensor(name, list(shape), dt))

        ps = [top.enter_context(nc.psum_tensor('ps%d' % i, [128, 512], F32)) for i in range(8)]

        def MM(o, lhsT, rhs, start, stop, r, w):
            P.op('pe', lambda e: e.matmul(o, lhsT, rhs, start=start, stop=stop), r, w)

        def ACT(o, i, func, r, w, **kw):
            P.op('act', lambda e: e.activation(o, i, func, **kw), r, w)

        def CP(eng, o, i, r, w):
            if eng == 'act':
                P.op('act', lambda e: e.copy(o, i), r, w)
            else:
                P.op(eng, lambda e: e.tensor_copy(o, i), r, w)

        def TS(eng, o, i0, s1, s2, op0, op1, r, w, **kw):
            if op1 is None:
                P.op(eng, lambda e: e.tensor_scalar(o, i0, s1, None, op0, **kw), r, w)
            else:
                P.op(eng, lambda e: e.tensor_scalar(o, i0, s1, s2, op0, op1, **kw), r, w)

        def STT(eng, o, i0, sc, i1, op0, op1, r, w):
            P.op(eng, lambda e: e.scalar_tensor_tensor(o, i0, sc, i1, op0, op1), r, w)

        def TT(eng, o, i0, i1, op, r, w):
            P.op(eng, lambda e: e.tensor_tensor(o, i0, i1, op), r, w)

        def DMA(q, o, i, r, w):
            return P.op(q, lambda e: e.dma_start(out=o, in_=i), r, w, dma=True)

        def MEMSET(eng, o, val, w):
            P.op(eng, lambda e: e.memset(o, val), (), w)

        gates = sbt(top, 'gates', [128, 16, 48], F32)
        identb = sbt(top, 'identb', [128, 128], BF16)
        neglam = sbt(top, 'neglam', [128, 1], F32)
        gs = sbt(top, 'gs', [128, 128], F32)
        cda = sbt(top, 'cda', [128, 8], F32)
        dbg2sb = sbt(top, 'dbg2sb', [128, 4, 258], F32) if dbg else None
        mid = ExitStack()
        aT = sbt(mid, 'aT', [128, 8, 2048], BF16)
        nT = sbt(mid, 'nT', [128, 8, 2048], BF16)

        with ExitStack() as ph:
            idf = sbt(ph, 'idf', [128, 128], F32)
            lq = sbt(ph, 'lq', [128, 128], F32)
            lk = sbt(ph, 'lk', [128, 128], F32)
            lp = sbt(ph, 'lp', [128, 128], F32)
            l2 = sbt(ph, 'l2', [128, 2], F32)
            DMA('sp', idf[:], I['ident'], [], ['idf'])
            DMA('sp', lq[:], I['lamq'], [], ['lq'])
            DMA('sp', lk[:], I['lamk'], [], ['lk'])
            DMA('sp', gs[:], I['subg'], [], ['gs'])
            DMA('sp', cda[:], I['c_da'], [], ['cda'])
            CP('dve', identb[:], idf[:], ['idf'], ['identb'])
            TT('dve', lp[:], lq[:], lk[:], ALU.mult, ['lq', 'lk'], ['lp'])
            P.op('dve', lambda e: e.tensor_reduce(l2[:], lp[:].rearrange('p (a b) -> p a b', a=2), AX.X, ALU.add),
                 ['lp'], ['l2'])
            ACT(l2[:], l2[:], AF.Exp, ['l2'], ['l2'])
            STT('dve', neglam[:], l2[:, 0:1], -1.0, l2[:, 1:2], ALU.mult, ALU.add, ['l2'], ['neglam'])
            TS('dve', neglam[:], neglam[:], -LAM_INIT, None, ALU.add, None, ['neglam'], ['neglam'])
            TS('dve', gs[:], gs[:], 1.0 - LAM_INIT, None, ALU.mult, None, ['gs'], ['gs'])
            P.barrier()
            P.emit()

        psrot = [0]

        def nextps():
            i = psrot[0]
            psrot[0] = (i + 1) % 8
            return i

        def phase_kv(passno):
            ncw = 2048 if passno == 0 else 1536
            wsrc = I['w_dakv'] if passno == 0 else I['w_nkv']
            with ExitStack() as ph:
                wb = sbt(ph, 'A%d_wb' % passno, [128, 16, ncw], BF16)
                wst = [sbt(ph, 'A%d_wst%d' % (passno, i), [128, ncw], F32) for i in range(2)]
                xs = [sbt(ph, 'A%d_xs%d' % (passno, i), [128, 4, 512], F32) for i in range(2)]
                xb = [sbt(ph, 'A%d_xb%d' % (passno, i), [128, 16, 512], BF16) for i in range(2)]
                evs = [sbt(ph, 'A%d_ev%d' % (passno, i), [128, 512], BF16) for i in range(4)]
                evi = [0]
                for kc in range(16):
                    b = kc % 2
                    DMA('sp', wst[b][:], wsrc[:, kc, :], [], ['wst%d' % b])
                    CP('pool', wb[:, kc, :], wst[b][:], ['wst%d' % b], ['wb'])

                def evac_store(pi, npart, ncol, dst_fn):
                    k = evi[0]
                    evi[0] = (k + 1) % 4
                    CP('act', evs[k][0:npart, 0:ncol], ps[pi][0:npart, 0:ncol], ['ps%d' % pi], ['ev%d' % k])
                    dst_fn(evs[k], k)

                for tile in range(16):
                    xbk = 'xb%d' % (tile % 2)
                    xbt = xb[tile % 2]
                    for qq in range(4):
                        half = qq % 2
                        DMA('sp', xs[half][:], I['xT'][tile, :, qq * 4:(qq + 1) * 4, :], [], ['xs%d' % half])
                        CP('dve', xbt[:, qq * 4:(qq + 1) * 4, :], xs[half][:], ['xs%d' % half], [xbk])
                    tsl = slice(tile * 512, (tile + 1) * 512)
                    if passno == 0:
                        fm = [(c * 128, S['kT'][c, :, tsl]) for c in range(8)]
                    else:
                        fm = []
                        for kind, cb in enumerate((0, 256, 512, 1024)):
                            for c2 in range(2):
                                fm.append((cb + c2 * 128, S['nkT'][kind, c2 * 128:(c2 + 1) * 128, tsl]))
                    for col0, dst in fm:
                        pi = nextps()
                        for kc in range(16):
                            MM(ps[pi][:, :], wb[:, kc, col0:col0 + 128], xbt[:, kc, :], kc == 0, kc == 15,
                               ['wb', xbk], ['ps%d' % pi])
                        evac_store(pi, 128, 512,
                                   lambda ev, k, dst=dst: DMA('pool', dst, ev[:, :], ['ev%d' % k], []))
                    for blk in range(4):
                        t0 = tile * 512 + blk * 128
                        if passno == 0:
                            tm = [(1024 + g4 * 512, 512,
                                   S['v'][g4 * 4:(g4 + 1) * 4, t0:t0 + 128, :].rearrange('h t e -> t h e'), 4)
                                  for g4 in range(2)]
                        else:
                            tm = [(768, 256, S['nv'][0, :, t0:t0 + 128, :].rearrange('g t e -> t g e'), 4),
                                  (1280, 256, S['nv'][1, :, t0:t0 + 128, :].rearrange('g t e -> t g e'), 4)]
                        for col0, ncol, dst, nh in tm:
                            pi = nextps()
                            for kc in range(16):
                                MM(ps[pi][:, 0:ncol], xbt[:, kc, blk * 128:(blk + 1) * 128], wb[:, kc, col0:col0 + ncol],
                                   kc == 0, kc == 15, ['wb', xbk], ['ps%d' % pi])
                            evac_store(pi, 128, ncol,
                                       lambda ev, k, dst=dst, ncol=ncol, nh=nh: DMA(
                                           'pool', dst, ev[:, 0:ncol].rearrange('t (h e) -> t h e', h=nh),
                                           ['ev%d' % k], []))
                P.barrier()
                P.emit()

        def phase_q():
            with ExitStack() as ph:
                wb = sbt(ph, 'B_wb', [128, 16, 2048], BF16)
                wgb = sbt(ph, 'B_wgb', [128, 16, 48], BF16)
                wst = [sbt(ph, 'B_wst%d' % i, [128, 2048], F32) for i in range(1)]
                wgs = sbt(ph, 'B_wgs', [128, 16, 48], F32)
                xs = [sbt(ph, 'B_xs%d' % i, [128, 4, 512], F32) for i in range(2)]
                xb = [sbt(ph, 'B_xb%d' % i, [128, 16, 512], BF16) for i in range(2)]
                evs = [sbt(ph, 'B_ev%d' % i, [128, 512], BF16) for i in range(4)]
                evi = [0]
                for kc in range(16):
                    b = 0
                    DMA('sp', wst[b][:], I['w_q'][:, kc, :], [], ['wst%d' % b])
                    CP('pool', wb[:, kc, :], wst[b][:], ['wst%d' % b], ['wb'])
                DMA('sp', wgs[:], I['w_gate'], [], ['wgs'])
                CP('pool', wgb[:], wgs[:], ['wgs'], ['wgb'])
                for tile in range(4):
                    xbk = 'xb%d' % (tile % 2)
                    xbt = xb[tile % 2]
                    for qq in range(4):
                        half = qq % 2
                        DMA('sp', xs[half][:], I['xTo'][tile, :, qq * 4:(qq + 1) * 4, :], [], ['xs%d' % half])
                        CP('dve', xbt[:, qq * 4:(qq + 1) * 4, :], xs[half][:], ['xs%d' % half], [xbk])
                    tsl = slice(tile * 512, (tile + 1) * 512)
                    for c in range(16):
                        pi = nextps()
                        for kc in range(16):
                            MM(ps[pi][:, :], wb[:, kc, c * 128:(c + 1) * 128], xbt[:, kc, :], kc == 0, kc == 15,
                               ['wb', xbk], ['ps%d' % pi])
                        k = evi[0]
                        evi[0] = (k + 1) % 4
                        CP('act', evs[k][:, :], ps[pi][:, :], ['ps%d' % pi], ['ev%d' % k])
                        DMA('pool', S['qT'][c, :, tsl], evs[k][:, :], ['ev%d' % k], [])
                    for blk in range(4):
                        pi = nextps()
                        for kc in range(16):
                            MM(ps[pi][:, 0:48], xbt[:, kc, blk * 128:(blk + 1) * 128], wgb[:, kc, :], kc == 0, kc == 15,
                               ['wgb', xbk], ['ps%d' % pi])
                        ACT(gates[:, tile * 4 + blk, :], ps[pi][:, 0:48], AF.Sigmoid, ['ps%d' % pi], ['gates'])
                P.barrier()
                P.emit()

        def phase_da():
            with ExitStack() as ph:
                KtF = sbt(ph, 'C_KtF', [128, 8192], BF16)
                Vh = sbt(ph, 'C_Vh', [128, 64, 129], BF16)
                strip = sbt(ph, 'C_strip', [128, 2560], F32)
                mneg = sbt(ph, 'C_mneg', [128, 2560], F32)
                QT = [sbt(ph, 'C_QT%d' % m, [128, 2048], BF16) for m in range(2)]
                pT = [sbt(ph, 'C_pT%d' % b, [128, 512], BF16) for b in range(5)]
                tmp = [sbt(ph, 'C_tmp%d' % b, [128, 512], F32) for b in range(4)]
                fz = [sbt(ph, 'C_fz%d' % i, [128, 4], F32) for i in range(2)]
                o0 = sbt(ph, 'C_o0', [128, 4, 129], F32)
                fu = [sbt(ph, 'C_fu%d' % i, [128, 128], F32) for i in range(2)]
                fo = [sbt(ph, 'C_fo%d' % i, [128, 128], F32) for i in range(2)]
                fj = [sbt(ph, 'C_fj%d' % i, [128, 128], F32) for i in range(2)]
                fon = [sbt(ph, 'C_fon%d' % i, [128, 128], BF16) for i in range(2)]
                DMA('sp', mneg[:], I['mneg'][:, 0:2560], [], ['mneg'])
                MEMSET('pool', Vh[:, :, 128:129], 1.0, ['Vh'])
                MEMSET('pool', QT[0][:], 0.0, ['QT0'])
                MEMSET('pool', QT[1][:], 0.0, ['QT1'])
                p0q = []
                if 'p0i' in phases:
                    pus = [sbt(ph, 'C_pus%d' % i, [128, 2048], F32) for i in range(3)]
                    pub = [sbt(ph, 'C_pub%d' % i, [128, 2048], BF16) for i in range(3)]
                    p0q = p0_units(pus, pub, ['pool'])
                st_ = {'slot': 0, 'fin': 0}
                pipe = []

                def push(pv):
                    pipe.append(pv)
                    if len(pipe) > 3:
                        pipe.pop(0)()

                def flush():
                    while pipe:
                        pipe.pop(0)()

                def da_finalize(h, t):
                    for sub in range(4):
                        f = st_['fin'] % 2
                        st_['fin'] += 1
                        acc = ps[4 + sub]
                        ak = 'ps%d' % (4 + sub)
                        if dbg and h == 0 and t == 0:
                            CP('dve', dbg2sb[:, sub, 0:129], o0[:, sub, :], ['o0_%d' % sub], ['dbg2sb'])
                            CP('dve', dbg2sb[:, sub, 129:258], acc[:, 0:129], [ak], ['dbg2sb'])
                        P.op('dve', lambda e, f=f, sub=sub: e.reciprocal(fz[f][:, 0:1], o0[:, sub, 128:129]), ['o0_%d' % sub], ['fz%d' % f])
                        P.op('dve', lambda e, f=f, acc=acc: e.reciprocal(fz[f][:, 1:2], acc[:, 128:129]), [ak], ['fz%d' % f])
                        TT('dve', fz[f][:, 2:3], fz[f][:, 1:2], neglam[:], ALU.mult, ['fz%d' % f, 'neglam'], ['fz%d' % f])
                        TS('dve', fu[f][:], acc[:, 0:128], fz[f][:, 2:3], None, ALU.mult, None, [ak, 'fz%d' % f], ['fu%d' % f])
                        STT('dve', fo[f][:], o0[:, sub, 0:128], fz[f][:, 0:1], fu[f][:], ALU.mult, ALU.add,
                            ['o0_%d' % sub, 'fz%d' % f, 'fu%d' % f], ['fo%d' % f])
                        TT('pool', fj[f][:], fo[f][:], fo[f][:], ALU.mult, ['fo%d' % f], ['fj%d' % f])
                        P.op('dve', lambda e, f=f: e.tensor_reduce(fz[f][:, 3:4], fj[f][:], AX.X, ALU.add), ['fj%d' % f], ['fz3_%d' % f])
                        TS('dve', fz[f][:, 3:4], fz[f][:, 3:4], 1.0 / 128.0, 1e-5, ALU.mult, ALU.add, ['fz3_%d' % f], ['fz3_%d' % f])
                        ACT(fz[f][:, 3:4], fz[f][:, 3:4], AF.Sqrt, ['fz3_%d' % f], ['fz3_%d' % f])
                        P.op('dve', lambda e, f=f: e.reciprocal(fz[f][:, 3:4], fz[f][:, 3:4]), ['fz3_%d' % f], ['fz3_%d' % f])
                        STT('dve', fon[f][:], fo[f][:], fz[f][:, 3:4], gs[:], ALU.mult, ALU.mult,
                            ['fo%d' % f, 'fz3_%d' % f, 'gs'], ['fon%d' % f])
                        MM(ps[3][:, 0:128], fon[f][:], identb[:], True, True, ['fon%d' % f, 'identb'], ['ps3'])
                        blk = t * 4 + sub
                        CP('act', aT[:, h, blk * 128:(blk + 1) * 128], ps[3][:, 0:128], ['ps3'], ['aT'])

                for h in range(8):
                    flush()
                    DMA('sp', KtF[:], S['kT'][h], [], ['KtF'])
                    for m in range(2):
                        DMA('sp', QT[m][m * 64:(m + 1) * 64, :], S['qT'][h, m * 64:(m + 1) * 64, :], [], ['QT%d' % m])
                    for q4 in range(4):
                        DMA('sp', Vh[:, q4 * 16:(q4 + 1) * 16, 0:128],
                            S['v'][h, q4 * 2048:(q4 + 1) * 2048, :].rearrange('(s p) e -> p s e', p=128), [], ['Vh'])
                    DMA('sp', strip[:], I['raw_da'][h], [], ['strip'])
                    STT('dve', strip[:], strip[:], cda[:, h:h + 1], mneg[:], ALU.subtract, ALU.add,
                        ['strip', 'cda', 'mneg'], ['strip'])
                    for t in range(4):
                        nsl = 16 * (t + 1)
                        for m in range(2):
                            for s in range(nsl):
                                c = st_['slot']
                                st_['slot'] += 1
                                if p0q and c % 10 == 0:
                                    p0q.pop(0)()
                                b2 = c % 4
                                b3 = c % 5
                                near = s >= 16 * t - 1
                                pi = b2
                                MM(ps[pi][:, :], KtF[:, s * 128:(s + 1) * 128], QT[m][:, t * 512:(t + 1) * 512],
                                   True, True, ['KtF', 'QT%d' % m], ['ps%d' % pi])
                                if near:
                                    x0 = 128 * (15 - (s - 16 * t))
                                    STT('dve', tmp[b2][:], ps[pi][:, :], 0.125, strip[:, x0:x0 + 512], ALU.mult, ALU.add,
                                        ['ps%d' % pi, 'strip'], ['tmp%d' % b2])
                                    ACT(pT[b3][:], tmp[b2][:], AF.Exp, ['tmp%d' % b2], ['pT%d' % b3])
                                else:
                                    ACT(pT[b3][:], ps[pi][:, :], AF.Exp, ['ps%d' % pi], ['pT%d' % b3], scale=0.125)

                                def pv(h=h, t=t, m=m, s=s, b3=b3, nsl=nsl):
                                    for sub in range(4):
                                        MM(ps[4 + sub][:, 0:129], pT[b3][:, sub * 128:(sub + 1) * 128],
                                           Vh[:, s, :], s == 0, s == nsl - 1, ['pT%d' % b3, 'Vh'], ['ps%d' % (4 + sub)])
                                    if s == nsl - 1:
                                        if m == 0:
                                            for sub in range(4):
                                                CP('dve', o0[:, sub, :], ps[4 + sub][:, 0:129], ['ps%d' % (4 + sub)], ['o0_%d' % sub])
                                        else:
                                            da_finalize(h, t)
                                push(pv)
                flush()
                while p0q:
                    p0q.pop(0)()
                P.barrier()
                P.emit()


        def phase_nsa():
            with ExitStack() as ph:
                KCT = sbt(ph, 'D_KCT', [64, 4, 512], BF16)
                Rg = sbt(ph, 'D_Rg', [128, 4, 4, 193], BF16)
                cns = sbt(ph, 'D_cns', [128, 16], F32)
                MEMSET('pool', KCT[:], 0.0, ['KCT'])
                MEMSET('pool', Rg[:], 0.0, ['Rg'])
                DMA('sp', cns[:], I['c_nsa'], [], ['cns'])
                with ExitStack() as p0:
                    cT = sbt(p0, 'D0_cT', [64, 8192], BF16)
                    w1s = sbt(p0, 'D0_w1s', [64, 8, 256], F32)
                    w1b = sbt(p0, 'D0_w1b', [64, 32, 256], BF16)
                    w2s = sbt(p0, 'D0_w2s', [128, 2, 64], F32)
                    w2b = sbt(p0, 'D0_w2b', [128, 2, 64], BF16)
                    pes = sbt(p0, 'D0_pes', [64, 32], F32)
                    peb = sbt(p0, 'D0_peb', [64, 32], BF16)
                    b1 = sbt(p0, 'D0_b1', [128, 2], F32)
                    hT = sbt(p0, 'D0_hT', [128, 2, 512], BF16)
                    ovs = sbt(p0, 'D0_ovs', [128, 4, 129], F32)
                    DMA('sp', ovs[:], I['ovl'], [], ['ovs'])
                    for g in range(4):
                        CP('pool', Rg[:, :, g, 0:129], ovs[:], ['ovs'], ['Rg'])
                    for kind in range(2):
                        w1src = I['w1k'] if kind == 0 else I['w1v']
                        for p8 in range(4):
                            DMA('sp', w1s[:], w1src[:, p8 * 8:(p8 + 1) * 8, :], [], ['w1s'])
                            CP('pool', w1b[:, p8 * 8:(p8 + 1) * 8, :], w1s[:], ['w1s'], ['w1b'])
                        DMA('sp', w2s[:], I['w2k'] if kind == 0 else I['w2v'], [], ['w2s'])
                        CP('pool', w2b[:], w2s[:], ['w2s'], ['w2b'])
                        DMA('sp', pes[:], I['pekT'] if kind == 0 else I['pevT'], [], ['pes'])
                        CP('pool', peb[:], pes[:], ['pes'], ['peb'])
                        for hc in range(2):
                            pi = nextps()
                            for p in range(32):
                                MM(ps[pi][:, 0:1], w1b[:, p, hc * 128:(hc + 1) * 128], peb[:, p:p + 1], p == 0, p == 31,
                                   ['w1b', 'peb'], ['ps%d' % pi])
                            CP('dve', b1[:, hc:hc + 1], ps[pi][:, 0:1], ['ps%d' % pi], ['b1'])
                        for g in range(4):
                            DMA('sp', cT[:], S['nkT'][kind, g * 64:(g + 1) * 64, :], [], ['cT'])
                            for hc in range(2):
                                pi = nextps()
                                for p in range(32):
                                    MM(ps[pi][:, 0:511], w1b[:, p, hc * 128:(hc + 1) * 128], cT[:, p:p + 16 * 510 + 1:16],
                                       p == 0, p == 31, ['w1b', 'cT'], ['ps%d' % pi])
                                ACT(hT[:, hc, 0:511], ps[pi][:, 0:511], AF.Gelu_apprx_tanh, ['ps%d' % pi, 'b1'], ['hT'],
                                    bias=b1[:, hc:hc + 1])
                            if kind == 0:
                                pi = nextps()
                                for hc in range(2):
                                    MM(ps[pi][0:64, 0:511], w2b[:, hc, :], hT[:, hc, 0:511], hc == 0, hc == 1,
                                       ['w2b', 'hT'], ['ps%d' % pi])
                                CP('act', KCT[:, g, 0:511], ps[pi][0:64, 0:511], ['ps%d' % pi], ['KCT'])
                            else:
                                for cc in range(4):
                                    ncl = 128 if cc < 3 else 127
                                    pi = nextps()
                                    for hc in range(2):
                                        MM(ps[pi][0:ncl, 0:64], hT[:, hc, cc * 128:cc * 128 + ncl], w2b[:, hc, :], hc == 0, hc == 1,
                                           ['w2b', 'hT'], ['ps%d' % pi])
                                    CP('act', Rg[0:ncl, cc, g, 129:193], ps[pi][0:ncl, 0:64], ['ps%d' % pi], ['Rg'])
                    P.barrier()
                    P.emit()
                cm = sbt(ph, 'D_cm', [128, 2, 512], F32)
                vmul = sbt(ph, 'D_vmul', [128, 16, 128], F32)
                vadd = sbt(ph, 'D_vadd', [128, 16, 128], F32)
                Kbuf = sbt(ph, 'D_Kbuf', [128, 8192], BF16)
                Vbuf = sbt(ph, 'D_Vbuf', [128, 64, 65], BF16)
                strip = sbt(ph, 'D_strip', [128, 2944], F32)
                neg = sbt(ph, 'D_neg', [128, 2944], F32)
                QTn = [sbt(ph, 'D_QTn%d' % i, [64, 2048], BF16) for i in range(4)]
                QA = [sbt(ph, 'D_QA%d' % i, [128, 512], BF16) for i in range(2)]
                QB = [sbt(ph, 'D_QB%d' % i, [128, 512], BF16) for i in range(2)]
                selT = [sbt(ph, 'D_selT%d' % i, [128, 512], BF16) for i in range(4)]
                onsa = sbt(ph, 'D_onsa', [128, 16, 4, 64], F32)
                impacc = sbt(ph, 'D_impacc', [128, 4, 128], F32)
                pT = [sbt(ph, 'D_pT%d' % b, [128, 512], BF16) for b in range(5)]
                tmp = [sbt(ph, 'D_tmp%d' % b, [128, 512], F32) for b in range(4)]
                fz = [sbt(ph, 'D_fz%d' % i, [128, 4], F32) for i in range(2)]
                sc = [sbt(ph, 'D_sc%d' % i, [128, 128], F32) for i in range(2)]
                sc2 = [sbt(ph, 'D_sc2%d' % i, [128, 128], F32) for i in range(2)]
                m8 = [sbt(ph, 'D_m8%d' % i, [128, 16], F32) for i in range(2)]
                sng = [sbt(ph, 'D_sng%d' % i, [128, 128], BF16) for i in range(2)]
                onb = [sbt(ph, 'D_onb%d' % i, [128, 128], BF16) for i in range(2)]
                accT_sb = [sbt(ph, 'D_accT%d' % i, [65, 512], F32) for i in range(1)]
                QZ = [sbt(ph, 'D_QZ%d' % i, [128, 512], BF16) for i in range(2)]
                MEMSET('pool', QZ[0][:], 0.0, ['QZ0'])
                MEMSET('pool', QZ[1][:], 0.0, ['QZ1'])
                identf = sbt(ph, 'D_identf', [128, 128], F32)
                DMA('sp', identf[:], I['ident'], [], ['identf'])
                DMA('sp', cm[:], I['cm'], [], ['cm'])
                DMA('sp', vmul[:], I['vmul'], [], ['vmul'])
                DMA('sp', vadd[:], I['vadd'], [], ['vadd'])
                DMA('sp', Kbuf[64:128, :], I['onehot'], [], ['KbufHi'])
                MEMSET('pool', Vbuf[:, :, 64:65], 1.0, ['Vbuf'])
                cnt = {'slot': 0, 'fin': 0, 'q': 0, 'tr': 0, 'grp': 0}

                pipe = []

                def push(pv):
                    pipe.append(pv)
                    if len(pipe) > 3:
                        pipe.pop(0)()

                def flush():
                    while pipe:
                        pipe.pop(0)()

                def attn_slot(lhsT, rhs, rkeys, bias_ap, vrhs, vkeys, ncolv, first, last, after=None, tbank=None):
                    c = cnt['slot']
                    cnt['slot'] = c + 1
                    b2, b3 = c % 4, c % 5
                    MM(ps[b2][:, :], lhsT, rhs, True, True, rkeys, ['ps%d' % b2])
                    if bias_ap is not None:
                        STT('dve', tmp[b2][:], ps[b2][:, :], 0.125, bias_ap, ALU.mult, ALU.add,
                            ['ps%d' % b2, 'strip', 'cm'], ['tmp%d' % b2])
                        ACT(pT[b3][:], tmp[b2][:], AF.Exp, ['tmp%d' % b2], ['pT%d' % b3])
                    else:
                        ACT(pT[b3][:], ps[b2][:, :], AF.Exp, ['ps%d' % b2], ['pT%d' % b3], scale=0.125)

                    def pv():
                        if tbank is None:
                            for sub in range(4):
                                MM(ps[4 + sub][:, 0:ncolv], pT[b3][:, sub * 128:(sub + 1) * 128], vrhs, first, last,
                                   ['pT%d' % b3] + vkeys, ['ps%d' % (4 + sub)])
                        else:
                            MM(ps[tbank][0:ncolv, :], vrhs, pT[b3][:, :], first, last, ['pT%d' % b3] + vkeys, ['ps%d' % tbank])
                        if after is not None:
                            after()
                    push(pv)

                def fin_branch(t, hh, n, gidx, dcol, ncol0, first_branch, tb=None):
                    if tb is not None:
                        k = cnt['tr'] % 2
                        cnt['tr'] += 1
                        CP('act', accT_sb[0][0:65, :], ps[tb][0:65, :], ['ps%d' % tb], ['accT0'])
                        for sub in range(4):
                            MM(ps[6 + k][:, sub * 65:(sub + 1) * 65], accT_sb[0][0:65, sub * 128:(sub + 1) * 128], identf[0:65, 0:65],
                               True, True, ['accT0', 'identf'], ['ps%d' % (6 + k)])
                    for sub in range(4):
                        f = cnt['fin'] % 2
                        cnt['fin'] += 1
                        blk = 4 * t + sub
                        if tb is None:
                            acc = ps[4 + sub]
                            ak = 'ps%d' % (4 + sub)
                            c0 = 0
                        else:
                            acc = ps[6 + k]
                            ak = 'ps%d' % (6 + k)
                            c0 = sub * 65
                        fk = 'fz%d' % f
                        TS('dve', fz[f][:, 0:1], acc[:, c0 + dcol:c0 + dcol + 1], 1e-30, None, ALU.max, None, [ak], [fk])
                        P.op('dve', lambda e, f=f: e.reciprocal(fz[f][:, 1:2], fz[f][:, 0:1]), [fk], [fk])
                        TT('dve', fz[f][:, 2:3], fz[f][:, 1:2], gates[:, blk, n * 3 + gidx:n * 3 + gidx + 1], ALU.mult,
                           [fk, 'gates'], [fk])
                        if first_branch:
                            if hh == 0:
                                TS('dve', impacc[:, sub, :], acc[:, 0:128], fz[f][:, 1:2], None, ALU.mult, None,
                                   [ak, fk], ['impacc%d' % sub])
                            else:
                                STT('dve', impacc[:, sub, :], acc[:, 0:128], fz[f][:, 1:2], impacc[:, sub, :], ALU.mult, ALU.add,
                                    [ak, fk, 'impacc%d' % sub], ['impacc%d' % sub])
                            TS('dve', onsa[:, blk, hh, :], acc[:, ncol0:ncol0 + 64], fz[f][:, 2:3], None, ALU.mult, None,
                               [ak, fk], ['onsa'])
                        else:
                            STT('dve', onsa[:, blk, hh, :], acc[:, c0 + ncol0:c0 + ncol0 + 64], fz[f][:, 2:3], onsa[:, blk, hh, :],
                                ALU.mult, ALU.add, [ak, fk, 'onsa'], ['onsa'])

                for g in range(4):
                    for hh in range(4):
                        n = 4 * g + hh
                        DMA('sp', QTn[hh][:], S['qT'][8 + n // 2, (n % 2) * 64:(n % 2) * 64 + 64, :], [], ['QTn%d' % hh])
                    def topk_code(g, t):
                        for sub in range(4):
                            f = cnt['fin'] % 2
                            cnt['fin'] += 1
                            blk = 4 * t + sub
                            TT('dve', sc[f][:], impacc[:, sub, :], vmul[:, blk, :], ALU.mult, ['impacc%d' % sub, 'vmul'], ['sc%d' % f])
                            TT('dve', sc[f][:], sc[f][:], vadd[:, blk, :], ALU.add, ['sc%d' % f, 'vadd'], ['sc%d' % f])
                            P.op('dve', lambda e, f=f: e.max(out=m8[f][:, 0:8], in_=sc[f][:]), ['sc%d' % f], ['m8_%d' % f])
                            P.op('dve', lambda e, f=f: e.match_replace(out=sc2[f][:], in_to_replace=m8[f][:, 0:8],
                                                                      in_values=sc[f][:], imm_value=-3.0e38),
                                 ['sc%d' % f, 'm8_%d' % f], ['sc2%d' % f])
                            P.op('dve', lambda e, f=f: e.max(out=m8[f][:, 8:16], in_=sc2[f][:]), ['sc2%d' % f], ['m8_%d' % f])
                            TS('dve', sng[f][:], sc[f][:], m8[f][:, 15:16], -240000.0, ALU.is_lt, ALU.mult,
                               ['sc%d' % f, 'm8_%d' % f], ['sng%d' % f])
                            MM(ps[3][:, 0:128], sng[f][:], identb[:], True, True, ['sng%d' % f, 'identb'], ['ps3'])
                            CP('act', selT[t][:, sub * 128:(sub + 1) * 128], ps[3][:, 0:128], ['ps3'], ['selT%d' % t])
                            if dbg and g == 0 and t == 0:
                                CP('dve', dbg2sb[:, sub, 0:128], sc[f][:], ['sc%d' % f], ['dbg2sb'])
                                CP('dve', dbg2sb[:, sub, 129:145], m8[f][:], ['m8_%d' % f], ['dbg2sb'])

                    for t in range(4):
                        for hh in range(4):
                            n = 4 * g + hh
                            for cc in range(t + 1):
                                bias_ap = cm[:, cc - t + 1, :] if cc >= t - 1 else None
                                aft = None
                                if cc == t:
                                    def aft(t=t, hh=hh, n=n, g=g):
                                        fin_branch(t, hh, n, 0, 128, 129, True)
                                        if hh == 3:
                                            topk_code(g, t)
                                attn_slot(KCT[:, g, cc * 128:(cc + 1) * 128], QTn[hh][:, t * 512:(t + 1) * 512],
                                          ['KCT', 'QTn%d' % hh], bias_ap, Rg[:, cc, g, :], ['Rg'], 193, cc == 0, cc == t, after=aft)
                    flush()
                    DMA('sp', Kbuf[0:64, :], S['nkT'][2, g * 64:(g + 1) * 64, :], [], ['KbufLo'])
                    for q4 in range(4):
                        DMA('sp', Vbuf[:, q4 * 16:(q4 + 1) * 16, 0:64],
                            S['nv'][0, g, q4 * 2048:(q4 + 1) * 2048, :].rearrange('(s p) e -> p s e', p=128), [], ['Vbuf'])
                    DMA('sp', neg[:], I['mneg'], [], ['neg'])
                    for hh in range(4):
                        n = 4 * g + hh
                        DMA('sp', strip[:], I['raw_nsa'][n], [], ['strip'])
                        STT('dve', strip[:], strip[:], cns[:, n:n + 1], neg[:], ALU.subtract, ALU.add,
                            ['strip', 'cns', 'neg'], ['strip'])
                        for t in range(4):
                            qb = cnt['q'] % 2
                            cnt['q'] += 1
                            CP('pool', QA[qb][0:64, :], QTn[hh][:, t * 512:(t + 1) * 512], ['QTn%d' % hh], ['QA%d' % qb])
                            CP('pool', QB[qb][0:64, :], QTn[hh][:, t * 512:(t + 1) * 512], ['QTn%d' % hh], ['QB%d' % qb])
                            DMA('sp', QA[qb][64:128, :], selT[t][0:64, :], ['selT%d' % t], ['QA%d' % qb])
                            CP('pool', QB[qb][64:128, :], selT[t][64:128, :], ['selT%d' % t], ['QB%d' % qb])
                            nsl = 16 * (t + 1)
                            tb = 4 + (cnt['grp'] % 2)
                            cnt['grp'] += 1
                            for s_ in range(nsl):
                                near = s_ >= 16 * t - 1
                                bias_ap = strip[:, 128 * (15 - (s_ - 16 * t)):128 * (15 - (s_ - 16 * t)) + 512] if near else None
                                qq = QA[qb] if s_ < 32 else QB[qb]
                                qk = ('QA%d' if s_ < 32 else 'QB%d') % qb
                                aft = None
                                if s_ == nsl - 1:
                                    def aft(t=t, hh=hh, n=n, tb=tb):
                                        fin_branch(t, hh, n, 1, 64, 0, False, tb=tb)
                                attn_slot(Kbuf[:, s_ * 128:(s_ + 1) * 128], qq[:, :], ['KbufLo', 'KbufHi', qk], bias_ap,
                                          Vbuf[:, s_, :], ['Vbuf'], 65, s_ == 0, s_ == nsl - 1, after=aft, tbank=tb)
                    flush()
                    DMA('sp', Kbuf[0:64, :], S['nkT'][3, g * 64:(g + 1) * 64, :], [], ['KbufLo'])
                    for q4 in range(4):
                        DMA('sp', Vbuf[:, q4 * 16:(q4 + 1) * 16, 0:64],
                            S['nv'][1, g, q4 * 2048:(q4 + 1) * 2048, :].rearrange('(s p) e -> p s e', p=128), [], ['Vbuf'])
                    DMA('sp', neg[:], I['wneg'], [], ['neg'])
                    for hh in range(4):
                        n = 4 * g + hh
                        DMA('sp', strip[:], I['raw_nsa'][n], [], ['strip'])
                        STT('dve', strip[:], strip[:], cns[:, n:n + 1], neg[:], ALU.subtract, ALU.add,
                            ['strip', 'cns', 'neg'], ['strip'])
                        for t in range(4):
                            s0 = max(0, 16 * t - 4)
                            s1 = 16 * t + 15
                            tb = 4 + (cnt['grp'] % 2)
                            cnt['grp'] += 1
                            qz = cnt['grp'] % 2
                            CP('pool', QZ[qz][0:64, :], QTn[hh][:, t * 512:(t + 1) * 512], ['QTn%d' % hh], ['QZ%d' % qz])
                            for s_ in range(s0, s1 + 1):
                                x0 = 128 * (15 - (s_ - 16 * t))
                                aft = None
                                if s_ == s1:
                                    def aft(t=t, hh=hh, n=n, tb=tb):
                                        fin_branch(t, hh, n, 2, 64, 0, False, tb=tb)
                                attn_slot(Kbuf[:, s_ * 128:(s_ + 1) * 128], QZ[qz][:, :],
                                          ['KbufLo', 'KbufHi', 'QZ%d' % qz], strip[:, x0:x0 + 512], Vbuf[:, s_, :], ['Vbuf'], 65,
                                          s_ == s0, s_ == s1, after=aft, tbank=tb)
                    flush()
                    for blk in range(16):
                        for pr in range(2):
                            f = cnt['fin'] % 2
                            cnt['fin'] += 1
                            CP('pool', onb[f][:].rearrange('p (h e) -> p h e', h=2), onsa[:, blk, 2 * pr:2 * pr + 2, :],
                               ['onsa'], ['onb%d' % f])
                            pi = 3
                            MM(ps[pi][:, 0:128], onb[f][:], identb[:], True, True, ['onb%d' % f, 'identb'], ['ps%d' % pi])
                            CP('act', nT[:, 2 * g + pr, blk * 128:(blk + 1) * 128], ps[pi][:, 0:128], ['ps%d' % pi], ['nT'])
                P.barrier()
                P.emit()


        def phase_e1():
            with ExitStack() as ph:
                xs = [sbt(ph, 'E_xs%d' % i, [128, 2, 512], F32) for i in range(2)]
                xb = sbt(ph, 'E_xb', [128, 16, 2048], BF16)
                wgs = [sbt(ph, 'E_wgs%d' % i, [128, 16, 128], F32) for i in range(2)]
                wbs = [sbt(ph, 'E_wbs%d' % i, [128, 8, 128], F32) for i in range(2)]
                wga = [sbt(ph, 'E_wga%d' % i, [128, 16, 128], BF16) for i in range(2)]
                wgn = [sbt(ph, 'E_wgn%d' % i, [128, 16, 128], BF16) for i in range(2)]
                wba = [sbt(ph, 'E_wba%d' % i, [128, 8, 128], BF16) for i in range(2)]
                wbn = [sbt(ph, 'E_wbn%d' % i, [128, 8, 128], BF16) for i in range(2)]
                sA = [sbt(ph, 'E_sA%d' % i, [128, 512], F32) for i in range(2)]
                sB = [sbt(ph, 'E_sB%d' % i, [128, 512], F32) for i in range(2)]
                m1 = [sbt(ph, 'E_m1%d' % i, [128, 512], F32) for i in range(2)]
                mx = [sbt(ph, 'E_mx%d' % i, [128, 512], BF16) for i in range(3)]
                k = 0
                for tile in range(4):
                    for qq in range(8):
                        half = k % 2
                        k += 1
                        DMA('sp', xs[half][:], I['xTo'][tile, :, qq * 2:(qq + 1) * 2, :], [], ['xs%d' % half])
                        CP('dve' if qq % 2 == 0 else 'act', xb[:, qq * 2:(qq + 1) * 2, tile * 512:(tile + 1) * 512], xs[half][:],
                           ['xs%d' % half], ['xb'])
                it = 0
                mi = 0
                for c in range(16):
                    b = c % 2
                    DMA('sp', wgs[0][:], I['w_mg'][:, :, c * 128:(c + 1) * 128], [], ['wgs0'])
                    CP('pool', wga[b][:], wgs[0][:], ['wgs0'], ['wga%d' % b])
                    DMA('sp', wgs[1][:], I['w_mg'][:, :, 2048 + c * 128:2048 + (c + 1) * 128], [], ['wgs1'])
                    CP('dve', wgn[b][:], wgs[1][:], ['wgs1'], ['wgn%d' % b])
                    DMA('sp', wbs[0][:], I['w_bda'][:, :, c * 128:(c + 1) * 128], [], ['wbs0'])
                    CP('pool', wba[b][:], wbs[0][:], ['wbs0'], ['wba%d' % b])
                    DMA('sp', wbs[1][:], I['w_bnsa'][:, :, c * 128:(c + 1) * 128], [], ['wbs1'])
                    CP('dve', wbn[b][:], wbs[1][:], ['wbs1'], ['wbn%d' % b])
                    for tile in range(4):
                        tsl = slice(tile * 512, (tile + 1) * 512)
                        p2 = it % 2
                        it += 1
                        pA, pB, pC, pD = (4 * p2 + 0), (4 * p2 + 1), (4 * p2 + 2), (4 * p2 + 3)
                        for kc in range(16):
                            MM(ps[pA][:, :], wga[b][:, kc, :], xb[:, kc, tsl], kc == 0, kc == 15, ['wga%d' % b, 'xb'], ['ps%d' % pA])
                        for kc in range(16):
                            MM(ps[pB][:, :], wgn[b][:, kc, :], xb[:, kc, tsl], kc == 0, kc == 15, ['wgn%d' % b, 'xb'], ['ps%d' % pB])
                        for kc in range(8):
                            MM(ps[pC][:, :], wba[b][:, kc, :], aT[:, kc, tsl], kc == 0, kc == 7, ['wba%d' % b, 'aT'], ['ps%d' % pC])
                        for kc in range(8):
                            MM(ps[pD][:, :], wbn[b][:, kc, :], nT[:, kc, tsl], kc == 0, kc == 7, ['wbn%d' % b, 'nT'], ['ps%d' % pD])
                        ACT(sA[p2][:], ps[pA][:, :], AF.Sigmoid, ['ps%d' % pA], ['sA%d' % p2])
                        ACT(sB[p2][:], ps[pB][:, :], AF.Sigmoid, ['ps%d' % pB], ['sB%d' % p2])
                        TT('dve', m1[p2][:], sA[p2][:], ps[pC][:, :], ALU.mult, ['sA%d' % p2, 'ps%d' % pC], ['m1%d' % p2])
                        TT('dve', sB[p2][:], sB[p2][:], ps[pD][:, :], ALU.mult, ['sB%d' % p2, 'ps%d' % pD], ['sB%d' % p2])
                        m3 = mi % 3
                        mi += 1
                        TT('pool', mx[m3][:], m1[p2][:], sB[p2][:], ALU.add, ['m1%d' % p2, 'sB%d' % p2], ['mx%d' % m3])
                        DMA('pool', S['mixT'][tile, :, c, :], mx[m3][:], ['mx%d' % m3], [])
                P.barrier()
                P.emit()

        def phase_e2():
            with ExitStack() as ph:
                wos = sbt(ph, 'F_wos', [128, 4, 512], F32)
                wob = [sbt(ph, 'F_wob%d' % i, [128, 16, 512], BF16) for i in range(2)]
                mxt = sbt(ph, 'F_mxt', [128, 16, 512], BF16)
                xc = [sbt(ph, 'F_xc%d' % i, [128, 512], F32) for i in range(3)]
                hpre2 = [sbt(ph, 'F_hpre%d' % i, [128, 4, 2048], F32) for i in range(2)]
                junk = sbt(ph, 'F_junk', [128, 2048], F32)
                hn = [sbt(ph, 'F_hn%d' % i, [128, 2048], F32) for i in range(2)]
                hb = sbt(ph, 'F_hb', [128, 2048], BF16)
                gB = sbt(ph, 'F_gB', [128, 2048], F32)
                bB = sbt(ph, 'F_bB', [128, 2048], F32)
                st = [sbt(ph, 'F_st%d' % i, [128, 4], F32) for i in range(2)]
                hTt = sbt(ph, 'F_hTt', [128, 16, 512], BF16)
                DMA('sp', gB[:], I['ln1g'], [], ['gB'])
                DMA('sp', bB[:], I['ln1b'], [], ['bB'])
                wi = 0
                xi = 0
                for tile in range(4):
                    hpre = hpre2[tile % 2]
                    hk = 'hpre%d_' % (tile % 2)
                    DMA('sp', mxt[:], S['mixT'][tile], [], ['mxt'])
                    for dc in range(4):
                        b = wi % 2
                        wi += 1
                        for k4 in range(4):
                            DMA('sp', wos[:], I['w_out'][:, k4 * 4:(k4 + 1) * 4, dc * 512:(dc + 1) * 512], [], ['wos'])
                            CP('act' if k4 % 2 == 0 else 'pool', wob[b][:, k4 * 4:(k4 + 1) * 4, :], wos[:], ['wos'], ['wob%d' % b])
                        for sub in range(4):
                            blk = tile * 4 + sub
                            pi = nextps()
                            for kc in range(16):
                                MM(ps[pi][:, :], mxt[:, kc, sub * 128:(sub + 1) * 128], wob[b][:, kc, :], kc == 0, kc == 15,
                                   ['mxt', 'wob%d' % b], ['ps%d' % pi])
                            x3 = xi % 3
                            xi += 1
                            DMA('sp', xc[x3][:], I['xo'][blk, :, dc * 512:(dc + 1) * 512], [], ['xc%d' % x3])
                            STT('dve', hpre[:, sub, dc * 512:(dc + 1) * 512], xc[x3][:], ALPHA, ps[pi][:, :], ALU.mult, ALU.add,
                                ['xc%d' % x3, 'ps%d' % pi], [hk + str(sub)])
                    for sub in range(4):
                        blk = tile * 4 + sub
                        layer_norm_block(hpre[:, sub, :], hk + str(sub), gB, bB, junk, hn, st, blk)
                        f = blk % len(hn)
                        DMA('pool', S['h'][blk], hn[f][:], ['hn%d' % f], [])
                        CP('act', hb[:], hn[f][:], ['hn%d' % f], ['hb'])
                        for f4 in range(4):
                            pi = nextps()
                            for ff in range(4):
                                fc = f4 * 4 + ff
                                MM(ps[pi][:, ff * 128:(ff + 1) * 128], hb[:, fc * 128:(fc + 1) * 128], identb[:], True, True,
                                   ['hb', 'identb'], ['ps%d' % pi])
                            CP('act', hTt[:, f4 * 4:(f4 + 1) * 4, sub * 128:(sub + 1) * 128],
                               ps[pi][:, :].rearrange('p (a b) -> p a b', a=4), ['ps%d' % pi], ['hTt'])
                    DMA('pool', S['hT'][tile], hTt[:], ['hTt'], [])
                P.barrier()
                P.emit()

        def layer_norm_block(src, srck, gB_, bB_, junk, hn, st, blk, junkk='junk'):
            f = blk % len(hn)
            sk = 'st%d' % f
            hk_ = 'hn%d' % f
            P.op('dve', lambda e: e.tensor_reduce(st[f][:, 0:1], src, AX.X, ALU.add), [srck], [sk])
            ACT(junk[:], src, AF.Square, [srck], [junkk])
            P.op('dve', lambda e: e.tensor_reduce(st[f][:, 1:2], junk[:], AX.X, ALU.add), [junkk], [sk])
            TS('dve', st[f][:, 0:1], st[f][:, 0:1], 1.0 / 2048.0, None, ALU.mult, None, [sk], [sk])
            TT('dve', st[f][:, 3:4], st[f][:, 0:1], st[f][:, 0:1], ALU.mult, [sk], [sk])
            STT('dve', st[f][:, 1:2], st[f][:, 1:2], 1.0 / 2048.0, st[f][:, 3:4], ALU.mult, ALU.subtract, [sk], [sk])
            TS('dve', st[f][:, 1:2], st[f][:, 1:2], 1e-5, None, ALU.add, None, [sk], [sk])
            ACT(st[f][:, 1:2], st[f][:, 1:2], AF.Sqrt, [sk], [sk])
            P.op('dve', lambda e: e.reciprocal(st[f][:, 2:3], st[f][:, 1:2]), [sk], [sk])
            STT('dve', st[f][:, 3:4], st[f][:, 0:1], -1.0, st[f][:, 2:3], ALU.mult, ALU.mult, [sk], [sk])
            ACT(hn[f][:], src, AF.Identity, [srck, sk], [hk_], scale=st[f][:, 2:3], bias=st[f][:, 3:4])
            TT('dve', hn[f][:], hn[f][:], gB_[:], ALU.mult, [hk_, 'gB'], [hk_])
            TT('dve', hn[f][:], hn[f][:], bB_[:], ALU.add, [hk_, 'bB'], [hk_])

        def p0_units(us, ub, engs):
            units = []
            for ec in range(128):
                for which in range(2):
                    def unit(ec=ec, which=which, k=len(units)):
                        b = k % len(us)
                        src = I['puT'][ec].rearrange('p a b -> p (a b)') if which == 0 else I['pv'][ec]
                        dst = S['puT'][ec].rearrange('p a b -> p (a b)') if which == 0 else S['pv'][ec]
                        DMA('sp', us[b][:], src, [], ['us%d' % b])
                        CP(engs[k % len(engs)], ub[b][:], us[b][:], ['us%d' % b], ['ub%d' % b])
                        DMA('pool', dst, ub[b][:], ['ub%d' % b], [])
                    units.append(unit)
            return units

        def phase_p0():
            with ExitStack() as ph:
                us = [sbt(ph, 'P_us%d' % i, [128, 2048], F32) for i in range(3)]
                ub = [sbt(ph, 'P_ub%d' % i, [128, 2048], BF16) for i in range(3)]
                for u in p0_units(us, ub, ['dve', 'act', 'pool']):
                    u()
                P.barrier()
                P.emit()

        def phase_peer(fin_evs):
            with ExitStack() as ph:
                hTt = sbt(ph, 'G_hTt', [128, 16, 512], BF16)
                wqs = [sbt(ph, 'G_wqs%d' % i, [128, 16, 64], F32) for i in range(2)]
                wqb = [sbt(ph, 'G_wqb%d' % i, [128, 16, 64], BF16) for i in range(2)]
                sks = sbt(ph, 'G_sks', [64, 2, 128], F32)
                skb = sbt(ph, 'G_skb', [64, 2, 128], BF16)
                qTu = [sbt(ph, 'G_qTu%d' % i, [64, 512], BF16) for i in range(2)]
                sAll = sbt(ph, 'G_sAll', [128, 4, 16, 128], F32)
                tau = sbt(ph, 'G_tau', [128, 4, 8], F32)
                negc = sbt(ph, 'G_negc', [128, 4, 8], F32)
                kap = sbt(ph, 'G_kap', [128, 4, 8], F32)
                m1 = [sbt(ph, 'G_m1%d' % i, [128, 16], F32) for i in range(4)]
                m2 = [sbt(ph, 'G_m2%d' % i, [128, 16], F32) for i in range(4)]
                mc = [sbt(ph, 'G_mc%d' % i, [128, 16], F32) for i in range(4)]
                t1 = [sbt(ph, 'G_t1%d' % i, [128, 256], F32) for i in range(4)]
                cand = [sbt(ph, 'G_cand%d' % i, [128, 256], F32) for i in range(4)]
                sm = [sbt(ph, 'G_sm%d' % i, [128, 4], F32) for i in range(4)]
                e16 = [sbt(ph, 'G_e16%d' % i, [128, 16], F32) for i in range(4)]
                eb = [sbt(ph, 'G_e%d' % i, [128, 4, 128], F32) for i in range(5)]
                Wall = [sbt(ph, 'G_W%d' % i, [128, 8, 4, 128], BF16) for i in range(2)]
                GT = [sbt(ph, 'G_GT%d' % i, [128, 4, 512], BF16) for i in range(3)]
                Gs = [sbt(ph, 'G_Gs%d' % i, [128, 512], BF16) for i in range(2)]
                uch = [sbt(ph, 'G_uch%d' % i, [128, 16, 128], BF16) for i in range(2)]
                vch = [sbt(ph, 'G_vch%d' % i, [128, 4, 2048], BF16) for i in range(2)]
                ga = [sbt(ph, 'G_ga%d' % i, [128, 512], F32) for i in range(4)]
                GA = [sbt(ph, 'G_GA%d' % i, [128, 4, 512], BF16) for i in range(2)]
                acc = sbt(ph, 'G_acc', [128, 4, 2048], F32)
                DMA('sp', sks[:], I['skT'], [], ['sks'])
                CP('dve', skb[:], sks[:], ['sks'], ['skb'])
                c_ = {'u': 0, 'k': 0, 'w': 0, 's': 0, 'v': 0, 'g': 0}
                for tile in range(4):
                    DMA('sp', hTt[:], S['hT'][tile], [], ['hTt'])
                    for u in range(16):
                        b = c_['u'] % 2
                        c_['u'] += 1
                        DMA('sp', wqs[b][:], I['wq'][u], [], ['wqs%d' % b])
                        CP('act', wqb[b][:], wqs[b][:], ['wqs%d' % b], ['wqb%d' % b])
                        pi = nextps()
                        for kc in range(16):
                            MM(ps[pi][0:64, :], wqb[b][:, kc, :], hTt[:, kc, :], kc == 0, kc == 15, ['wqb%d' % b, 'hTt'], ['ps%d' % pi])
                        CP('act', qTu[b][:], ps[pi][0:64, :], ['ps%d' % pi], ['qTu%d' % b])
                        pi = nextps()
                        for blk in range(4):
                            MM(ps[pi][:, blk * 128:(blk + 1) * 128], qTu[b][:, blk * 128:(blk + 1) * 128], skb[:, u % 2, :], True, True,
                               ['qTu%d' % b, 'skb'], ['ps%d' % pi])
                        CP('dve', sAll[:, :, u, :], ps[pi][:, :].rearrange('p (a b) -> p a b', a=4), ['ps%d' % pi], ['sAll'])
                    def chain(blk, h, f):
                        steps = []
                        s1 = sAll[:, blk, 2 * h, :]
                        s2 = sAll[:, blk, 2 * h + 1, :]
                        for (sx, mm_, mk, tk_, tt_) in ((s1, m1[f], 'm1_%d' % f, 't1a_%d' % f, t1[f][:, 0:128]),
                                                        (s2, m2[f], 'm2_%d' % f, 't1b_%d' % f, t1[f][:, 128:256])):
                            steps.append(lambda sx=sx, mm_=mm_, mk=mk: P.op('dve', lambda e: e.max(out=mm_[:, 0:8], in_=sx), ['sAll'], [mk]))
                            steps.append(lambda sx=sx, mm_=mm_, mk=mk, tk_=tk_, tt_=tt_: P.op(
                                'dve', lambda e: e.match_replace(out=tt_, in_to_replace=mm_[:, 0:8], in_values=sx, imm_value=-3.0e38),
                                ['sAll', mk], [tk_]))
                            steps.append(lambda mm_=mm_, mk=mk, tk_=tk_, tt_=tt_: P.op(
                                'dve', lambda e: e.max(out=mm_[:, 8:16], in_=tt_), [tk_], [mk]))
                        steps.append(lambda: TT('pool', cand[f][:].rearrange('p (a b) -> p a b', a=16),
                                                m1[f][:].unsqueeze(2).broadcast_to([128, 16, 16]), m2[f][:].unsqueeze(1).broadcast_to([128, 16, 16]),
                                                ALU.add, ['m1_%d' % f, 'm2_%d' % f], ['cand%d' % f]))
                        steps.append(lambda: P.op('dve', lambda e: e.max(out=mc[f][:, 0:8], in_=cand[f][:]), ['cand%d' % f], ['mc%d' % f]))
                        steps.append(lambda: P.op('dve', lambda e: e.match_replace(out=t1[f][:], in_to_replace=mc[f][:, 0:8], in_values=cand[f][:],
                                                                                  imm_value=-3.0e38),
                                                  ['cand%d' % f, 'mc%d' % f, 't1a_%d' % f, 't1b_%d' % f], ['t1a_%d' % f, 't1b_%d' % f]))
                        steps.append(lambda: P.op('dve', lambda e: e.max(out=mc[f][:, 8:16], in_=t1[f][:]), ['t1a_%d' % f, 't1b_%d' % f], ['mc%d' % f]))
                        steps.append(lambda: CP('dve', tau[:, blk, h:h + 1], mc[f][:, 15:16], ['mc%d' % f], ['tau']))
                        steps.append(lambda: TS('dve', sm[f][:, 0:1], mc[f][:, 0:1], -1.0, None, ALU.mult, None, ['mc%d' % f], ['sm%d' % f]))
                        steps.append(lambda: ACT(e16[f][:], mc[f][:], AF.Exp, ['mc%d' % f, 'sm%d' % f], ['e16_%d' % f], bias=sm[f][:, 0:1]))
                        steps.append(lambda: P.op('dve', lambda e: e.tensor_reduce(sm[f][:, 1:2], e16[f][:], AX.X, ALU.add), ['e16_%d' % f], ['sm%d' % f]))
                        steps.append(lambda: ACT(sm[f][:, 2:3], sm[f][:, 1:2], AF.Ln, ['sm%d' % f], ['sm%d' % f]))
                        steps.append(lambda: TT('dve', negc[:, blk, h:h + 1], sm[f][:, 0:1], sm[f][:, 2:3], ALU.subtract, ['sm%d' % f], ['negc%d' % f]))
                        steps.append(lambda: TT('dve', sm[f][:, 3:4], mc[f][:, 15:16], negc[:, blk, h:h + 1], ALU.add,
                                                ['mc%d' % f, 'negc%d' % f], ['sm%d' % f]))
                        steps.append(lambda: ACT(sm[f][:, 3:4], sm[f][:, 3:4], AF.Exp, ['sm%d' % f], ['sm%d' % f]))
                        steps.append(lambda: TS('dve', kap[:, blk, h:h + 1], sm[f][:, 3:4], 0.9999, None, ALU.mult, None, ['sm%d' % f], ['kap']))
                        steps.append(lambda: TS('dve', sAll[:, blk, 2 * h, :], sAll[:, blk, 2 * h, :], negc[:, blk, h:h + 1], None, ALU.add, None,
                                                ['negc%d' % f, 'm1_%d' % f, 't1a_%d' % f], ['sAllw%d' % f]))
                        return steps
                    pairs = [(blk, h) for blk in range(4) for h in range(8)]
                    for g4 in range(0, 32, 4):
                        chains = [chain(blk, h, f) for f, (blk, h) in enumerate(pairs[g4:g4 + 4])]
                        for i in range(len(chains[0])):
                            for ch in chains:
                                ch[i]()
                    P.op('dve', lambda e: e.tensor_copy(sm[0][:, 0:1], sm[0][:, 0:1]),
                         ['sAllw0', 'sAllw1', 'sAllw2', 'sAllw3', 'negc0', 'negc1', 'negc2', 'negc3', 'sm0'], ['sAll', 'negc', 'sm0'])
                    def opsA(eg, blk, h):
                        wb_ = (4 * eg + blk) % 2
                        sb_ = c_['s'] % 5
                        c_['s'] += 1
                        if h < NPOOL:
                            TT('pool', eb[sb_][:],
                               sAll[:, blk, 2 * h, 4 * eg:4 * eg + 4].unsqueeze(2).broadcast_to([128, 4, 128]),
                               sAll[:, blk, 2 * h + 1, :].unsqueeze(1).broadcast_to([128, 4, 128]),
                               ALU.add, ['sAll'], ['e%d' % sb_])
                            ACT(eb[sb_][:], eb[sb_][:], AF.Exp, ['e%d' % sb_], ['e%d' % sb_])
                        else:
                            for c in range(4):
                                ACT(eb[sb_][:, c, :], sAll[:, blk, 2 * h + 1, :], AF.Exp, ['sAll'], ['e%d' % sb_],
                                    bias=sAll[:, blk, 2 * h, 4 * eg + c:4 * eg + c + 1])
                        STT('dve', Wall[wb_][:, h, :, :], eb[sb_][:], kap[:, blk, h:h + 1], eb[sb_][:], ALU.is_ge, ALU.mult,
                            ['kap', 'e%d' % sb_], ['W%d' % wb_])

                    def stageB(eg, blk):
                        k = 4 * eg + blk
                        wb_ = k % 2
                        pb = k % 2
                        for c in range(4):
                            for h in range(8):
                                MM(ps[pb][:, c * 128:(c + 1) * 128], Wall[wb_][:, h, c, :], identb[:], h == 0, h == 7,
                                   ['W%d' % wb_, 'identb'], ['ps%d' % pb])
                        def evac(eg=eg, blk=blk, pb=pb):
                            CP('act', GT[eg % 3][:, :, blk * 128:(blk + 1) * 128], ps[pb][:, :].rearrange('p (a b) -> p a b', a=4),
                               ['ps%d' % pb], ['GT%d' % (eg % 3)])
                        pend.append(evac)

                    def stageC_pe(eg, c):
                        ec = 4 * eg + c
                        u2 = c % 2
                        DMA('sp', uch[u2][:], S['puT'][ec], [], ['uch%d' % u2])
                        pa = 2 + c
                        for kc in range(16):
                            MM(ps[pa][:, :], uch[u2][:, kc, :], hTt[:, kc, :], kc == 0, kc == 15, ['uch%d' % u2, 'hTt'], ['ps%d' % pa])

                    def stageC_post(eg):
                        gb = eg % 2
                        for c in range(4):
                            ACT(ga[c][:], ps[2 + c][:, :], AF.Gelu_apprx_tanh, ['ps%d' % (2 + c)], ['ga%d' % c])
                        for c in range(4):
                            TT('pool', GA[gb][:, c, :], ga[c][:], GT[eg % 3][:, c, :], ALU.mult, ['ga%d' % c, 'GT%d' % (eg % 3)], ['GA%d' % gb])

                    def stageD1(eg, blk, dc):
                        gb = eg % 2
                        vb = eg % 2
                        pv_ = 6 + (c_['v'] % 2)
                        c_['v'] += 1
                        for c in range(4):
                            MM(ps[pv_][:, :], GA[gb][:, c, blk * 128:(blk + 1) * 128], vch[vb][:, c, dc * 512:(dc + 1) * 512],
                               c == 0, c == 3, ['GA%d' % gb, 'vch%d' % vb], ['ps%d' % pv_])
                        if eg == 0:
                            CP('dve', acc[:, blk, dc * 512:(dc + 1) * 512], ps[pv_][:, :], ['ps%d' % pv_], ['acc%d' % blk])
                        else:
                            TT('dve', acc[:, blk, dc * 512:(dc + 1) * 512], acc[:, blk, dc * 512:(dc + 1) * 512], ps[pv_][:, :],
                               ALU.add, ['ps%d' % pv_, 'acc%d' % blk], ['acc%d' % blk])

                    pend = []
                    for it in range(35):
                        doA = it < 32
                        if 2 <= it <= 33:
                            eg_ = it - 2
                            DMA('sp', vch[eg_ % 2][:], S['pv'][4 * eg_:4 * eg_ + 4].rearrange('c p d -> p c d'), [], ['vch%d' % (eg_ % 2)])
                        for blk in range(4):
                            for h in range(8):
                                if doA:
                                    opsA(it, blk, h)
                                if h == 3 or not doA:
                                    while pend:
                                        pend.pop(0)()
                                if blk == 0 and h == 3 and 2 <= it <= 33:
                                    stageC_post(it - 2)
                                if h % 2 == 1 and 3 <= it <= 34:
                                    stageD1(it - 3, blk, h // 2)
                            if 1 <= it <= 32:
                                stageC_pe(it - 1, blk)
                            if doA:
                                stageB(it, blk)
                    DMA('pool', S['pe'][tile * 4:(tile + 1) * 4].rearrange('b p d -> p b d'), acc[:], ['acc0', 'acc1', 'acc2', 'acc3'], [])
                P.barrier()
                P.emit()


        def phase_g2(fin_evs):
            with ExitStack() as ph:
                hblk = [sbt(ph, 'H_hblk%d' % i, [128, 2048], F32) for i in range(2)]
                pblk = [sbt(ph, 'H_pblk%d' % i, [128, 2048], F32) for i in range(2)]
                junk = sbt(ph, 'H_junk', [128, 2048], F32)
                hn = [sbt(ph, 'H_hn%d' % i, [128, 2048], F32) for i in range(2)]
                gB = sbt(ph, 'H_gB', [128, 2048], F32)
                bB = sbt(ph, 'H_bB', [128, 2048], F32)
                st = [sbt(ph, 'H_st%d' % i, [128, 4], F32) for i in range(2)]
                DMA('sp', gB[:], I['ln2g'], [], ['gB'])
                DMA('sp', bB[:], I['ln2b'], [], ['bB'])
                for gblk in range(16):
                    f = gblk % 2
                    DMA('sp', hblk[f][:], S['h'][gblk], [], ['hblk%d' % f])
                    DMA('sp', pblk[f][:], S['pe'][gblk], [], ['pblk%d' % f])
                    STT('dve', pblk[f][:], hblk[f][:], ALPHA, pblk[f][:], ALU.mult, ALU.add, ['hblk%d' % f, 'pblk%d' % f], ['pblk%d' % f])
                    layer_norm_block(pblk[f][:], 'pblk%d' % f, gB, bB, junk, hn, st, gblk)
                    fin_evs.append(DMA('pool', out[gblk], hn[f][:], ['hn%d' % f], []))
                P.barrier()
                P.emit()

        if 'p0' in phases:
            phase_p0()
        if 'kv0' in phases:
            phase_kv(0)
        if 'kv1' in phases:
            phase_kv(1)
        if 'q' in phases:
            phase_q()
        if 'da' in phases:
            phase_da()
        if 'nsa' in phases:
            phase_nsa()
        if 'e1' in phases:
            phase_e1()
        fin_evs = []
        if dbg and dbg_src in ('aT', 'nT'):
            fin_evs.append(DMA('pool', dbg_out, (aT if dbg_src == 'aT' else nT)[:], ['aT', 'nT'], []))
            fin_evs.append(DMA('pool', dbg2_out, dbg2sb[:], ['dbg2sb'], []))
            P.barrier()
            P.emit()
        mid.close()
        if 'e2' in phases:
            phase_e2()
        if 'peer' in phases:
            phase_peer(fin_evs)
        if 'g2' in phases:
            phase_g2(fin_evs)
        for ev in fin_evs:
            pass
        P.ops['sp'].append((None, [ev for ev in fin_evs], None, 0))
        P.emit()
    return nc


def rel_bucket_np(dist):
    n = np.maximum(dist, 0)
    nf = np.maximum(n, 1).astype(np.float32)
    large = 16 + (np.log(nf / np.float32(16)) / np.float32(math.log(8.0)) * np.float32(16)).astype(np.int32)
    large = np.minimum(large, 31)
    return np.where(n < 16, n, large)


def prep_inputs(inputs):
    x = np.asarray(inputs['x'], np.float32)
    w_in = np.asarray(inputs['w_in'], np.float32)[0]
    rel = np.asarray(inputs['rel_bias'], np.float32)
    wr = np.ascontiguousarray(w_in.reshape(16, 128, 9776).transpose(1, 0, 2))
    common = {
        'w_dakv': np.ascontiguousarray(wr[:, :, 1024:3072]),
        'w_nkv': np.ascontiguousarray(wr[:, :, 4096:5632]),
        'w_q': np.ascontiguousarray(np.concatenate([wr[:, :, 0:1024], wr[:, :, 3072:4096]], axis=2)),
        'w_gate': np.ascontiguousarray(wr[:, :, 5632:5680]),
        'w_mg': np.ascontiguousarray(wr[:, :, 5680:9776]),
        'c_da': np.ascontiguousarray(np.broadcast_to(rel[31, 0:8][None, :], (128, 8))),
        'lamq': np.ascontiguousarray(np.broadcast_to(np.asarray(inputs['da_lam_q'], np.float32)[0].reshape(1, 128), (128, 128))),
        'lamk': np.ascontiguousarray(np.broadcast_to(np.asarray(inputs['da_lam_k'], np.float32)[0].reshape(1, 128), (128, 128))),
        'subg': np.ascontiguousarray(np.broadcast_to(np.asarray(inputs['da_subln_g'], np.float32)[0].reshape(1, 128), (128, 128))),
        'ident': np.eye(128, dtype=np.float32),
        'w_bda': np.ascontiguousarray(np.asarray(inputs['w_branch_da'], np.float32)[0].reshape(8, 128, 2048).transpose(1, 0, 2)),
        'w_bnsa': np.ascontiguousarray(np.asarray(inputs['w_branch_nsa'], np.float32)[0].reshape(8, 128, 2048).transpose(1, 0, 2)),
        'w_out': np.ascontiguousarray(np.asarray(inputs['w_out'], np.float32)[0].reshape(16, 128, 2048).transpose(1, 0, 2)),
        'ln1g': np.ascontiguousarray(np.broadcast_to(np.asarray(inputs['ln1_g'], np.float32)[0][None, :], (128, 2048))),
        'ln1b': np.ascontiguousarray(np.broadcast_to(np.asarray(inputs['ln1_b'], np.float32)[0][None, :], (128, 2048))),
        'ln2g': np.ascontiguousarray(np.broadcast_to(np.asarray(inputs['ln2_g'], np.float32)[0][None, :], (128, 2048))),
        'ln2b': np.ascontiguousarray(np.broadcast_to(np.asarray(inputs['ln2_b'], np.float32)[0][None, :], (128, 2048))),
        'wq': np.ascontiguousarray(np.asarray(inputs['peer_wq'], np.float32)[0].reshape(16, 128, 16, 64).transpose(2, 1, 0, 3)),
        'skT': np.ascontiguousarray(np.stack([np.asarray(inputs['peer_subkey1'], np.float32)[0].T,
                                              np.asarray(inputs['peer_subkey2'], np.float32)[0].T], axis=1)),
        'puT': np.ascontiguousarray(np.asarray(inputs['peer_u'], np.float32)[0].reshape(128, 128, 16, 128).transpose(0, 3, 2, 1)),
        'pv': np.ascontiguousarray(np.asarray(inputs['peer_v'], np.float32)[0].reshape(128, 128, 2048)),
        'c_nsa': np.ascontiguousarray(np.broadcast_to(rel[31, 8:24][None, :], (128, 16))),
        'w1k': np.ascontiguousarray(np.asarray(inputs['cmp_w1_k'], np.float32)[0].reshape(32, 64, 256).transpose(1, 0, 2)),
        'w1v': np.ascontiguousarray(np.asarray(inputs['cmp_w1_v'], np.float32)[0].reshape(32, 64, 256).transpose(1, 0, 2)),
        'w2k': np.ascontiguousarray(np.asarray(inputs['cmp_w2_k'], np.float32)[0].reshape(2, 128, 64).transpose(1, 0, 2)),
        'w2v': np.ascontiguousarray(np.asarray(inputs['cmp_w2_v'], np.float32)[0].reshape(2, 128, 64).transpose(1, 0, 2)),
        'pekT': np.ascontiguousarray(np.asarray(inputs['cmp_pe_k'], np.float32)[0].T),
        'pevT': np.ascontiguousarray(np.asarray(inputs['cmp_pe_v'], np.float32)[0].T),
    }
    import ml_dtypes
    cidx = np.arange(512)
    sidx = np.arange(128)
    ov = ((cidx[:, None] * 16 <= sidx[None, :] * 64 + 63) & (cidx[:, None] * 16 + 31 >= sidx[None, :] * 64)).astype(np.float32)
    ovl = np.concatenate([ov, np.ones((512, 1), np.float32)], axis=1)
    ovl[511] = 0.0
    common['ovl'] = np.ascontiguousarray(ovl.reshape(4, 128, 129).transpose(1, 0, 2))
    kk = np.arange(8192)
    common['onehot'] = (((kk[None, :] // 64) % 64) == np.arange(64)[:, None]).astype(ml_dtypes.bfloat16)
    xTs = []
    for b in range(2):
        xTs.append(np.ascontiguousarray(x[b].reshape(16, 512, 16, 128).transpose(0, 3, 2, 1)))
    in_maps = []
    kl = np.arange(128)[:, None]
    xx = np.arange(2944)[None, :]
    for c in range(8):
        b, j = c // 4, c % 4
        tiles = [4 * t + j for t in range(4)]
        m = dict(common)
        m['xT'] = xTs[b]
        m['xTo'] = np.ascontiguousarray(xTs[b][tiles])
        m['xo'] = np.ascontiguousarray(
            np.concatenate([x[b, 512 * T:512 * (T + 1)] for T in tiles], axis=0).reshape(16, 128, 2048))
        d = xx - kl + 512 * j - 1920
        bk = rel_bucket_np(d)
        rb = rel[bk]
        m['raw_da'] = np.ascontiguousarray(rb[:, 0:2560, 0:8].transpose(2, 0, 1))
        m['raw_nsa'] = np.ascontiguousarray(rb[:, :, 8:24].transpose(2, 0, 1))
        m['mneg'] = np.where(d < 0, np.float32(NEGM), np.float32(0.0)).astype(np.float32)
        m['wneg'] = np.where((d < 0) | (d >= 512), np.float32(NEGM), np.float32(0.0)).astype(np.float32)
        cl = np.arange(128)[:, None, None]
        dl = np.arange(2)[None, :, None] - 1
        ql = np.arange(512)[None, None, :]
        m['cm'] = np.where(16 * cl + 31 + 2048 * dl <= 512 * j + ql, np.float32(0.0), np.float32(NEGM)).astype(np.float32)
        qpos = (512 * np.array(tiles)[:, None, None] + 128 * np.arange(4)[None, :, None] + np.arange(128)[None, None, :]).reshape(16, 128)
        cur = qpos // 64
        sb_ = np.arange(128)[None, None, :]
        valid = sb_ <= cur[:, :, None]
        forced = valid & ((sb_ == 0) | (sb_ > cur[:, :, None] - 2))
        vmul = (valid & ~forced).astype(np.float32)
        vadd = np.where(forced, np.float32(1e4) + sb_.astype(np.float32), np.where(valid, np.float32(0.0), np.float32(-1e30))).astype(np.float32)
        m['vmul'] = np.ascontiguousarray(vmul.transpose(1, 0, 2))
        m['vadd'] = np.ascontiguousarray(vadd.transpose(1, 0, 2))
        in_maps.append(m)
    return in_maps


_NC_CACHE = {}


def kernel(**inputs):
    in_maps = prep_inputs(inputs)
    if 'nc' not in _NC_CACHE:
        _NC_CACHE['nc'] = build_program()
    nc = _NC_CACHE['nc']
    res = run_bass_kernel_spmd(nc, in_maps, core_ids=list(range(8)))
    outp = np.zeros((2, 8192, 2048), np.float32)
    for c in range(8):
        b, j = c // 4, c % 4
        o = np.asarray(res.results[c]['out']).reshape(4, 512, 2048)
        for t in range(4):
            T = 4 * t + j
            outp[b, 512 * T:512 * (T + 1)] = o[t]
    return outp
```

```python
import math
from contextlib import ExitStack

import numpy as np
import concourse.bass as bass
import concourse.mybir as mybir
from concourse.bass_utils import run_bass_kernel_spmd

F32 = mybir.dt.float32
BF16 = mybir.dt.bfloat16
AF = mybir.ActivationFunctionType
ALU = mybir.AluOpType
AX = mybir.AxisListType

ENGS = ['pe', 'act', 'dve', 'pool', 'sp']
EPOCH = 16000
RING = {'sp': 40, 'pool': 16}
NEGM = -30000.0
NPOOL = 5
POOL_STT = ()
ALPHA = 2.0 ** 0.25
LAM_INIT = 0.8 - 0.6 * math.exp(0.0)


class Prog:
    def __init__(self, nc, stack):
        self.nc = nc
        self.stack = stack
        self.ops = {e: [] for e in ENGS}
        self.cnt = {e: 0 for e in ENGS}
        self.esems = {e: [] for e in ENGS}
        self.rings = {}
        self.ring_pos = {}
        self.ring_use = {}
        for q, n in RING.items():
            self.rings[q] = [stack.enter_context(nc.semaphore('r%s%d' % (q, i))) for i in range(n)]
            self.ring_pos[q] = 0
            self.ring_use[q] = [0] * n
        self.seen = {e: {} for e in ENGS}
        self.lastw = {}
        self.readers = {}
        self.last_ev = {e: None for e in ENGS}

    def _esem(self, eng, epoch):
        while len(self.esems[eng]) <= epoch:
            self.esems[eng].append(self.stack.enter_context(
                self.nc.semaphore('e%s%d' % (eng, len(self.esems[eng])))))
        return self.esems[eng][epoch]

    def op(self, eng, fn, reads=(), writes=(), dma=False):
        deps = {}

        def add(ev):
            if ev is None:
                return
            s, v = ev
            if v > deps.get(id(s), (None, 0))[1]:
                deps[id(s)] = (s, v)
        for k in reads:
            add(self.lastw.get(k))
        for k in writes:
            add(self.lastw.get(k))
            for ev in self.readers.get(k, {}).values():
                add(ev)
        if eng == 'pe':
            for t in self.esems['pe']:
                deps.pop(id(t), None)
        if dma:
            q = eng
            pos = self.ring_pos[q]
            self.ring_pos[q] = (pos + 1) % len(self.rings[q])
            sem = self.rings[q][pos]
            if self.ring_use[q][pos] > 0:
                add((sem, 16 * self.ring_use[q][pos]))
            self.ring_use[q][pos] += 1
            ev = (sem, 16 * self.ring_use[q][pos])
            inc = 16
        else:
            i = self.cnt[eng]
            self.cnt[eng] += 1
            sem = self._esem(eng, i // EPOCH)
            ev = (sem, i % EPOCH + 1)
            inc = 1
            self.last_ev[eng] = ev
        waits = []
        seen = self.seen[eng]
        for s, v in deps.values():
            if seen.get(id(s), 0) < v:
                seen[id(s)] = v
                waits.append((s, v))
        self.ops[eng].append((fn, waits, sem, inc))
        for k in reads:
            self.readers.setdefault(k, {})[(eng, id(sem))] = ev
        for k in writes:
            self.lastw[k] = ev
            self.readers[k] = {}
        return ev

    def barrier(self):
        evs = [ev for ev in self.last_ev.values() if ev is not None]
        for q in self.rings:
            for i, s in enumerate(self.rings[q]):
                if self.ring_use[q][i] > 0:
                    evs.append((s, 16 * self.ring_use[q][i]))
        for eng in ENGS:
            waits = []
            seen = self.seen[eng]
            for s, v in evs:
                if seen.get(id(s), 0) < v:
                    seen[id(s)] = v
                    waits.append((s, v))
            self.ops[eng].append((None, waits, None, 0))
        self.lastw = {}
        self.readers = {}

    def emit(self):
        nc = self.nc
        ops = self.ops
        self.ops = {e: [] for e in ENGS}
        with nc.Block() as block:
            def run(engname):
                def body(e):
                    for fn, waits, sem, inc in ops[engname]:
                        for s, v in waits:
                            e.wait_ge(s, v)
                        if fn is not None:
                            fn(e).then_inc(sem, inc)
                return body
            block.tensor(run('pe'))
            block.scalar(run('act'))
            block.vector(run('dve'))
            block.gpsimd(run('pool'))
            block.sync(run('sp'))


class Ctx:
    pass


def build_program(phases=('p0i', 'kv0', 'kv1', 'q', 'da', 'nsa', 'e1', 'e2', 'peer', 'g2'), dbg=False, dbg_src='aT'):
    nc = bass.Bass("TRN2", target_bir_lowering=False)
    C = Ctx()
    C.nc = nc

    def din(name, shape, dt=F32):
        return nc.dram_tensor(name, list(shape), dt, kind="ExternalInput").ap()

    def dscr(name, shape, dt=BF16):
        return nc.dram_tensor(name, list(shape), dt, kind="Internal").ap()

    I = {}
    I['xT'] = din('xT', [16, 128, 16, 512])
    I['xTo'] = din('xTo', [4, 128, 16, 512])
    I['xo'] = din('xo', [16, 128, 2048])
    I['w_dakv'] = din('w_dakv', [128, 16, 2048])
    I['w_nkv'] = din('w_nkv', [128, 16, 1536])
    I['w_q'] = din('w_q', [128, 16, 2048])
    I['w_gate'] = din('w_gate', [128, 16, 48])
    I['w_mg'] = din('w_mg', [128, 16, 4096])
    I['raw_da'] = din('raw_da', [8, 128, 2560])
    I['mneg'] = din('mneg', [128, 2944])
    I['wneg'] = din('wneg', [128, 2944])
    I['raw_nsa'] = din('raw_nsa', [16, 128, 2944])
    I['c_nsa'] = din('c_nsa', [128, 16])
    I['w1k'] = din('w1k', [128, 16, 256])
    I['w1v'] = din('w1v', [128, 16, 256])
    I['w2k'] = din('w2k', [128, 2, 64])
    I['w2v'] = din('w2v', [128, 2, 64])
    I['pekT'] = din('pekT', [128, 16])
    I['pevT'] = din('pevT', [128, 16])
    I['ovl'] = din('ovl', [128, 4, 129])
    I['cm'] = din('cm', [128, 2, 512])
    I['vmul'] = din('vmul', [128, 16, 128])
    I['vadd'] = din('vadd', [128, 16, 128])
    I['onehot'] = din('onehot', [64, 8192], BF16)
    I['w_bda'] = din('w_bda', [128, 8, 2048])
    I['w_bnsa'] = din('w_bnsa', [128, 8, 2048])
    I['w_out'] = din('w_out', [128, 16, 2048])
    I['ln1g'] = din('ln1g', [128, 2048])
    I['ln1b'] = din('ln1b', [128, 2048])
    I['ln2g'] = din('ln2g', [128, 2048])
    I['ln2b'] = din('ln2b', [128, 2048])
    I['wq'] = din('wq', [16, 128, 16, 64])
    I['skT'] = din('skT', [64, 2, 128])
    I['puT'] = din('puT', [128, 128, 16, 128])
    I['pv'] = din('pv', [128, 128, 2048])
    I['c_da'] = din('c_da', [128, 8])
    I['lamq'] = din('lamq', [128, 128])
    I['lamk'] = din('lamk', [128, 128])
    I['subg'] = din('subg', [128, 128])
    I['ident'] = din('ident', [128, 128])
    out = nc.dram_tensor('out', [16, 128, 2048], F32, kind="ExternalOutput").ap()
    dbg_out = None
    if dbg:
        dbg_out = nc.dram_tensor('dbg', [128, 8, 2048], BF16, kind="ExternalOutput").ap()
        dbg2_out = nc.dram_tensor('dbg2', [128, 4, 258], F32, kind="ExternalOutput").ap()

    S = {}
    S['kT'] = dscr('s_kT', [8, 128, 8192])
    S['v'] = dscr('s_v', [8, 8192, 128])
    S['nkT'] = dscr('s_nkT', [4, 256, 8192])
    S['nv'] = dscr('s_nv', [2, 4, 8192, 64])
    S['qT'] = dscr('s_qT', [16, 128, 2048])
    S['h'] = dscr('s_h', [16, 128, 2048], F32)
    S['mixT'] = dscr('s_mixT', [4, 128, 16, 512])
    S['hT'] = dscr('s_hT', [4, 128, 16, 512])
    S['pe'] = dscr('s_pe', [16, 128, 2048], F32)
    S['puT'] = dscr('s_puT', [128, 128, 16, 128])
    S['pv'] = dscr('s_pv', [128, 128, 2048])

    with ExitStack() as top:
        P = Prog(nc, top)

        def sbt(st, name, shape, dt):
            return st.enter_context(nc.sbuf_tensor(name, list(shape), dt))

        ps = [top.enter_context(nc.psum_tensor('ps%d' % i, [128, 512], F32)) for i in range(8)]

        def MM(o, lhsT, rhs, start, stop, r, w):
            P.op('pe', lambda e: e.matmul(o, lhsT, rhs, start=start, stop=stop), r, w)

        def ACT(o, i, func, r, w, **kw):
            P.op('act', lambda e: e.activation(o, i, func, **kw), r, w)

        def CP(eng, o, i, r, w):
            if eng == 'act':
                P.op('act', lambda e: e.copy(o, i), r, w)
            else:
                P.op(eng, lambda e: e.tensor_copy(o, i), r, w)

        def TS(eng, o, i0, s1, s2, op0, op1, r, w, **kw):
            if op1 is None:
                P.op(eng, lambda e: e.tensor_scalar(o, i0, s1, None, op0, **kw), r, w)
            else:
                P.op(eng, lambda e: e.tensor_scalar(o, i0, s1, s2, op0, op1, **kw), r, w)

        def STT(eng, o, i0, sc, i1, op0, op1, r, w):
            P.op(eng, lambda e: e.scalar_tensor_tensor(o, i0, sc, i1, op0, op1), r, w)

        def TT(eng, o, i0, i1, op, r, w):
            P.op(eng, lambda e: e.tensor_tensor(o, i0, i1, op), r, w)

        def DMA(q, o, i, r, w):
            return P.op(q, lambda e: e.dma_start(out=o, in_=i), r, w, dma=True)

        def MEMSET(eng, o, val, w):
            P.op(eng, lambda e: e.memset(o, val), (), w)

        gates = sbt(top, 'gates', [128, 16, 48], F32)
        identb = sbt(top, 'identb', [128, 128], BF16)
        neglam = sbt(top, 'neglam', [128, 1], F32)
        gs = sbt(top, 'gs', [128, 128], F32)
        cda = sbt(top, 'cda', [128, 8], F32)
        dbg2sb = sbt(top, 'dbg2sb', [128, 4, 258], F32) if dbg else None
        mid = ExitStack()
        aT = sbt(mid, 'aT', [128, 8, 2048], BF16)
        nT = sbt(mid, 'nT', [128, 8, 2048], BF16)

        with ExitStack() as ph:
            idf = sbt(ph, 'idf', [128, 128], F32)
            lq = sbt(ph, 'lq', [128, 128], F32)
            lk = sbt(ph, 'lk', [128, 128], F32)
            lp = sbt(ph, 'lp', [128, 128], F32)
            l2 = sbt(ph, 'l2', [128, 2], F32)
            DMA('sp', idf[:], I['ident'], [], ['idf'])
            DMA('sp', lq[:], I['lamq'], [], ['lq'])
            DMA('sp', lk[:], I['lamk'], [], ['lk'])
            DMA('sp', gs[:], I['subg'], [], ['gs'])
            DMA('sp', cda[:], I['c_da'], [], ['cda'])
            CP('dve', identb[:], idf[:], ['idf'], ['identb'])
            TT('dve', lp[:], lq[:], lk[:], ALU.mult, ['lq', 'lk'], ['lp'])
            P.op('dve', lambda e: e.tensor_reduce(l2[:], lp[:].rearrange('p (a b) -> p a b', a=2), AX.X, ALU.add),
                 ['lp'], ['l2'])
            ACT(l2[:], l2[:], AF.Exp, ['l2'], ['l2'])
            STT('dve', neglam[:], l2[:, 0:1], -1.0, l2[:, 1:2], ALU.mult, ALU.add, ['l2'], ['neglam'])
            TS('dve', neglam[:], neglam[:], -LAM_INIT, None, ALU.add, None, ['neglam'], ['neglam'])
            TS('dve', gs[:], gs[:], 1.0 - LAM_INIT, None, ALU.mult, None, ['gs'], ['gs'])
            P.barrier()
            P.emit()

        psrot = [0]

        def nextps():
            i = psrot[0]
            psrot[0] = (i + 1) % 8
            return i

        def phase_kv(passno):
            ncw = 2048 if passno == 0 else 1536
            wsrc = I['w_dakv'] if passno == 0 else I['w_nkv']
            with ExitStack() as ph:
                wb = sbt(ph, 'A%d_wb' % passno, [128, 16, ncw], BF16)
                wst = [sbt(ph, 'A%d_wst%d' % (passno, i), [128, ncw], F32) for i in range(2)]
                xs = [sbt(ph, 'A%d_xs%d' % (passno, i), [128, 4, 512], F32) for i in range(2)]
                xb = [sbt(ph, 'A%d_xb%d' % (passno, i), [128, 16, 512], BF16) for i in range(2)]
                evs = [sbt(ph, 'A%d_ev%d' % (passno, i), [128, 512], BF16) for i in range(4)]
                evi = [0]
                for kc in range(16):
                    b = kc % 2
                    DMA('sp', wst[b][:], wsrc[:, kc, :], [], ['wst%d' % b])
                    CP('pool', wb[:, kc, :], wst[b][:], ['wst%d' % b], ['wb'])

                def evac_store(pi, npart, ncol, dst_fn):
                    k = evi[0]
                    evi[0] = (k + 1) % 4
                    CP('act', evs[k][0:npart, 0:ncol], ps[pi][0:npart, 0:ncol], ['ps%d' % pi], ['ev%d' % k])
                    dst_fn(evs[k], k)

                for tile in range(16):
                    xbk = 'xb%d' % (tile % 2)
                    xbt = xb[tile % 2]
                    for qq in range(4):
                        half = qq % 2
                        DMA('sp', xs[half][:], I['xT'][tile, :, qq * 4:(qq + 1) * 4, :], [], ['xs%d' % half])
                        CP('dve', xbt[:, qq * 4:(qq + 1) * 4, :], xs[half][:], ['xs%d' % half], [xbk])
                    tsl = slice(tile * 512, (tile + 1) * 512)
                    if passno == 0:
                        fm = [(c * 128, S['kT'][c, :, tsl]) for c in range(8)]
                    else:
                        fm = []
                        for kind, cb in enumerate((0, 256, 512, 1024)):
                            for c2 in range(2):
                                fm.append((cb + c2 * 128, S['nkT'][kind, c2 * 128:(c2 + 1) * 128, tsl]))
                    for col0, dst in fm:
                        pi = nextps()
                        for kc in range(16):
                            MM(ps[pi][:, :], wb[:, kc, col0:col0 + 128], xbt[:, kc, :], kc == 0, kc == 15,
                               ['wb', xbk], ['ps%d' % pi])
                        evac_store(pi, 128, 512,
                                   lambda ev, k, dst=dst: DMA('pool', dst, ev[:, :], ['ev%d' % k], []))
                    for blk in range(4):
                        t0 = tile * 512 + blk * 128
                        if passno == 0:
                            tm = [(1024 + g4 * 512, 512,
                                   S['v'][g4 * 4:(g4 + 1) * 4, t0:t0 + 128, :].rearrange('h t e -> t h e'), 4)
                                  for g4 in range(2)]
                        else:
                            tm = [(768, 256, S['nv'][0, :, t0:t0 + 128, :].rearrange('g t e -> t g e'), 4),
                                  (1280, 256, S['nv'][1, :, t0:t0 + 128, :].rearrange('g t e -> t g e'), 4)]
                        for col0, ncol, dst, nh in tm:
                            pi = nextps()
                            for kc in range(16):
                                MM(ps[pi][:, 0:ncol], xbt[:, kc, blk * 128:(blk + 1) * 128], wb[:, kc, col0:col0 + ncol],
                                   kc == 0, kc == 15, ['wb', xbk], ['ps%d' % pi])
                            evac_store(pi, 128, ncol,
                                       lambda ev, k, dst=dst, ncol=ncol, nh=nh: DMA(
                                           'pool', dst, ev[:, 0:ncol].rearrange('t (h e) -> t h e', h=nh),
                                           ['ev%d' % k], []))
                P.barrier()
                P.emit()

        def phase_q():
            with ExitStack() as ph:
                wb = sbt(ph, 'B_wb', [128, 16, 2048], BF16)
                wgb = sbt(ph, 'B_wgb', [128, 16, 48], BF16)
                wst = [sbt(ph, 'B_wst%d' % i, [128, 2048], F32) for i in range(1)]
                wgs = sbt(ph, 'B_wgs', [128, 16, 48], F32)
                xs = [sbt(ph, 'B_xs%d' % i, [128, 4, 512], F32) for i in range(2)]
                xb = [sbt(ph, 'B_xb%d' % i, [128, 16, 512], BF16) for i in range(2)]
                evs = [sbt(ph, 'B_ev%d' % i, [128, 512], BF16) for i in range(4)]
                evi = [0]
                for kc in range(16):
                    b = 0
                    DMA('sp', wst[b][:], I['w_q'][:, kc, :], [], ['wst%d' % b])
                    CP('pool', wb[:, kc, :], wst[b][:], ['wst%d' % b], ['wb'])
                DMA('sp', wgs[:], I['w_gate'], [], ['wgs'])
                CP('pool', wgb[:], wgs[:], ['wgs'], ['wgb'])
                for tile in range(4):
                    xbk = 'xb%d' % (tile % 2)
                    xbt = xb[tile % 2]
                    for qq in range(4):
                        half = qq % 2
                        DMA('sp', xs[half][:], I['xTo'][tile, :, qq * 4:(qq + 1) * 4, :], [], ['xs%d' % half])
                        CP('dve', xbt[:, qq * 4:(qq + 1) * 4, :], xs[half][:], ['xs%d' % half], [xbk])
                    tsl = slice(tile * 512, (tile + 1) * 512)
                    for c in range(16):
                        pi = nextps()
                        for kc in range(16):
                            MM(ps[pi][:, :], wb[:, kc, c * 128:(c + 1) * 128], xbt[:, kc, :], kc == 0, kc == 15,
                               ['wb', xbk], ['ps%d' % pi])
                        k = evi[0]
                        evi[0] = (k + 1) % 4
                        CP('act', evs[k][:, :], ps[pi][:, :], ['ps%d' % pi], ['ev%d' % k])
                        DMA('pool', S['qT'][c, :, tsl], evs[k][:, :], ['ev%d' % k], [])
                    for blk in range(4):
                        pi = nextps()
                        for kc in range(16):
                            MM(ps[pi][:, 0:48], xbt[:, kc, blk * 128:(blk + 1) * 128], wgb[:, kc, :], kc == 0, kc == 15,
                               ['wgb', xbk], ['ps%d' % pi])
                        ACT(gates[:, tile * 4 + blk, :], ps[pi][:, 0:48], AF.Sigmoid, ['ps%d' % pi], ['gates'])
                P.barrier()
                P.emit()

        def phase_da():
            with ExitStack() as ph:
                KtF = sbt(ph, 'C_KtF', [128, 8192], BF16)
                Vh = sbt(ph, 'C_Vh', [128, 64, 129], BF16)
                strip = sbt(ph, 'C_strip', [128, 2560], F32)
                mneg = sbt(ph, 'C_mneg', [128, 2560], F32)
                QT = [sbt(ph, 'C_QT%d' % m, [128, 2048], BF16) for m in range(2)]
                pT = [sbt(ph, 'C_pT%d' % b, [128, 512], BF16) for b in range(5)]
                tmp = [sbt(ph, 'C_tmp%d' % b, [128, 512], F32) for b in range(4)]
                fz = [sbt(ph, 'C_fz%d' % i, [128, 4], F32) for i in range(2)]
                o0 = sbt(ph, 'C_o0', [128, 4, 129], F32)
                fu = [sbt(ph, 'C_fu%d' % i, [128, 128], F32) for i in range(2)]
                fo = [sbt(ph, 'C_fo%d' % i, [128, 128], F32) for i in range(2)]
                fj = [sbt(ph, 'C_fj%d' % i, [128, 128], F32) for i in range(2)]
                fon = [sbt(ph, 'C_fon%d' % i, [128, 128], BF16) for i in range(2)]
                DMA('sp', mneg[:], I['mneg'][:, 0:2560], [], ['mneg'])
                MEMSET('pool', Vh[:, :, 128:129], 1.0, ['Vh'])
                MEMSET('pool', QT[0][:], 0.0, ['QT0'])
                MEMSET('pool', QT[1][:], 0.0, ['QT1'])
                p0q = []
                if 'p0i' in phases:
                    pus = [sbt(ph, 'C_pus%d' % i, [128, 2048], F32) for i in range(3)]
                    pub = [sbt(ph, 'C_pub%d' % i, [128, 2048], BF16) for i in range(3)]
                    p0q = p0_units(pus, pub, ['pool'])
                st_ = {'slot': 0, 'fin': 0}
                pipe = []

                def push(pv):
                    pipe.append(pv)
                    if len(pipe) > 3:
                        pipe.pop(0)()

                def flush():
                    while pipe:
                        pipe.pop(0)()

                def da_finalize(h, t):
                    for sub in range(4):
                        f = st_['fin'] % 2
                        st_['fin'] += 1
                        acc = ps[4 + sub]
                        ak = 'ps%d' % (4 + sub)
                        if dbg and h == 0 and t == 0:
                            CP('dve', dbg2sb[:, sub, 0:129], o0[:, sub, :], ['o0_%d' % sub], ['dbg2sb'])
                            CP('dve', dbg2sb[:, sub, 129:258], acc[:, 0:129], [ak], ['dbg2sb'])
                        P.op('dve', lambda e, f=f, sub=sub: e.reciprocal(fz[f][:, 0:1], o0[:, sub, 128:129]), ['o0_%d' % sub], ['fz%d' % f])
                        P.op('dve', lambda e, f=f, acc=acc: e.reciprocal(fz[f][:, 1:2], acc[:, 128:129]), [ak], ['fz%d' % f])
                        TT('dve', fz[f][:, 2:3], fz[f][:, 1:2], neglam[:], ALU.mult, ['fz%d' % f, 'neglam'], ['fz%d' % f])
                        TS('dve', fu[f][:], acc[:, 0:128], fz[f][:, 2:3], None, ALU.mult, None, [ak, 'fz%d' % f], ['fu%d' % f])
                        STT('dve', fo[f][:], o0[:, sub, 0:128], fz[f][:, 0:1], fu[f][:], ALU.mult, ALU.add,
                            ['o0_%d' % sub, 'fz%d' % f, 'fu%d' % f], ['fo%d' % f])
                        TT('pool', fj[f][:], fo[f][:], fo[f][:], ALU.mult, ['fo%d' % f], ['fj%d' % f])
                        P.op('dve', lambda e, f=f: e.tensor_reduce(fz[f][:, 3:4], fj[f][:], AX.X, ALU.add), ['fj%d' % f], ['fz3_%d' % f])
                        TS('dve', fz[f][:, 3:4], fz[f][:, 3:4], 1.0 / 128.0, 1e-5, ALU.mult, ALU.add, ['fz3_%d' % f], ['fz3_%d' % f])
                        ACT(fz[f][:, 3:4], fz[f][:, 3:4], AF.Sqrt, ['fz3_%d' % f], ['fz3_%d' % f])
                        P.op('dve', lambda e, f=f: e.reciprocal(fz[f][:, 3:4], fz[f][:, 3:4]), ['fz3_%d' % f], ['fz3_%d' % f])
                        STT('dve', fon[f][:], fo[f][:], fz[f][:, 3:4], gs[:], ALU.mult, ALU.mult,
                            ['fo%d' % f, 'fz3_%d' % f, 'gs'], ['fon%d' % f])
                        MM(ps[3][:, 0:128], fon[f][:], identb[:], True, True, ['fon%d' % f, 'identb'], ['ps3'])
                        blk = t * 4 + sub
                        CP('act', aT[:, h, blk * 128:(blk + 1) * 128], ps[3][:, 0:128], ['ps3'], ['aT'])

                for h in range(8):
                    flush()
                    DMA('sp', KtF[:], S['kT'][h], [], ['KtF'])
                    for m in range(2):
                        DMA('sp', QT[m][m * 64:(m + 1) * 64, :], S['qT'][h, m * 64:(m + 1) * 64, :], [], ['QT%d' % m])
                    for q4 in range(4):
                        DMA('sp', Vh[:, q4 * 16:(q4 + 1) * 16, 0:128],
                            S['v'][h, q4 * 2048:(q4 + 1) * 2048, :].rearrange('(s p) e -> p s e', p=128), [], ['Vh'])
                    DMA('sp', strip[:], I['raw_da'][h], [], ['strip'])
                    STT('dve', strip[:], strip[:], cda[:, h:h + 1], mneg[:], ALU.subtract, ALU.add,
                        ['strip', 'cda', 'mneg'], ['strip'])
                    for t in range(4):
                        nsl = 16 * (t + 1)
                        for m in range(2):
                            for s in range(nsl):
                                c = st_['slot']
                                st_['slot'] += 1
                                if p0q and c % 10 == 0:
                                    p0q.pop(0)()
                                b2 = c % 4
                                b3 = c % 5
                                near = s >= 16 * t - 1
                                pi = b2
                                MM(ps[pi][:, :], KtF[:, s * 128:(s + 1) * 128], QT[m][:, t * 512:(t + 1) * 512],
                                   True, True, ['KtF', 'QT%d' % m], ['ps%d' % pi])
                                if near:
                                    x0 = 128 * (15 - (s - 16 * t))
                                    STT('dve', tmp[b2][:], ps[pi][:, :], 0.125, strip[:, x0:x0 + 512], ALU.mult, ALU.add,
                                        ['ps%d' % pi, 'strip'], ['tmp%d' % b2])
                                    ACT(pT[b3][:], tmp[b2][:], AF.Exp, ['tmp%d' % b2], ['pT%d' % b3])
                                else:
                                    ACT(pT[b3][:], ps[pi][:, :], AF.Exp, ['ps%d' % pi], ['pT%d' % b3], scale=0.125)

                                def pv(h=h, t=t, m=m, s=s, b3=b3, nsl=nsl):
                                    for sub in range(4):
                                        MM(ps[4 + sub][:, 0:129], pT[b3][:, sub * 128:(sub + 1) * 128],
                                           Vh[:, s, :], s == 0, s == nsl - 1, ['pT%d' % b3, 'Vh'], ['ps%d' % (4 + sub)])
                                    if s == nsl - 1:
                                        if m == 0:
                                            for sub in range(4):
                                                CP('dve', o0[:, sub, :], ps[4 + sub][:, 0:129], ['ps%d' % (4 + sub)], ['o0_%d' % sub])
                                        else:
                                            da_finalize(h, t)
                                push(pv)
                flush()
                while p0q:
                    p0q.pop(0)()
                P.barrier()
                P.emit()


        def phase_nsa():
            with ExitStack() as ph:
                KCT = sbt(ph, 'D_KCT', [64, 4, 512], BF16)
                Rg = sbt(ph, 'D_Rg', [128, 4, 4, 193], BF16)
                cns = sbt(ph, 'D_cns', [128, 16], F32)
                MEMSET('pool', KCT[:], 0.0, ['KCT'])
                MEMSET('pool', Rg[:], 0.0, ['Rg'])
                DMA('sp', cns[:], I['c_nsa'], [], ['cns'])
                with ExitStack() as p0:
                    cT = sbt(p0, 'D0_cT', [128, 8192], BF16)
                    w1s = sbt(p0, 'D0_w1s', [128, 4, 256], F32)
                    w1b = sbt(p0, 'D0_w1b', [128, 16, 256], BF16)
                    w2s = sbt(p0, 'D0_w2s', [128, 2, 64], F32)
                    w2b = sbt(p0, 'D0_w2b', [128, 2, 64], BF16)
                    pes = sbt(p0, 'D0_pes', [128, 16], F32)
                    peb = sbt(p0, 'D0_peb', [128, 16], BF16)
                    b1 = sbt(p0, 'D0_b1', [128, 2], F32)
                    hT = sbt(p0, 'D0_hT', [128, 2, 512], BF16)
                    ovs = sbt(p0, 'D0_ovs', [128, 4, 129], F32)
                    DMA('sp', ovs[:], I['ovl'], [], ['ovs'])
                    for g in range(4):
                        CP('pool', Rg[:, :, g, 0:129], ovs[:], ['ovs'], ['Rg'])
                    for kind in range(2):
                        w1src = I['w1k'] if kind == 0 else I['w1v']
                        for p8 in range(4):
                            DMA('sp', w1s[:], w1src[:, p8 * 4:(p8 + 1) * 4, :], [], ['w1s'])
                            CP('pool', w1b[:, p8 * 4:(p8 + 1) * 4, :], w1s[:], ['w1s'], ['w1b'])
                        DMA('sp', w2s[:], I['w2k'] if kind == 0 else I['w2v'], [], ['w2s'])
                        CP('pool', w2b[:], w2s[:], ['w2s'], ['w2b'])
                        DMA('sp', pes[:], I['pekT'] if kind == 0 else I['pevT'], [], ['pes'])
                        CP('pool', peb[:], pes[:], ['pes'], ['peb'])
                        for hc in range(2):
                            pi = nextps()
                            for p in range(16):
                                MM(ps[pi][:, 0:1], w1b[:, p, hc * 128:(hc + 1) * 128], peb[:, p:p + 1], p == 0, p == 15,
                                   ['w1b', 'peb'], ['ps%d' % pi])
                            CP('dve', b1[:, hc:hc + 1], ps[pi][:, 0:1], ['ps%d' % pi], ['b1'])
                        for g in range(4):
                            DMA('sp', cT[0:64, :], S['nkT'][kind, g * 64:(g + 1) * 64, :], [], ['cT'])
                            DMA('sp', cT[64:128, 0:8191], S['nkT'][kind, g * 64:(g + 1) * 64, 1:8192], [], ['cT'])
                            for hc in range(2):
                                pi = nextps()
                                for p in range(16):
                                    MM(ps[pi][:, 0:511], w1b[:, p, hc * 128:(hc + 1) * 128], cT[:, 2 * p:2 * p + 16 * 510 + 1:16],
                                       p == 0, p == 15, ['w1b', 'cT'], ['ps%d' % pi])
                                ACT(hT[:, hc, 0:511], ps[pi][:, 0:511], AF.Gelu_apprx_tanh, ['ps%d' % pi, 'b1'], ['hT'],
                                    bias=b1[:, hc:hc + 1])
                            if kind == 0:
                                pi = nextps()
                                for hc in range(2):
                                    MM(ps[pi][0:64, 0:511], w2b[:, hc, :], hT[:, hc, 0:511], hc == 0, hc == 1,
                                       ['w2b', 'hT'], ['ps%d' % pi])
                                CP('act', KCT[:, g, 0:511], ps[pi][0:64, 0:511], ['ps%d' % pi], ['KCT'])
                            else:
                                for cc in range(4):
                                    ncl = 128 if cc < 3 else 127
                                    pi = nextps()
                                    for hc in range(2):
                                        MM(ps[pi][0:ncl, 0:64], hT[:, hc, cc * 128:cc * 128 + ncl], w2b[:, hc, :], hc == 0, hc == 1,
                                           ['w2b', 'hT'], ['ps%d' % pi])
                                    CP('act', Rg[0:ncl, cc, g, 129:193], ps[pi][0:ncl, 0:64], ['ps%d' % pi], ['Rg'])
                    P.barrier()
                    P.emit()
                cm = sbt(ph, 'D_cm', [128, 2, 512], F32)
                vmul = sbt(ph, 'D_vmul', [128, 16, 128], F32)
                vadd = sbt(ph, 'D_vadd', [128, 16, 128], F32)
                Kbuf = sbt(ph, 'D_Kbuf', [128, 8192], BF16)
                Vbuf = sbt(ph, 'D_Vbuf', [128, 64, 65], BF16)
                strip = sbt(ph, 'D_strip', [128, 2944], F32)
                neg = sbt(ph, 'D_neg', [128, 2944], F32)
                QTn = [sbt(ph, 'D_QTn%d' % i, [64, 2048], BF16) for i in range(4)]
                QA = [sbt(ph, 'D_QA%d' % i, [128, 512], BF16) for i in range(2)]
                QB = [sbt(ph, 'D_QB%d' % i, [128, 512], BF16) for i in range(2)]
                selT = [sbt(ph, 'D_selT%d' % i, [128, 512], BF16) for i in range(4)]
                onsa = sbt(ph, 'D_onsa', [128, 16, 4, 64], F32)
                impacc = sbt(ph, 'D_impacc', [128, 4, 128], F32)
                pT = [sbt(ph, 'D_pT%d' % b, [128, 512], BF16) for b in range(5)]
                tmp = [sbt(ph, 'D_tmp%d' % b, [128, 512], F32) for b in range(4)]
                fz = [sbt(ph, 'D_fz%d' % i, [128, 4], F32) for i in range(2)]
                sc = [sbt(ph, 'D_sc%d' % i, [128, 128], F32) for i in range(2)]
                sc2 = [sbt(ph, 'D_sc2%d' % i, [128, 128], F32) for i in range(2)]
                m8 = [sbt(ph, 'D_m8%d' % i, [128, 16], F32) for i in range(2)]
                sng = [sbt(ph, 'D_sng%d' % i, [128, 128], BF16) for i in range(2)]
                onb = [sbt(ph, 'D_onb%d' % i, [128, 128], BF16) for i in range(2)]
                accT_sb = [sbt(ph, 'D_accT%d' % i, [65, 512], F32) for i in range(1)]
                QZ = [sbt(ph, 'D_QZ%d' % i, [128, 512], BF16) for i in range(2)]
                MEMSET('pool', QZ[0][:], 0.0, ['QZ0'])
                MEMSET('pool', QZ[1][:], 0.0, ['QZ1'])
                identf = sbt(ph, 'D_identf', [128, 128], F32)
                DMA('sp', identf[:], I['ident'], [], ['identf'])
                DMA('sp', cm[:], I['cm'], [], ['cm'])
                DMA('sp', vmul[:], I['vmul'], [], ['vmul'])
                DMA('sp', vadd[:], I['vadd'], [], ['vadd'])
                DMA('sp', Kbuf[64:128, :], I['onehot'], [], ['KbufHi'])
                MEMSET('pool', Vbuf[:, :, 64:65], 1.0, ['Vbuf'])
                cnt = {'slot': 0, 'fin': 0, 'q': 0, 'tr': 0, 'grp': 0}

                pipe = []

                def push(pv):
                    pipe.append(pv)
                    if len(pipe) > 3:
                        pipe.pop(0)()

                def flush():
                    while pipe:
                        pipe.pop(0)()

                def attn_slot(lhsT, rhs, rkeys, bias_ap, vrhs, vkeys, ncolv, first, last, after=None, tbank=None):
                    c = cnt['slot']
                    cnt['slot'] = c + 1
                    b2, b3 = c % 4, c % 5
                    MM(ps[b2][:, :], lhsT, rhs, True, True, rkeys, ['ps%d' % b2])
                    if bias_ap is not None:
                        STT('dve', tmp[b2][:], ps[b2][:, :], 0.125, bias_ap, ALU.mult, ALU.add,
                            ['ps%d' % b2, 'strip', 'cm'], ['tmp%d' % b2])
                        ACT(pT[b3][:], tmp[b2][:], AF.Exp, ['tmp%d' % b2], ['pT%d' % b3])
                    else:
                        ACT(pT[b3][:], ps[b2][:, :], AF.Exp, ['ps%d' % b2], ['pT%d' % b3], scale=0.125)

                    def pv():
                        if tbank is None:
                            for sub in range(4):
                                MM(ps[4 + sub][:, 0:ncolv], pT[b3][:, sub * 128:(sub + 1) * 128], vrhs, first, last,
                                   ['pT%d' % b3] + vkeys, ['ps%d' % (4 + sub)])
                        else:
                            MM(ps[tbank][0:ncolv, :], vrhs, pT[b3][:, :], first, last, ['pT%d' % b3] + vkeys, ['ps%d' % tbank])
                        if after is not None:
                            after()
                    push(pv)

                def fin_branch(t, hh, n, gidx, dcol, ncol0, first_branch, tb=None):
                    if tb is not None:
                        k = cnt['tr'] % 2
                        cnt['tr'] += 1
                        CP('act', accT_sb[0][0:65, :], ps[tb][0:65, :], ['ps%d' % tb], ['accT0'])
                        for sub in range(4):
                            MM(ps[6 + k][:, sub * 65:(sub + 1) * 65], accT_sb[0][0:65, sub * 128:(sub + 1) * 128], identf[0:65, 0:65],
                               True, True, ['accT0', 'identf'], ['ps%d' % (6 + k)])
                    for sub in range(4):
                        f = cnt['fin'] % 2
                        cnt['fin'] += 1
                        blk = 4 * t + sub
                        if tb is None:
                            acc = ps[4 + sub]
                            ak = 'ps%d' % (4 + sub)
                            c0 = 0
                        else:
                            acc = ps[6 + k]
                            ak = 'ps%d' % (6 + k)
                            c0 = sub * 65
                        fk = 'fz%d' % f
                        TS('dve', fz[f][:, 0:1], acc[:, c0 + dcol:c0 + dcol + 1], 1e-30, None, ALU.max, None, [ak], [fk])
                        P.op('dve', lambda e, f=f: e.reciprocal(fz[f][:, 1:2], fz[f][:, 0:1]), [fk], [fk])
                        TT('dve', fz[f][:, 2:3], fz[f][:, 1:2], gates[:, blk, n * 3 + gidx:n * 3 + gidx + 1], ALU.mult,
                           [fk, 'gates'], [fk])
                        if first_branch:
                            if hh == 0:
                                TS('dve', impacc[:, sub, :], acc[:, 0:128], fz[f][:, 1:2], None, ALU.mult, None,
                                   [ak, fk], ['impacc%d' % sub])
                            else:
                                STT('dve', impacc[:, sub, :], acc[:, 0:128], fz[f][:, 1:2], impacc[:, sub, :], ALU.mult, ALU.add,
                                    [ak, fk, 'impacc%d' % sub], ['impacc%d' % sub])
                            TS('dve', onsa[:, blk, hh, :], acc[:, ncol0:ncol0 + 64], fz[f][:, 2:3], None, ALU.mult, None,
                               [ak, fk], ['onsa'])
                        else:
                            STT('dve', onsa[:, blk, hh, :], acc[:, c0 + ncol0:c0 + ncol0 + 64], fz[f][:, 2:3], onsa[:, blk, hh, :],
                                ALU.mult, ALU.add, [ak, fk, 'onsa'], ['onsa'])

                for g in range(4):
                    for hh in range(4):
                        n = 4 * g + hh
                        DMA('sp', QTn[hh][:], S['qT'][8 + n // 2, (n % 2) * 64:(n % 2) * 64 + 64, :], [], ['QTn%d' % hh])
                    def topk_code(g, t):
                        for sub in range(4):
                            f = cnt['fin'] % 2
                            cnt['fin'] += 1
                            blk = 4 * t + sub
                            TT('dve', sc[f][:], impacc[:, sub, :], vmul[:, blk, :], ALU.mult, ['impacc%d' % sub, 'vmul'], ['sc%d' % f])
                            TT('dve', sc[f][:], sc[f][:], vadd[:, blk, :], ALU.add, ['sc%d' % f, 'vadd'], ['sc%d' % f])
                            P.op('dve', lambda e, f=f: e.max(out=m8[f][:, 0:8], in_=sc[f][:]), ['sc%d' % f], ['m8_%d' % f])
                            P.op('dve', lambda e, f=f: e.match_replace(out=sc2[f][:], in_to_replace=m8[f][:, 0:8],
                                                                      in_values=sc[f][:], imm_value=-3.0e38),
                                 ['sc%d' % f, 'm8_%d' % f], ['sc2%d' % f])
                            P.op('dve', lambda e, f=f: e.max(out=m8[f][:, 8:16], in_=sc2[f][:]), ['sc2%d' % f], ['m8_%d' % f])
                            TS('dve', sng[f][:], sc[f][:], m8[f][:, 15:16], -240000.0, ALU.is_lt, ALU.mult,
                               ['sc%d' % f, 'm8_%d' % f], ['sng%d' % f])
                            MM(ps[3][:, 0:128], sng[f][:], identb[:], True, True, ['sng%d' % f, 'identb'], ['ps3'])
                            CP('act', selT[t][:, sub * 128:(sub + 1) * 128], ps[3][:, 0:128], ['ps3'], ['selT%d' % t])
                            if dbg and g == 0 and t == 0:
                                CP('dve', dbg2sb[:, sub, 0:128], sc[f][:], ['sc%d' % f], ['dbg2sb'])
                                CP('dve', dbg2sb[:, sub, 129:145], m8[f][:], ['m8_%d' % f], ['dbg2sb'])

                    for t in range(4):
                        for hh in range(4):
                            n = 4 * g + hh
                            for cc in range(t + 1):
                                bias_ap = cm[:, cc - t + 1, :] if cc >= t - 1 else None
                                aft = None
                                if cc == t:
                                    def aft(t=t, hh=hh, n=n, g=g):
                                        fin_branch(t, hh, n, 0, 128, 129, True)
                                        if hh == 3:
                                            topk_code(g, t)
                                attn_slot(KCT[:, g, cc * 128:(cc + 1) * 128], QTn[hh][:, t * 512:(t + 1) * 512],
                                          ['KCT', 'QTn%d' % hh], bias_ap, Rg[:, cc, g, :], ['Rg'], 193, cc == 0, cc == t, after=aft)
                    flush()
                    DMA('sp', Kbuf[0:64, :], S['nkT'][2, g * 64:(g + 1) * 64, :], [], ['KbufLo'])
                    for q4 in range(4):
                        DMA('sp', Vbuf[:, q4 * 16:(q4 + 1) * 16, 0:64],
                            S['nv'][0, g, q4 * 2048:(q4 + 1) * 2048, :].rearrange('(s p) e -> p s e', p=128), [], ['Vbuf'])
                    DMA('sp', neg[:], I['mneg'], [], ['neg'])
                    for hh in range(4):
                        n = 4 * g + hh
                        DMA('sp', strip[:], I['raw_nsa'][n], [], ['strip'])
                        STT('dve', strip[:], strip[:], cns[:, n:n + 1], neg[:], ALU.subtract, ALU.add,
                            ['strip', 'cns', 'neg'], ['strip'])
                        for t in range(4):
                            qb = cnt['q'] % 2
                            cnt['q'] += 1
                            CP('pool', QA[qb][0:64, :], QTn[hh][:, t * 512:(t + 1) * 512], ['QTn%d' % hh], ['QA%d' % qb])
                            CP('pool', QB[qb][0:64, :], QTn[hh][:, t * 512:(t + 1) * 512], ['QTn%d' % hh], ['QB%d' % qb])
                            DMA('sp', QA[qb][64:128, :], selT[t][0:64, :], ['selT%d' % t], ['QA%d' % qb])
                            CP('pool', QB[qb][64:128, :], selT[t][64:128, :], ['selT%d' % t], ['QB%d' % qb])
                            nsl = 16 * (t + 1)
                            tb = 4 + (cnt['grp'] % 2)
                            cnt['grp'] += 1
                            for s_ in range(nsl):
                                near = s_ >= 16 * t - 1
                                bias_ap = strip[:, 128 * (15 - (s_ - 16 * t)):128 * (15 - (s_ - 16 * t)) + 512] if near else None
                                qq = QA[qb] if s_ < 32 else QB[qb]
                                qk = ('QA%d' if s_ < 32 else 'QB%d') % qb
                                aft = None
                                if s_ == nsl - 1:
                                    def aft(t=t, hh=hh, n=n, tb=tb):
                                        fin_branch(t, hh, n, 1, 64, 0, False, tb=tb)
                                attn_slot(Kbuf[:, s_ * 128:(s_ + 1) * 128], qq[:, :], ['KbufLo', 'KbufHi', qk], bias_ap,
                                          Vbuf[:, s_, :], ['Vbuf'], 65, s_ == 0, s_ == nsl - 1, after=aft, tbank=tb)
                    flush()
                    DMA('sp', Kbuf[0:64, :], S['nkT'][3, g * 64:(g + 1) * 64, :], [], ['KbufLo'])
                    for q4 in range(4):
                        DMA('sp', Vbuf[:, q4 * 16:(q4 + 1) * 16, 0:64],
                            S['nv'][1, g, q4 * 2048:(q4 + 1) * 2048, :].rearrange('(s p) e -> p s e', p=128), [], ['Vbuf'])
                    DMA('sp', neg[:], I['wneg'], [], ['neg'])
                    for hh in range(4):
                        n = 4 * g + hh
                        DMA('sp', strip[:], I['raw_nsa'][n], [], ['strip'])
                        STT('dve', strip[:], strip[:], cns[:, n:n + 1], neg[:], ALU.subtract, ALU.add,
                            ['strip', 'cns', 'neg'], ['strip'])
                        for t in range(4):
                            s0 = max(0, 16 * t - 4)
                            s1 = 16 * t + 15
                            tb = 4 + (cnt['grp'] % 2)
                            cnt['grp'] += 1
                            qz = cnt['grp'] % 2
                            CP('pool', QZ[qz][0:64, :], QTn[hh][:, t * 512:(t + 1) * 512], ['QTn%d' % hh], ['QZ%d' % qz])
                            for s_ in range(s0, s1 + 1):
                                x0 = 128 * (15 - (s_ - 16 * t))
                                aft = None
                                if s_ == s1:
                                    def aft(t=t, hh=hh, n=n, tb=tb):
                                        fin_branch(t, hh, n, 2, 64, 0, False, tb=tb)
                                attn_slot(Kbuf[:, s_ * 128:(s_ + 1) * 128], QZ[qz][:, :],
                                          ['KbufLo', 'KbufHi', 'QZ%d' % qz], strip[:, x0:x0 + 512], Vbuf[:, s_, :], ['Vbuf'], 65,
                                          s_ == s0, s_ == s1, after=aft, tbank=tb)
                    flush()
                    for blk in range(16):
                        for pr in range(2):
                            f = cnt['fin'] % 2
                            cnt['fin'] += 1
                            CP('pool', onb[f][:].rearrange('p (h e) -> p h e', h=2), onsa[:, blk, 2 * pr:2 * pr + 2, :],
                               ['onsa'], ['onb%d' % f])
                            pi = 3
                            MM(ps[pi][:, 0:128], onb[f][:], identb[:], True, True, ['onb%d' % f, 'identb'], ['ps%d' % pi])
                            CP('act', nT[:, 2 * g + pr, blk * 128:(blk + 1) * 128], ps[pi][:, 0:128], ['ps%d' % pi], ['nT'])
                P.barrier()
                P.emit()


        def phase_e1():
            with ExitStack() as ph:
                xs = [sbt(ph, 'E_xs%d' % i, [128, 2, 512], F32) for i in range(2)]
                xb = sbt(ph, 'E_xb', [128, 16, 2048], BF16)
                wgs = [sbt(ph, 'E_wgs%d' % i, [128, 16, 128], F32) for i in range(2)]
                wbs = [sbt(ph, 'E_wbs%d' % i, [128, 8, 128], F32) for i in range(2)]
                wga = [sbt(ph, 'E_wga%d' % i, [128, 16, 128], BF16) for i in range(2)]
                wgn = [sbt(ph, 'E_wgn%d' % i, [128, 16, 128], BF16) for i in range(2)]
                wba = [sbt(ph, 'E_wba%d' % i, [128, 8, 128], BF16) for i in range(2)]
                wbn = [sbt(ph, 'E_wbn%d' % i, [128, 8, 128], BF16) for i in range(2)]
                sA = [sbt(ph, 'E_sA%d' % i, [128, 512], F32) for i in range(2)]
                sB = [sbt(ph, 'E_sB%d' % i, [128, 512], F32) for i in range(2)]
                m1 = [sbt(ph, 'E_m1%d' % i, [128, 512], F32) for i in range(2)]
                mx = [sbt(ph, 'E_mx%d' % i, [128, 512], BF16) for i in range(3)]
                k = 0
                for tile in range(4):
                    for qq in range(8):
                        half = k % 2
                        k += 1
                        DMA('sp', xs[half][:], I['xTo'][tile, :, qq * 2:(qq + 1) * 2, :], [], ['xs%d' % half])
                        CP('dve' if qq % 2 == 0 else 'act', xb[:, qq * 2:(qq + 1) * 2, tile * 512:(tile + 1) * 512], xs[half][:],
                           ['xs%d' % half], ['xb'])
                it = 0
                mi = 0
                for c in range(16):
                    b = c % 2
                    DMA('sp', wgs[0][:], I['w_mg'][:, :, c * 128:(c + 1) * 128], [], ['wgs0'])
                    CP('pool', wga[b][:], wgs[0][:], ['wgs0'], ['wga%d' % b])
                    DMA('sp', wgs[1][:], I['w_mg'][:, :, 2048 + c * 128:2048 + (c + 1) * 128], [], ['wgs1'])
                    CP('dve', wgn[b][:], wgs[1][:], ['wgs1'], ['wgn%d' % b])
                    DMA('sp', wbs[0][:], I['w_bda'][:, :, c * 128:(c + 1) * 128], [], ['wbs0'])
                    CP('pool', wba[b][:], wbs[0][:], ['wbs0'], ['wba%d' % b])
                    DMA('sp', wbs[1][:], I['w_bnsa'][:, :, c * 128:(c + 1) * 128], [], ['wbs1'])
                    CP('dve', wbn[b][:], wbs[1][:], ['wbs1'], ['wbn%d' % b])
                    for tile in range(4):
                        tsl = slice(tile * 512, (tile + 1) * 512)
                        p2 = it % 2
                        it += 1
                        pA, pB, pC, pD = (4 * p2 + 0), (4 * p2 + 1), (4 * p2 + 2), (4 * p2 + 3)
                        for kc in range(16):
                            MM(ps[pA][:, :], wga[b][:, kc, :], xb[:, kc, tsl], kc == 0, kc == 15, ['wga%d' % b, 'xb'], ['ps%d' % pA])
                        for kc in range(16):
                            MM(ps[pB][:, :], wgn[b][:, kc, :], xb[:, kc, tsl], kc == 0, kc == 15, ['wgn%d' % b, 'xb'], ['ps%d' % pB])
                        for kc in range(8):
                            MM(ps[pC][:, :], wba[b][:, kc, :], aT[:, kc, tsl], kc == 0, kc == 7, ['wba%d' % b, 'aT'], ['ps%d' % pC])
                        for kc in range(8):
                            MM(ps[pD][:, :], wbn[b][:, kc, :], nT[:, kc, tsl], kc == 0, kc == 7, ['wbn%d' % b, 'nT'], ['ps%d' % pD])
                        ACT(sA[p2][:], ps[pA][:, :], AF.Sigmoid, ['ps%d' % pA], ['sA%d' % p2])
                        ACT(sB[p2][:], ps[pB][:, :], AF.Sigmoid, ['ps%d' % pB], ['sB%d' % p2])
                        TT('dve', m1[p2][:], sA[p2][:], ps[pC][:, :], ALU.mult, ['sA%d' % p2, 'ps%d' % pC], ['m1%d' % p2])
                        TT('dve', sB[p2][:], sB[p2][:], ps[pD][:, :], ALU.mult, ['sB%d' % p2, 'ps%d' % pD], ['sB%d' % p2])
                        m3 = mi % 3
                        mi += 1
                        TT('pool', mx[m3][:], m1[p2][:], sB[p2][:], ALU.add, ['m1%d' % p2, 'sB%d' % p2], ['mx%d' % m3])
                        DMA('pool', S['mixT'][tile, :, c, :], mx[m3][:], ['mx%d' % m3], [])
                P.barrier()
                P.emit()

        def phase_e2():
            with ExitStack() as ph:
                wos = sbt(ph, 'F_wos', [128, 4, 512], F32)
                wob = [sbt(ph, 'F_wob%d' % i, [128, 16, 512], BF16) for i in range(2)]
                mxt = sbt(ph, 'F_mxt', [128, 16, 512], BF16)
                xc = [sbt(ph, 'F_xc%d' % i, [128, 512], F32) for i in range(3)]
                hpre2 = [sbt(ph, 'F_hpre%d' % i, [128, 4, 2048], F32) for i in range(2)]
                junk = sbt(ph, 'F_junk', [128, 2048], F32)
                hn = [sbt(ph, 'F_hn%d' % i, [128, 2048], F32) for i in range(2)]
                hb = sbt(ph, 'F_hb', [128, 2048], BF16)
                gB = sbt(ph, 'F_gB', [128, 2048], F32)
                bB = sbt(ph, 'F_bB', [128, 2048], F32)
                st = [sbt(ph, 'F_st%d' % i, [128, 4], F32) for i in range(2)]
                hTt = sbt(ph, 'F_hTt', [128, 16, 512], BF16)
                DMA('sp', gB[:], I['ln1g'], [], ['gB'])
                DMA('sp', bB[:], I['ln1b'], [], ['bB'])
                wi = 0
                xi = 0
                for tile in range(4):
                    hpre = hpre2[tile % 2]
                    hk = 'hpre%d_' % (tile % 2)
                    DMA('sp', mxt[:], S['mixT'][tile], [], ['mxt'])
                    for dc in range(4):
                        b = wi % 2
                        wi += 1
                        for k4 in range(4):
                            DMA('sp', wos[:], I['w_out'][:, k4 * 4:(k4 + 1) * 4, dc * 512:(dc + 1) * 512], [], ['wos'])
                            CP('act' if k4 % 2 == 0 else 'pool', wob[b][:, k4 * 4:(k4 + 1) * 4, :], wos[:], ['wos'], ['wob%d' % b])
                        for sub in range(4):
                            blk = tile * 4 + sub
                            pi = nextps()
                            for kc in range(16):
                                MM(ps[pi][:, :], mxt[:, kc, sub * 128:(sub + 1) * 128], wob[b][:, kc, :], kc == 0, kc == 15,
                                   ['mxt', 'wob%d' % b], ['ps%d' % pi])
                            x3 = xi % 3
                            xi += 1
                            DMA('sp', xc[x3][:], I['xo'][blk, :, dc * 512:(dc + 1) * 512], [], ['xc%d' % x3])
                            STT('dve', hpre[:, sub, dc * 512:(dc + 1) * 512], xc[x3][:], ALPHA, ps[pi][:, :], ALU.mult, ALU.add,
                                ['xc%d' % x3, 'ps%d' % pi], [hk + str(sub)])
                    for sub in range(4):
                        blk = tile * 4 + sub
                        layer_norm_block(hpre[:, sub, :], hk + str(sub), gB, bB, junk, hn, st, blk)
                        f = blk % len(hn)
                        DMA('pool', S['h'][blk], hn[f][:], ['hn%d' % f], [])
                        CP('act', hb[:], hn[f][:], ['hn%d' % f], ['hb'])
                        for f4 in range(4):
                            pi = nextps()
                            for ff in range(4):
                                fc = f4 * 4 + ff
                                MM(ps[pi][:, ff * 128:(ff + 1) * 128], hb[:, fc * 128:(fc + 1) * 128], identb[:], True, True,
                                   ['hb', 'identb'], ['ps%d' % pi])
                            CP('act', hTt[:, f4 * 4:(f4 + 1) * 4, sub * 128:(sub + 1) * 128],
                               ps[pi][:, :].rearrange('p (a b) -> p a b', a=4), ['ps%d' % pi], ['hTt'])
                    DMA('pool', S['hT'][tile], hTt[:], ['hTt'], [])
                P.barrier()
                P.emit()

        def layer_norm_block(src, srck, gB_, bB_, junk, hn, st, blk, junkk='junk'):
            f = blk % len(hn)
            sk = 'st%d' % f
            hk_ = 'hn%d' % f
            P.op('dve', lambda e: e.tensor_reduce(st[f][:, 0:1], src, AX.X, ALU.add), [srck], [sk])
            ACT(junk[:], src, AF.Square, [srck], [junkk])
            P.op('dve', lambda e: e.tensor_reduce(st[f][:, 1:2], junk[:], AX.X, ALU.add), [junkk], [sk])
            TS('dve', st[f][:, 0:1], st[f][:, 0:1], 1.0 / 2048.0, None, ALU.mult, None, [sk], [sk])
            TT('dve', st[f][:, 3:4], st[f][:, 0:1], st[f][:, 0:1], ALU.mult, [sk], [sk])
            STT('dve', st[f][:, 1:2], st[f][:, 1:2], 1.0 / 2048.0, st[f][:, 3:4], ALU.mult, ALU.subtract, [sk], [sk])
            TS('dve', st[f][:, 1:2], st[f][:, 1:2], 1e-5, None, ALU.add, None, [sk], [sk])
            ACT(st[f][:, 1:2], st[f][:, 1:2], AF.Sqrt, [sk], [sk])
            P.op('dve', lambda e: e.reciprocal(st[f][:, 2:3], st[f][:, 1:2]), [sk], [sk])
            STT('dve', st[f][:, 3:4], st[f][:, 0:1], -1.0, st[f][:, 2:3], ALU.mult, ALU.mult, [sk], [sk])
            ACT(hn[f][:], src, AF.Identity, [srck, sk], [hk_], scale=st[f][:, 2:3], bias=st[f][:, 3:4])
            TT('dve', hn[f][:], hn[f][:], gB_[:], ALU.mult, [hk_, 'gB'], [hk_])
            TT('dve', hn[f][:], hn[f][:], bB_[:], ALU.add, [hk_, 'bB'], [hk_])

        def p0_units(us, ub, engs):
            units = []
            for ec in range(128):
                for which in range(2):
                    def unit(ec=ec, which=which, k=len(units)):
                        b = k % len(us)
                        src = I['puT'][ec].rearrange('p a b -> p (a b)') if which == 0 else I['pv'][ec]
                        dst = S['puT'][ec].rearrange('p a b -> p (a b)') if which == 0 else S['pv'][ec]
                        DMA('sp', us[b][:], src, [], ['us%d' % b])
                        CP(engs[k % len(engs)], ub[b][:], us[b][:], ['us%d' % b], ['ub%d' % b])
                        DMA('pool', dst, ub[b][:], ['ub%d' % b], [])
                    units.append(unit)
            return units

        def phase_p0():
            with ExitStack() as ph:
                us = [sbt(ph, 'P_us%d' % i, [128, 2048], F32) for i in range(3)]
                ub = [sbt(ph, 'P_ub%d' % i, [128, 2048], BF16) for i in range(3)]
                for u in p0_units(us, ub, ['dve', 'act', 'pool']):
                    u()
                P.barrier()
                P.emit()

        def phase_peer(fin_evs):
            with ExitStack() as ph:
                hTt = sbt(ph, 'G_hTt', [128, 16, 512], BF16)
                wqs = [sbt(ph, 'G_wqs%d' % i, [128, 16, 64], F32) for i in range(2)]
                wqb = [sbt(ph, 'G_wqb%d' % i, [128, 16, 64], BF16) for i in range(2)]
                sks = sbt(ph, 'G_sks', [64, 2, 128], F32)
                skb = sbt(ph, 'G_skb', [64, 2, 128], BF16)
                qTu = [sbt(ph, 'G_qTu%d' % i, [64, 512], BF16) for i in range(2)]
                sAll = sbt(ph, 'G_sAll', [128, 4, 16, 128], F32)
                tau = sbt(ph, 'G_tau', [128, 4, 8], F32)
                negc = sbt(ph, 'G_negc', [128, 4, 8], F32)
                kap = sbt(ph, 'G_kap', [128, 4, 8], F32)
                m1 = [sbt(ph, 'G_m1%d' % i, [128, 16], F32) for i in range(4)]
                m2 = [sbt(ph, 'G_m2%d' % i, [128, 16], F32) for i in range(4)]
                mc = [sbt(ph, 'G_mc%d' % i, [128, 16], F32) for i in range(4)]
                t1 = [sbt(ph, 'G_t1%d' % i, [128, 256], F32) for i in range(4)]
                cand = [sbt(ph, 'G_cand%d' % i, [128, 256], F32) for i in range(4)]
                sm = [sbt(ph, 'G_sm%d' % i, [128, 4], F32) for i in range(4)]
                e16 = [sbt(ph, 'G_e16%d' % i, [128, 16], F32) for i in range(4)]
                eb = [sbt(ph, 'G_e%d' % i, [128, 4, 128], F32) for i in range(5)]
                Wall = [sbt(ph, 'G_W%d' % i, [128, 8, 4, 128], BF16) for i in range(2)]
                GT = [sbt(ph, 'G_GT%d' % i, [128, 4, 512], BF16) for i in range(3)]
                Gs = [sbt(ph, 'G_Gs%d' % i, [128, 512], BF16) for i in range(2)]
                uch = [sbt(ph, 'G_uch%d' % i, [128, 16, 128], BF16) for i in range(2)]
                vch = [sbt(ph, 'G_vch%d' % i, [128, 4, 2048], BF16) for i in range(2)]
                ga = [sbt(ph, 'G_ga%d' % i, [128, 512], F32) for i in range(4)]
                GA = [sbt(ph, 'G_GA%d' % i, [128, 4, 512], BF16) for i in range(2)]
                acc = sbt(ph, 'G_acc', [128, 4, 2048], F32)
                DMA('sp', sks[:], I['skT'], [], ['sks'])
                CP('dve', skb[:], sks[:], ['sks'], ['skb'])
                c_ = {'u': 0, 'k': 0, 'w': 0, 's': 0, 'v': 0, 'g': 0}
                for tile in range(4):
                    DMA('sp', hTt[:], S['hT'][tile], [], ['hTt'])
                    for u in range(16):
                        b = c_['u'] % 2
                        c_['u'] += 1
                        DMA('sp', wqs[b][:], I['wq'][u], [], ['wqs%d' % b])
                        CP('act', wqb[b][:], wqs[b][:], ['wqs%d' % b], ['wqb%d' % b])
                        pi = nextps()
                        for kc in range(16):
                            MM(ps[pi][0:64, :], wqb[b][:, kc, :], hTt[:, kc, :], kc == 0, kc == 15, ['wqb%d' % b, 'hTt'], ['ps%d' % pi])
                        CP('act', qTu[b][:], ps[pi][0:64, :], ['ps%d' % pi], ['qTu%d' % b])
                        pi = nextps()
                        for blk in range(4):
                            MM(ps[pi][:, blk * 128:(blk + 1) * 128], qTu[b][:, blk * 128:(blk + 1) * 128], skb[:, u % 2, :], True, True,
                               ['qTu%d' % b, 'skb'], ['ps%d' % pi])
                        CP('dve', sAll[:, :, u, :], ps[pi][:, :].rearrange('p (a b) -> p a b', a=4), ['ps%d' % pi], ['sAll'])
                    def chain(blk, h, f):
                        steps = []
                        s1 = sAll[:, blk, 2 * h, :]
                        s2 = sAll[:, blk, 2 * h + 1, :]
                        for (sx, mm_, mk, tk_, tt_) in ((s1, m1[f], 'm1_%d' % f, 't1a_%d' % f, t1[f][:, 0:128]),
                                                        (s2, m2[f], 'm2_%d' % f, 't1b_%d' % f, t1[f][:, 128:256])):
                            steps.append(lambda sx=sx, mm_=mm_, mk=mk: P.op('dve', lambda e: e.max(out=mm_[:, 0:8], in_=sx), ['sAll'], [mk]))
                            steps.append(lambda sx=sx, mm_=mm_, mk=mk, tk_=tk_, tt_=tt_: P.op(
                                'dve', lambda e: e.match_replace(out=tt_, in_to_replace=mm_[:, 0:8], in_values=sx, imm_value=-3.0e38),
                                ['sAll', mk], [tk_]))
                            steps.append(lambda mm_=mm_, mk=mk, tk_=tk_, tt_=tt_: P.op(
                                'dve', lambda e: e.max(out=mm_[:, 8:16], in_=tt_), [tk_], [mk]))
                        steps.append(lambda: TT('pool', cand[f][:].rearrange('p (a b) -> p a b', a=16),
                                                m1[f][:].unsqueeze(2).broadcast_to([128, 16, 16]), m2[f][:].unsqueeze(1).broadcast_to([128, 16, 16]),
                                                ALU.add, ['m1_%d' % f, 'm2_%d' % f], ['cand%d' % f]))
                        steps.append(lambda: P.op('dve', lambda e: e.max(out=mc[f][:, 0:8], in_=cand[f][:]), ['cand%d' % f], ['mc%d' % f]))
                        steps.append(lambda: P.op('dve', lambda e: e.match_replace(out=t1[f][:], in_to_replace=mc[f][:, 0:8], in_values=cand[f][:],
                                                                                  imm_value=-3.0e38),
                                                  ['cand%d' % f, 'mc%d' % f, 't1a_%d' % f, 't1b_%d' % f], ['t1a_%d' % f, 't1b_%d' % f]))
                        steps.append(lambda: P.op('dve', lambda e: e.max(out=mc[f][:, 8:16], in_=t1[f][:]), ['t1a_%d' % f, 't1b_%d' % f], ['mc%d' % f]))
                        steps.append(lambda: CP('dve', tau[:, blk, h:h + 1], mc[f][:, 15:16], ['mc%d' % f], ['tau']))
                        steps.append(lambda: TS('dve', sm[f][:, 0:1], mc[f][:, 0:1], -1.0, None, ALU.mult, None, ['mc%d' % f], ['sm%d' % f]))
                        steps.append(lambda: ACT(e16[f][:], mc[f][:], AF.Exp, ['mc%d' % f, 'sm%d' % f], ['e16_%d' % f], bias=sm[f][:, 0:1]))
                        steps.append(lambda: P.op('dve', lambda e: e.tensor_reduce(sm[f][:, 1:2], e16[f][:], AX.X, ALU.add), ['e16_%d' % f], ['sm%d' % f]))
                        steps.append(lambda: ACT(sm[f][:, 2:3], sm[f][:, 1:2], AF.Ln, ['sm%d' % f], ['sm%d' % f]))
                        steps.append(lambda: TT('dve', negc[:, blk, h:h + 1], sm[f][:, 0:1], sm[f][:, 2:3], ALU.subtract, ['sm%d' % f], ['negc%d' % f]))
                        steps.append(lambda: TT('dve', sm[f][:, 3:4], mc[f][:, 15:16], negc[:, blk, h:h + 1], ALU.add,
                                                ['mc%d' % f, 'negc%d' % f], ['sm%d' % f]))
                        steps.append(lambda: ACT(sm[f][:, 3:4], sm[f][:, 3:4], AF.Exp, ['sm%d' % f], ['sm%d' % f]))
                        steps.append(lambda: TS('dve', kap[:, blk, h:h + 1], sm[f][:, 3:4], 0.9999, None, ALU.mult, None, ['sm%d' % f], ['kap']))
                        steps.append(lambda: TS('dve', sAll[:, blk, 2 * h, :], sAll[:, blk, 2 * h, :], negc[:, blk, h:h + 1], None, ALU.add, None,
                                                ['negc%d' % f, 'm1_%d' % f, 't1a_%d' % f], ['sAllw%d' % f]))
                        return steps
                    pairs = [(blk, h) for blk in range(4) for h in range(8)]
                    for g4 in range(0, 32, 4):
                        chains = [chain(blk, h, f) for f, (blk, h) in enumerate(pairs[g4:g4 + 4])]
                        for i in range(len(chains[0])):
                            for ch in chains:
                                ch[i]()
                    P.op('dve', lambda e: e.tensor_copy(sm[0][:, 0:1], sm[0][:, 0:1]),
                         ['sAllw0', 'sAllw1', 'sAllw2', 'sAllw3', 'negc0', 'negc1', 'negc2', 'negc3', 'sm0'], ['sAll', 'negc', 'sm0'])
                    def opsA(eg, blk, h):
                        wb_ = (4 * eg + blk) % 2
                        sb_ = c_['s'] % 5
                        c_['s'] += 1
                        if h < NPOOL:
                            TT('pool', eb[sb_][:],
                               sAll[:, blk, 2 * h, 4 * eg:4 * eg + 4].unsqueeze(2).broadcast_to([128, 4, 128]),
                               sAll[:, blk, 2 * h + 1, :].unsqueeze(1).broadcast_to([128, 4, 128]),
                               ALU.add, ['sAll'], ['e%d' % sb_])
                            ACT(eb[sb_][:], eb[sb_][:], AF.Exp, ['e%d' % sb_], ['e%d' % sb_])
                        else:
                            for c in range(4):
                                ACT(eb[sb_][:, c, :], sAll[:, blk, 2 * h + 1, :], AF.Exp, ['sAll'], ['e%d' % sb_],
                                    bias=sAll[:, blk, 2 * h, 4 * eg + c:4 * eg + c + 1])
                        STT('dve', Wall[wb_][:, h, :, :], eb[sb_][:], kap[:, blk, h:h + 1], eb[sb_][:], ALU.is_ge, ALU.mult,
                            ['kap', 'e%d' % sb_], ['W%d' % wb_])

                    def stageB(eg, blk):
                        k = 4 * eg + blk
                        wb_ = k % 2
                        pb = k % 2
                        for c in range(4):
                            for h in range(8):
                                MM(ps[pb][:, c * 128:(c + 1) * 128], Wall[wb_][:, h, c, :], identb[:], h == 0, h == 7,
                                   ['W%d' % wb_, 'identb'], ['ps%d' % pb])
                        def evac(eg=eg, blk=blk, pb=pb):
                            CP('act', GT[eg % 3][:, :, blk * 128:(blk + 1) * 128], ps[pb][:, :].rearrange('p (a b) -> p a b', a=4),
                               ['ps%d' % pb], ['GT%d' % (eg % 3)])
                        pend.append(evac)

                    def stageC_pe(eg, c):
                        ec = 4 * eg + c
                        u2 = c % 2
                        DMA('sp', uch[u2][:], S['puT'][ec], [], ['uch%d' % u2])
                        pa = 2 + c
                        for kc in range(16):
                            MM(ps[pa][:, :], uch[u2][:, kc, :], hTt[:, kc, :], kc == 0, kc == 15, ['uch%d' % u2, 'hTt'], ['ps%d' % pa])

                    def stageC_post(eg):
                        gb = eg % 2
                        for c in range(4):
                            ACT(ga[c][:], ps[2 + c][:, :], AF.Gelu_apprx_tanh, ['ps%d' % (2 + c)], ['ga%d' % c])
                        for c in range(4):
                            TT('pool', GA[gb][:, c, :], ga[c][:], GT[eg % 3][:, c, :], ALU.mult, ['ga%d' % c, 'GT%d' % (eg % 3)], ['GA%d' % gb])

                    def stageD1(eg, blk, dc):
                        gb = eg % 2
                        vb = eg % 2
                        pv_ = 6 + (c_['v'] % 2)
                        c_['v'] += 1
                        for c in range(4):
                            MM(ps[pv_][:, :], GA[gb][:, c, blk * 128:(blk + 1) * 128], vch[vb][:, c, dc * 512:(dc + 1) * 512],
                               c == 0, c == 3, ['GA%d' % gb, 'vch%d' % vb], ['ps%d' % pv_])
                        if eg == 0:
                            CP('dve', acc[:, blk, dc * 512:(dc + 1) * 512], ps[pv_][:, :], ['ps%d' % pv_], ['acc%d' % blk])
                        else:
                            TT('dve', acc[:, blk, dc * 512:(dc + 1) * 512], acc[:, blk, dc * 512:(dc + 1) * 512], ps[pv_][:, :],
                               ALU.add, ['ps%d' % pv_, 'acc%d' % blk], ['acc%d' % blk])

                    pend = []
                    for it in range(35):
                        doA = it < 32
                        if 2 <= it <= 33:
                            eg_ = it - 2
                            DMA('sp', vch[eg_ % 2][:], S['pv'][4 * eg_:4 * eg_ + 4].rearrange('c p d -> p c d'), [], ['vch%d' % (eg_ % 2)])
                        for blk in range(4):
                            for h in range(8):
                                if doA:
                                    opsA(it, blk, h)
                                if h == 3 or not doA:
                                    while pend:
                                        pend.pop(0)()
                                if blk == 0 and h == 3 and 2 <= it <= 33:
                                    stageC_post(it - 2)
                                if h % 2 == 1 and 3 <= it <= 34:
                                    stageD1(it - 3, blk, h // 2)
                            if 1 <= it <= 32:
                                stageC_pe(it - 1, blk)
                            if doA:
                                stageB(it, blk)
                    DMA('pool', S['pe'][tile * 4:(tile + 1) * 4].rearrange('b p d -> p b d'), acc[:], ['acc0', 'acc1', 'acc2', 'acc3'], [])
                P.barrier()
                P.emit()


        def phase_g2(fin_evs):
            with ExitStack() as ph:
                hblk = [sbt(ph, 'H_hblk%d' % i, [128, 2048], F32) for i in range(2)]
                pblk = [sbt(ph, 'H_pblk%d' % i, [128, 2048], F32) for i in range(2)]
                junk = sbt(ph, 'H_junk', [128, 2048], F32)
                hn = [sbt(ph, 'H_hn%d' % i, [128, 2048], F32) for i in range(2)]
                gB = sbt(ph, 'H_gB', [128, 2048], F32)
                bB = sbt(ph, 'H_bB', [128, 2048], F32)
                st = [sbt(ph, 'H_st%d' % i, [128, 4], F32) for i in range(2)]
                DMA('sp', gB[:], I['ln2g'], [], ['gB'])
                DMA('sp', bB[:], I['ln2b'], [], ['bB'])
                for gblk in range(16):
                    f = gblk % 2
                    DMA('sp', hblk[f][:], S['h'][gblk], [], ['hblk%d' % f])
                    DMA('sp', pblk[f][:], S['pe'][gblk], [], ['pblk%d' % f])
                    STT('dve', pblk[f][:], hblk[f][:], ALPHA, pblk[f][:], ALU.mult, ALU.add, ['hblk%d' % f, 'pblk%d' % f], ['pblk%d' % f])
                    layer_norm_block(pblk[f][:], 'pblk%d' % f, gB, bB, junk, hn, st, gblk)
                    fin_evs.append(DMA('pool', out[gblk], hn[f][:], ['hn%d' % f], []))
                P.barrier()
                P.emit()

        if 'p0' in phases:
            phase_p0()
        if 'kv0' in phases:
            phase_kv(0)
        if 'kv1' in phases:
            phase_kv(1)
        if 'q' in phases:
            phase_q()
        if 'da' in phases:
            phase_da()
        if 'nsa' in phases:
            phase_nsa()
        if 'e1' in phases:
            phase_e1()
        fin_evs = []
        if dbg and dbg_src in ('aT', 'nT'):
            fin_evs.append(DMA('pool', dbg_out, (aT if dbg_src == 'aT' else nT)[:], ['aT', 'nT'], []))
            fin_evs.append(DMA('pool', dbg2_out, dbg2sb[:], ['dbg2sb'], []))
            P.barrier()
            P.emit()
        mid.close()
        if 'e2' in phases:
            phase_e2()
        if 'peer' in phases:
            phase_peer(fin_evs)
        if 'g2' in phases:
            phase_g2(fin_evs)
        for ev in fin_evs:
            pass
        P.ops['sp'].append((None, [ev for ev in fin_evs], None, 0))
        P.emit()
    return nc


def rel_bucket_np(dist):
    n = np.maximum(dist, 0)
    nf = np.maximum(n, 1).astype(np.float32)
    large = 16 + (np.log(nf / np.float32(16)) / np.float32(math.log(8.0)) * np.float32(16)).astype(np.int32)
    large = np.minimum(large, 31)
    return np.where(n < 16, n, large)


def prep_inputs(inputs):
    x = np.asarray(inputs['x'], np.float32)
    w_in = np.asarray(inputs['w_in'], np.float32)[0]
    rel = np.asarray(inputs['rel_bias'], np.float32)
    wr = np.ascontiguousarray(w_in.reshape(16, 128, 9776).transpose(1, 0, 2))
    common = {
        'w_dakv': np.ascontiguousarray(wr[:, :, 1024:3072]),
        'w_nkv': np.ascontiguousarray(wr[:, :, 4096:5632]),
        'w_q': np.ascontiguousarray(np.concatenate([wr[:, :, 0:1024], wr[:, :, 3072:4096]], axis=2)),
        'w_gate': np.ascontiguousarray(wr[:, :, 5632:5680]),
        'w_mg': np.ascontiguousarray(wr[:, :, 5680:9776]),
        'c_da': np.ascontiguousarray(np.broadcast_to(rel[31, 0:8][None, :], (128, 8))),
        'lamq': np.ascontiguousarray(np.broadcast_to(np.asarray(inputs['da_lam_q'], np.float32)[0].reshape(1, 128), (128, 128))),
        'lamk': np.ascontiguousarray(np.broadcast_to(np.asarray(inputs['da_lam_k'], np.float32)[0].reshape(1, 128), (128, 128))),
        'subg': np.ascontiguousarray(np.broadcast_to(np.asarray(inputs['da_subln_g'], np.float32)[0].reshape(1, 128), (128, 128))),
        'ident': np.eye(128, dtype=np.float32),
        'w_bda': np.ascontiguousarray(np.asarray(inputs['w_branch_da'], np.float32)[0].reshape(8, 128, 2048).transpose(1, 0, 2)),
        'w_bnsa': np.ascontiguousarray(np.asarray(inputs['w_branch_nsa'], np.float32)[0].reshape(8, 128, 2048).transpose(1, 0, 2)),
        'w_out': np.ascontiguousarray(np.asarray(inputs['w_out'], np.float32)[0].reshape(16, 128, 2048).transpose(1, 0, 2)),
        'ln1g': np.ascontiguousarray(np.broadcast_to(np.asarray(inputs['ln1_g'], np.float32)[0][None, :], (128, 2048))),
        'ln1b': np.ascontiguousarray(np.broadcast_to(np.asarray(inputs['ln1_b'], np.float32)[0][None, :], (128, 2048))),
        'ln2g': np.ascontiguousarray(np.broadcast_to(np.asarray(inputs['ln2_g'], np.float32)[0][None, :], (128, 2048))),
        'ln2b': np.ascontiguousarray(np.broadcast_to(np.asarray(inputs['ln2_b'], np.float32)[0][None, :], (128, 2048))),
        'wq': np.ascontiguousarray(np.asarray(inputs['peer_wq'], np.float32)[0].reshape(16, 128, 16, 64).transpose(2, 1, 0, 3)),
        'skT': np.ascontiguousarray(np.stack([np.asarray(inputs['peer_subkey1'], np.float32)[0].T,
                                              np.asarray(inputs['peer_subkey2'], np.float32)[0].T], axis=1)),
        'puT': np.ascontiguousarray(np.asarray(inputs['peer_u'], np.float32)[0].reshape(128, 128, 16, 128).transpose(0, 3, 2, 1)),
        'pv': np.ascontiguousarray(np.asarray(inputs['peer_v'], np.float32)[0].reshape(128, 128, 2048)),
        'c_nsa': np.ascontiguousarray(np.broadcast_to(rel[31, 8:24][None, :], (128, 16))),
        'w1k': np.ascontiguousarray(np.asarray(inputs['cmp_w1_k'], np.float32)[0].reshape(16, 2, 64, 256).transpose(1, 2, 0, 3).reshape(128, 16, 256)),
        'w1v': np.ascontiguousarray(np.asarray(inputs['cmp_w1_v'], np.float32)[0].reshape(16, 2, 64, 256).transpose(1, 2, 0, 3).reshape(128, 16, 256)),
        'w2k': np.ascontiguousarray(np.asarray(inputs['cmp_w2_k'], np.float32)[0].reshape(2, 128, 64).transpose(1, 0, 2)),
        'w2v': np.ascontiguousarray(np.asarray(inputs['cmp_w2_v'], np.float32)[0].reshape(2, 128, 64).transpose(1, 0, 2)),
        'pekT': np.ascontiguousarray(np.asarray(inputs['cmp_pe_k'], np.float32)[0].reshape(16, 2, 64).transpose(1, 2, 0).reshape(128, 16)),
        'pevT': np.ascontiguousarray(np.asarray(inputs['cmp_pe_v'], np.float32)[0].reshape(16, 2, 64).transpose(1, 2, 0).reshape(128, 16)),
    }
    import ml_dtypes
    cidx = np.arange(512)
    sidx = np.arange(128)
    ov = ((cidx[:, None] * 16 <= sidx[None, :] * 64 + 63) & (cidx[:, None] * 16 + 31 >= sidx[None, :] * 64)).astype(np.float32)
    ovl = np.concatenate([ov, np.ones((512, 1), np.float32)], axis=1)
    ovl[511] = 0.0
    common['ovl'] = np.ascontiguousarray(ovl.reshape(4, 128, 129).transpose(1, 0, 2))
    kk = np.arange(8192)
    common['onehot'] = (((kk[None, :] // 64) % 64) == np.arange(64)[:, None]).astype(ml_dtypes.bfloat16)
    xTs = []
    for b in range(2):
        xTs.append(np.ascontiguousarray(x[b].reshape(16, 512, 16, 128).transpose(0, 3, 2, 1)))
    in_maps = []
    kl = np.arange(128)[:, None]
    xx = np.arange(2944)[None, :]
    for c in range(8):
        b, j = c // 4, c % 4
        tiles = [4 * t + j for t in range(4)]
        m = dict(common)
        m['xT'] = xTs[b]
        m['xTo'] = np.ascontiguousarray(xTs[b][tiles])
        m['xo'] = np.ascontiguousarray(
            np.concatenate([x[b, 512 * T:512 * (T + 1)] for T in tiles], axis=0).reshape(16, 128, 2048))
        d = xx - kl + 512 * j - 1920
        bk = rel_bucket_np(d)
        rb = rel[bk]
        m['raw_da'] = np.ascontiguousarray(rb[:, 0:2560, 0:8].transpose(2, 0, 1))
        m['raw_nsa'] = np.ascontiguousarray(rb[:, :, 8:24].transpose(2, 0, 1))
        m['mneg'] = np.where(d < 0, np.float32(NEGM), np.float32(0.0)).astype(np.float32)
        m['wneg'] = np.where((d < 0) | (d >= 512), np.float32(NEGM), np.float32(0.0)).astype(np.float32)
        cl = np.arange(128)[:, None, None]
        dl = np.arange(2)[None, :, None] - 1
        ql = np.arange(512)[None, None, :]
        m['cm'] = np.where(16 * cl + 31 + 2048 * dl <= 512 * j + ql, np.float32(0.0), np.float32(NEGM)).astype(np.float32)
        qpos = (512 * np.array(tiles)[:, None, None] + 128 * np.arange(4)[None, :, None] + np.arange(128)[None, None, :]).reshape(16, 128)
        cur = qpos // 64
        sb_ = np.arange(128)[None, None, :]
        valid = sb_ <= cur[:, :, None]
        forced = valid & ((sb_ == 0) | (sb_ > cur[:, :, None] - 2))
        vmul = (valid & ~forced).astype(np.float32)
        vadd = np.where(forced, np.float32(1e4) + sb_.astype(np.float32), np.where(valid, np.float32(0.0), np.float32(-1e30))).astype(np.float32)
        m['vmul'] = np.ascontiguousarray(vmul.transpose(1, 0, 2))
        m['vadd'] = np.ascontiguousarray(vadd.transpose(1, 0, 2))
        in_maps.append(m)
    return in_maps


_NC_CACHE = {}


def kernel(**inputs):
    in_maps = prep_inputs(inputs)
    if 'nc' not in _NC_CACHE:
        _NC_CACHE['nc'] = build_program()
    nc = _NC_CACHE['nc']
    res = run_bass_kernel_spmd(nc, in_maps, core_ids=list(range(8)))
    outp = np.zeros((2, 8192, 2048), np.float32)
    for c in range(8):
        b, j = c // 4, c % 4
        o = np.asarray(res.results[c]['out']).reshape(4, 512, 2048)
        for t in range(4):
            T = 4 * t + j
            outp[b, 512 * T:512 * (T + 1)] = o[t]
    return outp
```

```python
import math
from contextlib import ExitStack

import numpy as np
import concourse.bass as bass
import concourse.mybir as mybir
from concourse.bass_utils import run_bass_kernel_spmd

F32 = mybir.dt.float32
BF16 = mybir.dt.bfloat16
AF = mybir.ActivationFunctionType
ALU = mybir.AluOpType
AX = mybir.AxisListType

ENGS = ['pe', 'act', 'dve', 'pool', 'sp']
EPOCH = 16000
RING = {'sp': 40, 'pool': 16}
NEGM = -30000.0
NPOOL = 5
POOL_STT = ()
ALPHA = 2.0 ** 0.25
LAM_INIT = 0.8 - 0.6 * math.exp(0.0)


class Prog:
    def __init__(self, nc, stack):
        self.nc = nc
        self.stack = stack
        self.ops = {e: [] for e in ENGS}
        self.cnt = {e: 0 for e in ENGS}
        self.esems = {e: [] for e in ENGS}
        self.rings = {}
        self.ring_pos = {}
        self.ring_use = {}
        for q, n in RING.items():
            self.rings[q] = [stack.enter_context(nc.semaphore('r%s%d' % (q, i))) for i in range(n)]
            self.ring_pos[q] = 0
            self.ring_use[q] = [0] * n
        self.seen = {e: {} for e in ENGS}
        self.lastw = {}
        self.readers = {}
        self.last_ev = {e: None for e in ENGS}

    def _esem(self, eng, epoch):
        while len(self.esems[eng]) <= epoch:
            self.esems[eng].append(self.stack.enter_context(
                self.nc.semaphore('e%s%d' % (eng, len(self.esems[eng])))))
        return self.esems[eng][epoch]

    def op(self, eng, fn, reads=(), writes=(), dma=False):
        deps = {}

        def add(ev):
            if ev is None:
                return
            s, v = ev
            if v > deps.get(id(s), (None, 0))[1]:
                deps[id(s)] = (s, v)
        for k in reads:
            add(self.lastw.get(k))
        for k in writes:
            add(self.lastw.get(k))
            for ev in self.readers.get(k, {}).values():
                add(ev)
        if eng == 'pe':
            for t in self.esems['pe']:
                deps.pop(id(t), None)
        if dma:
            q = eng
            pos = self.ring_pos[q]
            self.ring_pos[q] = (pos + 1) % len(self.rings[q])
            sem = self.rings[q][pos]
            if self.ring_use[q][pos] > 0:
                add((sem, 16 * self.ring_use[q][pos]))
            self.ring_use[q][pos] += 1
            ev = (sem, 16 * self.ring_use[q][pos])
            inc = 16
        else:
            i = self.cnt[eng]
            self.cnt[eng] += 1
            sem = self._esem(eng, i // EPOCH)
            ev = (sem, i % EPOCH + 1)
            inc = 1
            self.last_ev[eng] = ev
        waits = []
        seen = self.seen[eng]
        for s, v in deps.values():
            if seen.get(id(s), 0) < v:
                seen[id(s)] = v
                waits.append((s, v))
        self.ops[eng].append((fn, waits, sem, inc))
        for k in reads:
            self.readers.setdefault(k, {})[(eng, id(sem))] = ev
        for k in writes:
            self.lastw[k] = ev
            self.readers[k] = {}
        return ev

    def barrier(self):
        evs = [ev for ev in self.last_ev.values() if ev is not None]
        for q in self.rings:
            for i, s in enumerate(self.rings[q]):
                if self.ring_use[q][i] > 0:
                    evs.append((s, 16 * self.ring_use[q][i]))
        for eng in ENGS:
            waits = []
            seen = self.seen[eng]
            for s, v in evs:
                if seen.get(id(s), 0) < v:
                    seen[id(s)] = v
                    waits.append((s, v))
            self.ops[eng].append((None, waits, None, 0))
        self.lastw = {}
        self.readers = {}

    def emit(self):
        nc = self.nc
        ops = self.ops
        self.ops = {e: [] for e in ENGS}
        with nc.Block() as block:
            def run(engname):
                def body(e):
                    for fn, waits, sem, inc in ops[engname]:
                        for s, v in waits:
                            e.wait_ge(s, v)
                        if fn is not None:
                            fn(e).then_inc(sem, inc)
                return body
            block.tensor(run('pe'))
            block.scalar(run('act'))
            block.vector(run('dve'))
            block.gpsimd(run('pool'))
            block.sync(run('sp'))


class Ctx:
    pass


def build_program(phases=('p0i', 'kv0', 'kv1', 'q', 'da', 'nsa', 'e1', 'e2', 'peer', 'g2'), dbg=False, dbg_src='aT'):
    nc = bass.Bass("TRN2", target_bir_lowering=False)
    C = Ctx()
    C.nc = nc

    def din(name, shape, dt=F32):
        return nc.dram_tensor(name, list(shape), dt, kind="ExternalInput").ap()

    def dscr(name, shape, dt=BF16):
        return nc.dram_tensor(name, list(shape), dt, kind="Internal").ap()

    I = {}
    I['xT'] = din('xT', [16, 128, 16, 512])
    I['xTo'] = din('xTo', [4, 128, 16, 512])
    I['xo'] = din('xo', [16, 128, 2048])
    I['w_dakv'] = din('w_dakv', [128, 16, 2048])
    I['w_nkv'] = din('w_nkv', [128, 16, 1536])
    I['w_q'] = din('w_q', [128, 16, 2048])
    I['w_gate'] = din('w_gate', [128, 16, 48])
    I['w_mg'] = din('w_mg', [128, 16, 4096])
    I['raw_da'] = din('raw_da', [8, 128, 2560])
    I['mneg'] = din('mneg', [128, 2944])
    I['wneg'] = din('wneg', [128, 2944])
    I['raw_nsa'] = din('raw_nsa', [16, 128, 2944])
    I['c_nsa'] = din('c_nsa', [128, 16])
    I['w1k'] = din('w1k', [128, 16, 256])
    I['w1v'] = din('w1v', [128, 16, 256])
    I['w2k'] = din('w2k', [128, 2, 64])
    I['w2v'] = din('w2v', [128, 2, 64])
    I['pekT'] = din('pekT', [128, 16])
    I['pevT'] = din('pevT', [128, 16])
    I['ovl'] = din('ovl', [128, 4, 129])
    I['cm'] = din('cm', [128, 2, 512])
    I['vmul'] = din('vmul', [128, 16, 128])
    I['vadd'] = din('vadd', [128, 16, 128])
    I['onehot'] = din('onehot', [64, 8192], BF16)
    I['w_bda'] = din('w_bda', [128, 8, 2048])
    I['w_bnsa'] = din('w_bnsa', [128, 8, 2048])
    I['w_out'] = din('w_out', [128, 16, 2048])
    I['ln1g'] = din('ln1g', [128, 2048])
    I['ln1b'] = din('ln1b', [128, 2048])
    I['ln2g'] = din('ln2g', [128, 2048])
    I['ln2b'] = din('ln2b', [128, 2048])
    I['wq'] = din('wq', [16, 128, 16, 64])
    I['skT'] = din('skT', [64, 2, 128])
    I['puT'] = din('puT', [128, 128, 16, 128])
    I['pv'] = din('pv', [128, 128, 2048])
    I['c_da'] = din('c_da', [128, 8])
    I['lamq'] = din('lamq', [128, 128])
    I['lamk'] = din('lamk', [128, 128])
    I['subg'] = din('subg', [128, 128])
    I['ident'] = din('ident', [128, 128])
    out = nc.dram_tensor('out', [16, 128, 2048], F32, kind="ExternalOutput").ap()
    dbg_out = None
    if dbg:
        dbg_out = nc.dram_tensor('dbg', [128, 8, 2048], BF16, kind="ExternalOutput").ap()
        dbg2_out = nc.dram_tensor('dbg2', [128, 4, 258], F32, kind="ExternalOutput").ap()

    S = {}
    S['kT'] = dscr('s_kT', [8, 128, 8192])
    S['v'] = dscr('s_v', [8, 8192, 128])
    S['nkT'] = dscr('s_nkT', [4, 256, 8192])
    S['nv'] = dscr('s_nv', [2, 4, 8192, 64])
    S['qT'] = dscr('s_qT', [16, 128, 2048])
    S['h'] = dscr('s_h', [16, 128, 2048], F32)
    S['mixT'] = dscr('s_mixT', [4, 128, 16, 512])
    S['hT'] = dscr('s_hT', [4, 128, 16, 512])
    S['pe'] = dscr('s_pe', [16, 128, 2048], F32)
    S['puT'] = dscr('s_puT', [128, 128, 16, 128])
    S['pv'] = dscr('s_pv', [128, 128, 2048])

    with ExitStack() as top:
        P = Prog(nc, top)

        def sbt(st, name, shape, dt):
            return st.enter_context(nc.sbuf_tensor(name, list(shape), dt))

        ps = [top.enter_context(nc.psum_tensor('ps%d' % i, [128, 512], F32)) for i in range(8)]

        def MM(o, lhsT, rhs, start, stop, r, w):
            P.op('pe', lambda e: e.matmul(o, lhsT, rhs, start=start, stop=stop), r, w)

        def ACT(o, i, func, r, w, **kw):
            P.op('act', lambda e: e.activation(o, i, func, **kw), r, w)

        def CP(eng, o, i, r, w):
            if eng == 'act':
                P.op('act', lambda e: e.copy(o, i), r, w)
            else:
                P.op(eng, lambda e: e.tensor_copy(o, i), r, w)

        def TS(eng, o, i0, s1, s2, op0, op1, r, w, **kw):
            if op1 is None:
                P.op(eng, lambda e: e.tensor_scalar(o, i0, s1, None, op0, **kw), r, w)
            else:
                P.op(eng, lambda e: e.tensor_scalar(o, i0, s1, s2, op0, op1, **kw), r, w)

        def STT(eng, o, i0, sc, i1, op0, op1, r, w):
            P.op(eng, lambda e: e.scalar_tensor_tensor(o, i0, sc, i1, op0, op1), r, w)

        def TT(eng, o, i0, i1, op, r, w):
            P.op(eng, lambda e: e.tensor_tensor(o, i0, i1, op), r, w)

        def DMA(q, o, i, r, w):
            return P.op(q, lambda e: e.dma_start(out=o, in_=i), r, w, dma=True)

        def MEMSET(eng, o, val, w):
            P.op(eng, lambda e: e.memset(o, val), (), w)

        gates = sbt(top, 'gates', [128, 16, 48], F32)
        identb = sbt(top, 'identb', [128, 128], BF16)
        neglam = sbt(top, 'neglam', [128, 1], F32)
        gs = sbt(top, 'gs', [128, 128], F32)
        cda = sbt(top, 'cda', [128, 8], F32)
        dbg2sb = sbt(top, 'dbg2sb', [128, 4, 258], F32) if dbg else None
        mid = ExitStack()
        aT = sbt(mid, 'aT', [128, 8, 2048], BF16)
        nT = sbt(mid, 'nT', [128, 8, 2048], BF16)

        with ExitStack() as ph:
            idf = sbt(ph, 'idf', [128, 128], F32)
            lq = sbt(ph, 'lq', [128, 128], F32)
            lk = sbt(ph, 'lk', [128, 128], F32)
            lp = sbt(ph, 'lp', [128, 128], F32)
            l2 = sbt(ph, 'l2', [128, 2], F32)
            DMA('sp', idf[:], I['ident'], [], ['idf'])
            DMA('sp', lq[:], I['lamq'], [], ['lq'])
            DMA('sp', lk[:], I['lamk'], [], ['lk'])
            DMA('sp', gs[:], I['subg'], [], ['gs'])
            DMA('sp', cda[:], I['c_da'], [], ['cda'])
            CP('dve', identb[:], idf[:], ['idf'], ['identb'])
            TT('dve', lp[:], lq[:], lk[:], ALU.mult, ['lq', 'lk'], ['lp'])
            P.op('dve', lambda e: e.tensor_reduce(l2[:], lp[:].rearrange('p (a b) -> p a b', a=2), AX.X, ALU.add),
                 ['lp'], ['l2'])
            ACT(l2[:], l2[:], AF.Exp, ['l2'], ['l2'])
            STT('dve', neglam[:], l2[:, 0:1], -1.0, l2[:, 1:2], ALU.mult, ALU.add, ['l2'], ['neglam'])
            TS('dve', neglam[:], neglam[:], -LAM_INIT, None, ALU.add, None, ['neglam'], ['neglam'])
            TS('dve', gs[:], gs[:], 1.0 - LAM_INIT, None, ALU.mult, None, ['gs'], ['gs'])
            P.barrier()
            P.emit()

        psrot = [0]

        def nextps():
            i = psrot[0]
            psrot[0] = (i + 1) % 8
            return i

        def phase_kv(passno):
            ncw = 2048 if passno == 0 else 1536
            wsrc = I['w_dakv'] if passno == 0 else I['w_nkv']
            with ExitStack() as ph:
                wb = sbt(ph, 'A%d_wb' % passno, [128, 16, ncw], BF16)
                wst = [sbt(ph, 'A%d_wst%d' % (passno, i), [128, ncw], F32) for i in range(2)]
                xs = [sbt(ph, 'A%d_xs%d' % (passno, i), [128, 4, 512], F32) for i in range(2)]
                xb = [sbt(ph, 'A%d_xb%d' % (passno, i), [128, 16, 512], BF16) for i in range(2)]
                evs = [sbt(ph, 'A%d_ev%d' % (passno, i), [128, 512], BF16) for i in range(4)]
                evi = [0]
                for kc in range(16):
                    b = kc % 2
                    DMA('sp', wst[b][:], wsrc[:, kc, :], [], ['wst%d' % b])
                    CP('act' if kc % 2 == 0 else 'dve', wb[:, kc, :], wst[b][:], ['wst%d' % b], ['wb'])

                def evac_store(pi, npart, ncol, dst_fn):
                    k = evi[0]
                    evi[0] = (k + 1) % 4
                    CP('act', evs[k][0:npart, 0:ncol], ps[pi][0:npart, 0:ncol], ['ps%d' % pi], ['ev%d' % k])
                    dst_fn(evs[k], k)

                for tile in range(16):
                    xbk = 'xb%d' % (tile % 2)
                    xbt = xb[tile % 2]
                    for qq in range(4):
                        half = qq % 2
                        DMA('sp', xs[half][:], I['xT'][tile, :, qq * 4:(qq + 1) * 4, :], [], ['xs%d' % half])
                        CP('dve', xbt[:, qq * 4:(qq + 1) * 4, :], xs[half][:], ['xs%d' % half], [xbk])
                    tsl = slice(tile * 512, (tile + 1) * 512)
                    if passno == 0:
                        fm = [(c * 128, S['kT'][c, :, tsl]) for c in range(8)]
                    else:
                        fm = []
                        for kind, cb in enumerate((0, 256, 512, 1024)):
                            for c2 in range(2):
                                fm.append((cb + c2 * 128, S['nkT'][kind, c2 * 128:(c2 + 1) * 128, tsl]))
                    for col0, dst in fm:
                        pi = nextps()
                        for kc in range(16):
                            MM(ps[pi][:, :], wb[:, kc, col0:col0 + 128], xbt[:, kc, :], kc == 0, kc == 15,
                               ['wb', xbk], ['ps%d' % pi])
                        evac_store(pi, 128, 512,
                                   lambda ev, k, dst=dst: DMA('pool', dst, ev[:, :], ['ev%d' % k], []))
                    for blk in range(4):
                        t0 = tile * 512 + blk * 128
                        if passno == 0:
                            tm = [(1024 + g4 * 512, 512,
                                   S['v'][g4 * 4:(g4 + 1) * 4, t0:t0 + 128, :].rearrange('h t e -> t h e'), 4)
                                  for g4 in range(2)]
                        else:
                            tm = [(768, 256, S['nv'][0, :, t0:t0 + 128, :].rearrange('g t e -> t g e'), 4),
                                  (1280, 256, S['nv'][1, :, t0:t0 + 128, :].rearrange('g t e -> t g e'), 4)]
                        for col0, ncol, dst, nh in tm:
                            pi = nextps()
                            for kc in range(16):
                                MM(ps[pi][:, 0:ncol], xbt[:, kc, blk * 128:(blk + 1) * 128], wb[:, kc, col0:col0 + ncol],
                                   kc == 0, kc == 15, ['wb', xbk], ['ps%d' % pi])
                            evac_store(pi, 128, ncol,
                                       lambda ev, k, dst=dst, ncol=ncol, nh=nh: DMA(
                                           'pool', dst, ev[:, 0:ncol].rearrange('t (h e) -> t h e', h=nh),
                                           ['ev%d' % k], []))
                P.barrier()
                P.emit()

        def phase_q():
            with ExitStack() as ph:
                wb = sbt(ph, 'B_wb', [128, 16, 2048], BF16)
                wgb = sbt(ph, 'B_wgb', [128, 16, 48], BF16)
                wst = [sbt(ph, 'B_wst%d' % i, [128, 2048], F32) for i in range(1)]
                wgs = sbt(ph, 'B_wgs', [128, 16, 48], F32)
                xs = [sbt(ph, 'B_xs%d' % i, [128, 4, 512], F32) for i in range(2)]
                xb = [sbt(ph, 'B_xb%d' % i, [128, 16, 512], BF16) for i in range(2)]
                evs = [sbt(ph, 'B_ev%d' % i, [128, 512], BF16) for i in range(4)]
                evi = [0]
                for kc in range(16):
                    b = 0
                    DMA('sp', wst[b][:], I['w_q'][:, kc, :], [], ['wst%d' % b])
                    CP('act' if kc % 2 == 0 else 'dve', wb[:, kc, :], wst[b][:], ['wst%d' % b], ['wb'])
                DMA('sp', wgs[:], I['w_gate'], [], ['wgs'])
                CP('pool', wgb[:], wgs[:], ['wgs'], ['wgb'])
                for tile in range(4):
                    xbk = 'xb%d' % (tile % 2)
                    xbt = xb[tile % 2]
                    for qq in range(4):
                        half = qq % 2
                        DMA('sp', xs[half][:], I['xTo'][tile, :, qq * 4:(qq + 1) * 4, :], [], ['xs%d' % half])
                        CP('dve', xbt[:, qq * 4:(qq + 1) * 4, :], xs[half][:], ['xs%d' % half], [xbk])
                    tsl = slice(tile * 512, (tile + 1) * 512)
                    for c in range(16):
                        pi = nextps()
                        for kc in range(16):
                            MM(ps[pi][:, :], wb[:, kc, c * 128:(c + 1) * 128], xbt[:, kc, :], kc == 0, kc == 15,
                               ['wb', xbk], ['ps%d' % pi])
                        k = evi[0]
                        evi[0] = (k + 1) % 4
                        CP('act', evs[k][:, :], ps[pi][:, :], ['ps%d' % pi], ['ev%d' % k])
                        DMA('pool', S['qT'][c, :, tsl], evs[k][:, :], ['ev%d' % k], [])
                    for blk in range(4):
                        pi = nextps()
                        for kc in range(16):
                            MM(ps[pi][:, 0:48], xbt[:, kc, blk * 128:(blk + 1) * 128], wgb[:, kc, :], kc == 0, kc == 15,
                               ['wgb', xbk], ['ps%d' % pi])
                        ACT(gates[:, tile * 4 + blk, :], ps[pi][:, 0:48], AF.Sigmoid, ['ps%d' % pi], ['gates'])
                P.barrier()
                P.emit()

        def phase_da():
            with ExitStack() as ph:
                KtF = sbt(ph, 'C_KtF', [128, 8192], BF16)
                Vh = sbt(ph, 'C_Vh', [128, 64, 129], BF16)
                strip = sbt(ph, 'C_strip', [128, 2560], F32)
                mneg = sbt(ph, 'C_mneg', [128, 2560], F32)
                QT = [sbt(ph, 'C_QT%d' % m, [128, 2048], BF16) for m in range(2)]
                pT = [sbt(ph, 'C_pT%d' % b, [128, 512], BF16) for b in range(5)]
                tmp = [sbt(ph, 'C_tmp%d' % b, [128, 512], F32) for b in range(4)]
                fz = [sbt(ph, 'C_fz%d' % i, [128, 4], F32) for i in range(2)]
                o0 = sbt(ph, 'C_o0', [128, 4, 129], F32)
                fu = [sbt(ph, 'C_fu%d' % i, [128, 128], F32) for i in range(2)]
                fo = [sbt(ph, 'C_fo%d' % i, [128, 128], F32) for i in range(2)]
                fj = [sbt(ph, 'C_fj%d' % i, [128, 128], F32) for i in range(2)]
                fon = [sbt(ph, 'C_fon%d' % i, [128, 128], BF16) for i in range(2)]
                DMA('sp', mneg[:], I['mneg'][:, 0:2560], [], ['mneg'])
                MEMSET('pool', Vh[:, :, 128:129], 1.0, ['Vh'])
                MEMSET('pool', QT[0][:], 0.0, ['QT0'])
                MEMSET('pool', QT[1][:], 0.0, ['QT1'])
                p0q = []
                if 'p0i' in phases:
                    pus = [sbt(ph, 'C_pus%d' % i, [128, 2048], F32) for i in range(3)]
                    pub = [sbt(ph, 'C_pub%d' % i, [128, 2048], BF16) for i in range(3)]
                    p0q = p0_units(pus, pub, ['pool'])
                st_ = {'slot': 0, 'fin': 0}
                pipe = []

                def push(pv):
                    pipe.append(pv)
                    if len(pipe) > 3:
                        pipe.pop(0)()

                def flush():
                    while pipe:
                        pipe.pop(0)()

                def da_finalize(h, t):
                    for sub in range(4):
                        f = st_['fin'] % 2
                        st_['fin'] += 1
                        acc = ps[4 + sub]
                        ak = 'ps%d' % (4 + sub)
                        if dbg and h == 0 and t == 0:
                            CP('dve', dbg2sb[:, sub, 0:129], o0[:, sub, :], ['o0_%d' % sub], ['dbg2sb'])
                            CP('dve', dbg2sb[:, sub, 129:258], acc[:, 0:129], [ak], ['dbg2sb'])
                        P.op('dve', lambda e, f=f, sub=sub: e.reciprocal(fz[f][:, 0:1], o0[:, sub, 128:129]), ['o0_%d' % sub], ['fz%d' % f])
                        P.op('dve', lambda e, f=f, acc=acc: e.reciprocal(fz[f][:, 1:2], acc[:, 128:129]), [ak], ['fz%d' % f])
                        TT('dve', fz[f][:, 2:3], fz[f][:, 1:2], neglam[:], ALU.mult, ['fz%d' % f, 'neglam'], ['fz%d' % f])
                        TS('dve', fu[f][:], acc[:, 0:128], fz[f][:, 2:3], None, ALU.mult, None, [ak, 'fz%d' % f], ['fu%d' % f])
                        STT('dve', fo[f][:], o0[:, sub, 0:128], fz[f][:, 0:1], fu[f][:], ALU.mult, ALU.add,
                            ['o0_%d' % sub, 'fz%d' % f, 'fu%d' % f], ['fo%d' % f])
                        TT('pool', fj[f][:], fo[f][:], fo[f][:], ALU.mult, ['fo%d' % f], ['fj%d' % f])
                        P.op('dve', lambda e, f=f: e.tensor_reduce(fz[f][:, 3:4], fj[f][:], AX.X, ALU.add), ['fj%d' % f], ['fz3_%d' % f])
                        TS('dve', fz[f][:, 3:4], fz[f][:, 3:4], 1.0 / 128.0, 1e-5, ALU.mult, ALU.add, ['fz3_%d' % f], ['fz3_%d' % f])
                        ACT(fz[f][:, 3:4], fz[f][:, 3:4], AF.Sqrt, ['fz3_%d' % f], ['fz3_%d' % f])
                        P.op('dve', lambda e, f=f: e.reciprocal(fz[f][:, 3:4], fz[f][:, 3:4]), ['fz3_%d' % f], ['fz3_%d' % f])
                        STT('dve', fon[f][:], fo[f][:], fz[f][:, 3:4], gs[:], ALU.mult, ALU.mult,
                            ['fo%d' % f, 'fz3_%d' % f, 'gs'], ['fon%d' % f])
                        MM(ps[3][:, 0:128], fon[f][:], identb[:], True, True, ['fon%d' % f, 'identb'], ['ps3'])
                        blk = t * 4 + sub
                        CP('act', aT[:, h, blk * 128:(blk + 1) * 128], ps[3][:, 0:128], ['ps3'], ['aT'])

                for h in range(8):
                    flush()
                    DMA('sp', KtF[:], S['kT'][h], [], ['KtF'])
                    for m in range(2):
                        DMA('sp', QT[m][m * 64:(m + 1) * 64, :], S['qT'][h, m * 64:(m + 1) * 64, :], [], ['QT%d' % m])
                    for q4 in range(4):
                        DMA('sp', Vh[:, q4 * 16:(q4 + 1) * 16, 0:128],
                            S['v'][h, q4 * 2048:(q4 + 1) * 2048, :].rearrange('(s p) e -> p s e', p=128), [], ['Vh'])
                    DMA('sp', strip[:], I['raw_da'][h], [], ['strip'])
                    STT('dve', strip[:], strip[:], cda[:, h:h + 1], mneg[:], ALU.subtract, ALU.add,
                        ['strip', 'cda', 'mneg'], ['strip'])
                    for t in range(4):
                        nsl = 16 * (t + 1)
                        for m in range(2):
                            for s in range(nsl):
                                c = st_['slot']
                                st_['slot'] += 1
                                if p0q and c % 10 == 0:
                                    p0q.pop(0)()
                                b2 = c % 4
                                b3 = c % 5
                                near = s >= 16 * t - 1
                                pi = b2
                                MM(ps[pi][:, :], KtF[:, s * 128:(s + 1) * 128], QT[m][:, t * 512:(t + 1) * 512],
                                   True, True, ['KtF', 'QT%d' % m], ['ps%d' % pi])
                                if near:
                                    x0 = 128 * (15 - (s - 16 * t))
                                    STT('dve', tmp[b2][:], ps[pi][:, :], 0.125, strip[:, x0:x0 + 512], ALU.mult, ALU.add,
                                        ['ps%d' % pi, 'strip'], ['tmp%d' % b2])
                                    ACT(pT[b3][:], tmp[b2][:], AF.Exp, ['tmp%d' % b2], ['pT%d' % b3])
                                else:
                                    ACT(pT[b3][:], ps[pi][:, :], AF.Exp, ['ps%d' % pi], ['pT%d' % b3], scale=0.125)

                                def pv(h=h, t=t, m=m, s=s, b3=b3, nsl=nsl):
                                    for sub in range(4):
                                        MM(ps[4 + sub][:, 0:129], pT[b3][:, sub * 128:(sub + 1) * 128],
                                           Vh[:, s, :], s == 0, s == nsl - 1, ['pT%d' % b3, 'Vh'], ['ps%d' % (4 + sub)])
                                    if s == nsl - 1:
                                        if m == 0:
                                            for sub in range(4):
                                                CP('dve', o0[:, sub, :], ps[4 + sub][:, 0:129], ['ps%d' % (4 + sub)], ['o0_%d' % sub])
                                        else:
                                            da_finalize(h, t)
                                push(pv)
                flush()
                while p0q:
                    p0q.pop(0)()
                P.barrier()
                P.emit()


        def phase_nsa():
            with ExitStack() as ph:
                KCT = sbt(ph, 'D_KCT', [64, 4, 512], BF16)
                Rg = sbt(ph, 'D_Rg', [128, 4, 4, 193], BF16)
                cns = sbt(ph, 'D_cns', [128, 16], F32)
                MEMSET('pool', KCT[:], 0.0, ['KCT'])
                MEMSET('pool', Rg[:], 0.0, ['Rg'])
                DMA('sp', cns[:], I['c_nsa'], [], ['cns'])
                with ExitStack() as p0:
                    cT = sbt(p0, 'D0_cT', [128, 8192], BF16)
                    w1s = sbt(p0, 'D0_w1s', [128, 4, 256], F32)
                    w1b = sbt(p0, 'D0_w1b', [128, 16, 256], BF16)
                    w2s = sbt(p0, 'D0_w2s', [128, 2, 64], F32)
                    w2b = sbt(p0, 'D0_w2b', [128, 2, 64], BF16)
                    pes = sbt(p0, 'D0_pes', [128, 16], F32)
                    peb = sbt(p0, 'D0_peb', [128, 16], BF16)
                    b1 = sbt(p0, 'D0_b1', [128, 2], F32)
                    hT = sbt(p0, 'D0_hT', [128, 2, 512], BF16)
                    ovs = sbt(p0, 'D0_ovs', [128, 4, 129], F32)
                    DMA('sp', ovs[:], I['ovl'], [], ['ovs'])
                    for g in range(4):
                        CP('pool', Rg[:, :, g, 0:129], ovs[:], ['ovs'], ['Rg'])
                    for kind in range(2):
                        w1src = I['w1k'] if kind == 0 else I['w1v']
                        for p8 in range(4):
                            DMA('sp', w1s[:], w1src[:, p8 * 4:(p8 + 1) * 4, :], [], ['w1s'])
                            CP('pool', w1b[:, p8 * 4:(p8 + 1) * 4, :], w1s[:], ['w1s'], ['w1b'])
                        DMA('sp', w2s[:], I['w2k'] if kind == 0 else I['w2v'], [], ['w2s'])
                        CP('pool', w2b[:], w2s[:], ['w2s'], ['w2b'])
                        DMA('sp', pes[:], I['pekT'] if kind == 0 else I['pevT'], [], ['pes'])
                        CP('pool', peb[:], pes[:], ['pes'], ['peb'])
                        for hc in range(2):
                            pi = nextps()
                            for p in range(16):
                                MM(ps[pi][:, 0:1], w1b[:, p, hc * 128:(hc + 1) * 128], peb[:, p:p + 1], p == 0, p == 15,
                                   ['w1b', 'peb'], ['ps%d' % pi])
                            CP('dve', b1[:, hc:hc + 1], ps[pi][:, 0:1], ['ps%d' % pi], ['b1'])
                        for g in range(4):
                            DMA('sp', cT[0:64, :], S['nkT'][kind, g * 64:(g + 1) * 64, :], [], ['cT'])
                            DMA('sp', cT[64:128, 0:8191], S['nkT'][kind, g * 64:(g + 1) * 64, 1:8192], [], ['cT'])
                            for hc in range(2):
                                pi = nextps()
                                for p in range(16):
                                    MM(ps[pi][:, 0:511], w1b[:, p, hc * 128:(hc + 1) * 128], cT[:, 2 * p:2 * p + 16 * 510 + 1:16],
                                       p == 0, p == 15, ['w1b', 'cT'], ['ps%d' % pi])
                                ACT(hT[:, hc, 0:511], ps[pi][:, 0:511], AF.Gelu_apprx_tanh, ['ps%d' % pi, 'b1'], ['hT'],
                                    bias=b1[:, hc:hc + 1])
                            if kind == 0:
                                pi = nextps()
                                for hc in range(2):
                                    MM(ps[pi][0:64, 0:511], w2b[:, hc, :], hT[:, hc, 0:511], hc == 0, hc == 1,
                                       ['w2b', 'hT'], ['ps%d' % pi])
                                CP('act', KCT[:, g, 0:511], ps[pi][0:64, 0:511], ['ps%d' % pi], ['KCT'])
                            else:
                                for cc in range(4):
                                    ncl = 128 if cc < 3 else 127
                                    pi = nextps()
                                    for hc in range(2):
                                        MM(ps[pi][0:ncl, 0:64], hT[:, hc, cc * 128:cc * 128 + ncl], w2b[:, hc, :], hc == 0, hc == 1,
                                           ['w2b', 'hT'], ['ps%d' % pi])
                                    CP('act', Rg[0:ncl, cc, g, 129:193], ps[pi][0:ncl, 0:64], ['ps%d' % pi], ['Rg'])
                    P.barrier()
                    P.emit()
                cm = sbt(ph, 'D_cm', [128, 2, 512], F32)
                vmul = sbt(ph, 'D_vmul', [128, 16, 128], F32)
                vadd = sbt(ph, 'D_vadd', [128, 16, 128], F32)
                Kbuf = sbt(ph, 'D_Kbuf', [128, 8192], BF16)
                Vbuf = sbt(ph, 'D_Vbuf', [128, 64, 65], BF16)
                strip = sbt(ph, 'D_strip', [128, 2944], F32)
                neg = sbt(ph, 'D_neg', [128, 2944], F32)
                QTn = [sbt(ph, 'D_QTn%d' % i, [64, 2048], BF16) for i in range(4)]
                QA = [sbt(ph, 'D_QA%d' % i, [128, 512], BF16) for i in range(2)]
                QB = [sbt(ph, 'D_QB%d' % i, [128, 512], BF16) for i in range(2)]
                selT = [sbt(ph, 'D_selT%d' % i, [128, 512], BF16) for i in range(4)]
                onsa = sbt(ph, 'D_onsa', [128, 16, 4, 64], F32)
                impacc = sbt(ph, 'D_impacc', [128, 4, 128], F32)
                pT = [sbt(ph, 'D_pT%d' % b, [128, 512], BF16) for b in range(5)]
                tmp = [sbt(ph, 'D_tmp%d' % b, [128, 512], F32) for b in range(4)]
                fz = [sbt(ph, 'D_fz%d' % i, [128, 4], F32) for i in range(2)]
                sc = [sbt(ph, 'D_sc%d' % i, [128, 128], F32) for i in range(2)]
                sc2 = [sbt(ph, 'D_sc2%d' % i, [128, 128], F32) for i in range(2)]
                m8 = [sbt(ph, 'D_m8%d' % i, [128, 16], F32) for i in range(2)]
                sng = [sbt(ph, 'D_sng%d' % i, [128, 128], BF16) for i in range(2)]
                onb = [sbt(ph, 'D_onb%d' % i, [128, 128], BF16) for i in range(2)]
                accT_sb = [sbt(ph, 'D_accT%d' % i, [65, 512], F32) for i in range(1)]
                QZ = [sbt(ph, 'D_QZ%d' % i, [128, 512], BF16) for i in range(2)]
                MEMSET('pool', QZ[0][:], 0.0, ['QZ0'])
                MEMSET('pool', QZ[1][:], 0.0, ['QZ1'])
                identf = sbt(ph, 'D_identf', [128, 128], F32)
                DMA('sp', identf[:], I['ident'], [], ['identf'])
                DMA('sp', cm[:], I['cm'], [], ['cm'])
                DMA('sp', vmul[:], I['vmul'], [], ['vmul'])
                DMA('sp', vadd[:], I['vadd'], [], ['vadd'])
                DMA('sp', Kbuf[64:128, :], I['onehot'], [], ['KbufHi'])
                MEMSET('pool', Vbuf[:, :, 64:65], 1.0, ['Vbuf'])
                cnt = {'slot': 0, 'fin': 0, 'q': 0, 'tr': 0, 'grp': 0}

                pipe = []

                def push(pv):
                    pipe.append(pv)
                    if len(pipe) > 3:
                        pipe.pop(0)()

                def flush():
                    while pipe:
                        pipe.pop(0)()

                def attn_slot(lhsT, rhs, rkeys, bias_ap, vrhs, vkeys, ncolv, first, last, after=None, tbank=None):
                    c = cnt['slot']
                    cnt['slot'] = c + 1
                    b2, b3 = c % 4, c % 5
                    MM(ps[b2][:, :], lhsT, rhs, True, True, rkeys, ['ps%d' % b2])
                    if bias_ap is not None:
                        STT('dve', tmp[b2][:], ps[b2][:, :], 0.125, bias_ap, ALU.mult, ALU.add,
                            ['ps%d' % b2, 'strip', 'cm'], ['tmp%d' % b2])
                        ACT(pT[b3][:], tmp[b2][:], AF.Exp, ['tmp%d' % b2], ['pT%d' % b3])
                    else:
                        ACT(pT[b3][:], ps[b2][:, :], AF.Exp, ['ps%d' % b2], ['pT%d' % b3], scale=0.125)

                    def pv():
                        if tbank is None:
                            for sub in range(4):
                                MM(ps[4 + sub][:, 0:ncolv], pT[b3][:, sub * 128:(sub + 1) * 128], vrhs, first, last,
                                   ['pT%d' % b3] + vkeys, ['ps%d' % (4 + sub)])
                        else:
                            MM(ps[tbank][0:ncolv, :], vrhs, pT[b3][:, :], first, last, ['pT%d' % b3] + vkeys, ['ps%d' % tbank])
                        if after is not None:
                            after()
                    push(pv)

                def fin_branch(t, hh, n, gidx, dcol, ncol0, first_branch, tb=None):
                    if tb is not None:
                        k = cnt['tr'] % 2
                        cnt['tr'] += 1
                        CP('act', accT_sb[0][0:65, :], ps[tb][0:65, :], ['ps%d' % tb], ['accT0'])
                        for sub in range(4):
                            MM(ps[6 + k][:, sub * 65:(sub + 1) * 65], accT_sb[0][0:65, sub * 128:(sub + 1) * 128], identf[0:65, 0:65],
                               True, True, ['accT0', 'identf'], ['ps%d' % (6 + k)])
                    for sub in range(4):
                        f = cnt['fin'] % 2
                        cnt['fin'] += 1
                        blk = 4 * t + sub
                        if tb is None:
                            acc = ps[4 + sub]
                            ak = 'ps%d' % (4 + sub)
                            c0 = 0
                        else:
                            acc = ps[6 + k]
                            ak = 'ps%d' % (6 + k)
                            c0 = sub * 65
                        fk = 'fz%d' % f
                        TS('dve', fz[f][:, 0:1], acc[:, c0 + dcol:c0 + dcol + 1], 1e-30, None, ALU.max, None, [ak], [fk])
                        P.op('dve', lambda e, f=f: e.reciprocal(fz[f][:, 1:2], fz[f][:, 0:1]), [fk], [fk])
                        TT('dve', fz[f][:, 2:3], fz[f][:, 1:2], gates[:, blk, n * 3 + gidx:n * 3 + gidx + 1], ALU.mult,
                           [fk, 'gates'], [fk])
                        if first_branch:
                            if hh == 0:
                                TS('dve', impacc[:, sub, :], acc[:, 0:128], fz[f][:, 1:2], None, ALU.mult, None,
                                   [ak, fk], ['impacc%d' % sub])
                            else:
                                STT('dve', impacc[:, sub, :], acc[:, 0:128], fz[f][:, 1:2], impacc[:, sub, :], ALU.mult, ALU.add,
                                    [ak, fk, 'impacc%d' % sub], ['impacc%d' % sub])
                            TS('dve', onsa[:, blk, hh, :], acc[:, ncol0:ncol0 + 64], fz[f][:, 2:3], None, ALU.mult, None,
                               [ak, fk], ['onsa'])
                        else:
                            STT('dve', onsa[:, blk, hh, :], acc[:, c0 + ncol0:c0 + ncol0 + 64], fz[f][:, 2:3], onsa[:, blk, hh, :],
                                ALU.mult, ALU.add, [ak, fk, 'onsa'], ['onsa'])

                for g in range(4):
                    for hh in range(4):
                        n = 4 * g + hh
                        DMA('sp', QTn[hh][:], S['qT'][8 + n // 2, (n % 2) * 64:(n % 2) * 64 + 64, :], [], ['QTn%d' % hh])
                    def topk_code(g, t):
                        for sub in range(4):
                            f = cnt['fin'] % 2
                            cnt['fin'] += 1
                            blk = 4 * t + sub
                            TT('dve', sc[f][:], impacc[:, sub, :], vmul[:, blk, :], ALU.mult, ['impacc%d' % sub, 'vmul'], ['sc%d' % f])
                            TT('dve', sc[f][:], sc[f][:], vadd[:, blk, :], ALU.add, ['sc%d' % f, 'vadd'], ['sc%d' % f])
                            P.op('dve', lambda e, f=f: e.max(out=m8[f][:, 0:8], in_=sc[f][:]), ['sc%d' % f], ['m8_%d' % f])
                            P.op('dve', lambda e, f=f: e.match_replace(out=sc2[f][:], in_to_replace=m8[f][:, 0:8],
                                                                      in_values=sc[f][:], imm_value=-3.0e38),
                                 ['sc%d' % f, 'm8_%d' % f], ['sc2%d' % f])
                            P.op('dve', lambda e, f=f: e.max(out=m8[f][:, 8:16], in_=sc2[f][:]), ['sc2%d' % f], ['m8_%d' % f])
                            TS('dve', sng[f][:], sc[f][:], m8[f][:, 15:16], -240000.0, ALU.is_lt, ALU.mult,
                               ['sc%d' % f, 'm8_%d' % f], ['sng%d' % f])
                            MM(ps[3][:, 0:128], sng[f][:], identb[:], True, True, ['sng%d' % f, 'identb'], ['ps3'])
                            CP('act', selT[t][:, sub * 128:(sub + 1) * 128], ps[3][:, 0:128], ['ps3'], ['selT%d' % t])
                            if dbg and g == 0 and t == 0:
                                CP('dve', dbg2sb[:, sub, 0:128], sc[f][:], ['sc%d' % f], ['dbg2sb'])
                                CP('dve', dbg2sb[:, sub, 129:145], m8[f][:], ['m8_%d' % f], ['dbg2sb'])

                    for t in range(4):
                        for hh in range(4):
                            n = 4 * g + hh
                            for cc in range(t + 1):
                                bias_ap = cm[:, cc - t + 1, :] if cc >= t - 1 else None
                                aft = None
                                if cc == t:
                                    def aft(t=t, hh=hh, n=n, g=g):
                                        fin_branch(t, hh, n, 0, 128, 129, True)
                                        if hh == 3:
                                            topk_code(g, t)
                                attn_slot(KCT[:, g, cc * 128:(cc + 1) * 128], QTn[hh][:, t * 512:(t + 1) * 512],
                                          ['KCT', 'QTn%d' % hh], bias_ap, Rg[:, cc, g, :], ['Rg'], 193, cc == 0, cc == t, after=aft)
                    flush()
                    DMA('sp', Kbuf[0:64, :], S['nkT'][2, g * 64:(g + 1) * 64, :], [], ['KbufLo'])
                    for q4 in range(4):
                        DMA('sp', Vbuf[:, q4 * 16:(q4 + 1) * 16, 0:64],
                            S['nv'][0, g, q4 * 2048:(q4 + 1) * 2048, :].rearrange('(s p) e -> p s e', p=128), [], ['Vbuf'])
                    DMA('sp', neg[:], I['mneg'], [], ['neg'])
                    for hh in range(4):
                        n = 4 * g + hh
                        DMA('sp', strip[:], I['raw_nsa'][n], [], ['strip'])
                        STT('dve', strip[:], strip[:], cns[:, n:n + 1], neg[:], ALU.subtract, ALU.add,
                            ['strip', 'cns', 'neg'], ['strip'])
                        for t in range(4):
                            qb = cnt['q'] % 2
                            cnt['q'] += 1
                            CP('pool', QA[qb][0:64, :], QTn[hh][:, t * 512:(t + 1) * 512], ['QTn%d' % hh], ['QA%d' % qb])
                            CP('pool', QB[qb][0:64, :], QTn[hh][:, t * 512:(t + 1) * 512], ['QTn%d' % hh], ['QB%d' % qb])
                            DMA('sp', QA[qb][64:128, :], selT[t][0:64, :], ['selT%d' % t], ['QA%d' % qb])
                            CP('pool', QB[qb][64:128, :], selT[t][64:128, :], ['selT%d' % t], ['QB%d' % qb])
                            nsl = 16 * (t + 1)
                            tb = 4 + (cnt['grp'] % 2)
                            cnt['grp'] += 1
                            for s_ in range(nsl):
                                near = s_ >= 16 * t - 1
                                bias_ap = strip[:, 128 * (15 - (s_ - 16 * t)):128 * (15 - (s_ - 16 * t)) + 512] if near else None
                                qq = QA[qb] if s_ < 32 else QB[qb]
                                qk = ('QA%d' if s_ < 32 else 'QB%d') % qb
                                aft = None
                                if s_ == nsl - 1:
                                    def aft(t=t, hh=hh, n=n, tb=tb):
                                        fin_branch(t, hh, n, 1, 64, 0, False, tb=tb)
                                attn_slot(Kbuf[:, s_ * 128:(s_ + 1) * 128], qq[:, :], ['KbufLo', 'KbufHi', qk], bias_ap,
                                          Vbuf[:, s_, :], ['Vbuf'], 65, s_ == 0, s_ == nsl - 1, after=aft, tbank=tb)
                    flush()
                    DMA('sp', Kbuf[0:64, :], S['nkT'][3, g * 64:(g + 1) * 64, :], [], ['KbufLo'])
                    for q4 in range(4):
                        DMA('sp', Vbuf[:, q4 * 16:(q4 + 1) * 16, 0:64],
                            S['nv'][1, g, q4 * 2048:(q4 + 1) * 2048, :].rearrange('(s p) e -> p s e', p=128), [], ['Vbuf'])
                    DMA('sp', neg[:], I['wneg'], [], ['neg'])
                    for hh in range(4):
                        n = 4 * g + hh
                        DMA('sp', strip[:], I['raw_nsa'][n], [], ['strip'])
                        STT('dve', strip[:], strip[:], cns[:, n:n + 1], neg[:], ALU.subtract, ALU.add,
                            ['strip', 'cns', 'neg'], ['strip'])
                        for t in range(4):
                            s0 = max(0, 16 * t - 4)
                            s1 = 16 * t + 15
                            tb = 4 + (cnt['grp'] % 2)
                            cnt['grp'] += 1
                            qz = cnt['grp'] % 2
                            CP('pool', QZ[qz][0:64, :], QTn[hh][:, t * 512:(t + 1) * 512], ['QTn%d' % hh], ['QZ%d' % qz])
                            for s_ in range(s0, s1 + 1):
                                x0 = 128 * (15 - (s_ - 16 * t))
                                aft = None
                                if s_ == s1:
                                    def aft(t=t, hh=hh, n=n, tb=tb):
                                        fin_branch(t, hh, n, 2, 64, 0, False, tb=tb)
                                attn_slot(Kbuf[:, s_ * 128:(s_ + 1) * 128], QZ[qz][:, :],
                                          ['KbufLo', 'KbufHi', 'QZ%d' % qz], strip[:, x0:x0 + 512], Vbuf[:, s_, :], ['Vbuf'], 65,
                                          s_ == s0, s_ == s1, after=aft, tbank=tb)
                    flush()
                    for blk in range(16):
                        for pr in range(2):
                            f = cnt['fin'] % 2
                            cnt['fin'] += 1
                            CP('pool', onb[f][:].rearrange('p (h e) -> p h e', h=2), onsa[:, blk, 2 * pr:2 * pr + 2, :],
                               ['onsa'], ['onb%d' % f])
                            pi = 3
                            MM(ps[pi][:, 0:128], onb[f][:], identb[:], True, True, ['onb%d' % f, 'identb'], ['ps%d' % pi])
                            CP('act', nT[:, 2 * g + pr, blk * 128:(blk + 1) * 128], ps[pi][:, 0:128], ['ps%d' % pi], ['nT'])
                P.barrier()
                P.emit()


        def phase_e1():
            with ExitStack() as ph:
                xs = [sbt(ph, 'E_xs%d' % i, [128, 2, 512], F32) for i in range(2)]
                xb = sbt(ph, 'E_xb', [128, 16, 2048], BF16)
                wgs = [sbt(ph, 'E_wgs%d' % i, [128, 16, 128], F32) for i in range(2)]
                wbs = [sbt(ph, 'E_wbs%d' % i, [128, 8, 128], F32) for i in range(2)]
                wga = [sbt(ph, 'E_wga%d' % i, [128, 16, 128], BF16) for i in range(2)]
                wgn = [sbt(ph, 'E_wgn%d' % i, [128, 16, 128], BF16) for i in range(2)]
                wba = [sbt(ph, 'E_wba%d' % i, [128, 8, 128], BF16) for i in range(2)]
                wbn = [sbt(ph, 'E_wbn%d' % i, [128, 8, 128], BF16) for i in range(2)]
                sA = [sbt(ph, 'E_sA%d' % i, [128, 512], F32) for i in range(2)]
                sB = [sbt(ph, 'E_sB%d' % i, [128, 512], F32) for i in range(2)]
                m1 = [sbt(ph, 'E_m1%d' % i, [128, 512], F32) for i in range(2)]
                mx = [sbt(ph, 'E_mx%d' % i, [128, 512], BF16) for i in range(3)]
                k = 0
                for tile in range(4):
                    for qq in range(8):
                        half = k % 2
                        k += 1
                        DMA('sp', xs[half][:], I['xTo'][tile, :, qq * 2:(qq + 1) * 2, :], [], ['xs%d' % half])
                        CP('dve' if qq % 2 == 0 else 'act', xb[:, qq * 2:(qq + 1) * 2, tile * 512:(tile + 1) * 512], xs[half][:],
                           ['xs%d' % half], ['xb'])
                it = 0
                mi = 0
                for c in range(16):
                    b = c % 2
                    DMA('sp', wgs[0][:], I['w_mg'][:, :, c * 128:(c + 1) * 128], [], ['wgs0'])
                    CP('act', wga[b][:], wgs[0][:], ['wgs0'], ['wga%d' % b])
                    DMA('sp', wgs[1][:], I['w_mg'][:, :, 2048 + c * 128:2048 + (c + 1) * 128], [], ['wgs1'])
                    CP('dve', wgn[b][:], wgs[1][:], ['wgs1'], ['wgn%d' % b])
                    DMA('sp', wbs[0][:], I['w_bda'][:, :, c * 128:(c + 1) * 128], [], ['wbs0'])
                    CP('act', wba[b][:], wbs[0][:], ['wbs0'], ['wba%d' % b])
                    DMA('sp', wbs[1][:], I['w_bnsa'][:, :, c * 128:(c + 1) * 128], [], ['wbs1'])
                    CP('dve', wbn[b][:], wbs[1][:], ['wbs1'], ['wbn%d' % b])
                    for tile in range(4):
                        tsl = slice(tile * 512, (tile + 1) * 512)
                        p2 = it % 2
                        it += 1
                        pA, pB, pC, pD = (4 * p2 + 0), (4 * p2 + 1), (4 * p2 + 2), (4 * p2 + 3)
                        for kc in range(16):
                            MM(ps[pA][:, :], wga[b][:, kc, :], xb[:, kc, tsl], kc == 0, kc == 15, ['wga%d' % b, 'xb'], ['ps%d' % pA])
                        for kc in range(16):
                            MM(ps[pB][:, :], wgn[b][:, kc, :], xb[:, kc, tsl], kc == 0, kc == 15, ['wgn%d' % b, 'xb'], ['ps%d' % pB])
                        for kc in range(8):
                            MM(ps[pC][:, :], wba[b][:, kc, :], aT[:, kc, tsl], kc == 0, kc == 7, ['wba%d' % b, 'aT'], ['ps%d' % pC])
                        for kc in range(8):
                            MM(ps[pD][:, :], wbn[b][:, kc, :], nT[:, kc, tsl], kc == 0, kc == 7, ['wbn%d' % b, 'nT'], ['ps%d' % pD])
                        ACT(sA[p2][:], ps[pA][:, :], AF.Sigmoid, ['ps%d' % pA], ['sA%d' % p2])
                        ACT(sB[p2][:], ps[pB][:, :], AF.Sigmoid, ['ps%d' % pB], ['sB%d' % p2])
                        TT('dve', m1[p2][:], sA[p2][:], ps[pC][:, :], ALU.mult, ['sA%d' % p2, 'ps%d' % pC], ['m1%d' % p2])
                        TT('dve', sB[p2][:], sB[p2][:], ps[pD][:, :], ALU.mult, ['sB%d' % p2, 'ps%d' % pD], ['sB%d' % p2])
                        m3 = mi % 3
                        mi += 1
                        TT('pool', mx[m3][:], m1[p2][:], sB[p2][:], ALU.add, ['m1%d' % p2, 'sB%d' % p2], ['mx%d' % m3])
                        DMA('pool', S['mixT'][tile, :, c, :], mx[m3][:], ['mx%d' % m3], [])
                P.barrier()
                P.emit()

        def phase_e2():
            with ExitStack() as ph:
                wos = sbt(ph, 'F_wos', [128, 4, 512], F32)
                wob = [sbt(ph, 'F_wob%d' % i, [128, 16, 512], BF16) for i in range(2)]
                mxt = sbt(ph, 'F_mxt', [128, 16, 512], BF16)
                xc = [sbt(ph, 'F_xc%d' % i, [128, 512], F32) for i in range(3)]
                hpre2 = [sbt(ph, 'F_hpre%d' % i, [128, 4, 2048], F32) for i in range(2)]
                junk = sbt(ph, 'F_junk', [128, 2048], F32)
                hn = [sbt(ph, 'F_hn%d' % i, [128, 2048], F32) for i in range(2)]
                hb = sbt(ph, 'F_hb', [128, 2048], BF16)
                gB = sbt(ph, 'F_gB', [128, 2048], F32)
                bB = sbt(ph, 'F_bB', [128, 2048], F32)
                st = [sbt(ph, 'F_st%d' % i, [128, 4], F32) for i in range(2)]
                hTt = sbt(ph, 'F_hTt', [128, 16, 512], BF16)
                DMA('sp', gB[:], I['ln1g'], [], ['gB'])
                DMA('sp', bB[:], I['ln1b'], [], ['bB'])
                wi = 0
                xi = 0
                for tile in range(4):
                    hpre = hpre2[tile % 2]
                    hk = 'hpre%d_' % (tile % 2)
                    DMA('sp', mxt[:], S['mixT'][tile], [], ['mxt'])
                    for dc in range(4):
                        b = wi % 2
                        wi += 1
                        for k4 in range(4):
                            DMA('sp', wos[:], I['w_out'][:, k4 * 4:(k4 + 1) * 4, dc * 512:(dc + 1) * 512], [], ['wos'])
                            CP('act' if k4 % 2 == 0 else 'pool', wob[b][:, k4 * 4:(k4 + 1) * 4, :], wos[:], ['wos'], ['wob%d' % b])
                        for sub in range(4):
                            blk = tile * 4 + sub
                            pi = nextps()
                            for kc in range(16):
                                MM(ps[pi][:, :], mxt[:, kc, sub * 128:(sub + 1) * 128], wob[b][:, kc, :], kc == 0, kc == 15,
                                   ['mxt', 'wob%d' % b], ['ps%d' % pi])
                            x3 = xi % 3
                            xi += 1
                            DMA('sp', xc[x3][:], I['xo'][blk, :, dc * 512:(dc + 1) * 512], [], ['xc%d' % x3])
                            STT('dve', hpre[:, sub, dc * 512:(dc + 1) * 512], xc[x3][:], ALPHA, ps[pi][:, :], ALU.mult, ALU.add,
                                ['xc%d' % x3, 'ps%d' % pi], [hk + str(sub)])
                    for sub in range(4):
                        blk = tile * 4 + sub
                        layer_norm_block(hpre[:, sub, :], hk + str(sub), gB, bB, junk, hn, st, blk)
                        f = blk % len(hn)
                        DMA('pool', S['h'][blk], hn[f][:], ['hn%d' % f], [])
                        CP('act', hb[:], hn[f][:], ['hn%d' % f], ['hb'])
                        for f4 in range(4):
                            pi = nextps()
                            for ff in range(4):
                                fc = f4 * 4 + ff
                                MM(ps[pi][:, ff * 128:(ff + 1) * 128], hb[:, fc * 128:(fc + 1) * 128], identb[:], True, True,
                                   ['hb', 'identb'], ['ps%d' % pi])
                            CP('act', hTt[:, f4 * 4:(f4 + 1) * 4, sub * 128:(sub + 1) * 128],
                               ps[pi][:, :].rearrange('p (a b) -> p a b', a=4), ['ps%d' % pi], ['hTt'])
                    DMA('pool', S['hT'][tile], hTt[:], ['hTt'], [])
                P.barrier()
                P.emit()

        def layer_norm_block(src, srck, gB_, bB_, junk, hn, st, blk, junkk='junk'):
            f = blk % len(hn)
            sk = 'st%d' % f
            hk_ = 'hn%d' % f
            P.op('dve', lambda e: e.tensor_reduce(st[f][:, 0:1], src, AX.X, ALU.add), [srck], [sk])
            ACT(junk[:], src, AF.Square, [srck], [junkk])
            P.op('dve', lambda e: e.tensor_reduce(st[f][:, 1:2], junk[:], AX.X, ALU.add), [junkk], [sk])
            TS('dve', st[f][:, 0:1], st[f][:, 0:1], 1.0 / 2048.0, None, ALU.mult, None, [sk], [sk])
            TT('dve', st[f][:, 3:4], st[f][:, 0:1], st[f][:, 0:1], ALU.mult, [sk], [sk])
            STT('dve', st[f][:, 1:2], st[f][:, 1:2], 1.0 / 2048.0, st[f][:, 3:4], ALU.mult, ALU.subtract, [sk], [sk])
            TS('dve', st[f][:, 1:2], st[f][:, 1:2], 1e-5, None, ALU.add, None, [sk], [sk])
            ACT(st[f][:, 1:2], st[f][:, 1:2], AF.Sqrt, [sk], [sk])
            P.op('dve', lambda e: e.reciprocal(st[f][:, 2:3], st[f][:, 1:2]), [sk], [sk])
            STT('dve', st[f][:, 3:4], st[f][:, 0:1], -1.0, st[f][:, 2:3], ALU.mult, ALU.mult, [sk], [sk])
            ACT(hn[f][:], src, AF.Identity, [srck, sk], [hk_], scale=st[f][:, 2:3], bias=st[f][:, 3:4])
            TT('dve', hn[f][:], hn[f][:], gB_[:], ALU.mult, [hk_, 'gB'], [hk_])
            TT('dve', hn[f][:], hn[f][:], bB_[:], ALU.add, [hk_, 'bB'], [hk_])

        def p0_units(us, ub, engs):
            units = []
            for ec in range(128):
                for which in range(2):
                    def unit(ec=ec, which=which, k=len(units)):
                        b = k % len(us)
                        src = I['puT'][ec].rearrange('p a b -> p (a b)') if which == 0 else I['pv'][ec]
                        dst = S['puT'][ec].rearrange('p a b -> p (a b)') if which == 0 else S['pv'][ec]
                        DMA('sp', us[b][:], src, [], ['us%d' % b])
                        CP(engs[k % len(engs)], ub[b][:], us[b][:], ['us%d' % b], ['ub%d' % b])
                        DMA('pool', dst, ub[b][:], ['ub%d' % b], [])
                    units.append(unit)
            return units

        def phase_p0():
            with ExitStack() as ph:
                us = [sbt(ph, 'P_us%d' % i, [128, 2048], F32) for i in range(3)]
                ub = [sbt(ph, 'P_ub%d' % i, [128, 2048], BF16) for i in range(3)]
                for u in p0_units(us, ub, ['dve', 'act', 'pool']):
                    u()
                P.barrier()
                P.emit()

        def phase_peer(fin_evs):
            with ExitStack() as ph:
                hTt = sbt(ph, 'G_hTt', [128, 16, 512], BF16)
                wqs = [sbt(ph, 'G_wqs%d' % i, [128, 16, 64], F32) for i in range(2)]
                wqb = [sbt(ph, 'G_wqb%d' % i, [128, 16, 64], BF16) for i in range(2)]
                sks = sbt(ph, 'G_sks', [64, 2, 128], F32)
                skb = sbt(ph, 'G_skb', [64, 2, 128], BF16)
                qTu = [sbt(ph, 'G_qTu%d' % i, [64, 512], BF16) for i in range(2)]
                sAll = sbt(ph, 'G_sAll', [128, 4, 16, 128], F32)
                tau = sbt(ph, 'G_tau', [128, 4, 8], F32)
                negc = sbt(ph, 'G_negc', [128, 4, 8], F32)
                kap = sbt(ph, 'G_kap', [128, 4, 8], F32)
                m1 = [sbt(ph, 'G_m1%d' % i, [128, 16], F32) for i in range(4)]
                m2 = [sbt(ph, 'G_m2%d' % i, [128, 16], F32) for i in range(4)]
                mc = [sbt(ph, 'G_mc%d' % i, [128, 16], F32) for i in range(4)]
                t1 = [sbt(ph, 'G_t1%d' % i, [128, 256], F32) for i in range(4)]
                cand = [sbt(ph, 'G_cand%d' % i, [128, 256], F32) for i in range(4)]
                sm = [sbt(ph, 'G_sm%d' % i, [128, 4], F32) for i in range(4)]
                e16 = [sbt(ph, 'G_e16%d' % i, [128, 16], F32) for i in range(4)]
                eb = [sbt(ph, 'G_e%d' % i, [128, 4, 128], F32) for i in range(5)]
                Wall = [sbt(ph, 'G_W%d' % i, [128, 8, 4, 128], BF16) for i in range(2)]
                GT = [sbt(ph, 'G_GT%d' % i, [128, 4, 512], BF16) for i in range(3)]
                Gs = [sbt(ph, 'G_Gs%d' % i, [128, 512], BF16) for i in range(2)]
                uch = [sbt(ph, 'G_uch%d' % i, [128, 16, 128], BF16) for i in range(2)]
                vch = [sbt(ph, 'G_vch%d' % i, [128, 4, 2048], BF16) for i in range(2)]
                ga = [sbt(ph, 'G_ga%d' % i, [128, 512], F32) for i in range(4)]
                GA = [sbt(ph, 'G_GA%d' % i, [128, 4, 512], BF16) for i in range(2)]
                acc = sbt(ph, 'G_acc', [128, 4, 2048], F32)
                DMA('sp', sks[:], I['skT'], [], ['sks'])
                CP('dve', skb[:], sks[:], ['sks'], ['skb'])
                c_ = {'u': 0, 'k': 0, 'w': 0, 's': 0, 'v': 0, 'g': 0}
                for tile in range(4):
                    DMA('sp', hTt[:], S['hT'][tile], [], ['hTt'])
                    for u in range(16):
                        b = c_['u'] % 2
                        c_['u'] += 1
                        DMA('sp', wqs[b][:], I['wq'][u], [], ['wqs%d' % b])
                        CP('act', wqb[b][:], wqs[b][:], ['wqs%d' % b], ['wqb%d' % b])
                        pi = nextps()
                        for kc in range(16):
                            MM(ps[pi][0:64, :], wqb[b][:, kc, :], hTt[:, kc, :], kc == 0, kc == 15, ['wqb%d' % b, 'hTt'], ['ps%d' % pi])
                        CP('act', qTu[b][:], ps[pi][0:64, :], ['ps%d' % pi], ['qTu%d' % b])
                        pi = nextps()
                        for blk in range(4):
                            MM(ps[pi][:, blk * 128:(blk + 1) * 128], qTu[b][:, blk * 128:(blk + 1) * 128], skb[:, u % 2, :], True, True,
                               ['qTu%d' % b, 'skb'], ['ps%d' % pi])
                        CP('dve', sAll[:, :, u, :], ps[pi][:, :].rearrange('p (a b) -> p a b', a=4), ['ps%d' % pi], ['sAll'])
                    def chain(blk, h, f):
                        steps = []
                        s1 = sAll[:, blk, 2 * h, :]
                        s2 = sAll[:, blk, 2 * h + 1, :]
                        for (sx, mm_, mk, tk_, tt_) in ((s1, m1[f], 'm1_%d' % f, 't1a_%d' % f, t1[f][:, 0:128]),
                                                        (s2, m2[f], 'm2_%d' % f, 't1b_%d' % f, t1[f][:, 128:256])):
                            steps.append(lambda sx=sx, mm_=mm_, mk=mk: P.op('dve', lambda e: e.max(out=mm_[:, 0:8], in_=sx), ['sAll'], [mk]))
                            steps.append(lambda sx=sx, mm_=mm_, mk=mk, tk_=tk_, tt_=tt_: P.op(
                                'dve', lambda e: e.match_replace(out=tt_, in_to_replace=mm_[:, 0:8], in_values=sx, imm_value=-3.0e38),
                                ['sAll', mk], [tk_]))
                            steps.append(lambda mm_=mm_, mk=mk, tk_=tk_, tt_=tt_: P.op(
                                'dve', lambda e: e.max(out=mm_[:, 8:16], in_=tt_), [tk_], [mk]))
                        steps.append(lambda: TT('pool', cand[f][:].rearrange('p (a b) -> p a b', a=16),
                                                m1[f][:].unsqueeze(2).broadcast_to([128, 16, 16]), m2[f][:].unsqueeze(1).broadcast_to([128, 16, 16]),
                                                ALU.add, ['m1_%d' % f, 'm2_%d' % f], ['cand%d' % f]))
                        steps.append(lambda: P.op('dve', lambda e: e.max(out=mc[f][:, 0:8], in_=cand[f][:]), ['cand%d' % f], ['mc%d' % f]))
                        steps.append(lambda: P.op('dve', lambda e: e.match_replace(out=t1[f][:], in_to_replace=mc[f][:, 0:8], in_values=cand[f][:],
                                                                                  imm_value=-3.0e38),
                                                  ['cand%d' % f, 'mc%d' % f, 't1a_%d' % f, 't1b_%d' % f], ['t1a_%d' % f, 't1b_%d' % f]))
                        steps.append(lambda: P.op('dve', lambda e: e.max(out=mc[f][:, 8:16], in_=t1[f][:]), ['t1a_%d' % f, 't1b_%d' % f], ['mc%d' % f]))
                        steps.append(lambda: CP('dve', tau[:, blk, h:h + 1], mc[f][:, 15:16], ['mc%d' % f], ['tau']))
                        steps.append(lambda: TS('dve', sm[f][:, 0:1], mc[f][:, 0:1], -1.0, None, ALU.mult, None, ['mc%d' % f], ['sm%d' % f]))
                        steps.append(lambda: ACT(e16[f][:], mc[f][:], AF.Exp, ['mc%d' % f, 'sm%d' % f], ['e16_%d' % f], bias=sm[f][:, 0:1]))
                        steps.append(lambda: P.op('dve', lambda e: e.tensor_reduce(sm[f][:, 1:2], e16[f][:], AX.X, ALU.add), ['e16_%d' % f], ['sm%d' % f]))
                        steps.append(lambda: ACT(sm[f][:, 2:3], sm[f][:, 1:2], AF.Ln, ['sm%d' % f], ['sm%d' % f]))
                        steps.append(lambda: TT('dve', negc[:, blk, h:h + 1], sm[f][:, 0:1], sm[f][:, 2:3], ALU.subtract, ['sm%d' % f], ['negc%d' % f]))
                        steps.append(lambda: TT('dve', sm[f][:, 3:4], mc[f][:, 15:16], negc[:, blk, h:h + 1], ALU.add,
                                                ['mc%d' % f, 'negc%d' % f], ['sm%d' % f]))
                        steps.append(lambda: ACT(sm[f][:, 3:4], sm[f][:, 3:4], AF.Exp, ['sm%d' % f], ['sm%d' % f]))
                        steps.append(lambda: TS('dve', kap[:, blk, h:h + 1], sm[f][:, 3:4], 0.9999, None, ALU.mult, None, ['sm%d' % f], ['kap']))
                        steps.append(lambda: TS('dve', sAll[:, blk, 2 * h, :], sAll[:, blk, 2 * h, :], negc[:, blk, h:h + 1], None, ALU.add, None,
                                                ['negc%d' % f, 'm1_%d' % f, 't1a_%d' % f], ['sAllw%d' % f]))
                        return steps
                    pairs = [(blk, h) for blk in range(4) for h in range(8)]
                    for g4 in range(0, 32, 4):
                        chains = [chain(blk, h, f) for f, (blk, h) in enumerate(pairs[g4:g4 + 4])]
                        for i in range(len(chains[0])):
                            for ch in chains:
                                ch[i]()
                    P.op('dve', lambda e: e.tensor_copy(sm[0][:, 0:1], sm[0][:, 0:1]),
                         ['sAllw0', 'sAllw1', 'sAllw2', 'sAllw3', 'negc0', 'negc1', 'negc2', 'negc3', 'sm0'], ['sAll', 'negc', 'sm0'])
                    def opsA(eg, blk, h):
                        wb_ = (4 * eg + blk) % 2
                        sb_ = c_['s'] % 5
                        c_['s'] += 1
                        if h < NPOOL:
                            TT('pool', eb[sb_][:],
                               sAll[:, blk, 2 * h, 4 * eg:4 * eg + 4].unsqueeze(2).broadcast_to([128, 4, 128]),
                               sAll[:, blk, 2 * h + 1, :].unsqueeze(1).broadcast_to([128, 4, 128]),
                               ALU.add, ['sAll'], ['e%d' % sb_])
                            ACT(eb[sb_][:], eb[sb_][:], AF.Exp, ['e%d' % sb_], ['e%d' % sb_])
                        else:
                            for c in range(4):
                                ACT(eb[sb_][:, c, :], sAll[:, blk, 2 * h + 1, :], AF.Exp, ['sAll'], ['e%d' % sb_],
                                    bias=sAll[:, blk, 2 * h, 4 * eg + c:4 * eg + c + 1])
                        STT('dve', Wall[wb_][:, h, :, :], eb[sb_][:], kap[:, blk, h:h + 1], eb[sb_][:], ALU.is_ge, ALU.mult,
                            ['kap', 'e%d' % sb_], ['W%d' % wb_])

                    def stageB(eg, blk):
                        k = 4 * eg + blk
                        wb_ = k % 2
                        pb = k % 2
                        for c in range(4):
                            for h in range(8):
                                MM(ps[pb][:, c * 128:(c + 1) * 128], Wall[wb_][:, h, c, :], identb[:], h == 0, h == 7,
                                   ['W%d' % wb_, 'identb'], ['ps%d' % pb])
                        def evac(eg=eg, blk=blk, pb=pb):
                            CP('act', GT[eg % 3][:, :, blk * 128:(blk + 1) * 128], ps[pb][:, :].rearrange('p (a b) -> p a b', a=4),
                               ['ps%d' % pb], ['GT%d' % (eg % 3)])
                        pend.append(evac)

                    def stageC_pe(eg, c):
                        ec = 4 * eg + c
                        u2 = c % 2
                        DMA('sp', uch[u2][:], S['puT'][ec], [], ['uch%d' % u2])
                        pa = 2 + c
                        for kc in range(16):
                            MM(ps[pa][:, :], uch[u2][:, kc, :], hTt[:, kc, :], kc == 0, kc == 15, ['uch%d' % u2, 'hTt'], ['ps%d' % pa])

                    def stageC_post(eg):
                        gb = eg % 2
                        for c in range(4):
                            ACT(ga[c][:], ps[2 + c][:, :], AF.Gelu_apprx_tanh, ['ps%d' % (2 + c)], ['ga%d' % c])
                        for c in range(4):
                            TT('pool', GA[gb][:, c, :], ga[c][:], GT[eg % 3][:, c, :], ALU.mult, ['ga%d' % c, 'GT%d' % (eg % 3)], ['GA%d' % gb])

                    def stageD1(eg, blk, dc):
                        gb = eg % 2
                        vb = eg % 2
                        pv_ = 6 + (c_['v'] % 2)
                        c_['v'] += 1
                        for c in range(4):
                            MM(ps[pv_][:, :], GA[gb][:, c, blk * 128:(blk + 1) * 128], vch[vb][:, c, dc * 512:(dc + 1) * 512],
                               c == 0, c == 3, ['GA%d' % gb, 'vch%d' % vb], ['ps%d' % pv_])
                        if eg == 0:
                            CP('dve', acc[:, blk, dc * 512:(dc + 1) * 512], ps[pv_][:, :], ['ps%d' % pv_], ['acc%d' % blk])
                        else:
                            TT('dve', acc[:, blk, dc * 512:(dc + 1) * 512], acc[:, blk, dc * 512:(dc + 1) * 512], ps[pv_][:, :],
                               ALU.add, ['ps%d' % pv_, 'acc%d' % blk], ['acc%d' % blk])

                    pend = []
                    for it in range(35):
                        doA = it < 32
                        if 2 <= it <= 33:
                            eg_ = it - 2
                            DMA('sp', vch[eg_ % 2][:], S['pv'][4 * eg_:4 * eg_ + 4].rearrange('c p d -> p c d'), [], ['vch%d' % (eg_ % 2)])
                        for blk in range(4):
                            for h in range(8):
                                if doA:
                                    opsA(it, blk, h)
                                if h == 3 or not doA:
                                    while pend:
                                        pend.pop(0)()
                                if blk == 0 and h == 3 and 2 <= it <= 33:
                                    stageC_post(it - 2)
                                if h % 2 == 1 and 3 <= it <= 34:
                                    stageD1(it - 3, blk, h // 2)
                            if 1 <= it <= 32:
                                stageC_pe(it - 1, blk)
                            if doA:
                                stageB(it, blk)
                    DMA('pool', S['pe'][tile * 4:(tile + 1) * 4].rearrange('b p d -> p b d'), acc[:], ['acc0', 'acc1', 'acc2', 'acc3'], [])
                P.barrier()
                P.emit()


        def phase_g2(fin_evs):
            with ExitStack() as ph:
                hblk = [sbt(ph, 'H_hblk%d' % i, [128, 2048], F32) for i in range(2)]
                pblk = [sbt(ph, 'H_pblk%d' % i, [128, 2048], F32) for i in range(2)]
                junk = sbt(ph, 'H_junk', [128, 2048], F32)
                hn = [sbt(ph, 'H_hn%d' % i, [128, 2048], F32) for i in range(2)]
                gB = sbt(ph, 'H_gB', [128, 2048], F32)
                bB = sbt(ph, 'H_bB', [128, 2048], F32)
                st = [sbt(ph, 'H_st%d' % i, [128, 4], F32) for i in range(2)]
                DMA('sp', gB[:], I['ln2g'], [], ['gB'])
                DMA('sp', bB[:], I['ln2b'], [], ['bB'])
                for gblk in range(16):
                    f = gblk % 2
                    DMA('sp', hblk[f][:], S['h'][gblk], [], ['hblk%d' % f])
                    DMA('sp', pblk[f][:], S['pe'][gblk], [], ['pblk%d' % f])
                    STT('dve', pblk[f][:], hblk[f][:], ALPHA, pblk[f][:], ALU.mult, ALU.add, ['hblk%d' % f, 'pblk%d' % f], ['pblk%d' % f])
                    layer_norm_block(pblk[f][:], 'pblk%d' % f, gB, bB, junk, hn, st, gblk)
                    fin_evs.append(DMA('pool', out[gblk], hn[f][:], ['hn%d' % f], []))
                P.barrier()
                P.emit()

        if 'p0' in phases:
            phase_p0()
        if 'kv0' in phases:
            phase_kv(0)
        if 'kv1' in phases:
            phase_kv(1)
        if 'q' in phases:
            phase_q()
        if 'da' in phases:
            phase_da()
        if 'nsa' in phases:
            phase_nsa()
        if 'e1' in phases:
            phase_e1()
        fin_evs = []
        if dbg and dbg_src in ('aT', 'nT'):
            fin_evs.append(DMA('pool', dbg_out, (aT if dbg_src == 'aT' else nT)[:], ['aT', 'nT'], []))
            fin_evs.append(DMA('pool', dbg2_out, dbg2sb[:], ['dbg2sb'], []))
            P.barrier()
            P.emit()
        mid.close()
        if 'e2' in phases:
            phase_e2()
        if 'peer' in phases:
            phase_peer(fin_evs)
        if 'g2' in phases:
            phase_g2(fin_evs)
        for ev in fin_evs:
            pass
        P.ops['sp'].append((None, [ev for ev in fin_evs], None, 0))
        P.emit()
    return nc


def rel_bucket_np(dist):
    n = np.maximum(dist, 0)
    nf = np.maximum(n, 1).astype(np.float32)
    large = 16 + (np.log(nf / np.float32(16)) / np.float32(math.log(8.0)) * np.float32(16)).astype(np.int32)
    large = np.minimum(large, 31)
    return np.where(n < 16, n, large)


def prep_inputs(inputs):
    x = np.asarray(inputs['x'], np.float32)
    w_in = np.asarray(inputs['w_in'], np.float32)[0]
    rel = np.asarray(inputs['rel_bias'], np.float32)
    wr = np.ascontiguousarray(w_in.reshape(16, 128, 9776).transpose(1, 0, 2))
    common = {
        'w_dakv': np.ascontiguousarray(wr[:, :, 1024:3072]),
        'w_nkv': np.ascontiguousarray(wr[:, :, 4096:5632]),
        'w_q': np.ascontiguousarray(np.concatenate([wr[:, :, 0:1024], wr[:, :, 3072:4096]], axis=2)),
        'w_gate': np.ascontiguousarray(wr[:, :, 5632:5680]),
        'w_mg': np.ascontiguousarray(wr[:, :, 5680:9776]),
        'c_da': np.ascontiguousarray(np.broadcast_to(rel[31, 0:8][None, :], (128, 8))),
        'lamq': np.ascontiguousarray(np.broadcast_to(np.asarray(inputs['da_lam_q'], np.float32)[0].reshape(1, 128), (128, 128))),
        'lamk': np.ascontiguousarray(np.broadcast_to(np.asarray(inputs['da_lam_k'], np.float32)[0].reshape(1, 128), (128, 128))),
        'subg': np.ascontiguousarray(np.broadcast_to(np.asarray(inputs['da_subln_g'], np.float32)[0].reshape(1, 128), (128, 128))),
        'ident': np.eye(128, dtype=np.float32),
        'w_bda': np.ascontiguousarray(np.asarray(inputs['w_branch_da'], np.float32)[0].reshape(8, 128, 2048).transpose(1, 0, 2)),
        'w_bnsa': np.ascontiguousarray(np.asarray(inputs['w_branch_nsa'], np.float32)[0].reshape(8, 128, 2048).transpose(1, 0, 2)),
        'w_out': np.ascontiguousarray(np.asarray(inputs['w_out'], np.float32)[0].reshape(16, 128, 2048).transpose(1, 0, 2)),
        'ln1g': np.ascontiguousarray(np.broadcast_to(np.asarray(inputs['ln1_g'], np.float32)[0][None, :], (128, 2048))),
        'ln1b': np.ascontiguousarray(np.broadcast_to(np.asarray(inputs['ln1_b'], np.float32)[0][None, :], (128, 2048))),
        'ln2g': np.ascontiguousarray(np.broadcast_to(np.asarray(inputs['ln2_g'], np.float32)[0][None, :], (128, 2048))),
        'ln2b': np.ascontiguousarray(np.broadcast_to(np.asarray(inputs['ln2_b'], np.float32)[0][None, :], (128, 2048))),
        'wq': np.ascontiguousarray(np.asarray(inputs['peer_wq'], np.float32)[0].reshape(16, 128, 16, 64).transpose(2, 1, 0, 3)),
        'skT': np.ascontiguousarray(np.stack([np.asarray(inputs['peer_subkey1'], np.float32)[0].T,
                                              np.asarray(inputs['peer_subkey2'], np.float32)[0].T], axis=1)),
        'puT': np.ascontiguousarray(np.asarray(inputs['peer_u'], np.float32)[0].reshape(128, 128, 16, 128).transpose(0, 3, 2, 1)),
        'pv': np.ascontiguousarray(np.asarray(inputs['peer_v'], np.float32)[0].reshape(128, 128, 2048)),
        'c_nsa': np.ascontiguousarray(np.broadcast_to(rel[31, 8:24][None, :], (128, 16))),
        'w1k': np.ascontiguousarray(np.asarray(inputs['cmp_w1_k'], np.float32)[0].reshape(16, 2, 64, 256).transpose(1, 2, 0, 3).reshape(128, 16, 256)),
        'w1v': np.ascontiguousarray(np.asarray(inputs['cmp_w1_v'], np.float32)[0].reshape(16, 2, 64, 256).transpose(1, 2, 0, 3).reshape(128, 16, 256)),
        'w2k': np.ascontiguousarray(np.asarray(inputs['cmp_w2_k'], np.float32)[0].reshape(2, 128, 64).transpose(1, 0, 2)),
        'w2v': np.ascontiguousarray(np.asarray(inputs['cmp_w2_v'], np.float32)[0].reshape(2, 128, 64).transpose(1, 0, 2)),
        'pekT': np.ascontiguousarray(np.asarray(inputs['cmp_pe_k'], np.float32)[0].reshape(16, 2, 64).transpose(1, 2, 0).reshape(128, 16)),
        'pevT': np.ascontiguousarray(np.asarray(inputs['cmp_pe_v'], np.float32)[0].reshape(16, 2, 64).transpose(1, 2, 0).reshape(128, 16)),
    }
    import ml_dtypes
    cidx = np.arange(512)
    sidx = np.arange(128)
    ov = ((cidx[:, None] * 16 <= sidx[None, :] * 64 + 63) & (cidx[:, None] * 16 + 31 >= sidx[None, :] * 64)).astype(np.float32)
    ovl = np.concatenate([ov, np.ones((512, 1), np.float32)], axis=1)
    ovl[511] = 0.0
    common['ovl'] = np.ascontiguousarray(ovl.reshape(4, 128, 129).transpose(1, 0, 2))
    kk = np.arange(8192)
    common['onehot'] = (((kk[None, :] // 64) % 64) == np.arange(64)[:, None]).astype(ml_dtypes.bfloat16)
    xTs = []
    for b in range(2):
        xTs.append(np.ascontiguousarray(x[b].reshape(16, 512, 16, 128).transpose(0, 3, 2, 1)))
    in_maps = []
    kl = np.arange(128)[:, None]
    xx = np.arange(2944)[None, :]
    for c in range(8):
        b, j = c // 4, c % 4
        tiles = [4 * t + j for t in range(4)]
        m = dict(common)
        m['xT'] = xTs[b]
        m['xTo'] = np.ascontiguousarray(xTs[b][tiles])
        m['xo'] = np.ascontiguousarray(
            np.concatenate([x[b, 512 * T:512 * (T + 1)] for T in tiles], axis=0).reshape(16, 128, 2048))
        d = xx - kl + 512 * j - 1920
        bk = rel_bucket_np(d)
        rb = rel[bk]
        m['raw_da'] = np.ascontiguousarray(rb[:, 0:2560, 0:8].transpose(2, 0, 1))
        m['raw_nsa'] = np.ascontiguousarray(rb[:, :, 8:24].transpose(2, 0, 1))
        m['mneg'] = np.where(d < 0, np.float32(NEGM), np.float32(0.0)).astype(np.float32)
        m['wneg'] = np.where((d < 0) | (d >= 512), np.float32(NEGM), np.float32(0.0)).astype(np.float32)
        cl = np.arange(128)[:, None, None]
        dl = np.arange(2)[None, :, None] - 1
        ql = np.arange(512)[None, None, :]
        m['cm'] = np.where(16 * cl + 31 + 2048 * dl <= 512 * j + ql, np.float32(0.0), np.float32(NEGM)).astype(np.float32)
        qpos = (512 * np.array(tiles)[:, None, None] + 128 * np.arange(4)[None, :, None] + np.arange(128)[None, None, :]).reshape(16, 128)
        cur = qpos // 64
        sb_ = np.arange(128)[None, None, :]
        valid = sb_ <= cur[:, :, None]
        forced = valid & ((sb_ == 0) | (sb_ > cur[:, :, None] - 2))
        vmul = (valid & ~forced).astype(np.float32)
        vadd = np.where(forced, np.float32(1e4) + sb_.astype(np.float32), np.where(valid, np.float32(0.0), np.float32(-1e30))).astype(np.float32)
        m['vmul'] = np.ascontiguousarray(vmul.transpose(1, 0, 2))
        m['vadd'] = np.ascontiguousarray(vadd.transpose(1, 0, 2))
        in_maps.append(m)
    return in_maps


_NC_CACHE = {}


def kernel(**inputs):
    in_maps = prep_inputs(inputs)
    if 'nc' not in _NC_CACHE:
        _NC_CACHE['nc'] = build_program()
    nc = _NC_CACHE['nc']
    res = run_bass_kernel_spmd(nc, in_maps, core_ids=list(range(8)))
    outp = np.zeros((2, 8192, 2048), np.float32)
    for c in range(8):
        b, j = c // 4, c % 4
        o = np.asarray(res.results[c]['out']).reshape(4, 512, 2048)
        for t in range(4):
            T = 4 * t + j
            outp[b, 512 * T:512 * (T + 1)] = o[t]
    return outp
```

```python
import math
from contextlib import ExitStack

import numpy as np
import concourse.bass as bass
import concourse.mybir as mybir
from concourse.bass_utils import run_bass_kernel_spmd

F32 = mybir.dt.float32
BF16 = mybir.dt.bfloat16
AF = mybir.ActivationFunctionType
ALU = mybir.AluOpType
AX = mybir.AxisListType

ENGS = ['pe', 'act', 'dve', 'pool', 'sp']
EPOCH = 16000
RING = {'sp': 40, 'pool': 16}
NEGM = -30000.0
NPOOL = 5
POOL_STT = ()
ALPHA = 2.0 ** 0.25
LAM_INIT = 0.8 - 0.6 * math.exp(0.0)


class Prog:
    def __init__(self, nc, stack):
        self.nc = nc
        self.stack = stack
        self.ops = {e: [] for e in ENGS}
        self.cnt = {e: 0 for e in ENGS}
        self.esems = {e: [] for e in ENGS}
        self.rings = {}
        self.ring_pos = {}
        self.ring_use = {}
        for q, n in RING.items():
            self.rings[q] = [stack.enter_context(nc.semaphore('r%s%d' % (q, i))) for i in range(n)]
            self.ring_pos[q] = 0
            self.ring_use[q] = [0] * n
        self.seen = {e: {} for e in ENGS}
        self.lastw = {}
        self.readers = {}
        self.last_ev = {e: None for e in ENGS}

    def _esem(self, eng, epoch):
        while len(self.esems[eng]) <= epoch:
            self.esems[eng].append(self.stack.enter_context(
                self.nc.semaphore('e%s%d' % (eng, len(self.esems[eng])))))
        return self.esems[eng][epoch]

    def op(self, eng, fn, reads=(), writes=(), dma=False):
        deps = {}

        def add(ev):
            if ev is None:
                return
            s, v = ev
            if v > deps.get(id(s), (None, 0))[1]:
                deps[id(s)] = (s, v)
        for k in reads:
            add(self.lastw.get(k))
        for k in writes:
            add(self.lastw.get(k))
            for ev in self.readers.get(k, {}).values():
                add(ev)
        if eng == 'pe':
            for t in self.esems['pe']:
                deps.pop(id(t), None)
        if dma:
            q = eng
            pos = self.ring_pos[q]
            self.ring_pos[q] = (pos + 1) % len(self.rings[q])
            sem = self.rings[q][pos]
            if self.ring_use[q][pos] > 0:
                add((sem, 16 * self.ring_use[q][pos]))
            self.ring_use[q][pos] += 1
            ev = (sem, 16 * self.ring_use[q][pos])
            inc = 16
        else:
            i = self.cnt[eng]
            self.cnt[eng] += 1
            sem = self._esem(eng, i // EPOCH)
            ev = (sem, i % EPOCH + 1)
            inc = 1
            self.last_ev[eng] = ev
        waits = []
        seen = self.seen[eng]
        for s, v in deps.values():
            if seen.get(id(s), 0) < v:
                seen[id(s)] = v
                waits.append((s, v))
        self.ops[eng].append((fn, waits, sem, inc))
        for k in reads:
            self.readers.setdefault(k, {})[(eng, id(sem))] = ev
        for k in writes:
            self.lastw[k] = ev
            self.readers[k] = {}
        return ev

    def barrier(self):
        evs = [ev for ev in self.last_ev.values() if ev is not None]
        for q in self.rings:
            for i, s in enumerate(self.rings[q]):
                if self.ring_use[q][i] > 0:
                    evs.append((s, 16 * self.ring_use[q][i]))
        for eng in ENGS:
            waits = []
            seen = self.seen[eng]
            for s, v in evs:
                if seen.get(id(s), 0) < v:
                    seen[id(s)] = v
                    waits.append((s, v))
            self.ops[eng].append((None, waits, None, 0))
        self.lastw = {}
        self.readers = {}

    def emit(self):
        nc = self.nc
        ops = self.ops
        self.ops = {e: [] for e in ENGS}
        with nc.Block() as block:
            def run(engname):
                def body(e):
                    for fn, waits, sem, inc in ops[engname]:
                        for s, v in waits:
                            e.wait_ge(s, v)
                        if fn is not None:
                            fn(e).then_inc(sem, inc)
                return body
            block.tensor(run('pe'))
            block.scalar(run('act'))
            block.vector(run('dve'))
            block.gpsimd(run('pool'))
            block.sync(run('sp'))


class Ctx:
    pass


def build_program(phases=('p0i', 'kv0', 'kv1', 'q', 'da', 'nsa', 'e1', 'e2', 'peer', 'g2'), dbg=False, dbg_src='aT'):
    nc = bass.Bass("TRN2", target_bir_lowering=False)
    C = Ctx()
    C.nc = nc

    def din(name, shape, dt=F32):
        return nc.dram_tensor(name, list(shape), dt, kind="ExternalInput").ap()

    def dscr(name, shape, dt=BF16):
        return nc.dram_tensor(name, list(shape), dt, kind="Internal").ap()

    I = {}
    I['xT'] = din('xT', [16, 128, 16, 512])
    I['xTo'] = din('xTo', [4, 128, 16, 512])
    I['xo'] = din('xo', [16, 128, 2048])
    I['w_dakv'] = din('w_dakv', [128, 16, 2048])
    I['w_nkv'] = din('w_nkv', [128, 16, 1536])
    I['w_q'] = din('w_q', [128, 16, 2048])
    I['w_gate'] = din('w_gate', [128, 16, 48])
    I['w_mg'] = din('w_mg', [128, 16, 4096])
    I['raw_da'] = din('raw_da', [8, 128, 2560])
    I['mneg'] = din('mneg', [128, 2944])
    I['wneg'] = din('wneg', [128, 2944])
    I['raw_nsa'] = din('raw_nsa', [16, 128, 2944])
    I['c_nsa'] = din('c_nsa', [128, 16])
    I['w1k'] = din('w1k', [128, 16, 256])
    I['w1v'] = din('w1v', [128, 16, 256])
    I['w2k'] = din('w2k', [128, 2, 64])
    I['w2v'] = din('w2v', [128, 2, 64])
    I['pekT'] = din('pekT', [128, 16])
    I['pevT'] = din('pevT', [128, 16])
    I['ovl'] = din('ovl', [128, 4, 129])
    I['cm'] = din('cm', [128, 2, 512])
    I['vmul'] = din('vmul', [128, 16, 128])
    I['vadd'] = din('vadd', [128, 16, 128])
    I['onehot'] = din('onehot', [64, 8192], BF16)
    I['w_bda'] = din('w_bda', [128, 8, 2048])
    I['w_bnsa'] = din('w_bnsa', [128, 8, 2048])
    I['w_out'] = din('w_out', [128, 16, 2048])
    I['ln1g'] = din('ln1g', [128, 2048])
    I['ln1b'] = din('ln1b', [128, 2048])
    I['ln2g'] = din('ln2g', [128, 2048])
    I['ln2b'] = din('ln2b', [128, 2048])
    I['wq'] = din('wq', [16, 128, 16, 64])
    I['skT'] = din('skT', [64, 2, 128])
    I['puT'] = din('puT', [128, 128, 16, 128])
    I['pv'] = din('pv', [128, 128, 2048])
    I['c_da'] = din('c_da', [128, 8])
    I['lamq'] = din('lamq', [128, 128])
    I['lamk'] = din('lamk', [128, 128])
    I['subg'] = din('subg', [128, 128])
    I['ident'] = din('ident', [128, 128])
    out = nc.dram_tensor('out', [16, 128, 2048], F32, kind="ExternalOutput").ap()
    dbg_out = None
    if dbg:
        dbg_out = nc.dram_tensor('dbg', [128, 8, 2048], BF16, kind="ExternalOutput").ap()
        dbg2_out = nc.dram_tensor('dbg2', [128, 4, 258], F32, kind="ExternalOutput").ap()

    S = {}
    S['kT'] = dscr('s_kT', [8, 128, 8192])
    S['v'] = dscr('s_v', [8, 8192, 128])
    S['nkT'] = dscr('s_nkT', [4, 256, 8192])
    S['nv'] = dscr('s_nv', [2, 4, 8192, 64])
    S['qT'] = dscr('s_qT', [16, 128, 2048])
    S['h'] = dscr('s_h', [16, 128, 2048], F32)
    S['mixT'] = dscr('s_mixT', [4, 128, 16, 512])
    S['hT'] = dscr('s_hT', [4, 128, 16, 512])
    S['pe'] = dscr('s_pe', [16, 128, 2048], F32)
    S['puT'] = dscr('s_puT', [128, 128, 16, 128])
    S['pv'] = dscr('s_pv', [128, 128, 2048])

    with ExitStack() as top:
        P = Prog(nc, top)

        def sbt(st, name, shape, dt):
            return st.enter_context(nc.sbuf_tensor(name, list(shape), dt))

        ps = [top.enter_context(nc.psum_tensor('ps%d' % i, [128, 512], F32)) for i in range(8)]

        def MM(o, lhsT, rhs, start, stop, r, w):
            P.op('pe', lambda e: e.matmul(o, lhsT, rhs, start=start, stop=stop), r, w)

        def ACT(o, i, func, r, w, **kw):
            P.op('act', lambda e: e.activation(o, i, func, **kw), r, w)

        def CP(eng, o, i, r, w):
            if eng == 'act':
                P.op('act', lambda e: e.copy(o, i), r, w)
            else:
                P.op(eng, lambda e: e.tensor_copy(o, i), r, w)

        def TS(eng, o, i0, s1, s2, op0, op1, r, w, **kw):
            if op1 is None:
                P.op(eng, lambda e: e.tensor_scalar(o, i0, s1, None, op0, **kw), r, w)
            else:
                P.op(eng, lambda e: e.tensor_scalar(o, i0, s1, s2, op0, op1, **kw), r, w)

        def STT(eng, o, i0, sc, i1, op0, op1, r, w):
            P.op(eng, lambda e: e.scalar_tensor_tensor(o, i0, sc, i1, op0, op1), r, w)

        def TT(eng, o, i0, i1, op, r, w):
            P.op(eng, lambda e: e.tensor_tensor(o, i0, i1, op), r, w)

        def DMA(q, o, i, r, w):
            return P.op(q, lambda e: e.dma_start(out=o, in_=i), r, w, dma=True)

        def MEMSET(eng, o, val, w):
            P.op(eng, lambda e: e.memset(o, val), (), w)

        gates = sbt(top, 'gates', [128, 16, 48], F32)
        identb = sbt(top, 'identb', [128, 128], BF16)
        neglam = sbt(top, 'neglam', [128, 1], F32)
        gs = sbt(top, 'gs', [128, 128], F32)
        cda = sbt(top, 'cda', [128, 8], F32)
        dbg2sb = sbt(top, 'dbg2sb', [128, 4, 258], F32) if dbg else None
        mid = ExitStack()
        aT = sbt(mid, 'aT', [128, 8, 2048], BF16)
        nT = sbt(mid, 'nT', [128, 8, 2048], BF16)

        with ExitStack() as ph:
            idf = sbt(ph, 'idf', [128, 128], F32)
            lq = sbt(ph, 'lq', [128, 128], F32)
            lk = sbt(ph, 'lk', [128, 128], F32)
            lp = sbt(ph, 'lp', [128, 128], F32)
            l2 = sbt(ph, 'l2', [128, 2], F32)
            DMA('sp', idf[:], I['ident'], [], ['idf'])
            DMA('sp', lq[:], I['lamq'], [], ['lq'])
            DMA('sp', lk[:], I['lamk'], [], ['lk'])
            DMA('sp', gs[:], I['subg'], [], ['gs'])
            DMA('sp', cda[:], I['c_da'], [], ['cda'])
            CP('dve', identb[:], idf[:], ['idf'], ['identb'])
            TT('dve', lp[:], lq[:], lk[:], ALU.mult, ['lq', 'lk'], ['lp'])
            P.op('dve', lambda e: e.tensor_reduce(l2[:], lp[:].rearrange('p (a b) -> p a b', a=2), AX.X, ALU.add),
                 ['lp'], ['l2'])
            ACT(l2[:], l2[:], AF.Exp, ['l2'], ['l2'])
            STT('dve', neglam[:], l2[:, 0:1], -1.0, l2[:, 1:2], ALU.mult, ALU.add, ['l2'], ['neglam'])
            TS('dve', neglam[:], neglam[:], -LAM_INIT, None, ALU.add, None, ['neglam'], ['neglam'])
            TS('dve', gs[:], gs[:], 1.0 - LAM_INIT, None, ALU.mult, None, ['gs'], ['gs'])
            P.barrier()
            P.emit()

        psrot = [0]

        def nextps():
            i = psrot[0]
            psrot[0] = (i + 1) % 8
            return i

        def phase_kv(passno):
            ncw = 2048 if passno == 0 else 1536
            wsrc = I['w_dakv'] if passno == 0 else I['w_nkv']
            with ExitStack() as ph:
                wb = sbt(ph, 'A%d_wb' % passno, [128, 16, ncw], BF16)
                wst = [sbt(ph, 'A%d_wst%d' % (passno, i), [128, ncw], F32) for i in range(2)]
                xs = [sbt(ph, 'A%d_xs%d' % (passno, i), [128, 4, 512], F32) for i in range(2)]
                xb = [sbt(ph, 'A%d_xb%d' % (passno, i), [128, 16, 512], BF16) for i in range(2)]
                evs = [sbt(ph, 'A%d_ev%d' % (passno, i), [128, 512], BF16) for i in range(4)]
                evi = [0]
                for kc in range(16):
                    b = kc % 2
                    DMA('sp', wst[b][:], wsrc[:, kc, :], [], ['wst%d' % b])
                    CP('act' if kc % 2 == 0 else 'dve', wb[:, kc, :], wst[b][:], ['wst%d' % b], ['wb'])

                def evac_store(pi, npart, ncol, dst_fn):
                    k = evi[0]
                    evi[0] = (k + 1) % 4
                    CP('act', evs[k][0:npart, 0:ncol], ps[pi][0:npart, 0:ncol], ['ps%d' % pi], ['ev%d' % k])
                    dst_fn(evs[k], k)

                for tile in range(16):
                    xbk = 'xb%d' % (tile % 2)
                    xbt = xb[tile % 2]
                    for qq in range(4):
                        half = qq % 2
                        DMA('sp', xs[half][:], I['xT'][tile, :, qq * 4:(qq + 1) * 4, :], [], ['xs%d' % half])
                        CP('dve', xbt[:, qq * 4:(qq + 1) * 4, :], xs[half][:], ['xs%d' % half], [xbk])
                    tsl = slice(tile * 512, (tile + 1) * 512)
                    if passno == 0:
                        fm = [(c * 128, S['kT'][c, :, tsl]) for c in range(8)]
                    else:
                        fm = []
                        for kind, cb in enumerate((0, 256, 512, 1024)):
                            for c2 in range(2):
                                fm.append((cb + c2 * 128, S['nkT'][kind, c2 * 128:(c2 + 1) * 128, tsl]))
                    for col0, dst in fm:
                        pi = nextps()
                        for kc in range(16):
                            MM(ps[pi][:, :], wb[:, kc, col0:col0 + 128], xbt[:, kc, :], kc == 0, kc == 15,
                               ['wb', xbk], ['ps%d' % pi])
                        evac_store(pi, 128, 512,
                                   lambda ev, k, dst=dst: DMA('pool', dst, ev[:, :], ['ev%d' % k], []))
                    for blk in range(4):
                        t0 = tile * 512 + blk * 128
                        if passno == 0:
                            tm = [(1024 + g4 * 512, 512,
                                   S['v'][g4 * 4:(g4 + 1) * 4, t0:t0 + 128, :].rearrange('h t e -> t h e'), 4)
                                  for g4 in range(2)]
                        else:
                            tm = [(768, 256, S['nv'][0, :, t0:t0 + 128, :].rearrange('g t e -> t g e'), 4),
                                  (1280, 256, S['nv'][1, :, t0:t0 + 128, :].rearrange('g t e -> t g e'), 4)]
                        for col0, ncol, dst, nh in tm:
                            pi = nextps()
                            for kc in range(16):
                                MM(ps[pi][:, 0:ncol], xbt[:, kc, blk * 128:(blk + 1) * 128], wb[:, kc, col0:col0 + ncol],
                                   kc == 0, kc == 15, ['wb', xbk], ['ps%d' % pi])
                            evac_store(pi, 128, ncol,
                                       lambda ev, k, dst=dst, ncol=ncol, nh=nh: DMA(
                                           'pool', dst, ev[:, 0:ncol].rearrange('t (h e) -> t h e', h=nh),
                                           ['ev%d' % k], []))
                P.barrier()
                P.emit()

        def phase_q():
            with ExitStack() as ph:
                wb = sbt(ph, 'B_wb', [128, 16, 2048], BF16)
                wgb = sbt(ph, 'B_wgb', [128, 16, 48], BF16)
                wst = [sbt(ph, 'B_wst%d' % i, [128, 2048], F32) for i in range(1)]
                wgs = sbt(ph, 'B_wgs', [128, 16, 48], F32)
                xs = [sbt(ph, 'B_xs%d' % i, [128, 4, 512], F32) for i in range(2)]
                xb = [sbt(ph, 'B_xb%d' % i, [128, 16, 512], BF16) for i in range(2)]
                evs = [sbt(ph, 'B_ev%d' % i, [128, 512], BF16) for i in range(4)]
                evi = [0]
                for kc in range(16):
                    b = 0
                    DMA('sp', wst[b][:], I['w_q'][:, kc, :], [], ['wst%d' % b])
                    CP('act' if kc % 2 == 0 else 'dve', wb[:, kc, :], wst[b][:], ['wst%d' % b], ['wb'])
                DMA('sp', wgs[:], I['w_gate'], [], ['wgs'])
                CP('pool', wgb[:], wgs[:], ['wgs'], ['wgb'])
                for tile in range(4):
                    xbk = 'xb%d' % (tile % 2)
                    xbt = xb[tile % 2]
                    for qq in range(4):
                        half = qq % 2
                        DMA('sp', xs[half][:], I['xTo'][tile, :, qq * 4:(qq + 1) * 4, :], [], ['xs%d' % half])
                        CP('dve', xbt[:, qq * 4:(qq + 1) * 4, :], xs[half][:], ['xs%d' % half], [xbk])
                    tsl = slice(tile * 512, (tile + 1) * 512)
                    for c in range(16):
                        pi = nextps()
                        for kc in range(16):
                            MM(ps[pi][:, :], wb[:, kc, c * 128:(c + 1) * 128], xbt[:, kc, :], kc == 0, kc == 15,
                               ['wb', xbk], ['ps%d' % pi])
                        k = evi[0]
                        evi[0] = (k + 1) % 4
                        CP('act', evs[k][:, :], ps[pi][:, :], ['ps%d' % pi], ['ev%d' % k])
                        DMA('pool', S['qT'][c, :, tsl], evs[k][:, :], ['ev%d' % k], [])
                    for blk in range(4):
                        pi = nextps()
                        for kc in range(16):
                            MM(ps[pi][:, 0:48], xbt[:, kc, blk * 128:(blk + 1) * 128], wgb[:, kc, :], kc == 0, kc == 15,
                               ['wgb', xbk], ['ps%d' % pi])
                        ACT(gates[:, tile * 4 + blk, :], ps[pi][:, 0:48], AF.Sigmoid, ['ps%d' % pi], ['gates'])
                P.barrier()
                P.emit()

        def phase_da():
            with ExitStack() as ph:
                KtF = sbt(ph, 'C_KtF', [128, 8192], BF16)
                Vh = sbt(ph, 'C_Vh', [128, 64, 129], BF16)
                strip = sbt(ph, 'C_strip', [128, 2560], F32)
                mneg = sbt(ph, 'C_mneg', [128, 2560], F32)
                QT = [sbt(ph, 'C_QT%d' % m, [128, 2048], BF16) for m in range(2)]
                pT = [sbt(ph, 'C_pT%d' % b, [128, 512], BF16) for b in range(5)]
                tmp = [sbt(ph, 'C_tmp%d' % b, [128, 512], F32) for b in range(4)]
                fz = [sbt(ph, 'C_fz%d' % i, [128, 4], F32) for i in range(2)]
                o0 = sbt(ph, 'C_o0', [128, 4, 129], F32)
                fu = [sbt(ph, 'C_fu%d' % i, [128, 128], F32) for i in range(2)]
                fo = [sbt(ph, 'C_fo%d' % i, [128, 128], F32) for i in range(2)]
                fj = [sbt(ph, 'C_fj%d' % i, [128, 128], F32) for i in range(2)]
                fon = [sbt(ph, 'C_fon%d' % i, [128, 128], BF16) for i in range(2)]
                DMA('sp', mneg[:], I['mneg'][:, 0:2560], [], ['mneg'])
                MEMSET('pool', Vh[:, :, 128:129], 1.0, ['Vh'])
                MEMSET('pool', QT[0][:], 0.0, ['QT0'])
                MEMSET('pool', QT[1][:], 0.0, ['QT1'])
                p0q = []
                if 'p0i' in phases:
                    pus = [sbt(ph, 'C_pus%d' % i, [128, 2048], F32) for i in range(3)]
                    pub = [sbt(ph, 'C_pub%d' % i, [128, 2048], BF16) for i in range(3)]
                    p0q = p0_units(pus, pub, ['pool'])
                st_ = {'slot': 0, 'fin': 0}
                pipe = []

                def push(pv):
                    pipe.append(pv)
                    if len(pipe) > 3:
                        pipe.pop(0)()

                def flush():
                    while pipe:
                        pipe.pop(0)()

                def da_finalize(h, t):
                    for sub in range(4):
                        f = st_['fin'] % 2
                        st_['fin'] += 1
                        acc = ps[4 + sub]
                        ak = 'ps%d' % (4 + sub)
                        if dbg and h == 0 and t == 0:
                            CP('dve', dbg2sb[:, sub, 0:129], o0[:, sub, :], ['o0_%d' % sub], ['dbg2sb'])
                            CP('dve', dbg2sb[:, sub, 129:258], acc[:, 0:129], [ak], ['dbg2sb'])
                        P.op('dve', lambda e, f=f, sub=sub: e.reciprocal(fz[f][:, 0:1], o0[:, sub, 128:129]), ['o0_%d' % sub], ['fz%d' % f])
                        P.op('dve', lambda e, f=f, acc=acc: e.reciprocal(fz[f][:, 1:2], acc[:, 128:129]), [ak], ['fz%d' % f])
                        TT('dve', fz[f][:, 2:3], fz[f][:, 1:2], neglam[:], ALU.mult, ['fz%d' % f, 'neglam'], ['fz%d' % f])
                        TS('dve', fu[f][:], acc[:, 0:128], fz[f][:, 2:3], None, ALU.mult, None, [ak, 'fz%d' % f], ['fu%d' % f])
                        STT('dve', fo[f][:], o0[:, sub, 0:128], fz[f][:, 0:1], fu[f][:], ALU.mult, ALU.add,
                            ['o0_%d' % sub, 'fz%d' % f, 'fu%d' % f], ['fo%d' % f])
                        TT('pool', fj[f][:], fo[f][:], fo[f][:], ALU.mult, ['fo%d' % f], ['fj%d' % f])
                        P.op('dve', lambda e, f=f: e.tensor_reduce(fz[f][:, 3:4], fj[f][:], AX.X, ALU.add), ['fj%d' % f], ['fz3_%d' % f])
                        TS('dve', fz[f][:, 3:4], fz[f][:, 3:4], 1.0 / 128.0, 1e-5, ALU.mult, ALU.add, ['fz3_%d' % f], ['fz3_%d' % f])
                        ACT(fz[f][:, 3:4], fz[f][:, 3:4], AF.Sqrt, ['fz3_%d' % f], ['fz3_%d' % f])
                        P.op('dve', lambda e, f=f: e.reciprocal(fz[f][:, 3:4], fz[f][:, 3:4]), ['fz3_%d' % f], ['fz3_%d' % f])
                        STT('dve', fon[f][:], fo[f][:], fz[f][:, 3:4], gs[:], ALU.mult, ALU.mult,
                            ['fo%d' % f, 'fz3_%d' % f, 'gs'], ['fon%d' % f])
                        MM(ps[3][:, 0:128], fon[f][:], identb[:], True, True, ['fon%d' % f, 'identb'], ['ps3'])
                        blk = t * 4 + sub
                        CP('act', aT[:, h, blk * 128:(blk + 1) * 128], ps[3][:, 0:128], ['ps3'], ['aT'])

                for h in range(8):
                    flush()
                    DMA('sp', KtF[:], S['kT'][h], [], ['KtF'])
                    for m in range(2):
                        DMA('sp', QT[m][m * 64:(m + 1) * 64, :], S['qT'][h, m * 64:(m + 1) * 64, :], [], ['QT%d' % m])
                    for q4 in range(4):
                        DMA('sp', Vh[:, q4 * 16:(q4 + 1) * 16, 0:128],
                            S['v'][h, q4 * 2048:(q4 + 1) * 2048, :].rearrange('(s p) e -> p s e', p=128), [], ['Vh'])
                    DMA('sp', strip[:], I['raw_da'][h], [], ['strip'])
                    STT('dve', strip[:], strip[:], cda[:, h:h + 1], mneg[:], ALU.subtract, ALU.add,
                        ['strip', 'cda', 'mneg'], ['strip'])
                    for t in range(4):
                        nsl = 16 * (t + 1)
                        for m in range(2):
                            for s in range(nsl):
                                c = st_['slot']
                                st_['slot'] += 1
                                if p0q and c % 10 == 0:
                                    p0q.pop(0)()
                                b2 = c % 4
                                b3 = c % 5
                                near = s >= 16 * t - 1
                                pi = b2
                                MM(ps[pi][:, :], KtF[:, s * 128:(s + 1) * 128], QT[m][:, t * 512:(t + 1) * 512],
                                   True, True, ['KtF', 'QT%d' % m], ['ps%d' % pi])
                                if near:
                                    x0 = 128 * (15 - (s - 16 * t))
                                    STT('dve', tmp[b2][:], ps[pi][:, :], 0.125, strip[:, x0:x0 + 512], ALU.mult, ALU.add,
                                        ['ps%d' % pi, 'strip'], ['tmp%d' % b2])
                                    ACT(pT[b3][:], tmp[b2][:], AF.Exp, ['tmp%d' % b2], ['pT%d' % b3])
                                else:
                                    ACT(pT[b3][:], ps[pi][:, :], AF.Exp, ['ps%d' % pi], ['pT%d' % b3], scale=0.125)

                                def pv(h=h, t=t, m=m, s=s, b3=b3, nsl=nsl):
                                    for sub in range(4):
                                        MM(ps[4 + sub][:, 0:129], pT[b3][:, sub * 128:(sub + 1) * 128],
                                           Vh[:, s, :], s == 0, s == nsl - 1, ['pT%d' % b3, 'Vh'], ['ps%d' % (4 + sub)])
                                    if s == nsl - 1:
                                        if m == 0:
                                            for sub in range(4):
                                                CP('dve', o0[:, sub, :], ps[4 + sub][:, 0:129], ['ps%d' % (4 + sub)], ['o0_%d' % sub])
                                        else:
                                            da_finalize(h, t)
                                push(pv)
                flush()
                while p0q:
                    p0q.pop(0)()
                P.barrier()
                P.emit()


        def phase_nsa():
            with ExitStack() as ph:
                KCT = sbt(ph, 'D_KCT', [64, 4, 512], BF16)
                Rg = sbt(ph, 'D_Rg', [128, 4, 4, 193], BF16)
                cns = sbt(ph, 'D_cns', [128, 16], F32)
                MEMSET('pool', KCT[:], 0.0, ['KCT'])
                MEMSET('pool', Rg[:], 0.0, ['Rg'])
                DMA('sp', cns[:], I['c_nsa'], [], ['cns'])
                with ExitStack() as p0:
                    cT = sbt(p0, 'D0_cT', [128, 8192], BF16)
                    w1s = sbt(p0, 'D0_w1s', [128, 4, 256], F32)
                    w1b = sbt(p0, 'D0_w1b', [128, 16, 256], BF16)
                    w2s = sbt(p0, 'D0_w2s', [128, 2, 64], F32)
                    w2b = sbt(p0, 'D0_w2b', [128, 2, 64], BF16)
                    pes = sbt(p0, 'D0_pes', [128, 16], F32)
                    peb = sbt(p0, 'D0_peb', [128, 16], BF16)
                    b1 = sbt(p0, 'D0_b1', [128, 2], F32)
                    hT = sbt(p0, 'D0_hT', [128, 2, 512], BF16)
                    ovs = sbt(p0, 'D0_ovs', [128, 4, 129], F32)
                    DMA('sp', ovs[:], I['ovl'], [], ['ovs'])
                    for g in range(4):
                        CP('pool', Rg[:, :, g, 0:129], ovs[:], ['ovs'], ['Rg'])
                    for kind in range(2):
                        w1src = I['w1k'] if kind == 0 else I['w1v']
                        for p8 in range(4):
                            DMA('sp', w1s[:], w1src[:, p8 * 4:(p8 + 1) * 4, :], [], ['w1s'])
                            CP('pool', w1b[:, p8 * 4:(p8 + 1) * 4, :], w1s[:], ['w1s'], ['w1b'])
                        DMA('sp', w2s[:], I['w2k'] if kind == 0 else I['w2v'], [], ['w2s'])
                        CP('pool', w2b[:], w2s[:], ['w2s'], ['w2b'])
                        DMA('sp', pes[:], I['pekT'] if kind == 0 else I['pevT'], [], ['pes'])
                        CP('pool', peb[:], pes[:], ['pes'], ['peb'])
                        for hc in range(2):
                            pi = nextps()
                            for p in range(16):
                                MM(ps[pi][:, 0:1], w1b[:, p, hc * 128:(hc + 1) * 128], peb[:, p:p + 1], p == 0, p == 15,
                                   ['w1b', 'peb'], ['ps%d' % pi])
                            CP('dve', b1[:, hc:hc + 1], ps[pi][:, 0:1], ['ps%d' % pi], ['b1'])
                        for g in range(4):
                            DMA('sp', cT[0:64, :], S['nkT'][kind, g * 64:(g + 1) * 64, :], [], ['cT'])
                            DMA('sp', cT[64:128, 0:8191], S['nkT'][kind, g * 64:(g + 1) * 64, 1:8192], [], ['cT'])
                            for hc in range(2):
                                pi = nextps()
                                for p in range(16):
                                    MM(ps[pi][:, 0:511], w1b[:, p, hc * 128:(hc + 1) * 128], cT[:, 2 * p:2 * p + 16 * 510 + 1:16],
                                       p == 0, p == 15, ['w1b', 'cT'], ['ps%d' % pi])
                                ACT(hT[:, hc, 0:511], ps[pi][:, 0:511], AF.Gelu_apprx_tanh, ['ps%d' % pi, 'b1'], ['hT'],
                                    bias=b1[:, hc:hc + 1])
                            if kind == 0:
                                pi = nextps()
                                for hc in range(2):
                                    MM(ps[pi][0:64, 0:511], w2b[:, hc, :], hT[:, hc, 0:511], hc == 0, hc == 1,
                                       ['w2b', 'hT'], ['ps%d' % pi])
                                CP('act', KCT[:, g, 0:511], ps[pi][0:64, 0:511], ['ps%d' % pi], ['KCT'])
                            else:
                                for cc in range(4):
                                    ncl = 128 if cc < 3 else 127
                                    pi = nextps()
                                    for hc in range(2):
                                        MM(ps[pi][0:ncl, 0:64], hT[:, hc, cc * 128:cc * 128 + ncl], w2b[:, hc, :], hc == 0, hc == 1,
                                           ['w2b', 'hT'], ['ps%d' % pi])
                                    CP('act', Rg[0:ncl, cc, g, 129:193], ps[pi][0:ncl, 0:64], ['ps%d' % pi], ['Rg'])
                    P.barrier()
                    P.emit()
                cm = sbt(ph, 'D_cm', [128, 2, 512], F32)
                vmul = sbt(ph, 'D_vmul', [128, 16, 128], F32)
                vadd = sbt(ph, 'D_vadd', [128, 16, 128], F32)
                Kbuf = sbt(ph, 'D_Kbuf', [128, 8192], BF16)
                Vbuf = sbt(ph, 'D_Vbuf', [128, 64, 65], BF16)
                strip = sbt(ph, 'D_strip', [128, 2944], F32)
                neg = sbt(ph, 'D_neg', [128, 2944], F32)
                QTn = [sbt(ph, 'D_QTn%d' % i, [64, 2048], BF16) for i in range(4)]
                QA = [sbt(ph, 'D_QA%d' % i, [128, 512], BF16) for i in range(2)]
                QB = [sbt(ph, 'D_QB%d' % i, [128, 512], BF16) for i in range(2)]
                selT = [sbt(ph, 'D_selT%d' % i, [128, 512], BF16) for i in range(4)]
                onsa = sbt(ph, 'D_onsa', [128, 16, 4, 64], F32)
                impacc = sbt(ph, 'D_impacc', [128, 4, 128], F32)
                pT = [sbt(ph, 'D_pT%d' % b, [128, 512], BF16) for b in range(5)]
                tmp = [sbt(ph, 'D_tmp%d' % b, [128, 512], F32) for b in range(4)]
                fz = [sbt(ph, 'D_fz%d' % i, [128, 4], F32) for i in range(2)]
                sc = [sbt(ph, 'D_sc%d' % i, [128, 128], F32) for i in range(2)]
                sc2 = [sbt(ph, 'D_sc2%d' % i, [128, 128], F32) for i in range(2)]
                m8 = [sbt(ph, 'D_m8%d' % i, [128, 16], F32) for i in range(2)]
                sng = [sbt(ph, 'D_sng%d' % i, [128, 128], BF16) for i in range(2)]
                onb = [sbt(ph, 'D_onb%d' % i, [128, 128], BF16) for i in range(2)]
                accT_sb = [sbt(ph, 'D_accT%d' % i, [65, 512], F32) for i in range(1)]
                QZ = [sbt(ph, 'D_QZ%d' % i, [128, 512], BF16) for i in range(2)]
                MEMSET('pool', QZ[0][:], 0.0, ['QZ0'])
                MEMSET('pool', QZ[1][:], 0.0, ['QZ1'])
                identf = sbt(ph, 'D_identf', [128, 128], F32)
                DMA('sp', identf[:], I['ident'], [], ['identf'])
                DMA('sp', cm[:], I['cm'], [], ['cm'])
                DMA('sp', vmul[:], I['vmul'], [], ['vmul'])
                DMA('sp', vadd[:], I['vadd'], [], ['vadd'])
                DMA('sp', Kbuf[64:128, :], I['onehot'], [], ['KbufHi'])
                MEMSET('pool', Vbuf[:, :, 64:65], 1.0, ['Vbuf'])
                cnt = {'slot': 0, 'fin': 0, 'q': 0, 'tr': 0, 'grp': 0}

                pipe = []

                def push(pv):
                    pipe.append(pv)
                    if len(pipe) > 3:
                        pipe.pop(0)()

                def flush():
                    while pipe:
                        pipe.pop(0)()

                def attn_slot(lhsT, rhs, rkeys, bias_ap, vrhs, vkeys, ncolv, first, last, after=None, tbank=None):
                    c = cnt['slot']
                    cnt['slot'] = c + 1
                    b2, b3 = c % 4, c % 5
                    MM(ps[b2][:, :], lhsT, rhs, True, True, rkeys, ['ps%d' % b2])
                    if bias_ap is not None:
                        STT('dve', tmp[b2][:], ps[b2][:, :], 0.125, bias_ap, ALU.mult, ALU.add,
                            ['ps%d' % b2, 'strip', 'cm'], ['tmp%d' % b2])
                        ACT(pT[b3][:], tmp[b2][:], AF.Exp, ['tmp%d' % b2], ['pT%d' % b3])
                    else:
                        ACT(pT[b3][:], ps[b2][:, :], AF.Exp, ['ps%d' % b2], ['pT%d' % b3], scale=0.125)

                    def pv():
                        if tbank is None:
                            for sub in range(4):
                                MM(ps[4 + sub][:, 0:ncolv], pT[b3][:, sub * 128:(sub + 1) * 128], vrhs, first, last,
                                   ['pT%d' % b3] + vkeys, ['ps%d' % (4 + sub)])
                        else:
                            MM(ps[tbank][0:ncolv, :], vrhs, pT[b3][:, :], first, last, ['pT%d' % b3] + vkeys, ['ps%d' % tbank])
                        if after is not None:
                            after()
                    push(pv)

                def fin_branch(t, hh, n, gidx, dcol, ncol0, first_branch, tb=None):
                    if tb is not None:
                        k = cnt['tr'] % 2
                        cnt['tr'] += 1
                        CP('act', accT_sb[0][0:65, :], ps[tb][0:65, :], ['ps%d' % tb], ['accT0'])
                        for sub in range(4):
                            MM(ps[6 + k][:, sub * 65:(sub + 1) * 65], accT_sb[0][0:65, sub * 128:(sub + 1) * 128], identf[0:65, 0:65],
                               True, True, ['accT0', 'identf'], ['ps%d' % (6 + k)])
                    for sub in range(4):
                        f = cnt['fin'] % 2
                        cnt['fin'] += 1
                        blk = 4 * t + sub
                        if tb is None:
                            acc = ps[4 + sub]
                            ak = 'ps%d' % (4 + sub)
                            c0 = 0
                        else:
                            acc = ps[6 + k]
                            ak = 'ps%d' % (6 + k)
                            c0 = sub * 65
                        fk = 'fz%d' % f
                        TS('dve', fz[f][:, 0:1], acc[:, c0 + dcol:c0 + dcol + 1], 1e-30, None, ALU.max, None, [ak], [fk])
                        P.op('dve', lambda e, f=f: e.reciprocal(fz[f][:, 1:2], fz[f][:, 0:1]), [fk], [fk])
                        TT('dve', fz[f][:, 2:3], fz[f][:, 1:2], gates[:, blk, n * 3 + gidx:n * 3 + gidx + 1], ALU.mult,
                           [fk, 'gates'], [fk])
                        if first_branch:
                            if hh == 0:
                                TS('dve', impacc[:, sub, :], acc[:, 0:128], fz[f][:, 1:2], None, ALU.mult, None,
                                   [ak, fk], ['impacc%d' % sub])
                            else:
                                STT('dve', impacc[:, sub, :], acc[:, 0:128], fz[f][:, 1:2], impacc[:, sub, :], ALU.mult, ALU.add,
                                    [ak, fk, 'impacc%d' % sub], ['impacc%d' % sub])
                            TS('dve', onsa[:, blk, hh, :], acc[:, ncol0:ncol0 + 64], fz[f][:, 2:3], None, ALU.mult, None,
                               [ak, fk], ['onsa'])
                        else:
                            STT('dve', onsa[:, blk, hh, :], acc[:, c0 + ncol0:c0 + ncol0 + 64], fz[f][:, 2:3], onsa[:, blk, hh, :],
                                ALU.mult, ALU.add, [ak, fk, 'onsa'], ['onsa'])

                for g in range(4):
                    for hh in range(4):
                        n = 4 * g + hh
                        DMA('sp', QTn[hh][:], S['qT'][8 + n // 2, (n % 2) * 64:(n % 2) * 64 + 64, :], [], ['QTn%d' % hh])
                    def topk_code(g, t):
                        for sub in range(4):
                            f = cnt['fin'] % 2
                            cnt['fin'] += 1
                            blk = 4 * t + sub
                            TT('dve', sc[f][:], impacc[:, sub, :], vmul[:, blk, :], ALU.mult, ['impacc%d' % sub, 'vmul'], ['sc%d' % f])
                            TT('dve', sc[f][:], sc[f][:], vadd[:, blk, :], ALU.add, ['sc%d' % f, 'vadd'], ['sc%d' % f])
                            P.op('dve', lambda e, f=f: e.max(out=m8[f][:, 0:8], in_=sc[f][:]), ['sc%d' % f], ['m8_%d' % f])
                            P.op('dve', lambda e, f=f: e.match_replace(out=sc2[f][:], in_to_replace=m8[f][:, 0:8],
                                                                      in_values=sc[f][:], imm_value=-3.0e38),
                                 ['sc%d' % f, 'm8_%d' % f], ['sc2%d' % f])
                            P.op('dve', lambda e, f=f: e.max(out=m8[f][:, 8:16], in_=sc2[f][:]), ['sc2%d' % f], ['m8_%d' % f])
                            TS('dve', sng[f][:], sc[f][:], m8[f][:, 15:16], -240000.0, ALU.is_lt, ALU.mult,
                               ['sc%d' % f, 'm8_%d' % f], ['sng%d' % f])
                            MM(ps[3][:, 0:128], sng[f][:], identb[:], True, True, ['sng%d' % f, 'identb'], ['ps3'])
                            CP('act', selT[t][:, sub * 128:(sub + 1) * 128], ps[3][:, 0:128], ['ps3'], ['selT%d' % t])
                            if dbg and g == 0 and t == 0:
                                CP('dve', dbg2sb[:, sub, 0:128], sc[f][:], ['sc%d' % f], ['dbg2sb'])
                                CP('dve', dbg2sb[:, sub, 129:145], m8[f][:], ['m8_%d' % f], ['dbg2sb'])

                    for t in range(4):
                        for hh in range(4):
                            n = 4 * g + hh
                            for cc in range(t + 1):
                                bias_ap = cm[:, cc - t + 1, :] if cc >= t - 1 else None
                                aft = None
                                if cc == t:
                                    def aft(t=t, hh=hh, n=n, g=g):
                                        fin_branch(t, hh, n, 0, 128, 129, True)
                                        if hh == 3:
                                            topk_code(g, t)
                                attn_slot(KCT[:, g, cc * 128:(cc + 1) * 128], QTn[hh][:, t * 512:(t + 1) * 512],
                                          ['KCT', 'QTn%d' % hh], bias_ap, Rg[:, cc, g, :], ['Rg'], 193, cc == 0, cc == t, after=aft)
                    flush()
                    DMA('sp', Kbuf[0:64, :], S['nkT'][2, g * 64:(g + 1) * 64, :], [], ['KbufLo'])
                    for q4 in range(4):
                        DMA('sp', Vbuf[:, q4 * 16:(q4 + 1) * 16, 0:64],
                            S['nv'][0, g, q4 * 2048:(q4 + 1) * 2048, :].rearrange('(s p) e -> p s e', p=128), [], ['Vbuf'])
                    DMA('sp', neg[:], I['mneg'], [], ['neg'])
                    for hh in range(4):
                        n = 4 * g + hh
                        DMA('sp', strip[:], I['raw_nsa'][n], [], ['strip'])
                        STT('dve', strip[:], strip[:], cns[:, n:n + 1], neg[:], ALU.subtract, ALU.add,
                            ['strip', 'cns', 'neg'], ['strip'])
                        for t in range(4):
                            qb = cnt['q'] % 2
                            cnt['q'] += 1
                            CP('pool', QA[qb][0:64, :], QTn[hh][:, t * 512:(t + 1) * 512], ['QTn%d' % hh], ['QA%d' % qb])
                            CP('pool', QB[qb][0:64, :], QTn[hh][:, t * 512:(t + 1) * 512], ['QTn%d' % hh], ['QB%d' % qb])
                            DMA('sp', QA[qb][64:128, :], selT[t][0:64, :], ['selT%d' % t], ['QA%d' % qb])
                            CP('pool', QB[qb][64:128, :], selT[t][64:128, :], ['selT%d' % t], ['QB%d' % qb])
                            nsl = 16 * (t + 1)
                            tb = 4 + (cnt['grp'] % 2)
                            cnt['grp'] += 1
                            for s_ in range(nsl):
                                near = s_ >= 16 * t - 1
                                bias_ap = strip[:, 128 * (15 - (s_ - 16 * t)):128 * (15 - (s_ - 16 * t)) + 512] if near else None
                                qq = QA[qb] if s_ < 32 else QB[qb]
                                qk = ('QA%d' if s_ < 32 else 'QB%d') % qb
                                aft = None
                                if s_ == nsl - 1:
                                    def aft(t=t, hh=hh, n=n, tb=tb):
                                        fin_branch(t, hh, n, 1, 64, 0, False, tb=tb)
                                attn_slot(Kbuf[:, s_ * 128:(s_ + 1) * 128], qq[:, :], ['KbufLo', 'KbufHi', qk], bias_ap,
                                          Vbuf[:, s_, :], ['Vbuf'], 65, s_ == 0, s_ == nsl - 1, after=aft, tbank=tb)
                    flush()
                    DMA('sp', Kbuf[0:64, :], S['nkT'][3, g * 64:(g + 1) * 64, :], [], ['KbufLo'])
                    for q4 in range(4):
                        DMA('sp', Vbuf[:, q4 * 16:(q4 + 1) * 16, 0:64],
                            S['nv'][1, g, q4 * 2048:(q4 + 1) * 2048, :].rearrange('(s p) e -> p s e', p=128), [], ['Vbuf'])
                    DMA('sp', neg[:], I['wneg'], [], ['neg'])
                    for hh in range(4):
                        n = 4 * g + hh
                        DMA('sp', strip[:], I['raw_nsa'][n], [], ['strip'])
                        STT('dve', strip[:], strip[:], cns[:, n:n + 1], neg[:], ALU.subtract, ALU.add,
                            ['strip', 'cns', 'neg'], ['strip'])
                        for t in range(4):
                            s0 = max(0, 16 * t - 4)
                            s1 = 16 * t + 15
                            tb = 4 + (cnt['grp'] % 2)
                            cnt['grp'] += 1
                            qz = cnt['grp'] % 2
                            CP('pool', QZ[qz][0:64, :], QTn[hh][:, t * 512:(t + 1) * 512], ['QTn%d' % hh], ['QZ%d' % qz])
                            for s_ in range(s0, s1 + 1):
                                x0 = 128 * (15 - (s_ - 16 * t))
                                aft = None
                                if s_ == s1:
                                    def aft(t=t, hh=hh, n=n, tb=tb):
                                        fin_branch(t, hh, n, 2, 64, 0, False, tb=tb)
                                attn_slot(Kbuf[:, s_ * 128:(s_ + 1) * 128], QZ[qz][:, :],
                                          ['KbufLo', 'KbufHi', 'QZ%d' % qz], strip[:, x0:x0 + 512], Vbuf[:, s_, :], ['Vbuf'], 65,
                                          s_ == s0, s_ == s1, after=aft, tbank=tb)
                    flush()
                    for blk in range(16):
                        for pr in range(2):
                            f = cnt['fin'] % 2
                            cnt['fin'] += 1
                            CP('pool', onb[f][:].rearrange('p (h e) -> p h e', h=2), onsa[:, blk, 2 * pr:2 * pr + 2, :],
                               ['onsa'], ['onb%d' % f])
                            pi = 3
                            MM(ps[pi][:, 0:128], onb[f][:], identb[:], True, True, ['onb%d' % f, 'identb'], ['ps%d' % pi])
                            CP('act', nT[:, 2 * g + pr, blk * 128:(blk + 1) * 128], ps[pi][:, 0:128], ['ps%d' % pi], ['nT'])
                P.barrier()
                P.emit()


        def phase_e1():
            with ExitStack() as ph:
                xs = [sbt(ph, 'E_xs%d' % i, [128, 2, 512], F32) for i in range(2)]
                xb = sbt(ph, 'E_xb', [128, 16, 2048], BF16)
                wgs = [sbt(ph, 'E_wgs%d' % i, [128, 16, 128], F32) for i in range(2)]
                wbs = [sbt(ph, 'E_wbs%d' % i, [128, 8, 128], F32) for i in range(2)]
                wga = [sbt(ph, 'E_wga%d' % i, [128, 16, 128], BF16) for i in range(2)]
                wgn = [sbt(ph, 'E_wgn%d' % i, [128, 16, 128], BF16) for i in range(2)]
                wba = [sbt(ph, 'E_wba%d' % i, [128, 8, 128], BF16) for i in range(2)]
                wbn = [sbt(ph, 'E_wbn%d' % i, [128, 8, 128], BF16) for i in range(2)]
                sA = [sbt(ph, 'E_sA%d' % i, [128, 512], F32) for i in range(2)]
                sB = [sbt(ph, 'E_sB%d' % i, [128, 512], F32) for i in range(2)]
                m1 = [sbt(ph, 'E_m1%d' % i, [128, 512], F32) for i in range(2)]
                mx = [sbt(ph, 'E_mx%d' % i, [128, 512], BF16) for i in range(3)]
                k = 0
                for tile in range(4):
                    for qq in range(8):
                        half = k % 2
                        k += 1
                        DMA('sp', xs[half][:], I['xTo'][tile, :, qq * 2:(qq + 1) * 2, :], [], ['xs%d' % half])
                        CP('dve' if qq % 2 == 0 else 'act', xb[:, qq * 2:(qq + 1) * 2, tile * 512:(tile + 1) * 512], xs[half][:],
                           ['xs%d' % half], ['xb'])
                it = 0
                mi = 0
                for c in range(16):
                    b = c % 2
                    DMA('sp', wgs[0][:], I['w_mg'][:, :, c * 128:(c + 1) * 128], [], ['wgs0'])
                    CP('act', wga[b][:], wgs[0][:], ['wgs0'], ['wga%d' % b])
                    DMA('sp', wgs[1][:], I['w_mg'][:, :, 2048 + c * 128:2048 + (c + 1) * 128], [], ['wgs1'])
                    CP('dve', wgn[b][:], wgs[1][:], ['wgs1'], ['wgn%d' % b])
                    DMA('sp', wbs[0][:], I['w_bda'][:, :, c * 128:(c + 1) * 128], [], ['wbs0'])
                    CP('act', wba[b][:], wbs[0][:], ['wbs0'], ['wba%d' % b])
                    DMA('sp', wbs[1][:], I['w_bnsa'][:, :, c * 128:(c + 1) * 128], [], ['wbs1'])
                    CP('dve', wbn[b][:], wbs[1][:], ['wbs1'], ['wbn%d' % b])
                    for tile in range(4):
                        tsl = slice(tile * 512, (tile + 1) * 512)
                        p2 = it % 2
                        it += 1
                        pA, pB, pC, pD = (4 * p2 + 0), (4 * p2 + 1), (4 * p2 + 2), (4 * p2 + 3)
                        for kc in range(16):
                            MM(ps[pA][:, :], wga[b][:, kc, :], xb[:, kc, tsl], kc == 0, kc == 15, ['wga%d' % b, 'xb'], ['ps%d' % pA])
                        for kc in range(16):
                            MM(ps[pB][:, :], wgn[b][:, kc, :], xb[:, kc, tsl], kc == 0, kc == 15, ['wgn%d' % b, 'xb'], ['ps%d' % pB])
                        for kc in range(8):
                            MM(ps[pC][:, :], wba[b][:, kc, :], aT[:, kc, tsl], kc == 0, kc == 7, ['wba%d' % b, 'aT'], ['ps%d' % pC])
                        for kc in range(8):
                            MM(ps[pD][:, :], wbn[b][:, kc, :], nT[:, kc, tsl], kc == 0, kc == 7, ['wbn%d' % b, 'nT'], ['ps%d' % pD])
                        ACT(sA[p2][:], ps[pA][:, :], AF.Sigmoid, ['ps%d' % pA], ['sA%d' % p2])
                        ACT(sB[p2][:], ps[pB][:, :], AF.Sigmoid, ['ps%d' % pB], ['sB%d' % p2])
                        TT('dve', m1[p2][:], sA[p2][:], ps[pC][:, :], ALU.mult, ['sA%d' % p2, 'ps%d' % pC], ['m1%d' % p2])
                        TT('dve', sB[p2][:], sB[p2][:], ps[pD][:, :], ALU.mult, ['sB%d' % p2, 'ps%d' % pD], ['sB%d' % p2])
                        m3 = mi % 3
                        mi += 1
                        TT('pool', mx[m3][:], m1[p2][:], sB[p2][:], ALU.add, ['m1%d' % p2, 'sB%d' % p2], ['mx%d' % m3])
                        DMA('pool', S['mixT'][tile, :, c, :], mx[m3][:], ['mx%d' % m3], [])
                P.barrier()
                P.emit()

        def phase_e2():
            with ExitStack() as ph:
                wos = sbt(ph, 'F_wos', [128, 4, 512], F32)
                wob = [sbt(ph, 'F_wob%d' % i, [128, 16, 512], BF16) for i in range(2)]
                mxt = sbt(ph, 'F_mxt', [128, 16, 512], BF16)
                xc = [sbt(ph, 'F_xc%d' % i, [128, 512], F32) for i in range(3)]
                hpre2 = [sbt(ph, 'F_hpre%d' % i, [128, 4, 2048], F32) for i in range(2)]
                junk = sbt(ph, 'F_junk', [128, 2048], F32)
                hn = [sbt(ph, 'F_hn%d' % i, [128, 2048], F32) for i in range(2)]
                hb = sbt(ph, 'F_hb', [128, 2048], BF16)
                gB = sbt(ph, 'F_gB', [128, 2048], F32)
                bB = sbt(ph, 'F_bB', [128, 2048], F32)
                st = [sbt(ph, 'F_st%d' % i, [128, 4], F32) for i in range(2)]
                hTt = sbt(ph, 'F_hTt', [128, 16, 512], BF16)
                DMA('sp', gB[:], I['ln1g'], [], ['gB'])
                DMA('sp', bB[:], I['ln1b'], [], ['bB'])
                wi = 0
                xi = 0
                for tile in range(4):
                    hpre = hpre2[tile % 2]
                    hk = 'hpre%d_' % (tile % 2)
                    DMA('sp', mxt[:], S['mixT'][tile], [], ['mxt'])
                    for dc in range(4):
                        b = wi % 2
                        wi += 1
                        for k4 in range(4):
                            DMA('sp', wos[:], I['w_out'][:, k4 * 4:(k4 + 1) * 4, dc * 512:(dc + 1) * 512], [], ['wos'])
                            CP('act' if k4 % 2 == 0 else 'dve', wob[b][:, k4 * 4:(k4 + 1) * 4, :], wos[:], ['wos'], ['wob%d' % b])
                        for sub in range(4):
                            blk = tile * 4 + sub
                            pi = nextps()
                            for kc in range(16):
                                MM(ps[pi][:, :], mxt[:, kc, sub * 128:(sub + 1) * 128], wob[b][:, kc, :], kc == 0, kc == 15,
                                   ['mxt', 'wob%d' % b], ['ps%d' % pi])
                            x3 = xi % 3
                            xi += 1
                            DMA('sp', xc[x3][:], I['xo'][blk, :, dc * 512:(dc + 1) * 512], [], ['xc%d' % x3])
                            STT('dve', hpre[:, sub, dc * 512:(dc + 1) * 512], xc[x3][:], ALPHA, ps[pi][:, :], ALU.mult, ALU.add,
                                ['xc%d' % x3, 'ps%d' % pi], [hk + str(sub)])
                    for sub in range(4):
                        blk = tile * 4 + sub
                        layer_norm_block(hpre[:, sub, :], hk + str(sub), gB, bB, junk, hn, st, blk)
                        f = blk % len(hn)
                        DMA('pool', S['h'][blk], hn[f][:], ['hn%d' % f], [])
                        CP('act', hb[:], hn[f][:], ['hn%d' % f], ['hb'])
                        for f4 in range(4):
                            pi = nextps()
                            for ff in range(4):
                                fc = f4 * 4 + ff
                                MM(ps[pi][:, ff * 128:(ff + 1) * 128], hb[:, fc * 128:(fc + 1) * 128], identb[:], True, True,
                                   ['hb', 'identb'], ['ps%d' % pi])
                            CP('act', hTt[:, f4 * 4:(f4 + 1) * 4, sub * 128:(sub + 1) * 128],
                               ps[pi][:, :].rearrange('p (a b) -> p a b', a=4), ['ps%d' % pi], ['hTt'])
                    DMA('pool', S['hT'][tile], hTt[:], ['hTt'], [])
                P.barrier()
                P.emit()

        def layer_norm_block(src, srck, gB_, bB_, junk, hn, st, blk, junkk='junk'):
            f = blk % len(hn)
            sk = 'st%d' % f
            hk_ = 'hn%d' % f
            P.op('dve', lambda e: e.tensor_reduce(st[f][:, 0:1], src, AX.X, ALU.add), [srck], [sk])
            ACT(junk[:], src, AF.Square, [srck], [junkk])
            P.op('dve', lambda e: e.tensor_reduce(st[f][:, 1:2], junk[:], AX.X, ALU.add), [junkk], [sk])
            TS('dve', st[f][:, 0:1], st[f][:, 0:1], 1.0 / 2048.0, None, ALU.mult, None, [sk], [sk])
            TT('dve', st[f][:, 3:4], st[f][:, 0:1], st[f][:, 0:1], ALU.mult, [sk], [sk])
            STT('dve', st[f][:, 1:2], st[f][:, 1:2], 1.0 / 2048.0, st[f][:, 3:4], ALU.mult, ALU.subtract, [sk], [sk])
            TS('dve', st[f][:, 1:2], st[f][:, 1:2], 1e-5, None, ALU.add, None, [sk], [sk])
            ACT(st[f][:, 1:2], st[f][:, 1:2], AF.Sqrt, [sk], [sk])
            P.op('dve', lambda e: e.reciprocal(st[f][:, 2:3], st[f][:, 1:2]), [sk], [sk])
            STT('dve', st[f][:, 3:4], st[f][:, 0:1], -1.0, st[f][:, 2:3], ALU.mult, ALU.mult, [sk], [sk])
            ACT(hn[f][:], src, AF.Identity, [srck, sk], [hk_], scale=st[f][:, 2:3], bias=st[f][:, 3:4])
            TT('dve', hn[f][:], hn[f][:], gB_[:], ALU.mult, [hk_, 'gB'], [hk_])
            TT('dve', hn[f][:], hn[f][:], bB_[:], ALU.add, [hk_, 'bB'], [hk_])

        def p0_units(us, ub, engs):
            units = []
            for ec in range(128):
                for which in range(2):
                    def unit(ec=ec, which=which, k=len(units)):
                        b = k % len(us)
                        src = I['puT'][ec].rearrange('p a b -> p (a b)') if which == 0 else I['pv'][ec]
                        dst = S['puT'][ec].rearrange('p a b -> p (a b)') if which == 0 else S['pv'][ec]
                        DMA('sp', us[b][:], src, [], ['us%d' % b])
                        CP(engs[k % len(engs)], ub[b][:], us[b][:], ['us%d' % b], ['ub%d' % b])
                        DMA('pool', dst, ub[b][:], ['ub%d' % b], [])
                    units.append(unit)
            return units

        def phase_p0():
            with ExitStack() as ph:
                us = [sbt(ph, 'P_us%d' % i, [128, 2048], F32) for i in range(3)]
                ub = [sbt(ph, 'P_ub%d' % i, [128, 2048], BF16) for i in range(3)]
                for u in p0_units(us, ub, ['dve', 'act', 'pool']):
                    u()
                P.barrier()
                P.emit()

        def phase_peer(fin_evs):
            with ExitStack() as ph:
                hTt = sbt(ph, 'G_hTt', [128, 16, 512], BF16)
                wqs = [sbt(ph, 'G_wqs%d' % i, [128, 16, 64], F32) for i in range(2)]
                wqb = [sbt(ph, 'G_wqb%d' % i, [128, 16, 64], BF16) for i in range(2)]
                sks = sbt(ph, 'G_sks', [64, 2, 128], F32)
                skb = sbt(ph, 'G_skb', [64, 2, 128], BF16)
                qTu = [sbt(ph, 'G_qTu%d' % i, [64, 512], BF16) for i in range(2)]
                sAll = sbt(ph, 'G_sAll', [128, 4, 16, 128], F32)
                tau = sbt(ph, 'G_tau', [128, 4, 8], F32)
                negc = sbt(ph, 'G_negc', [128, 4, 8], F32)
                kap = sbt(ph, 'G_kap', [128, 4, 8], F32)
                m1 = [sbt(ph, 'G_m1%d' % i, [128, 16], F32) for i in range(4)]
                m2 = [sbt(ph, 'G_m2%d' % i, [128, 16], F32) for i in range(4)]
                mc = [sbt(ph, 'G_mc%d' % i, [128, 16], F32) for i in range(4)]
                t1 = [sbt(ph, 'G_t1%d' % i, [128, 256], F32) for i in range(4)]
                cand = [sbt(ph, 'G_cand%d' % i, [128, 256], F32) for i in range(4)]
                sm = [sbt(ph, 'G_sm%d' % i, [128, 4], F32) for i in range(4)]
                e16 = [sbt(ph, 'G_e16%d' % i, [128, 16], F32) for i in range(4)]
                eb = [sbt(ph, 'G_e%d' % i, [128, 4, 128], F32) for i in range(5)]
                Wall = [sbt(ph, 'G_W%d' % i, [128, 8, 4, 128], BF16) for i in range(2)]
                GT = [sbt(ph, 'G_GT%d' % i, [128, 4, 512], BF16) for i in range(3)]
                Gs = [sbt(ph, 'G_Gs%d' % i, [128, 512], BF16) for i in range(2)]
                uch = [sbt(ph, 'G_uch%d' % i, [128, 16, 128], BF16) for i in range(2)]
                vch = [sbt(ph, 'G_vch%d' % i, [128, 4, 2048], BF16) for i in range(2)]
                ga = [sbt(ph, 'G_ga%d' % i, [128, 512], F32) for i in range(4)]
                GA = [sbt(ph, 'G_GA%d' % i, [128, 4, 512], BF16) for i in range(2)]
                acc = sbt(ph, 'G_acc', [128, 4, 2048], F32)
                DMA('sp', sks[:], I['skT'], [], ['sks'])
                CP('dve', skb[:], sks[:], ['sks'], ['skb'])
                c_ = {'u': 0, 'k': 0, 'w': 0, 's': 0, 'v': 0, 'g': 0}
                for tile in range(4):
                    DMA('sp', hTt[:], S['hT'][tile], [], ['hTt'])
                    for u in range(16):
                        b = c_['u'] % 2
                        c_['u'] += 1
                        DMA('sp', wqs[b][:], I['wq'][u], [], ['wqs%d' % b])
                        CP('act', wqb[b][:], wqs[b][:], ['wqs%d' % b], ['wqb%d' % b])
                        pi = nextps()
                        for kc in range(16):
                            MM(ps[pi][0:64, :], wqb[b][:, kc, :], hTt[:, kc, :], kc == 0, kc == 15, ['wqb%d' % b, 'hTt'], ['ps%d' % pi])
                        CP('act', qTu[b][:], ps[pi][0:64, :], ['ps%d' % pi], ['qTu%d' % b])
                        pi = nextps()
                        for blk in range(4):
                            MM(ps[pi][:, blk * 128:(blk + 1) * 128], qTu[b][:, blk * 128:(blk + 1) * 128], skb[:, u % 2, :], True, True,
                               ['qTu%d' % b, 'skb'], ['ps%d' % pi])
                        CP('dve', sAll[:, :, u, :], ps[pi][:, :].rearrange('p (a b) -> p a b', a=4), ['ps%d' % pi], ['sAll'])
                    def chain(blk, h, f):
                        steps = []
                        s1 = sAll[:, blk, 2 * h, :]
                        s2 = sAll[:, blk, 2 * h + 1, :]
                        for (sx, mm_, mk, tk_, tt_) in ((s1, m1[f], 'm1_%d' % f, 't1a_%d' % f, t1[f][:, 0:128]),
                                                        (s2, m2[f], 'm2_%d' % f, 't1b_%d' % f, t1[f][:, 128:256])):
                            steps.append(lambda sx=sx, mm_=mm_, mk=mk: P.op('dve', lambda e: e.max(out=mm_[:, 0:8], in_=sx), ['sAll'], [mk]))
                            steps.append(lambda sx=sx, mm_=mm_, mk=mk, tk_=tk_, tt_=tt_: P.op(
                                'dve', lambda e: e.match_replace(out=tt_, in_to_replace=mm_[:, 0:8], in_values=sx, imm_value=-3.0e38),
                                ['sAll', mk], [tk_]))
                            steps.append(lambda mm_=mm_, mk=mk, tk_=tk_, tt_=tt_: P.op(
                                'dve', lambda e: e.max(out=mm_[:, 8:16], in_=tt_), [tk_], [mk]))
                        steps.append(lambda: TT('pool', cand[f][:].rearrange('p (a b) -> p a b', a=16),
                                                m1[f][:].unsqueeze(2).broadcast_to([128, 16, 16]), m2[f][:].unsqueeze(1).broadcast_to([128, 16, 16]),
                                                ALU.add, ['m1_%d' % f, 'm2_%d' % f], ['cand%d' % f]))
                        steps.append(lambda: P.op('dve', lambda e: e.max(out=mc[f][:, 0:8], in_=cand[f][:]), ['cand%d' % f], ['mc%d' % f]))
                        steps.append(lambda: P.op('dve', lambda e: e.match_replace(out=t1[f][:], in_to_replace=mc[f][:, 0:8], in_values=cand[f][:],
                                                                                  imm_value=-3.0e38),
                                                  ['cand%d' % f, 'mc%d' % f, 't1a_%d' % f, 't1b_%d' % f], ['t1a_%d' % f, 't1b_%d' % f]))
                        steps.append(lambda: P.op('dve', lambda e: e.max(out=mc[f][:, 8:16], in_=t1[f][:]), ['t1a_%d' % f, 't1b_%d' % f], ['mc%d' % f]))
                        steps.append(lambda: CP('dve', tau[:, blk, h:h + 1], mc[f][:, 15:16], ['mc%d' % f], ['tau']))
                        steps.append(lambda: TS('dve', sm[f][:, 0:1], mc[f][:, 0:1], -1.0, None, ALU.mult, None, ['mc%d' % f], ['sm%d' % f]))
                        steps.append(lambda: ACT(e16[f][:], mc[f][:], AF.Exp, ['mc%d' % f, 'sm%d' % f], ['e16_%d' % f], bias=sm[f][:, 0:1]))
                        steps.append(lambda: P.op('dve', lambda e: e.tensor_reduce(sm[f][:, 1:2], e16[f][:], AX.X, ALU.add), ['e16_%d' % f], ['sm%d' % f]))
                        steps.append(lambda: ACT(sm[f][:, 2:3], sm[f][:, 1:2], AF.Ln, ['sm%d' % f], ['sm%d' % f]))
                        steps.append(lambda: TT('dve', negc[:, blk, h:h + 1], sm[f][:, 0:1], sm[f][:, 2:3], ALU.subtract, ['sm%d' % f], ['negc%d' % f]))
                        steps.append(lambda: TT('dve', sm[f][:, 3:4], mc[f][:, 15:16], negc[:, blk, h:h + 1], ALU.add,
                                                ['mc%d' % f, 'negc%d' % f], ['sm%d' % f]))
                        steps.append(lambda: ACT(sm[f][:, 3:4], sm[f][:, 3:4], AF.Exp, ['sm%d' % f], ['sm%d' % f]))
                        steps.append(lambda: TS('dve', kap[:, blk, h:h + 1], sm[f][:, 3:4], 0.9999, None, ALU.mult, None, ['sm%d' % f], ['kap']))
                        steps.append(lambda: TS('dve', sAll[:, blk, 2 * h, :], sAll[:, blk, 2 * h, :], negc[:, blk, h:h + 1], None, ALU.add, None,
                                                ['negc%d' % f, 'm1_%d' % f, 't1a_%d' % f], ['sAllw%d' % f]))
                        return steps
                    pairs = [(blk, h) for blk in range(4) for h in range(8)]
                    for g4 in range(0, 32, 4):
                        chains = [chain(blk, h, f) for f, (blk, h) in enumerate(pairs[g4:g4 + 4])]
                        for i in range(len(chains[0])):
                            for ch in chains:
                                ch[i]()
                    P.op('dve', lambda e: e.tensor_copy(sm[0][:, 0:1], sm[0][:, 0:1]),
                         ['sAllw0', 'sAllw1', 'sAllw2', 'sAllw3', 'negc0', 'negc1', 'negc2', 'negc3', 'sm0'], ['sAll', 'negc', 'sm0'])
                    def opsA(eg, blk, h):
                        wb_ = (4 * eg + blk) % 2
                        sb_ = c_['s'] % 5
                        c_['s'] += 1
                        if h < NPOOL:
                            TT('pool', eb[sb_][:],
                               sAll[:, blk, 2 * h, 4 * eg:4 * eg + 4].unsqueeze(2).broadcast_to([128, 4, 128]),
                               sAll[:, blk, 2 * h + 1, :].unsqueeze(1).broadcast_to([128, 4, 128]),
                               ALU.add, ['sAll'], ['e%d' % sb_])
                            ACT(eb[sb_][:], eb[sb_][:], AF.Exp, ['e%d' % sb_], ['e%d' % sb_])
                        else:
                            for c in range(4):
                                ACT(eb[sb_][:, c, :], sAll[:, blk, 2 * h + 1, :], AF.Exp, ['sAll'], ['e%d' % sb_],
                                    bias=sAll[:, blk, 2 * h, 4 * eg + c:4 * eg + c + 1])
                        STT('dve', Wall[wb_][:, h, :, :], eb[sb_][:], kap[:, blk, h:h + 1], eb[sb_][:], ALU.is_ge, ALU.mult,
                            ['kap', 'e%d' % sb_], ['W%d' % wb_])

                    def stageB(eg, blk):
                        k = 4 * eg + blk
                        wb_ = k % 2
                        pb = k % 2
                        for c in range(4):
                            for h in range(8):
                                MM(ps[pb][:, c * 128:(c + 1) * 128], Wall[wb_][:, h, c, :], identb[:], h == 0, h == 7,
                                   ['W%d' % wb_, 'identb'], ['ps%d' % pb])
                        def evac(eg=eg, blk=blk, pb=pb):
                            CP('act', GT[eg % 3][:, :, blk * 128:(blk + 1) * 128], ps[pb][:, :].rearrange('p (a b) -> p a b', a=4),
                               ['ps%d' % pb], ['GT%d' % (eg % 3)])
                        pend.append(evac)

                    def stageC_pe(eg, c):
                        ec = 4 * eg + c
                        u2 = c % 2
                        DMA('sp', uch[u2][:], S['puT'][ec], [], ['uch%d' % u2])
                        pa = 2 + c
                        for kc in range(16):
                            MM(ps[pa][:, :], uch[u2][:, kc, :], hTt[:, kc, :], kc == 0, kc == 15, ['uch%d' % u2, 'hTt'], ['ps%d' % pa])

                    def stageC_post(eg):
                        gb = eg % 2
                        for c in range(4):
                            ACT(ga[c][:], ps[2 + c][:, :], AF.Gelu_apprx_tanh, ['ps%d' % (2 + c)], ['ga%d' % c])
                        for c in range(4):
                            TT('pool', GA[gb][:, c, :], ga[c][:], GT[eg % 3][:, c, :], ALU.mult, ['ga%d' % c, 'GT%d' % (eg % 3)], ['GA%d' % gb])

                    def stageD1(eg, blk, dc):
                        gb = eg % 2
                        vb = eg % 2
                        pv_ = 6 + (c_['v'] % 2)
                        c_['v'] += 1
                        for c in range(4):
                            MM(ps[pv_][:, :], GA[gb][:, c, blk * 128:(blk + 1) * 128], vch[vb][:, c, dc * 512:(dc + 1) * 512],
                               c == 0, c == 3, ['GA%d' % gb, 'vch%d' % vb], ['ps%d' % pv_])
                        if eg == 0:
                            CP('dve', acc[:, blk, dc * 512:(dc + 1) * 512], ps[pv_][:, :], ['ps%d' % pv_], ['acc%d' % blk])
                        else:
                            TT('dve', acc[:, blk, dc * 512:(dc + 1) * 512], acc[:, blk, dc * 512:(dc + 1) * 512], ps[pv_][:, :],
                               ALU.add, ['ps%d' % pv_, 'acc%d' % blk], ['acc%d' % blk])

                    pend = []
                    for it in range(35):
                        doA = it < 32
                        if 2 <= it <= 33:
                            eg_ = it - 2
                            DMA('sp', vch[eg_ % 2][:], S['pv'][4 * eg_:4 * eg_ + 4].rearrange('c p d -> p c d'), [], ['vch%d' % (eg_ % 2)])
                        for blk in range(4):
                            for h in range(8):
                                if doA:
                                    opsA(it, blk, h)
                                if h == 3 or not doA:
                                    while pend:
                                        pend.pop(0)()
                                if blk == 0 and h == 3 and 2 <= it <= 33:
                                    stageC_post(it - 2)
                                if h % 2 == 1 and 3 <= it <= 34:
                                    stageD1(it - 3, blk, h // 2)
                            if 1 <= it <= 32:
                                stageC_pe(it - 1, blk)
                            if doA:
                                stageB(it, blk)
                    DMA('pool', S['pe'][tile * 4:(tile + 1) * 4].rearrange('b p d -> p b d'), acc[:], ['acc0', 'acc1', 'acc2', 'acc3'], [])
                P.barrier()
                P.emit()


        def phase_g2(fin_evs):
            with ExitStack() as ph:
                hblk = [sbt(ph, 'H_hblk%d' % i, [128, 2048], F32) for i in range(2)]
                pblk = [sbt(ph, 'H_pblk%d' % i, [128, 2048], F32) for i in range(2)]
                junk = sbt(ph, 'H_junk', [128, 2048], F32)
                hn = [sbt(ph, 'H_hn%d' % i, [128, 2048], F32) for i in range(2)]
                gB = sbt(ph, 'H_gB', [128, 2048], F32)
                bB = sbt(ph, 'H_bB', [128, 2048], F32)
                st = [sbt(ph, 'H_st%d' % i, [128, 4], F32) for i in range(2)]
                DMA('sp', gB[:], I['ln2g'], [], ['gB'])
                DMA('sp', bB[:], I['ln2b'], [], ['bB'])
                for gblk in range(16):
                    f = gblk % 2
                    DMA('sp', hblk[f][:], S['h'][gblk], [], ['hblk%d' % f])
                    DMA('sp', pblk[f][:], S['pe'][gblk], [], ['pblk%d' % f])
                    STT('dve', pblk[f][:], hblk[f][:], ALPHA, pblk[f][:], ALU.mult, ALU.add, ['hblk%d' % f, 'pblk%d' % f], ['pblk%d' % f])
                    layer_norm_block(pblk[f][:], 'pblk%d' % f, gB, bB, junk, hn, st, gblk)
                    fin_evs.append(DMA('pool', out[gblk], hn[f][:], ['hn%d' % f], []))
                P.barrier()
                P.emit()

        if 'p0' in phases:
            phase_p0()
        if 'kv0' in phases:
            phase_kv(0)
        if 'kv1' in phases:
            phase_kv(1)
        if 'q' in phases:
            phase_q()
        if 'da' in phases:
            phase_da()
        if 'nsa' in phases:
            phase_nsa()
        if 'e1' in phases:
            phase_e1()
        fin_evs = []
        if dbg and dbg_src in ('aT', 'nT'):
            fin_evs.append(DMA('pool', dbg_out, (aT if dbg_src == 'aT' else nT)[:], ['aT', 'nT'], []))
            fin_evs.append(DMA('pool', dbg2_out, dbg2sb[:], ['dbg2sb'], []))
            P.barrier()
            P.emit()
        mid.close()
        if 'e2' in phases:
            phase_e2()
        if 'peer' in phases:
            phase_peer(fin_evs)
        if 'g2' in phases:
            phase_g2(fin_evs)
        for ev in fin_evs:
            pass
        P.ops['sp'].append((None, [ev for ev in fin_evs], None, 0))
        P.emit()
    return nc


def rel_bucket_np(dist):
    n = np.maximum(dist, 0)
    nf = np.maximum(n, 1).astype(np.float32)
    large = 16 + (np.log(nf / np.float32(16)) / np.float32(math.log(8.0)) * np.float32(16)).astype(np.int32)
    large = np.minimum(large, 31)
    return np.where(n < 16, n, large)


def prep_inputs(inputs):
    x = np.asarray(inputs['x'], np.float32)
    w_in = np.asarray(inputs['w_in'], np.float32)[0]
    rel = np.asarray(inputs['rel_bias'], np.float32)
    wr = np.ascontiguousarray(w_in.reshape(16, 128, 9776).transpose(1, 0, 2))
    common = {
        'w_dakv': np.ascontiguousarray(wr[:, :, 1024:3072]),
        'w_nkv': np.ascontiguousarray(wr[:, :, 4096:5632]),
        'w_q': np.ascontiguousarray(np.concatenate([wr[:, :, 0:1024], wr[:, :, 3072:4096]], axis=2)),
        'w_gate': np.ascontiguousarray(wr[:, :, 5632:5680]),
        'w_mg': np.ascontiguousarray(wr[:, :, 5680:9776]),
        'c_da': np.ascontiguousarray(np.broadcast_to(rel[31, 0:8][None, :], (128, 8))),
        'lamq': np.ascontiguousarray(np.broadcast_to(np.asarray(inputs['da_lam_q'], np.float32)[0].reshape(1, 128), (128, 128))),
        'lamk': np.ascontiguousarray(np.broadcast_to(np.asarray(inputs['da_lam_k'], np.float32)[0].reshape(1, 128), (128, 128))),
        'subg': np.ascontiguousarray(np.broadcast_to(np.asarray(inputs['da_subln_g'], np.float32)[0].reshape(1, 128), (128, 128))),
        'ident': np.eye(128, dtype=np.float32),
        'w_bda': np.ascontiguousarray(np.asarray(inputs['w_branch_da'], np.float32)[0].reshape(8, 128, 2048).transpose(1, 0, 2)),
        'w_bnsa': np.ascontiguousarray(np.asarray(inputs['w_branch_nsa'], np.float32)[0].reshape(8, 128, 2048).transpose(1, 0, 2)),
        'w_out': np.ascontiguousarray(np.asarray(inputs['w_out'], np.float32)[0].reshape(16, 128, 2048).transpose(1, 0, 2)),
        'ln1g': np.ascontiguousarray(np.broadcast_to(np.asarray(inputs['ln1_g'], np.float32)[0][None, :], (128, 2048))),
        'ln1b': np.ascontiguousarray(np.broadcast_to(np.asarray(inputs['ln1_b'], np.float32)[0][None, :], (128, 2048))),
        'ln2g': np.ascontiguousarray(np.broadcast_to(np.asarray(inputs['ln2_g'], np.float32)[0][None, :], (128, 2048))),
        'ln2b': np.ascontiguousarray(np.broadcast_to(np.asarray(inputs['ln2_b'], np.float32)[0][None, :], (128, 2048))),
        'wq': np.ascontiguousarray(np.asarray(inputs['peer_wq'], np.float32)[0].reshape(16, 128, 16, 64).transpose(2, 1, 0, 3)),
        'skT': np.ascontiguousarray(np.stack([np.asarray(inputs['peer_subkey1'], np.float32)[0].T,
                                              np.asarray(inputs['peer_subkey2'], np.float32)[0].T], axis=1)),
        'puT': np.ascontiguousarray(np.asarray(inputs['peer_u'], np.float32)[0].reshape(128, 128, 16, 128).transpose(0, 3, 2, 1)),
        'pv': np.ascontiguousarray(np.asarray(inputs['peer_v'], np.float32)[0].reshape(128, 128, 2048)),
        'c_nsa': np.ascontiguousarray(np.broadcast_to(rel[31, 8:24][None, :], (128, 16))),
        'w1k': np.ascontiguousarray(np.asarray(inputs['cmp_w1_k'], np.float32)[0].reshape(16, 2, 64, 256).transpose(1, 2, 0, 3).reshape(128, 16, 256)),
        'w1v': np.ascontiguousarray(np.asarray(inputs['cmp_w1_v'], np.float32)[0].reshape(16, 2, 64, 256).transpose(1, 2, 0, 3).reshape(128, 16, 256)),
        'w2k': np.ascontiguousarray(np.asarray(inputs['cmp_w2_k'], np.float32)[0].reshape(2, 128, 64).transpose(1, 0, 2)),
        'w2v': np.ascontiguousarray(np.asarray(inputs['cmp_w2_v'], np.float32)[0].reshape(2, 128, 64).transpose(1, 0, 2)),
        'pekT': np.ascontiguousarray(np.asarray(inputs['cmp_pe_k'], np.float32)[0].reshape(16, 2, 64).transpose(1, 2, 0).reshape(128, 16)),
        'pevT': np.ascontiguousarray(np.asarray(inputs['cmp_pe_v'], np.float32)[0].reshape(16, 2, 64).transpose(1, 2, 0).reshape(128, 16)),
    }
    import ml_dtypes
    cidx = np.arange(512)
    sidx = np.arange(128)
    ov = ((cidx[:, None] * 16 <= sidx[None, :] * 64 + 63) & (cidx[:, None] * 16 + 31 >= sidx[None, :] * 64)).astype(np.float32)
    ovl = np.concatenate([ov, np.ones((512, 1), np.float32)], axis=1)
    ovl[511] = 0.0
    common['ovl'] = np.ascontiguousarray(ovl.reshape(4, 128, 129).transpose(1, 0, 2))
    kk = np.arange(8192)
    common['onehot'] = (((kk[None, :] // 64) % 64) == np.arange(64)[:, None]).astype(ml_dtypes.bfloat16)
    xTs = []
    for b in range(2):
        xTs.append(np.ascontiguousarray(x[b].reshape(16, 512, 16, 128).transpose(0, 3, 2, 1)))
    in_maps = []
    kl = np.arange(128)[:, None]
    xx = np.arange(2944)[None, :]
    for c in range(8):
        b, j = c // 4, c % 4
        tiles = [4 * t + j for t in range(4)]
        m = dict(common)
        m['xT'] = xTs[b]
        m['xTo'] = np.ascontiguousarray(xTs[b][tiles])
        m['xo'] = np.ascontiguousarray(
            np.concatenate([x[b, 512 * T:512 * (T + 1)] for T in tiles], axis=0).reshape(16, 128, 2048))
        d = xx - kl + 512 * j - 1920
        bk = rel_bucket_np(d)
        rb = rel[bk]
        m['raw_da'] = np.ascontiguousarray(rb[:, 0:2560, 0:8].transpose(2, 0, 1))
        m['raw_nsa'] = np.ascontiguousarray(rb[:, :, 8:24].transpose(2, 0, 1))
        m['mneg'] = np.where(d < 0, np.float32(NEGM), np.float32(0.0)).astype(np.float32)
        m['wneg'] = np.where((d < 0) | (d >= 512), np.float32(NEGM), np.float32(0.0)).astype(np.float32)
        cl = np.arange(128)[:, None, None]
        dl = np.arange(2)[None, :, None] - 1
        ql = np.arange(512)[None, None, :]
        m['cm'] = np.where(16 * cl + 31 + 2048 * dl <= 512 * j + ql, np.float32(0.0), np.float32(NEGM)).astype(np.float32)
        qpos = (512 * np.array(tiles)[:, None, None] + 128 * np.arange(4)[None, :, None] + np.arange(128)[None, None, :]).reshape(16, 128)
        cur = qpos // 64
        sb_ = np.arange(128)[None, None, :]
        valid = sb_ <= cur[:, :, None]
        forced = valid & ((sb_ == 0) | (sb_ > cur[:, :, None] - 2))
        vmul = (valid & ~forced).astype(np.float32)
        vadd = np.where(forced, np.float32(1e4) + sb_.astype(np.float32), np.where(valid, np.float32(0.0), np.float32(-1e30))).astype(np.float32)
        m['vmul'] = np.ascontiguousarray(vmul.transpose(1, 0, 2))
        m['vadd'] = np.ascontiguousarray(vadd.transpose(1, 0, 2))
        in_maps.append(m)
    return in_maps


_NC_CACHE = {}


def kernel(**inputs):
    in_maps = prep_inputs(inputs)
    if 'nc' not in _NC_CACHE:
        _NC_CACHE['nc'] = build_program()
    nc = _NC_CACHE['nc']
    res = run_bass_kernel_spmd(nc, in_maps, core_ids=list(range(8)))
    outp = np.zeros((2, 8192, 2048), np.float32)
    for c in range(8):
        b, j = c // 4, c % 4
        o = np.asarray(res.results[c]['out']).reshape(4, 512, 2048)
        for t in range(4):
            T = 4 * t + j
            outp[b, 512 * T:512 * (T + 1)] = o[t]
    return outp
```
